# Optimizing a Trainium2 kernel written in Bass

```python
import math
import jax, jax.numpy as jnp
from jax import lax
import numpy as np

D_MODEL = 1024
BATCH = 4
SEQ = 4096
DEPTH = 1

HEAD_DIM = 64
ATT_HEADS = 8
ATT_WIDTH = ATT_HEADS * HEAD_DIM
RWKV_HEADS = 8
RWKV_WIDTH = RWKV_HEADS * HEAD_DIM
MIX_WIDTH = ATT_WIDTH + RWKV_WIDTH
D_FF = 2816
N_DIRS = 2
DECAY_LORA = 64
ICLR_LORA = 64
GATE_LORA = 128
DILATED_PATTERNS = ((128, 1), (512, 4), (2048, 16))
ROPE_THETA = 10000.0
NORM_EPS = 1e-6
GN_EPS = 64e-5
ATT_IN = 3 * ATT_WIDTH
RWKV_IN = 3 * RWKV_WIDTH + N_DIRS * DECAY_LORA + N_DIRS * ICLR_LORA + GATE_LORA
IN_WIDTH = ATT_IN + RWKV_IN

kernel_name = "hymba_rwkv7_dilated_macaron_block"


def _rmsnorm(x, g):
    xf = x.astype(jnp.float32)
    return xf * lax.rsqrt(jnp.mean(xf * xf, axis=-1, keepdims=True) + NORM_EPS) * g.astype(jnp.float32)


def _swiglu(h, w_gate, w_up, w_down):
    return (jax.nn.silu(h @ w_gate) * (h @ w_up)) @ w_down


def _rope(t, cos, sin):
    half = t.shape[-1] // 2
    t1, t2 = t[..., :half], t[..., half:]
    c, s = cos[None, :, None, :], sin[None, :, None, :]
    return jnp.concatenate([t1 * c - t2 * s, t2 * c + t1 * s], axis=-1)


def _dilated_attention(q, k, v, window, dilation):
    B, S, H, Dh = q.shape
    L = S // dilation
    W = window // (2 * dilation)
    nb = -(-L // W)
    Lp = nb * W
    pad = Lp - L

    def to_sub(t):
        return t.reshape(B, L, dilation, H, Dh).transpose(0, 3, 2, 1, 4)

    qb = jnp.pad(to_sub(q), ((0, 0), (0, 0), (0, 0), (0, pad), (0, 0))).reshape(B, H, dilation, nb, W, Dh)

    def band(t):
        tp = jnp.pad(to_sub(t), ((0, 0), (0, 0), (0, 0), (W, W + pad), (0, 0))).reshape(B, H, dilation, nb + 2, W, Dh)
        return jnp.concatenate([tp[:, :, :, :-2], tp[:, :, :, 1:-1], tp[:, :, :, 2:]], axis=4)

    kb, vb = band(k), band(v)
    blk = jnp.arange(nb)[:, None, None] * W
    qi = blk + jnp.arange(W)[None, :, None]
    ki = blk - W + jnp.arange(3 * W)[None, None, :]
    valid = (jnp.abs(ki - qi) <= W) & (ki >= 0) & (ki < L)

    s = jnp.einsum('bhrnqd,bhrnkd->bhrnqk', qb, kb).astype(jnp.float32)
    s = jnp.where(valid, s, -jnp.inf)
    m = jnp.max(s, axis=-1, keepdims=True)
    p = jnp.exp(s - m)
    den = jnp.sum(p, axis=-1, keepdims=True)
    o = jnp.einsum('bhrnqk,bhrnkd->bhrnqd', p, vb.astype(jnp.float32)) / den
    lse = (m + jnp.log(den))[..., 0]
    o = o.reshape(B, H, dilation, Lp, Dh)[:, :, :, :L].transpose(0, 3, 2, 1, 4).reshape(B, S, H, Dh)
    lse = lse.reshape(B, H, dilation, Lp)[:, :, :, :L].transpose(0, 3, 2, 1).reshape(B, S, H)
    return o, lse


def _attention_group(za, cos, sin, out_g):
    B, S, _ = za.shape
    q, k, v = jnp.split(za, 3, axis=-1)
    q = _rope(q.reshape(B, S, ATT_HEADS, HEAD_DIM), cos, sin) * (HEAD_DIM ** -0.5)
    k = _rope(k.reshape(B, S, ATT_HEADS, HEAD_DIM), cos, sin)
    v = v.reshape(B, S, ATT_HEADS, HEAD_DIM)
    outs, lses = [], []
    for window, dilation in DILATED_PATTERNS:
        o, l = _dilated_attention(q, k, v, window, dilation)
        outs.append(o)
        lses.append(l)
    wts = jax.nn.softmax(jnp.stack(lses, axis=0), axis=0)
    o = jnp.sum(wts[..., None] * jnp.stack(outs, axis=0), axis=0)
    o = o * lax.rsqrt(jnp.mean(o * o, axis=-1, keepdims=True) + NORM_EPS)
    return o.reshape(B, S, ATT_WIDTH) * out_g


def _rwkv7_step(state, inp):
    r, w, k, v, kk, kka = inp
    sa = jnp.einsum('dbhvk,dbhk->dbhv', state, kk)
    state = state * w[..., None, :] - sa[..., :, None] * kka[..., None, :] + v[..., :, None] * k[..., None, :]
    y = jnp.einsum('dbhvk,dbhk->dbhv', state, r)
    return state, y


def _rwkv7_group(u, mu_prev, mu_next, w0, w2, a0, a2, g2, k_k, k_a, r_k, lnx_w, lnx_b):
    B, S, _ = u.shape
    H, N, C = RWKV_HEADS, HEAD_DIM, RWKV_WIDTH
    u = u.astype(jnp.float32)
    prev = jnp.pad(u, ((0, 0), (1, 0), (0, 0)))[:, :-1]
    nxt = jnp.pad(u, ((0, 0), (0, 1), (0, 0)))[:, 1:]
    u = u + mu_prev * (prev - u) + mu_next * (nxt - u)
    cuts = np.cumsum([C, C, C, N_DIRS * DECAY_LORA, N_DIRS * ICLR_LORA]).tolist()
    r, k, v, wd, ad, gd = jnp.split(u, cuts, axis=-1)
    wd = wd.reshape(B, S, N_DIRS, DECAY_LORA)
    ad = ad.reshape(B, S, N_DIRS, ICLR_LORA)
    w_logit = w0 + jnp.einsum('bsdr,drc->bsdc', jnp.tanh(wd), w2)
    decay = jnp.exp(-jnp.exp(-jax.nn.softplus(-w_logit) - 0.5))
    a = jax.nn.sigmoid(a0 + jnp.einsum('bsdr,drc->bsdc', ad, a2))
    g = jax.nn.sigmoid(gd) @ g2
    kk = (k * k_k).reshape(B, S, H, N)
    kk = kk / jnp.maximum(jnp.sqrt(jnp.sum(kk * kk, axis=-1, keepdims=True)), 1e-12)
    kk = kk.reshape(B, S, C)
    k_dir = k[:, :, None, :] * (1.0 + (a - 1.0) * k_a)

    def both(t):
        return jnp.broadcast_to(t[:, :, None, :], (B, S, N_DIRS, C))

    def orient(t):
        t = jnp.concatenate([t[:, :, :1], jnp.flip(t[:, :, 1:], axis=1)], axis=2)
        return t.reshape(B, S, N_DIRS, H, N).transpose(1, 2, 0, 3, 4)

    xs = (orient(both(r)), orient(decay), orient(k_dir), orient(both(v)),
          orient(both(kk)), orient(kk[:, :, None, :] * a))
    state0 = jnp.zeros((N_DIRS, B, H, N, N), jnp.float32)
    _, ys = lax.scan(_rwkv7_step, state0, xs)
    ys = ys.transpose(2, 0, 1, 3, 4)
    y = ys[:, :, 0] + jnp.flip(ys[:, :, 1], axis=1)
    mu = jnp.mean(y, axis=-1, keepdims=True)
    var = jnp.mean(jnp.square(y - mu), axis=-1, keepdims=True)
    y = ((y - mu) * lax.rsqrt(var + GN_EPS)).reshape(B, S, C) * lnx_w + lnx_b
    bonus = jnp.einsum('bshn,bsdhn,hn->bsh', r.reshape(B, S, H, N),
                       k_dir.reshape(B, S, N_DIRS, H, N), r_k) / N_DIRS
    y = y + (bonus[..., None] * v.reshape(B, S, H, N)).reshape(B, S, C)
    return y * g


def setup_inputs(seed: int = 0) -> dict:
    key = jax.random.key(seed)
    keys = iter(jax.random.split(key, 40))

    def rnd(shape, scale):
        return scale * jax.random.normal(next(keys), (DEPTH,) + shape, jnp.float32)

    def gain(shape):
        return 1.0 + rnd(shape, 0.05)

    def unif(shape, lo, hi):
        return jax.random.uniform(next(keys), (DEPTH,) + shape, jnp.float32, lo, hi)

    x = jax.random.normal(next(keys), (BATCH, SEQ, D_MODEL), jnp.float32)
    return {
        "x": x,
        "ffn1_pre_g": gain((D_MODEL,)),
        "ffn1_w_gate": rnd((D_MODEL, D_FF), D_MODEL ** -0.5),
        "ffn1_w_up": rnd((D_MODEL, D_FF), D_MODEL ** -0.5),
        "ffn1_w_down": rnd((D_FF, D_MODEL), D_FF ** -0.5),
        "ffn1_post_g": gain((D_MODEL,)),
        "mix_pre_g": gain((D_MODEL,)),
        "w_in": rnd((D_MODEL, IN_WIDTH), D_MODEL ** -0.5),
        "rwkv_mu_prev": unif((RWKV_IN,), 0.0, 0.5),
        "rwkv_mu_next": unif((RWKV_IN,), 0.0, 0.5),
        "rwkv_w0": unif((N_DIRS, RWKV_WIDTH), -4.0, 1.0),
        "rwkv_w2": rnd((N_DIRS, DECAY_LORA, RWKV_WIDTH), 0.5 * DECAY_LORA ** -0.5),
        "rwkv_a0": rnd((N_DIRS, RWKV_WIDTH), 0.3),
        "rwkv_a2": rnd((N_DIRS, ICLR_LORA, RWKV_WIDTH), 0.5 * ICLR_LORA ** -0.5),
        "rwkv_g2": rnd((GATE_LORA, RWKV_WIDTH), GATE_LORA ** -0.5),
        "rwkv_k_k": 0.85 + rnd((RWKV_WIDTH,), 0.05),
        "rwkv_k_a": gain((RWKV_WIDTH,)),
        "rwkv_r_k": rnd((RWKV_HEADS, HEAD_DIM), 0.1),
        "rwkv_lnx_w": gain((RWKV_WIDTH,)),
        "rwkv_lnx_b": rnd((RWKV_WIDTH,), 0.02),
        "attn_out_g": gain((ATT_WIDTH,)),
        "w_out": rnd((MIX_WIDTH, D_MODEL), MIX_WIDTH ** -0.5),
        "mix_post_g": gain((D_MODEL,)),
        "ffn2_pre_g": gain((D_MODEL,)),
        "ffn2_w_gate": rnd((D_MODEL, D_FF), D_MODEL ** -0.5),
        "ffn2_w_up": rnd((D_MODEL, D_FF), D_MODEL ** -0.5),
        "ffn2_w_down": rnd((D_FF, D_MODEL), D_FF ** -0.5),
        "ffn2_post_g": gain((D_MODEL,)),
    }


def reference(x, ffn1_pre_g, ffn1_w_gate, ffn1_w_up, ffn1_w_down, ffn1_post_g,
              mix_pre_g, w_in,
              rwkv_mu_prev, rwkv_mu_next, rwkv_w0, rwkv_w2, rwkv_a0, rwkv_a2, rwkv_g2,
              rwkv_k_k, rwkv_k_a, rwkv_r_k, rwkv_lnx_w, rwkv_lnx_b,
              attn_out_g, w_out, mix_post_g,
              ffn2_pre_g, ffn2_w_gate, ffn2_w_up, ffn2_w_down, ffn2_post_g):
    S = x.shape[1]
    pos = jnp.arange(S, dtype=jnp.float32)
    inv_freq = ROPE_THETA ** (-jnp.arange(0, HEAD_DIM, 2, dtype=jnp.float32) / HEAD_DIM)
    ang = pos[:, None] * inv_freq[None, :]
    cos, sin = jnp.cos(ang), jnp.sin(ang)

    h = x.astype(jnp.float32)
    for l in range(DEPTH):
        f = _swiglu(_rmsnorm(h, ffn1_pre_g[l]), ffn1_w_gate[l], ffn1_w_up[l], ffn1_w_down[l])
        h = h + 0.5 * _rmsnorm(f, ffn1_post_g[l])
        z = _rmsnorm(h, mix_pre_g[l]) @ w_in[l]
        y_att = _attention_group(z[..., :ATT_IN], cos, sin, attn_out_g[l])
        y_rwkv = _rwkv7_group(z[..., ATT_IN:], rwkv_mu_prev[l], rwkv_mu_next[l], rwkv_w0[l], rwkv_w2[l],
                              rwkv_a0[l], rwkv_a2[l], rwkv_g2[l], rwkv_k_k[l], rwkv_k_a[l], rwkv_r_k[l],
                              rwkv_lnx_w[l], rwkv_lnx_b[l])
        mix = jnp.concatenate([y_att, y_rwkv], axis=-1) @ w_out[l]
        h = h + _rmsnorm(mix, mix_post_g[l])
        f = _swiglu(_rmsnorm(h, ffn2_pre_g[l]), ffn2_w_gate[l], ffn2_w_up[l], ffn2_w_down[l])
        h = h + 0.5 * _rmsnorm(f, ffn2_post_g[l])
    return h.astype(x.dtype)
```

```python
import contextlib
import numpy as np
import ml_dtypes
import concourse.bass as bass
import concourse.mybir as mybir
from concourse.bass_utils import run_bass_kernel_spmd

F32 = mybir.dt.float32
BF16 = mybir.dt.bfloat16
AF = mybir.ActivationFunctionType
ALU = mybir.AluOpType
AX = mybir.AxisListType

D = 1024
DFF = 2816
NF = DFF // 128
S = 4096
OWN = 2048
TT = 512
EPS = 1e-6
GN_EPS = 64e-5
CDEC = float(np.exp(-0.5))
NKT = 24


class Prog:
    NDMA = 16

    def __init__(self):
        self.ops = []
        self.last_w = {}
        self.readers = {}
        self.last_barrier = 0
        self.warn = []
        self.pe_hist = []

    def add(self, eng, fn, r=(), w=(), dma=False):
        i = len(self.ops)
        deps = set()
        for k in r:
            j = self.last_w.get(k)
            if j is not None:
                deps.add((j, 'raw'))
        for k in w:
            j = self.last_w.get(k)
            if j is not None:
                deps.add((j, 'waw'))
            for j in self.readers.get(k, ()):
                deps.add((j, 'war'))
        self.ops.append(dict(eng=eng, fn=fn, deps=deps, dma=dma))
        for k in w:
            if isinstance(k, str) and k.startswith('ps'):
                engs = set(self.ops[j]['eng'] for j in self.readers.get(k, ()))
                if len(engs) > 1:
                    self.warn.append(('multi-engine psum readers', k, sorted(engs), i))
        for k in r:
            self.readers.setdefault(k, []).append(i)
        for k in w:
            self.last_w[k] = i
            self.readers[k] = []
        return i

    def barrier(self):
        n = len(self.ops)
        deps = set()
        last = {}
        for i in range(n):
            op = self.ops[i]
            if op['dma']:
                if i >= self.last_barrier:
                    deps.add((i, 'raw'))
            elif op['fn'] is not None:
                last[op['eng']] = i
        for e, i in last.items():
            deps.add((i, 'raw'))
        for e in ['pe', 'act', 'dve', 'pool', 'sp']:
            self.ops.append(dict(eng=e, fn=None, deps=set(deps), dma=False))
        self.last_barrier = n

    def emit(self, nc, stack):
        ops = self.ops
        n = len(ops)
        need = [set() for _ in range(n)]
        signaled = [False] * n
        for i, op in enumerate(ops):
            for (j, kind) in op['deps']:
                if j == i:
                    continue
                pj = ops[j]
                same = (pj['eng'] == op['eng'])
                if same and op['eng'] == 'pe' and not pj['dma'] and not op['dma'] and op['fn'] is not None:
                    if kind != 'raw':
                        continue
                need[i].add(j)
            latest = {}
            for j in need[i]:
                pj = ops[j]
                if pj['dma']:
                    continue
                e2 = pj['eng']
                if e2 not in latest or j > latest[e2]:
                    latest[e2] = j
            need[i] = set(j for j in need[i] if ops[j]['dma'] or latest[ops[j]['eng']] == j)
            for j in need[i]:
                signaled[j] = True
        engs = ['pe', 'act', 'dve', 'pool', 'sp']
        csem = {e: stack.enter_context(nc.semaphore("s_" + e)) for e in engs[:4]}
        dsem = {e: [stack.enter_context(nc.semaphore("d_%s%d" % (e, k))) for k in range(self.NDMA)]
                for e in ['sp', 'pool']}
        sig = [None] * n
        ccount = {e: 0 for e in engs}
        dcount = {e: 0 for e in engs}
        prevuse = [None] * n
        for i, op in enumerate(ops):
            e = op['eng']
            if op['dma']:
                k = dcount[e]
                dcount[e] += 1
                s = dsem[e][k % self.NDMA]
                sig[i] = (s, 16 * (k // self.NDMA + 1))
                if k >= self.NDMA:
                    prevuse[i] = (s, 16 * (k // self.NDMA))
            elif signaled[i]:
                ccount[e] += 1
                sig[i] = (csem[e], ccount[e])
        per = {e: [i for i in range(n) if ops[i]['eng'] == e] for e in engs}
        self.stats = {e: len(per[e]) for e in engs}
        self.stats['sem'] = dict(ccount)

        def run(e, eng):
            waited = {}
            for i in per[e]:
                op = ops[i]
                ws = []
                if prevuse[i] is not None:
                    ws.append(prevuse[i])
                for j in need[i]:
                    ws.append(sig[j])
                best = {}
                for (s, v) in ws:
                    key = id(s)
                    if v > best.get(key, (None, -1))[1]:
                        best[key] = (s, v)
                for key, (s, v) in best.items():
                    if waited.get(key, -1) >= v:
                        continue
                    eng.wait_ge(s, v)
                    waited[key] = v
                if op['fn'] is None:
                    continue
                ins = op['fn'](eng)
                if sig[i] is not None:
                    ins.then_inc(sig[i][0], 16 if op['dma'] else 1)

        with nc.Block() as block:
            @block.tensor
            def _(eng):
                run('pe', eng)

            @block.scalar
            def _(eng):
                run('act', eng)

            @block.vector
            def _(eng):
                run('dve', eng)

            @block.gpsimd
            def _(eng):
                run('pool', eng)

            @block.sync
            def _(eng):
                run('sp', eng)


class Arena:
    def __init__(self, tensor, nbytes):
        self.t = tensor
        self.n = nbytes
        self.off = 0

    def alloc(self, shape, dt=F32):
        esz = mybir.dt.size(dt)
        nel = int(np.prod(shape[1:]))
        nb = (nel * esz + 31) // 32 * 32
        assert self.off + nb <= self.n, ("arena overflow", self.off, nb, self.n)
        ap = self.t[:, self.off // 4:(self.off + nb) // 4]
        self.off += nb
        if dt != F32:
            ap = ap.bitcast(dt)
        ap = ap[:, 0:nel]
        if len(shape) == 3:
            ap = ap.rearrange("p (a b) -> p a b", a=shape[1])
        elif len(shape) == 4:
            ap = ap.rearrange("p (a b c) -> p a b c", a=shape[1], b=shape[2])
        return ap


class Env:
    pass


def build(debug=False, phases=('p1', 'att', 'rwkv', 'p4'), ntiles1=8, rwkv_stop=None):
    nc = bass.Bass("TRN2", target_bir_lowering=False)
    P = Prog()
    stack = contextlib.ExitStack()
    E = Env()

    def din(name, shape, dt=F32):
        return nc.dram_tensor(name, list(shape), dt, kind="ExternalInput").ap()

    def dscr(name, shape, dt=F32, out=False):
        if out:
            return nc.dram_tensor(name, list(shape), dt, kind="ExternalOutput").ap()
        return nc.dram_tensor(name, list(shape), dt).ap()

    x_d = din("x", [S, D])
    wgu_d = [din("wgu%d" % k, [NF, 128, 2, 8, 128]) for k in (1, 2)]
    wdc_d = [din("wdc%d" % k, [8, 128, NF, 128]) for k in (1, 2)]
    win_d = din("win", [31, 128, 8, 128])
    wv_d = din("wv", [128, 8, 512])
    wo_d = din("wo", [8, 128, 8, 128])
    gv_d = din("gv", [128, 6, 8])
    ident_d = din("ident", [128, 128])
    cos_d = din("cosT", [128, S])
    sin_d = din("sinT", [128, S])
    wm_d = din("wm", [20, 128, 512], BF16)
    rep_d = din("rep", [128, 3, 512])
    rwm_d = din("rwmask", [128, 2, 5, 128])
    rwc_d = din("rwconst", [128, 1218])
    rwp_d = din("rwpar", [128, 64])
    lora_d = din("lora", [128, 3, 512])
    out_d = nc.dram_tensor("out", [OWN, D], F32, kind="ExternalOutput").ap()
    h1s_d = dscr("h1s", [8, 128, OWN], out=debug)
    zr_d = dscr("zr", [15, 128, S], out=debug)
    qT_d = dscr("qTs", [4, 128, OWN], BF16, out=debug)
    kT_d = dscr("kTs", [4, 128, NKT * 128], BF16, out=debug)
    v_d = dscr("vs", [NKT * 128, 520], BF16, out=debug)
    yc_d = dscr("yc", [OWN, D], out=debug)
    yf_d = dscr("yf", [OWN, 512], out=debug)

    ARENA_BYTES = 204 * 1024
    arena_t = stack.enter_context(nc.sbuf_tensor("arena", [128, ARENA_BYTES // 4], F32))
    A = Arena(arena_t, ARENA_BYTES)
    ps = [stack.enter_context(nc.psum_tensor("ps%d" % i, [128, 512], F32)) for i in range(8)]

    def MM(out, lhsT, rhs, start, stop, r, w):
        rb0, kk_ = lhsT.base_partition(), lhsT.shape[0]
        for (pb0, pk, pw) in P.pe_hist[-1:]:
            if (rb0 + kk_ <= pb0 or pb0 + pk <= rb0) and pw == w[0]:
                P.warn.append(('row-group conflict', w[0], (pb0, pk), (rb0, kk_), len(P.ops)))
        P.pe_hist.append((rb0, kk_, w[0]))
        P.add('pe', lambda e: e.matmul(out, lhsT, rhs, start=start, stop=stop), r=r, w=w)

    def TR(out, in_, idt, r, w):
        P.pe_hist.append((in_.base_partition(), in_.shape[0], w[0]))
        P.add('pe', lambda e: e.transpose(out, in_, idt), r=r, w=w)

    def ACT(out, in_, func, r, w, bias=None, scale=None):
        kw = {}
        if bias is not None:
            kw['bias'] = bias
        if scale is not None:
            kw['scale'] = scale
        P.add('act', lambda e: e.activation(out, in_, func, **kw), r=r, w=w)

    def TTo(eng, out, in0, in1, op, r, w):
        P.add(eng, lambda e: e.tensor_tensor(out=out, in0=in0, in1=in1, op=op), r=r, w=w)

    def TS(eng, out, in0, s1, s2, op0, op1, r, w):
        if op1 is None:
            P.add(eng, lambda e: e.tensor_scalar(out=out, in0=in0, scalar1=s1, scalar2=None, op0=op0), r=r, w=w)
        else:
            P.add(eng, lambda e: e.tensor_scalar(out=out, in0=in0, scalar1=s1, scalar2=s2, op0=op0, op1=op1),
                  r=r, w=w)

    def STT(out, in0, scalar, in1, op0, op1, r, w):
        P.add('dve', lambda e: e.scalar_tensor_tensor(out=out, in0=in0, scalar=scalar, in1=in1, op0=op0, op1=op1),
              r=r, w=w)

    def CP(eng, out, in_, r, w):
        if eng == 'act':
            P.add('act', lambda e: e.copy(out, in_), r=r, w=w)
        else:
            P.add(eng, lambda e: e.tensor_copy(out, in_), r=r, w=w)

    def RECIP(out, in_, r, w):
        P.add('dve', lambda e: e.reciprocal(out, in_), r=r, w=w)

    def DMA(q, out, in_, r, w):
        return P.add(q, lambda e: e.dma_start(out=out, in_=in_), r=r, w=w, dma=True)

    def MEMSET(eng, ap, val, w):
        P.add(eng, lambda e: e.memset(ap, val), w=w)

    ident = A.alloc([128, 128])
    identb = A.alloc([128, 128], BF16)
    ones = A.alloc([128, 128], BF16)
    gv = A.alloc([128, 6, 8])
    eps_t = A.alloc([128, 1])
    gneps_t = A.alloc([128, 1])
    rep = A.alloc([128, 3, 512])
    rwm = A.alloc([128, 2, 5, 128])
    rwc = A.alloc([128, 1218])
    rwp = A.alloc([128, 64])
    lora = A.alloc([128, 3, 512])
    DMA('sp', ident, ident_d, [], ['ident'])
    DMA('sp', gv, gv_d, [], ['gv'])
    DMA('sp', rep, rep_d, [], ['rep'])
    DMA('sp', rwm, rwm_d, [], ['rwm'])
    DMA('sp', rwc, rwc_d, [], ['rwc'])
    DMA('sp', rwp, rwp_d, [], ['rwp'])
    DMA('sp', lora, lora_d, [], ['lora'])
    MEMSET('pool', ones, 1.0, ['ones'])
    MEMSET('pool', eps_t, EPS, ['eps'])
    MEMSET('pool', gneps_t, GN_EPS, ['eps'])
    CP('dve', identb, ident, ['ident'], ['identb'])
    A_MARK = A.off

    cnt = {'ps01': 0, 'x': 0, 'z': 0, 'st': 0}
    for k_, v_ in list(locals().items()):
        setattr(E, k_, v_)

    def alloc_ffn_bufs():
        B = {}
        B['xin'] = [A.alloc([128, D]) for _ in range(2)]
        B['xT'] = A.alloc([128, 8, TT])
        B['hn'] = A.alloc([128, 8, TT], BF16)
        B['sq'] = A.alloc([128, 8, TT], BF16)
        B['fb'] = A.alloc([128, 8, TT])
        B['aT'] = A.alloc([128, NF, TT], BF16)
        B['rstd'] = A.alloc([128, TT])
        B['tmp'] = A.alloc([128, TT])
        B['wgu'] = [A.alloc([128, 2, 8, 128], BF16) for _ in range(3)]
        B['wdc'] = [A.alloc([128, NF, 128], BF16) for _ in range(2)]
        B['sgl'] = [A.alloc([128, TT]) for _ in range(2)]
        return B

    def load_T(B, src_d, tt, dst, dstkey):
        xin = B['xin']
        for sub in range(4):
            b = cnt['x'] % 2
            cnt['x'] += 1
            t0 = tt * TT + sub * 128
            DMA('sp', xin[b], src_d[t0:t0 + 128, :], ['yc'] if src_d is yc_d else [], ['xin%d' % b])
            for half in range(2):
                pb = cnt['ps01'] % 2
                cnt['ps01'] += 1
                for q in range(4):
                    c = half * 4 + q
                    TR(ps[pb][:, q * 128:(q + 1) * 128], xin[b][:, c * 128:(c + 1) * 128], ident,
                       ['xin%d' % b, 'ident'], ['ps%d' % pb])
                src = ps[pb][:].rearrange("p (q t) -> p q t", q=4)
                d_ = dst[:, half * 4:(half + 1) * 4, sub * 128:(sub + 1) * 128]
                CP('act' if half == 0 else 'dve', d_, src, ['ps%d' % pb], [dstkey])

    def rmsnorm_stats(B, src, srckey):
        sq, tmp, rstd = B['sq'], B['tmp'], B['rstd']
        ACT(sq, src, AF.Square, [srckey], ['sq'])
        pb = cnt['ps01'] % 2
        cnt['ps01'] += 1
        for c in range(8):
            MM(ps[pb][:], ones, sq[:, c, :], c == 0, c == 7, ['sq', 'ones'], ['ps%d' % pb])
        ACT(tmp, ps[pb][:], AF.Sqrt, ['ps%d' % pb, 'eps'], ['tmp'], bias=eps_t, scale=1.0 / D)
        RECIP(rstd, tmp, ['tmp'], ['rstd'])

    def prenorm(B, gidx):
        xT, hn, rstd = B['xT'], B['hn'], B['rstd']
        rmsnorm_stats(B, xT, 'xT')
        for c in range(8):
            STT(hn[:, c, :], xT[:, c, :], gv[:, gidx, c:c + 1], rstd, ALU.mult, ALU.mult,
                ['xT', 'gv', 'rstd'], ['hn'])

    def ffn(B, k, gpre, gpost):
        xT, hn, fb, aT, rstd = B['xT'], B['hn'], B['fb'], B['aT'], B['rstd']
        wgu, wdc, sgl = B['wgu'], B['wdc'], B['sgl']
        prenorm(B, gpre)

        def ld_gu(j):
            b = j % 3
            DMA('pool', wgu[b], wgu_d[k][j], [], ['wgu%d' % b])

        def ld_d(c):
            b = c % 2
            DMA('pool', wdc[b], wdc_d[k][c], [], ['wdc%d' % b])
        ld_gu(0)
        ld_gu(1)
        for j in range(NF):
            if j + 2 < NF:
                ld_gu(j + 2)
            if j == NF - 2:
                ld_d(0)
            b = j % 3
            pg = 2 + (j % 2) * 2
            pu = pg + 1
            for which, pbank in ((0, pg), (1, pu)):
                for c in range(8):
                    MM(ps[pbank][:], wgu[b][:, which, c, :], hn[:, c, :], c == 0, c == 7,
                       ['wgu%d' % b, 'hn'], ['ps%d' % pbank])
            sgb = j % 2
            ACT(sgl[sgb], ps[pg][:], AF.Silu, ['ps%d' % pg], ['sgl%d' % sgb])
            TTo('dve', aT[:, j, :], sgl[sgb], ps[pu][:], ALU.mult, ['sgl%d' % sgb, 'ps%d' % pu], [('aT', j)])
        for c in range(8):
            if c + 1 < 8:
                ld_d(c + 1)
            b = c % 2
            pf = 6 + (c % 2)
            for j in range(NF):
                MM(ps[pf][:], wdc[b][:, j, :], aT[:, j, :], j == 0, j == NF - 1,
                   ['wdc%d' % b, ('aT', j)], ['ps%d' % pf])
            CP('act' if c % 2 == 0 else 'dve', fb[:, c, :], ps[pf][:], ['ps%d' % pf], ['fb'])
        rmsnorm_stats(B, fb, 'fb')
        for c in range(8):
            STT(fb[:, c, :], fb[:, c, :], gv[:, gpost, c:c + 1], rstd, ALU.mult, ALU.mult,
                ['fb', 'gv', 'rstd'], ['fb'])
            STT(xT[:, c, :], fb[:, c, :], 0.5, xT[:, c, :], ALU.mult, ALU.add, ['fb', 'xT'], ['xT'])

    if 'p1' in phases:
        B = alloc_ffn_bufs()
        winb = [A.alloc([128, 8, 128], BF16) for _ in range(4)]
        wvb = A.alloc([128, 8, 512], BF16)
        cosb = A.alloc([128, TT])
        sinb = A.alloc([128, TT])
        ra = A.alloc([128, TT])
        rb = A.alloc([128, TT])
        qst = [A.alloc([128, TT], BF16) for _ in range(2)]
        vst = [A.alloc([128, 8, 65], BF16) for _ in range(2)]
        zst = [A.alloc([128, TT]) for _ in range(2)]
        xT, hn = B['xT'], B['hn']
        DMA('pool', wvb, wv_d, [], ['wvb'])
        for b in range(2):
            MEMSET('pool', vst[b], 1.0, ['vst%d' % b])
        zbank = [2, 3, 4, 5]
        for tt in range(ntiles1):
            load_T(B, x_d, tt, xT, 'xT')
            ffn(B, 0, 0, 1)
            if tt < 4:
                DMA('sp', h1s_d[:, :, tt * TT:(tt + 1) * TT].rearrange("c p t -> p c t"), xT, ['xT'], ['h1s'])
            prenorm(B, 2)
            if tt < 6:
                DMA('sp', cosb, cos_d[:, tt * TT:(tt + 1) * TT], [], ['cosb'])
                DMA('sp', sinb, sin_d[:, tt * TT:(tt + 1) * TT], [], ['sinb'])
            jobs = []
            if tt < 4:
                jobs += [('q', g, [2 * g, 2 * g + 1]) for g in range(4)]
            if tt < 6:
                jobs += [('k', g, [8 + 2 * g, 9 + 2 * g]) for g in range(4)]
            jobs += [('z', j, [16 + j]) for j in range(15)]
            loads = [ci for (_, _, cis) in jobs for ci in cis]

            def ldw(n):
                if n < len(loads):
                    b = n % 4
                    DMA('pool', winb[b], win_d[loads[n]], [], ['winb%d' % b])
            for n in range(3):
                ldw(n)
            li = 0
            for (kind, g, cis) in jobs:
                banks = []
                for ci in cis:
                    ldw(li + 3)
                    b = li % 4
                    li += 1
                    zb = zbank[cnt['z'] % 4]
                    cnt['z'] += 1
                    banks.append(zb)
                    for c in range(8):
                        MM(ps[zb][:], winb[b][:, c, :], hn[:, c, :], c == 0, c == 7,
                           ['winb%d' % b, 'hn'], ['ps%d' % zb])
                sb_ = cnt['st'] % 2
                cnt['st'] += 1
                if kind in ('q', 'k'):
                    TTo('dve', ra, ps[banks[0]][:], cosb, ALU.mult, ['ps%d' % banks[0], 'cosb'], ['ra'])
                    TTo('dve', rb, ps[banks[1]][:], sinb, ALU.mult, ['ps%d' % banks[1], 'sinb'], ['rb'])
                    TTo('pool', qst[sb_], ra, rb, ALU.add, ['ra', 'rb'], ['qst%d' % sb_])
                    dst = qT_d if kind == 'q' else kT_d
                    DMA('sp', dst[g, :, tt * TT:(tt + 1) * TT], qst[sb_], ['qst%d' % sb_], ['qk_d'])
                else:
                    CP('act' if g % 2 == 0 else 'dve', zst[sb_], ps[banks[0]][:], ['ps%d' % banks[0]],
                       ['zst%d' % sb_])
                    DMA('sp', zr_d[g, :, tt * TT:(tt + 1) * TT], zst[sb_], ['zst%d' % sb_], ['zr'])
            if tt < 6:
                for sub in range(4):
                    zb = zbank[cnt['z'] % 4]
                    cnt['z'] += 1
                    for c in range(8):
                        MM(ps[zb][:], hn[:, c, sub * 128:(sub + 1) * 128], wvb[:, c, :], c == 0, c == 7,
                           ['hn', 'wvb'], ['ps%d' % zb])
                    sb_ = cnt['st'] % 2
                    cnt['st'] += 1
                    CP('act' if sub % 2 == 0 else 'dve', vst[sb_][:, :, 0:64],
                       ps[zb][:].rearrange("p (h d) -> p h d", h=8), ['ps%d' % zb], ['vst%d' % sb_])
                    r0 = (tt * 4 + sub) * 128
                    DMA('sp', v_d[r0:r0 + 128, :], vst[sb_].rearrange("p h e -> p (h e)"),
                        ['vst%d' % sb_], ['v_d'])
        P.barrier()
        A.off = A_MARK

    if 'att' in phases:
        qT = A.alloc([128, 4, OWN], BF16)
        kT = A.alloc([128, 4, NKT * 128], BF16)
        va = A.alloc([128, NKT, 520], BF16)
        wm = A.alloc([128, 20, 512], BF16)
        oall = A.alloc([128, 16, 512])
        eb = [A.alloc([128, 512], BF16) for _ in range(4)]
        pm = [A.alloc([128, 512], BF16) for _ in range(4)]
        rden = [A.alloc([128, 4]) for _ in range(2)]
        sq32 = A.alloc([128, 512])
        ss = A.alloc([128, 8])
        rs = A.alloc([128, 8])
        DMA('sp', qT, qT_d.rearrange("g p t -> p g t"), ['qk_d'], ['qT'])
        DMA('sp', kT, kT_d.rearrange("g p t -> p g t"), ['qk_d'], ['kT'])
        DMA('sp', va, v_d.rearrange("(n p) f -> p n f", p=128), ['v_d'], ['va'])
        DMA('sp', wm, wm_d.rearrange("j p c -> p j c"), [], ['wm'])
        items = []
        for hh in range(8):
            for qb in range(4):
                blk_id = hh * 4 + qb
                kts = list(range(max(0, 4 * qb - 8), 4 * qb + 12))
                pvl = [(kt, i) for kt in kts for i in range(4) if abs(kt - 4 * qb - i) <= 8]
                for kt in kts:
                    items.append(dict(hh=hh, qb=qb, kt=kt, J=kt - 4 * qb + 8, ob=6 + blk_id % 2,
                                      first=pvl[0], last=pvl[-1], endblk=(kt == kts[-1]), blk=blk_id))
        NB = 4
        SKEW = 2

        def stage_a(n):
            d = items[n]
            hh, qb, kt, J = d['hh'], d['qb'], d['kt'], d['J']
            g, base = hh // 2, 64 * (hh % 2)
            sbk = 2 + (n % NB)
            b = n % NB
            MM(ps[sbk][:], kT[base:base + 64, g, kt * 128:(kt + 1) * 128],
               qT[base:base + 64, g, qb * 512:(qb + 1) * 512], True, True, ['kT', 'qT'], ['ps%d' % sbk])
            ACT(eb[b], ps[sbk][:], AF.Exp, ['ps%d' % sbk], ['eb%d' % b], scale=0.125)
            TTo('dve', pm[b], eb[b], wm[:, J, :], ALU.mult, ['eb%d' % b, 'wm'], ['pm%d' % b])

        def stage_b(n):
            d = items[n]
            hh, qb, kt, J, ob = d['hh'], d['qb'], d['kt'], d['J'], d['ob']
            b = n % NB
            for i in range(4):
                if abs(J - 8 - i) > 8:
                    continue
                MM(ps[ob][:, i * 65:(i + 1) * 65], pm[b][:, i * 128:(i + 1) * 128],
                   va[:, kt, hh * 65:(hh + 1) * 65], (kt, i) == d['first'], (kt, i) == d['last'],
                   ['pm%d' % b, 'va'], ['ps%d' % ob])
            if d['endblk']:
                o4 = ps[ob][:, 0:260].rearrange("p (i e) -> p i e", e=65)
                rb_ = d['blk'] % 2
                RECIP(rden[rb_].unsqueeze(2), o4[:, :, 64:65], ['ps%d' % ob], ['rden%d' % rb_])
                TTo('dve', oall[:, qb * 4:(qb + 1) * 4, hh * 64:(hh + 1) * 64], o4[:, :, 0:64],
                    rden[rb_].unsqueeze(2).broadcast_to([128, 4, 64]), ALU.mult,
                    ['ps%d' % ob, 'rden%d' % rb_], [('oall', qb)])
        for n in range(len(items) + SKEW):
            if n < len(items):
                stage_a(n)
            if n >= SKEW:
                stage_b(n - SKEW)
        for qt in range(16):
            o = oall[:, qt, :]
            o3 = o.rearrange("p (h d) -> p h d", h=8)
            ACT(sq32, o, AF.Square, [('oall', qt // 4)], ['sq32'])
            P.add('dve', lambda e: e.tensor_reduce(out=ss, in_=sq32.rearrange("p (h d) -> p h d", h=8),
                                                   axis=AX.X, op=ALU.add), r=['sq32'], w=['ss'])
            ACT(rs, ss, AF.Sqrt, ['ss', 'eps'], ['rs'], bias=eps_t, scale=1.0 / 64)
            RECIP(rs, rs, ['rs'], ['rs'])
            TTo('dve', o3, o3, rs.unsqueeze(2).broadcast_to([128, 8, 64]), ALU.mult,
                [('oall', qt // 4), 'rs'], [('oall', qt // 4)])
            TTo('dve', o, o, rep[:, 0, :], ALU.mult, [('oall', qt // 4), 'rep'], [('oall', qt // 4)])
            DMA('sp', yc_d[qt * 128:(qt + 1) * 128, 0:512], o, [('oall', qt // 4)], ['yc'])
        P.barrier()
        A.off = A_MARK

    if 'rwkv' in phases:
        rwkv_phase(E, rwkv_stop)
        P.barrier()
        A.off = A_MARK

    if 'p4' in phases:
        B = alloc_ffn_bufs()
        wob = [A.alloc([128, 8, 128], BF16) for _ in range(2)]
        ost = [A.alloc([128, D]) for _ in range(2)]
        xT, hn, fb, rstd = B['xT'], B['hn'], B['fb'], B['rstd']
        for tt in range(4):
            load_T(B, yc_d, tt, hn, 'hn')
            DMA('pool', wob[0], wo_d[0], [], ['wob0'])
            for dc in range(8):
                if dc + 1 < 8:
                    DMA('pool', wob[(dc + 1) % 2], wo_d[dc + 1], [], ['wob%d' % ((dc + 1) % 2)])
                b = dc % 2
                pf = 6 + (dc % 2)
                for cc in range(8):
                    MM(ps[pf][:], wob[b][:, cc, :], hn[:, cc, :], cc == 0, cc == 7,
                       ['wob%d' % b, 'hn'], ['ps%d' % pf])
                CP('act' if dc % 2 == 0 else 'dve', fb[:, dc, :], ps[pf][:], ['ps%d' % pf], ['fb'])
            rmsnorm_stats(B, fb, 'fb')
            DMA('sp', xT, h1s_d[:, :, tt * TT:(tt + 1) * TT].rearrange("c p t -> p c t"), ['h1s'], ['xT'])
            for c in range(8):
                STT(fb[:, c, :], fb[:, c, :], gv[:, 3, c:c + 1], rstd, ALU.mult, ALU.mult,
                    ['fb', 'gv', 'rstd'], ['fb'])
                TTo('dve', xT[:, c, :], fb[:, c, :], xT[:, c, :], ALU.add, ['fb', 'xT'], ['xT'])
            ffn(B, 1, 4, 5)
            for sub in range(4):
                ob = cnt['st'] % 2
                cnt['st'] += 1
                for half in range(2):
                    pb = cnt['ps01'] % 2
                    cnt['ps01'] += 1
                    for q in range(4):
                        c = half * 4 + q
                        TR(ps[pb][:, q * 128:(q + 1) * 128], xT[:, c, sub * 128:(sub + 1) * 128], ident,
                           ['xT', 'ident'], ['ps%d' % pb])
                    CP('act' if half == 0 else 'dve', ost[ob][:, half * 512:(half + 1) * 512], ps[pb][:],
                       ['ps%d' % pb], ['ost%d' % ob])
                r0 = tt * TT + sub * 128
                DMA('sp', out_d[r0:r0 + 128, :], ost[ob], ['ost%d' % ob], ['out'])

    P.add('sp', None, r=['h1s', 'out', 'yc', 'zr', 'qk_d', 'v_d', 'yf'])
    P.emit(nc, stack)
    stack.close()
    return nc, P


def rwkv_phase(E, debug_stop=None):
    P, A, ps = E.P, E.A, E.ps
    MM, TR, ACT, TTo, TS, STT, CP, RECIP, DMA, MEMSET = (E.MM, E.TR, E.ACT, E.TTo, E.TS, E.STT, E.CP, E.RECIP,
                                                          E.DMA, E.MEMSET)
    ident, identb, rep, rwm, rwc, rwp, lora = E.ident, E.identb, E.rep, E.rwm, E.rwc, E.rwp, E.lora
    eps_t, gneps_t = E.eps_t, E.gneps_t
    zr_d, yc_d, yf_d = E.zr_d, E.yc_d, E.yf_d

    rmask = rwc[:, 0:512]
    blockones = rwc[:, 1024:1152]
    identblk = rwc[:, 1152:1216]
    hsel = rwc[:, 1216:1218]
    mp, mn = rwp[:, 0:15], rwp[:, 15:30]
    k_k, k_a, r_k = rwp[:, 46:50], rwp[:, 50:54], rwp[:, 54:58]
    w2sb, a2sb, g2sb = lora[:, 0, :], lora[:, 1, :], lora[:, 2, :]

    def w0T(di, c):
        return rwp[:, 30 + 4 * di + c:31 + 4 * di + c]

    def a0T(di, c):
        return rwp[:, 38 + 4 * di + c:39 + 4 * di + c]

    c0 = A.alloc([128, 15])
    omka = A.alloc([128, 4])
    rk2 = A.alloc([128, 4])
    GW = 256
    NTG = GW // 128
    NG = 4096 // GW
    NG_OWN = 2048 // GW
    zsb = [A.alloc([128, GW + 2]) for _ in range(3)]
    zcnt = {'z': 0}
    ub = A.alloc([128, 15, GW])
    tw = A.alloc([128, GW])
    T_ = {nm: A.alloc([128, GW]) for nm in
          ('sg', 'av', 'kx', 't1', 't2', 'kkv', 'kd', 'ka', 'cs', 'cs2', 'e0', 'e1', 'e2', 'e3', 'kd0')}
    tot_s = A.alloc([128, GW // 64])
    OB = []
    for _ in range(2):
        d_ = {nm: A.alloc([128, 4, GW], BF16) for nm in ('Rb', 'Kb', 'Kd', 'Ad', 'Kh', 'Ahn', 'vb')}
        d_['gc'] = A.alloc([128, 4, GW // 64])
        d_['bprod'] = A.alloc([128, 4, GW])
        d_['sgd'] = A.alloc([128, GW])
        d_['uvf'] = A.alloc([128, 4, GW])
        OB.append(d_)
    KbT, KhT, AhT, VT = [A.alloc([128, 512], BF16) for _ in range(4)]
    VT32 = A.alloc([128, 512])
    Xs = [[A.alloc([128, 4, 128], BF16) for _ in range(2)] for _ in range(2)]
    Ys = [[A.alloc([128, 4, 128], BF16) for _ in range(2)] for _ in range(2)]
    Rms = [[A.alloc([128, 4, 128], BF16) for _ in range(2)] for _ in range(2)]
    pkks, qrks, qras = [[A.alloc([128, 4, 128], BF16) for _ in range(2)] for _ in range(3)]
    pvs = [A.alloc([128, 4, 64], BF16) for _ in range(2)]
    u0 = A.alloc([128, 8, 64], BF16)
    wt = A.alloc([128, 8, 64], BF16)
    y0 = A.alloc([128, 512])
    rpT = A.alloc([128, 4, 128])
    GT = A.alloc([128, 4, 2, 64])
    Hs = A.alloc([128, 4, 2, 64])
    Tst = [A.alloc([128, 4, 64]) for _ in range(2)]
    ytile = A.alloc([128, 512])
    yfb = A.alloc([128, 512])
    sqb = A.alloc([128, 512])
    tmpv = A.alloc([128, 512])
    s1 = A.alloc([128, 8])
    s2 = A.alloc([128, 8])
    bon = A.alloc([128, 8])
    psb2 = ps[2][:].bitcast(BF16)

    TTo('dve', c0, mp, mn, ALU.add, ['rwp'], ['c0'])
    TS('dve', c0, c0, -1.0, 1.0, ALU.mult, ALU.add, ['c0'], ['c0'])
    TS('dve', omka, k_a, -1.0, 1.0, ALU.mult, ALU.add, ['rwp'], ['omka'])
    TS('dve', rk2, r_k, 0.5, None, ALU.mult, None, ['rwp'], ['rk2'])

    pcnt = {'pa': 0}

    def pbank(lo=0):
        b = (6 if lo == 0 else 0) + pcnt['pa'] % 2
        pcnt['pa'] += 1
        return b

    def prep_group(g, di, own, final, pb):
        O_ = OB[pb]
        Rb, Kb, Kd, Ad, Kh, Ahn, vb = (O_[n] for n in ('Rb', 'Kb', 'Kd', 'Ad', 'Kh', 'Ahn', 'vb'))
        gc, bprod, sgd, uvf = O_['gc'], O_['bprod'], O_['sgd'], O_['uvf']
        lo, hi = GW * g - 1, GW * g + GW + 1
        clo, chi = max(lo, 0), min(hi, 4096)
        need = list(range(4, 12)) + [12, 13] + ([0, 1, 2, 3] if own else []) + ([14] if final else [])
        for n_, j in enumerate(need):
            zb = zsb[zcnt['z'] % 3]
            zk = 'zsb%d' % (zcnt['z'] % 3)
            zcnt['z'] += 1
            DMA('sp', zb[:, clo - lo:GW + 2 - (hi - chi)], zr_d[j, :, clo:chi], ['zr'], [zk])
            if g == 0:
                MEMSET('pool', zb[:, 0:1], 0.0, [zk])
            if g == NG - 1:
                MEMSET('pool', zb[:, GW + 1:GW + 2], 0.0, [zk])
            k = ('ub', j)
            TS('dve', ub[:, j, :], zb[:, 0:GW], mp[:, j:j + 1], None, ALU.mult, None, [zk, 'rwp'], [k])
            STT(ub[:, j, :], zb[:, 2:GW + 2], mn[:, j:j + 1], ub[:, j, :], ALU.mult, ALU.add, [zk, 'rwp', k], [k])
            STT(ub[:, j, :], zb[:, 1:GW + 1], c0[:, j:j + 1], ub[:, j, :], ALU.mult, ALU.add, [zk, 'c0', k], [k])
            yield
        ACT(tw, ub[:, 12, :], AF.Tanh, [('ub', 12)], ['tw'])
        if final:
            ACT(sgd, ub[:, 14, :], AF.Sigmoid, [('ub', 14)], [('sgd', pb)])
        sg, av, kx, t1, t2, kkv, kd, ka = (T_[n] for n in ('sg', 'av', 'kx', 't1', 't2', 'kkv', 'kd', 'ka'))
        cs, cs2, e0, e1, e2, e3, kd0 = (T_[n] for n in ('cs', 'cs2', 'e0', 'e1', 'e2', 'e3', 'kd0'))
        lo64 = 64 * di
        NC64 = GW // 64
        for c in range(4):
            ku = ub[:, 4 + c, :]
            kkey = ('ub', 4 + c)
            cc = slice(c * 128, (c + 1) * 128)
            TS('dve', kx, ku, k_k[:, c:c + 1], None, ALU.mult, None, [kkey, 'rwp'], ['kx'])
            ACT(t1, kx, AF.Square, ['kx'], ['t1'])
            b = pbank()
            MM(ps[b][:, 0:GW], blockones, t1, True, True, ['rwc', 't1'], ['ps%d' % b])
            ACT(t1, ps[b][:, 0:GW], AF.Sqrt, ['ps%d' % b], ['t1'])
            yield
            TS('dve', t1, t1, 1e-12, None, ALU.max, None, ['t1'], ['t1'])
            RECIP(t1, t1, ['t1'], ['t1'])
            TTo('dve', kkv, kx, t1, ALU.mult, ['kx', 't1'], ['kkv'])
            b = pbank(lo64)
            MM(ps[b][:, 0:GW], w2sb[lo64:lo64 + 64, cc], tw[lo64:lo64 + 64, :], True, True, ['lora', 'tw'],
               ['ps%d' % b])
            ACT(sg, ps[b][:, 0:GW], AF.Sigmoid, ['ps%d' % b, 'rwp'], ['sg'], bias=w0T(di, c))
            yield
            b = pbank(lo64)
            MM(ps[b][:, 0:GW], a2sb[lo64:lo64 + 64, cc], ub[lo64:lo64 + 64, 13, :], True, True,
               ['lora', ('ub', 13)], ['ps%d' % b])
            ACT(av, ps[b][:, 0:GW], AF.Sigmoid, ['ps%d' % b, 'rwp'], ['av'], bias=a0T(di, c))
            TS('dve', t2, av, k_a[:, c:c + 1], omka[:, c:c + 1], ALU.mult, ALU.add, ['av', 'rwp', 'omka'], ['t2'])
            TTo('dve', kd, ku, t2, ALU.mult, [kkey, 't2'], ['kd'])
            TTo('pool', ka, kkv, av, ALU.mult, ['kkv', 'av'], ['ka'])
            yield
            if final:
                od = 1 - di
                b = pbank(64 * od)
                MM(ps[b][:, 0:GW], a2sb[64 * od:64 * od + 64, cc], ub[64 * od:64 * od + 64, 13, :], True, True,
                   ['lora', ('ub', 13)], ['ps%d' % b])
                ACT(kd0, ps[b][:, 0:GW], AF.Sigmoid, ['ps%d' % b, 'rwp'], ['kd0'], bias=a0T(od, c))
                TS('dve', kd0, kd0, k_a[:, c:c + 1], omka[:, c:c + 1], ALU.mult, ALU.add,
                   ['kd0', 'rwp', 'omka'], ['kd0'])
                TTo('dve', kd0, kd0, ku, ALU.mult, ['kd0', kkey], ['kd0'])
                yield
                TTo('dve', kd0, kd0, kd, ALU.add, ['kd0', 'kd'], ['kd0'])
                TTo('dve', kd0, kd0, ub[:, c, :], ALU.mult, ['kd0', ('ub', c)], ['kd0'])
                TS('dve', bprod[:, c, :], kd0, rk2[:, c:c + 1], None, ALU.mult, None, ['kd0', 'rk2'],
                   [('bprod', c, pb)])
                CP('pool', uvf[:, c, :], ub[:, 8 + c, :], [('ub', 8 + c)], [('uvf', c, pb)])
                yield
            P.add('dve', lambda e, cs=cs, sg=sg: e.tensor_tensor_scan(out=cs, data0=rmask[:, 0:GW], data1=sg,
                                                                      initial=0.0, op0=ALU.mult, op1=ALU.add),
                  r=['rwc', 'sg'], w=['cs'])
            CP('dve', tot_s, cs[:, 63::64], ['cs'], ['tot'])
            totb = tot_s.unsqueeze(2).broadcast_to([128, NC64, 64])
            v3 = lambda ap: ap.rearrange("p (a b) -> p a b", a=NC64)
            if di == 0:
                csx, cskey = cs, 'cs'
            else:
                TTo('dve', t2, sg, cs, ALU.subtract, ['sg', 'cs'], ['t2'])
                TTo('dve', v3(cs2), v3(t2), totb, ALU.add, ['t2', 'tot'], ['cs2'])
                csx, cskey = cs2, 'cs2'
            yield
            TTo('dve', e0, csx, sg, ALU.subtract, [cskey, 'sg'], ['e0'])
            TTo('dve', v3(e3), totb, v3(csx), ALU.subtract, ['tot', cskey], ['e3'])
            ACT(e1, csx, AF.Exp, [cskey], ['e1'], scale=-CDEC)
            ACT(e2, csx, AF.Exp, [cskey], ['e2'], scale=CDEC)
            yield
            ACT(e0, e0, AF.Exp, ['e0'], ['e0'], scale=-CDEC)
            ACT(e3, e3, AF.Exp, ['e3'], ['e3'], scale=-CDEC)
            ACT(gc[:, c, :], tot_s, AF.Exp, ['tot'], [('gc', c, pb)], scale=-CDEC)
            yield
            if own:
                TTo('pool', Rb[:, c, :], ub[:, c, :], e1, ALU.mult, [('ub', c), 'e1'], [('Rb', c, pb)])
            TTo('pool', Kb[:, c, :], kkv, e0, ALU.mult, ['kkv', 'e0'], [('Kb', c, pb)])
            TTo('pool', Kd[:, c, :], kd, e2, ALU.mult, ['kd', 'e2'], [('Kd', c, pb)])
            yield
            TTo('pool', Ad[:, c, :], ka, e2, ALU.mult, ['ka', 'e2'], [('Ad', c, pb)])
            TTo('pool', Kh[:, c, :], kd, e3, ALU.mult, ['kd', 'e3'], [('Kh', c, pb)])
            STT(Ahn[:, c, :], ka, -1.0, e3, ALU.mult, ALU.mult, ['ka', 'e3'], [('Ahn', c, pb)])
            CP('pool', vb[:, c, :], ub[:, 8 + c, :], [('ub', 8 + c)], [('vb', c, pb)])
            yield

    M = lambda di, k: rwm[:, di, k, :].unsqueeze(1).broadcast_to([128, 4, 128])

    def tile_proc(di, g, tl, outp, final, cur, pb, pump):
        cols = slice(tl * 128, (tl + 1) * 128)
        gt = NTG * g + tl
        O_ = OB[pb]
        Rb, Kb, Kd, Ad, Kh, Ahn, vb = (O_[n] for n in ('Rb', 'Kb', 'Kd', 'Ad', 'Kh', 'Ahn', 'vb'))
        gc, bprod, sgd, uvf = O_['gc'], O_['bprod'], O_['sgd'], O_['uvf']
        allk = lambda nm: [(nm, c, pb) for c in range(4)]
        slot_of = lambda hh: 4 * (hh % 2) + hh // 2
        for (src, dstT, nm, eng) in ((Kb, KbT, 'Kb', 'act'), (Kh, KhT, 'Kh', 'dve'), (Ahn, AhT, 'Ahn', 'act'),
                                      (vb, VT, 'vb', 'dve')):
            for c in range(4):
                TR(psb2[:, c * 128:(c + 1) * 128], src[:, c, cols], identb, [(nm, c, pb), 'identb'], ['ps2'])
            CP(eng, dstT, psb2[:, 0:512], ['ps2'], [nm + 'T'])
        if final:
            for c in range(4):
                TR(ps[2][:, c * 128:(c + 1) * 128], uvf[:, c, cols], ident, [('uvf', c, pb), 'ident'], ['ps2'])
            CP('act', VT32, ps[2][:], ['ps2'], ['VT32'])
        def hg_stages(hg):
            BA, BB, BC = ((4, 5, 3), (1, 0, 2))[hg]
            X, Y, Rm = Xs[hg], Ys[hg], Rms[hg]
            pkk, qrk, qra, pv = pkks[hg], qrks[hg], qras[hg], pvs[hg]
            sx = 'g%d' % hg
            base = 64 * hg
            hl = [(2 * i + hg, i) for i in range(4)]

            def blk(bank, i):
                return ps[bank][:, i * 128:(i + 1) * 128]

            def b3(bank):
                return ps[bank][:].rearrange("p (a b) -> p a b", a=4)
            for i, (hh, c) in enumerate(hl):
                MM(blk(BA, i), Kb[base:base + 64, c, cols], Ad[base:base + 64, c, cols], True, True,
                   [('Kb', c, pb), ('Ad', c, pb)], ['ps%d' % BA])
            TTo('dve', X[0], b3(BA), M(di, 0), ALU.mult, ['ps%d' % BA, 'rwm'], ['X0' + sx])
            yield
            for i, (hh, c) in enumerate(hl):
                MM(blk(BB, i), Ad[base:base + 64, c, cols], Kb[base:base + 64, c, cols], True, True,
                   [('Kb', c, pb), ('Ad', c, pb)], ['ps%d' % BB])
            TTo('dve', Y[0], b3(BB), M(di, 1), ALU.mult, ['ps%d' % BB, 'rwm'], ['Y0' + sx])
            TTo('pool', Rm[0], Y[0], identb.unsqueeze(1).broadcast_to([128, 4, 128]), ALU.add,
                ['Y0' + sx, 'identb'], ['R0' + sx])
            yield
            for j in range(1, 6):
                a, b = (j - 1) % 2, j % 2
                for i in range(4):
                    MM(blk(BA, i), Y[a][:, i, :], X[a][:, i, :], True, True, ['X%d' % a + sx, 'Y%d' % a + sx], ['ps%d' % BA])
                CP('act', X[b], b3(BA), ['ps%d' % BA], ['X%d' % b + sx])
                yield
                if j < 5:
                    for i in range(4):
                        MM(blk(BB, i), X[a][:, i, :], Y[a][:, i, :], True, True, ['X%d' % a + sx, 'Y%d' % a + sx], ['ps%d' % BB])
                    CP('act', Y[b], b3(BB), ['ps%d' % BB], ['Y%d' % b + sx])
                    yield
                for i in range(4):
                    MM(blk(BC, i), X[b][:, i, :], Rm[a][:, i, :], True, True, ['X%d' % b + sx, 'R%d' % a + sx], ['ps%d' % BC])
                TTo('dve', Rm[b], b3(BC), Rm[a], ALU.add, ['ps%d' % BC, 'R%d' % a + sx], ['R%d' % b + sx])
                yield
            MinvT, mkey = Rm[1], 'R1' + sx
            for i, (hh, c) in enumerate(hl):
                MM(blk(BA, i), Kd[base:base + 64, c, cols], Kb[base:base + 64, c, cols], True, True,
                   [('Kd', c, pb), ('Kb', c, pb)], ['ps%d' % BA])
            TTo('dve', pkk, b3(BA), M(di, 2), ALU.mult, ['ps%d' % BA, 'rwm'], ['pkk' + sx])
            yield
            if outp:
                for i, (hh, c) in enumerate(hl):
                    MM(blk(BB, i), Kd[base:base + 64, c, cols], Rb[base:base + 64, c, cols], True, True,
                       [('Kd', c, pb), ('Rb', c, pb)], ['ps%d' % BB])
                TTo('dve', qrk, b3(BB), M(di, 3), ALU.mult, ['ps%d' % BB, 'rwm'], ['qrk' + sx])
                yield
                for i, (hh, c) in enumerate(hl):
                    MM(blk(BC, i), Ad[base:base + 64, c, cols], Rb[base:base + 64, c, cols], True, True,
                       [('Ad', c, pb), ('Rb', c, pb)], ['ps%d' % BC])
                TTo('dve', qra, b3(BC), M(di, 4), ALU.mult, ['ps%d' % BC, 'rwm'], ['qra' + sx])
                yield
            for i, (hh, c) in enumerate(hl):
                MM(ps[BA][:, i * 64:(i + 1) * 64], pkk[:, i, :], VT[:, hh * 64:(hh + 1) * 64], True, True,
                   ['pkk' + sx, 'vbT'], ['ps%d' % BA])
            CP('act', pv, ps[BA][:, 0:256].rearrange("p (a b) -> p a b", a=4), ['ps%d' % BA], ['pv' + sx])
            yield
            for i, (hh, c) in enumerate(hl):
                MM(ps[BB][:, i * 64:(i + 1) * 64], MinvT[:, i, :], pv[:, i, :], True, True, [mkey, 'pv' + sx], ['ps%d' % BB])
            for i, (hh, c) in enumerate(hl):
                MM(ps[BC][:, i * 64:(i + 1) * 64], MinvT[:, i, :], KbT[:, hh * 64:(hh + 1) * 64],
                   True, True, [mkey, 'KbT'], ['ps%d' % BC])
            CP('act', u0[:, 4 * hg:4 * hg + 4, :], ps[BB][:, 0:256].rearrange("p (a b) -> p a b", a=4),
               ['ps%d' % BB], [('u0', hg)])
            CP('dve', wt[:, 4 * hg:4 * hg + 4, :], ps[BC][:, 0:256].rearrange("p (a b) -> p a b", a=4),
               ['ps%d' % BC], [('wt', hg)])
            yield
            if outp:
                for i, (hh, c) in enumerate(hl):
                    MM(ps[BB][:, i * 64:(i + 1) * 64], qrk[:, i, :], VT[:, hh * 64:(hh + 1) * 64], True, False,
                       ['qrk' + sx, 'vbT'], ['ps%d' % BB])
                    MM(ps[BB][:, i * 64:(i + 1) * 64], qra[:, i, :], u0[:, 4 * hg + i, :], False, True,
                       ['qra' + sx, ('u0', hg)], ['ps%d' % BB])
                CP('dve', y0.rearrange("p (i q d) -> p i q d", i=4, q=2)[:, :, hg, :],
                   ps[BB][:, 0:256].rearrange("p (a b) -> p a b", a=4), ['ps%d' % BB], [('y0', hg)])
                yield
                for i, (hh, c) in enumerate(hl):
                    MM(ps[BA][base:base + 64, i * 128:(i + 1) * 128], wt[:, 4 * hg + i, :], qra[:, i, :],
                       True, True, [('wt', hg), 'qra' + sx], ['ps%d' % BA])
                TTo('dve', rpT[base:base + 64, :, :],
                    ps[BA][base:base + 64, :].rearrange("p (a b) -> p a b", a=4),
                    Rb[base:base + 64, :, cols], ALU.add, ['ps%d' % BA] + allk('Rb'), [('rpT', hg)])
            yield
        gens = [hg_stages(0), hg_stages(1)]
        alive = [True, True]
        while any(alive):
            pump(1)
            for k_ in (0, 1):
                if alive[k_]:
                    try:
                        next(gens[k_])
                    except StopIteration:
                        alive[k_] = False
        gbank = {0: 5, 1: 3}
        hbank = {0: 4, 1: 2}
        for ch2 in range(2):
            rows = slice(ch2 * 64, ch2 * 64 + 64)
            gb, hb = gbank[ch2], hbank[ch2]
            for hh in range(8):
                c, base = hh // 2, 64 * (hh % 2)
                sl = slot_of(hh)
                o5 = ps[gb][base:base + 64, c * 64:(c + 1) * 64]
                MM(o5, wt[rows, sl, :], AhT[rows, hh * 64:(hh + 1) * 64], True, True,
                   [('wt', hh % 2), 'AhnT'], ['ps%d' % gb])
                o4 = ps[hb][base:base + 64, c * 64:(c + 1) * 64]
                MM(o4, KhT[rows, hh * 64:(hh + 1) * 64], VT[rows, hh * 64:(hh + 1) * 64], True, False,
                   ['KhT', 'vbT'], ['ps%d' % hb])
                MM(o4, AhT[rows, hh * 64:(hh + 1) * 64], u0[rows, sl, :], False, True,
                   ['AhnT', ('u0', hh % 2)], ['ps%d' % hb])
        for ch2 in range(2):
            gb, hb = gbank[ch2], hbank[ch2]
            for c in range(4):
                slot = 2 * tl + ch2
                STT(GT[:, c, ch2, :], identblk, gc[:, c, slot:slot + 1], ps[gb][:, c * 64:(c + 1) * 64],
                    ALU.mult, ALU.add, ['rwc', ('gc', c, pb), 'ps%d' % gb], ['GT'])
            CP('act', Hs[:, :, ch2, :], ps[hb][:, 0:256].rearrange("p (a b) -> p a b", a=4),
               ['ps%d' % hb], ['Hs'])
        pump(2)
        order = [0, 1] if di == 0 else [1, 0]
        tb = {0: 6, 1: 0}
        yb_ = {0: 7, 1: 1}
        for ch2 in order:
            rows = slice(ch2 * 64, ch2 * 64 + 64)
            Tc, Tn = Tst[cur], Tst[1 - cur]
            kc, kn = 'T%d' % cur, 'T%d' % (1 - cur)
            for par in range(2):
                base = 64 * par
                for c in range(4):
                    hh = 2 * c + par
                    if outp:
                        MM(ps[yb_[par]][rows, hh * 64:(hh + 1) * 64],
                           rpT[base:base + 64, c, ch2 * 64:ch2 * 64 + 64],
                           Tc[base:base + 64, c, :], True, True, [('rpT', par), kc], ['ps%d' % yb_[par]])
                    MM(ps[tb[par]][base:base + 64, c * 64:(c + 1) * 64], GT[base:base + 64, c, ch2, :],
                       Tc[base:base + 64, c, :], True, True, ['GT', kc], ['ps%d' % tb[par]])
            for par in range(2):
                base = 64 * par
                if outp:
                    yv = ytile.rearrange("p (i q d) -> p i q d", i=4, q=2)
                    y0v = y0.rearrange("p (i q d) -> p i q d", i=4, q=2)
                    pyv = ps[yb_[par]][:].rearrange("p (i q d) -> p i q d", i=4, q=2)
                    TTo('dve', yv[rows, :, par, :], pyv[rows, :, par, :], y0v[rows, :, par, :], ALU.add,
                        ['ps%d' % yb_[par], ('y0', 0), ('y0', 1)], ['ytile'])
                TTo('dve', Tn[base:base + 64, :, :],
                    ps[tb[par]][base:base + 64, 0:256].rearrange("p (a b) -> p a b", a=4),
                    Hs[base:base + 64, :, ch2, :], ALU.add, ['ps%d' % tb[par], 'Hs'], [kn])
            cur = 1 - cur
            pump(1)
        if outp and not final:
            DMA('sp', yf_d[gt * 128:(gt + 1) * 128, :], ytile, ['ytile'], ['yf'])
        if outp and final:
            DMA('sp', yfb, yf_d[gt * 128:(gt + 1) * 128, :], ['yf'], ['yfb'])
            y = ytile
            y3 = y.rearrange("p (h d) -> p h d", h=8)
            TTo('dve', y, y, yfb, ALU.add, ['ytile', 'yfb'], ['ytile'])
            P.add('dve', lambda e: e.tensor_reduce(out=s1, in_=y3, axis=AX.X, op=ALU.add), r=['ytile'], w=['s1'])
            TS('dve', s1, s1, -1.0 / 64, None, ALU.mult, None, ['s1'], ['s1'])
            TTo('dve', y3, y3, s1.unsqueeze(2).broadcast_to([128, 8, 64]), ALU.add, ['ytile', 's1'], ['ytile'])
            ACT(sqb, y, AF.Square, ['ytile'], ['sqb'])
            P.add('dve', lambda e: e.tensor_reduce(out=s2, in_=sqb.rearrange("p (h d) -> p h d", h=8), axis=AX.X,
                                                   op=ALU.add), r=['sqb'], w=['s2'])
            ACT(s2, s2, AF.Sqrt, ['s2', 'eps'], ['s2'], bias=gneps_t, scale=1.0 / 64)
            RECIP(s2, s2, ['s2'], ['s2'])
            TTo('dve', y3, y3, s2.unsqueeze(2).broadcast_to([128, 8, 64]), ALU.mult, ['ytile', 's2'], ['ytile'])
            TTo('dve', y, y, rep[:, 1, :], ALU.mult, ['ytile', 'rep'], ['ytile'])
            TTo('dve', y, y, rep[:, 2, :], ALU.add, ['ytile', 'rep'], ['ytile'])
            for c in range(4):
                MM(ps[2][:, 2 * c:2 * c + 2], bprod[:, c, cols], hsel, True, True, [('bprod', c, pb), 'rwc'], ['ps2'])
            CP('dve', bon, ps[2][:, 0:8], ['ps2'], ['bon'])
            TTo('dve', tmpv.rearrange("p (h d) -> p h d", h=8), VT32.rearrange("p (h d) -> p h d", h=8),
                bon.unsqueeze(2).broadcast_to([128, 8, 64]), ALU.mult, ['VT32', 'bon'], ['tmpv'])
            TTo('dve', y, y, tmpv, ALU.add, ['ytile', 'tmpv'], ['ytile'])
            MM(ps[3][:], sgd[:, cols], g2sb, True, True, [('sgd', pb), 'lora'], ['ps3'])
            TTo('dve', y, y, ps[3][:], ALU.mult, ['ytile', 'ps3'], ['ytile'])
            DMA('sp', yc_d[gt * 128:(gt + 1) * 128, 512:1024], y, ['ytile'], ['yc'])
        return cur

    sched = [(g, 0, True, False) for g in range(NG_OWN)]
    if debug_stop != 'F':
        sched += [(g, 1, g < NG_OWN, g < NG_OWN) for g in range(NG - 1, -1, -1)]
    cur = 0
    MEMSET('pool', Tst[0], 0.0, ['T0'])
    for _ in prep_group(*sched[0], 0):
        pass
    for idx, (g, di, own, final) in enumerate(sched):
        pb = idx % 2
        pg = prep_group(*sched[idx + 1], (idx + 1) % 2) if idx + 1 < len(sched) else None
        st = {'alive': pg is not None}

        def pump(n, pg=pg, st=st):
            for _ in range(n):
                if st['alive']:
                    try:
                        next(pg)
                    except StopIteration:
                        st['alive'] = False
        if idx > 0 and sched[idx - 1][1] != di:
            MEMSET('pool', Tst[cur], 0.0, ['T%d' % cur])
        tls = range(NTG) if di == 0 else range(NTG - 1, -1, -1)
        for tl in tls:
            cur = tile_proc(di, g, tl, own, final, cur, pb, pump)
        pump(10 ** 6)


def _f(a):
    return np.ascontiguousarray(a, dtype=np.float32)


def _lhs_chunks(W):
    n = W.shape[1] // 128
    return _f(W.reshape(8, 128, n, 128).transpose(2, 1, 0, 3))


def _attn_masks():
    p = np.arange(128)[:, None]
    c = np.arange(128)[None, :]
    cntm = {}
    for j in range(-8, 9):
        d = 128 * j + p - c
        m = (np.abs(d) <= 64).astype(np.float32)
        m += ((d % 4 == 0) & (np.abs(d) <= 256)).astype(np.float32)
        m += ((d % 16 == 0) & (np.abs(d) <= 1024)).astype(np.float32)
        cntm[j] = m
    wm = np.zeros((20, 128, 512), np.float32)
    for J in range(20):
        for i in range(4):
            j = J - 8 - i
            if abs(j) <= 8:
                wm[J, :, i * 128:(i + 1) * 128] = cntm[j]
    return wm.astype(ml_dtypes.bfloat16)


def _rope_tables(rev):
    pos = np.arange(S, dtype=np.float32)
    if rev:
        pos = pos[::-1].copy()
    inv_freq = (np.float32(10000.0) ** (-np.arange(0, 64, 2, dtype=np.float32) / np.float32(64))).astype(np.float32)
    ang = (pos[:, None] * inv_freq[None, :]).astype(np.float32)
    cos, sin = np.cos(ang).astype(np.float32), np.sin(ang).astype(np.float32)
    idx = np.arange(128) % 32
    sign = np.where((np.arange(128) % 64) < 32, -1.0, 1.0).astype(np.float32)
    cosT = cos[:, idx].T
    sinT = (sin[:, idx] * sign[None, :]).T
    return _f(cosT), _f(sinT)


def host_inputs(inputs):
    g = lambda k: np.asarray(inputs[k][0], np.float32)
    x = np.asarray(inputs["x"], dtype=np.float32)
    shared = {}
    ffn_names = {1: ("ffn1_w_gate", "ffn1_w_up", "ffn1_w_down"), 2: ("ffn2_w_gate", "ffn2_w_up", "ffn2_w_down")}
    for k in (1, 2):
        wg, wu, wd = (g(nm) for nm in ffn_names[k])
        gg = wg.reshape(8, 128, NF, 128).transpose(2, 1, 0, 3)
        uu = wu.reshape(8, 128, NF, 128).transpose(2, 1, 0, 3)
        shared["wgu%d" % k] = _f(np.stack([gg, uu], axis=2))
        shared["wdc%d" % k] = _f(wd.reshape(NF, 128, 8, 128).transpose(2, 1, 0, 3))
    gnames = ["ffn1_pre_g", "ffn1_post_g", "mix_pre_g", "mix_post_g", "ffn2_pre_g", "ffn2_post_g"]
    shared["gv"] = _f(np.stack([g(nm).reshape(8, 128).T for nm in gnames], axis=1))
    shared["ident"] = np.eye(128, dtype=np.float32)
    w_in = g("w_in")
    swap = np.concatenate([(np.arange(64) + 32) % 64 + 64 * h for h in range(8)])
    shared["wv"] = _f(w_in[:, 1024:1536].reshape(8, 128, 512).transpose(1, 0, 2))
    shared["wo"] = _lhs_chunks(g("w_out"))
    shared["wm"] = _attn_masks()
    shared["rep"] = _f(np.stack([np.broadcast_to(g(nm)[None, :], (128, 512))
                                 for nm in ("attn_out_g", "rwkv_lnx_w", "rwkv_lnx_b")], axis=1))
    maps = []
    for c in range(8):
        b, h = c // 2, c % 2
        m = dict(shared)
        m["x"] = _f(x[b] if h == 0 else x[b, ::-1])
        cosT, sinT = _rope_tables(h == 1)
        m["cosT"], m["sinT"] = cosT, sinT
        m.update(_rwkv_host(inputs, h, w_in, swap))
        maps.append(m)
    return maps


def _rwkv_host(inputs, h, w_in, swap):
    g = lambda k: np.asarray(inputs[k][0], np.float32)
    dirs = [0, 1] if h == 0 else [1, 0]
    wq, wk = w_in[:, 0:512], w_in[:, 512:1024]
    cols = []
    for gi in range(4):
        cols.append(wq[:, gi * 128:(gi + 1) * 128])
        cols.append(wq[:, swap][:, gi * 128:(gi + 1) * 128])
    for gi in range(4):
        cols.append(wk[:, gi * 128:(gi + 1) * 128])
        cols.append(wk[:, swap][:, gi * 128:(gi + 1) * 128])
    wr = w_in[:, 1536:]
    rw = [wr[:, 0:1536]]
    for off in (1536, 1664):
        blk = wr[:, off:off + 128]
        rw.append(np.concatenate([blk[:, 64 * dirs[0]:64 * dirs[0] + 64], blk[:, 64 * dirs[1]:64 * dirs[1] + 64]], axis=1))
    rw.append(wr[:, 1792:1920])
    Wall = np.concatenate(cols + rw, axis=1)
    out = {"win": _lhs_chunks(Wall)}
    mp, mn = g("rwkv_mu_prev"), g("rwkv_mu_next")
    if h == 1:
        mp, mn = mn, mp

    def fix(v):
        v = v.copy()
        for off in (1536, 1664):
            blk = v[off:off + 128].copy()
            v[off:off + 128] = np.concatenate([blk[64 * dirs[0]:64 * dirs[0] + 64], blk[64 * dirs[1]:64 * dirs[1] + 64]])
        return v
    mp, mn = fix(mp), fix(mn)
    rwp = np.zeros((128, 64), np.float32)
    rwp[:, 0:15] = mp.reshape(15, 128).T
    rwp[:, 15:30] = mn.reshape(15, 128).T
    w0, a0 = g("rwkv_w0"), g("rwkv_a0")
    for di, d in enumerate(dirs):
        rwp[:, 30 + 4 * di:34 + 4 * di] = w0[d].reshape(4, 128).T
        rwp[:, 38 + 4 * di:42 + 4 * di] = a0[d].reshape(4, 128).T
    rwp[:, 46:50] = g("rwkv_k_k").reshape(4, 128).T
    rwp[:, 50:54] = g("rwkv_k_a").reshape(4, 128).T
    rwp[:, 54:58] = g("rwkv_r_k").reshape(4, 128).T
    out["rwpar"] = rwp
    w2, a2 = g("rwkv_w2"), g("rwkv_a2")
    lora = np.zeros((128, 3, 512), np.float32)
    lora[:, 0, :] = np.concatenate([w2[dirs[0]], w2[dirs[1]]], axis=0)
    lora[:, 1, :] = np.concatenate([a2[dirs[0]], a2[dirs[1]]], axis=0)
    lora[:, 2, :] = g("rwkv_g2")
    out["lora"] = lora
    idx = np.arange(128)
    same = (idx[:, None] // 64) == (idx[None, :] // 64)
    B0 = ((idx[None, :] < idx[:, None]) & same).astype(np.float32)
    I = np.eye(128, dtype=np.float32)
    rwm = np.zeros((128, 2, 5, 128), np.float32)
    for d, Bd in enumerate((B0, B0.T)):
        rwm[:, d, 0] = -Bd
        rwm[:, d, 1] = -Bd.T
        rwm[:, d, 2] = Bd.T
        rwm[:, d, 3] = Bd.T + I
        rwm[:, d, 4] = -(Bd.T + I)
    out["rwmask"] = rwm
    rwc = np.zeros((128, 1218), np.float32)
    rwc[:, 0:512] = (np.arange(512) % 64 != 0).astype(np.float32)[None, :]
    rwc[:, 1024:1152] = same.astype(np.float32)
    rwc[:, 1152:1216] = (idx[:, None] % 64 == np.arange(64)[None, :]).astype(np.float32)
    rwc[:, 1216] = (idx < 64)
    rwc[:, 1217] = (idx >= 64)
    out["rwconst"] = rwc
    return out


_CACHE = {}


def kernel(**inputs):
    if 'nc' not in _CACHE:
        _CACHE['nc'] = build()[0]
    nc = _CACHE['nc']
    maps = host_inputs(inputs)
    res = run_bass_kernel_spmd(nc, maps, core_ids=list(range(8)))
    out = np.zeros((4, S, D), np.float32)
    for c in range(8):
        b, h = c // 2, c % 2
        o = np.asarray(res.results[c]["out"])
        if h == 0:
            out[b, :OWN] = o
        else:
            out[b, OWN:] = o[::-1]
    return out
```

```python
import contextlib
import numpy as np
import ml_dtypes
import concourse.bass as bass
import concourse.mybir as mybir
from concourse.bass_utils import run_bass_kernel_spmd

F32 = mybir.dt.float32
BF16 = mybir.dt.bfloat16
AF = mybir.ActivationFunctionType
ALU = mybir.AluOpType
AX = mybir.AxisListType

D = 1024
DFF = 2816
NF = DFF // 128
S = 4096
OWN = 2048
TT = 512
EPS = 1e-6
GN_EPS = 64e-5
CDEC = float(np.exp(-0.5))
NKT = 16


class Prog:
    NDMA = 16

    def __init__(self):
        self.ops = []
        self.last_w = {}
        self.readers = {}
        self.last_barrier = 0
        self.warn = []
        self.pe_hist = []

    def add(self, eng, fn, r=(), w=(), dma=False, cc=False):
        i = len(self.ops)
        deps = set()
        for k in r:
            j = self.last_w.get(k)
            if j is not None:
                deps.add((j, 'raw'))
        for k in w:
            j = self.last_w.get(k)
            if j is not None:
                deps.add((j, 'waw'))
            for j in self.readers.get(k, ()):
                deps.add((j, 'war'))
        self.ops.append(dict(eng=eng, fn=fn, deps=deps, dma=dma, cc=cc))
        for k in w:
            if isinstance(k, str) and k.startswith('ps'):
                engs = set(self.ops[j]['eng'] for j in self.readers.get(k, ()))
                if len(engs) > 1:
                    self.warn.append(('multi-engine psum readers', k, sorted(engs), i))
        for k in r:
            self.readers.setdefault(k, []).append(i)
        for k in w:
            self.last_w[k] = i
            self.readers[k] = []
        return i

    def barrier(self):
        n = len(self.ops)
        deps = set()
        last = {}
        for i in range(n):
            op = self.ops[i]
            if op['dma']:
                if i >= self.last_barrier:
                    deps.add((i, 'raw'))
            elif op['fn'] is not None:
                last[op['eng']] = i
        for e, i in last.items():
            deps.add((i, 'raw'))
        for e in ['pe', 'act', 'dve', 'pool', 'sp']:
            self.ops.append(dict(eng=e, fn=None, deps=set(deps), dma=False, cc=False))
        self.last_barrier = n

    def emit(self, nc, stack):
        ops = self.ops
        n = len(ops)
        need = [set() for _ in range(n)]
        signaled = [False] * n
        for i, op in enumerate(ops):
            for (j, kind) in op['deps']:
                if j == i:
                    continue
                pj = ops[j]
                same = (pj['eng'] == op['eng'])
                if same and op['eng'] == 'pe' and not pj['dma'] and not op['dma'] and op['fn'] is not None:
                    if kind != 'raw':
                        continue
                need[i].add(j)
            latest = {}
            for j in need[i]:
                pj = ops[j]
                if pj['dma']:
                    continue
                e2 = pj['eng']
                if e2 not in latest or j > latest[e2]:
                    latest[e2] = j
            need[i] = set(j for j in need[i] if ops[j]['dma'] or latest[ops[j]['eng']] == j)
            for j in need[i]:
                signaled[j] = True
        engs = ['pe', 'act', 'dve', 'pool', 'sp']
        csem = {e: stack.enter_context(nc.semaphore("s_" + e)) for e in engs[:4]}
        dsem = {e: [stack.enter_context(nc.semaphore("d_%s%d" % (e, k))) for k in range(self.NDMA)]
                for e in ['sp', 'pool']}
        sig = [None] * n
        ccount = {e: 0 for e in engs}
        dcount = {e: 0 for e in engs}
        prevuse = [None] * n
        for i, op in enumerate(ops):
            e = op['eng']
            if op.get('cc'):
                sig[i] = (stack.enter_context(nc.semaphore("cc_%d" % i)), 1)
            elif op['dma']:
                k = dcount[e]
                dcount[e] += 1
                s = dsem[e][k % self.NDMA]
                sig[i] = (s, 16 * (k // self.NDMA + 1))
                if k >= self.NDMA:
                    prevuse[i] = (s, 16 * (k // self.NDMA))
            elif signaled[i]:
                ccount[e] += 1
                sig[i] = (csem[e], ccount[e])
        per = {e: [i for i in range(n) if ops[i]['eng'] == e] for e in engs}
        self.stats = {e: len(per[e]) for e in engs}
        self.stats['sem'] = dict(ccount)

        def run(e, eng):
            waited = {}
            for i in per[e]:
                op = ops[i]
                ws = []
                if prevuse[i] is not None:
                    ws.append(prevuse[i])
                for j in need[i]:
                    ws.append(sig[j])
                best = {}
                for (s, v) in ws:
                    key = id(s)
                    if v > best.get(key, (None, -1))[1]:
                        best[key] = (s, v)
                for key, (s, v) in best.items():
                    if waited.get(key, -1) >= v:
                        continue
                    eng.wait_ge(s, v)
                    waited[key] = v
                if op['fn'] is None:
                    continue
                ins = op['fn'](eng)
                if sig[i] is not None:
                    if op.get('cc'):
                        ins.then_inc(sig[i][0])
                    else:
                        ins.then_inc(sig[i][0], 16 if op['dma'] else 1)

        with nc.Block() as block:
            @block.tensor
            def _(eng):
                run('pe', eng)

            @block.scalar
            def _(eng):
                run('act', eng)

            @block.vector
            def _(eng):
                run('dve', eng)

            @block.gpsimd
            def _(eng):
                run('pool', eng)

            @block.sync
            def _(eng):
                run('sp', eng)


class Arena:
    def __init__(self, tensor, nbytes):
        self.t = tensor
        self.n = nbytes
        self.off = 0

    def alloc(self, shape, dt=F32):
        esz = mybir.dt.size(dt)
        nel = int(np.prod(shape[1:]))
        nb = (nel * esz + 31) // 32 * 32
        assert self.off + nb <= self.n, ("arena overflow", self.off, nb, self.n)
        ap = self.t[:, self.off // 4:(self.off + nb) // 4]
        self.off += nb
        if dt != F32:
            ap = ap.bitcast(dt)
        ap = ap[:, 0:nel]
        if len(shape) == 3:
            ap = ap.rearrange("p (a b) -> p a b", a=shape[1])
        elif len(shape) == 4:
            ap = ap.rearrange("p (a b c) -> p a b c", a=shape[1], b=shape[2])
        return ap


class Env:
    pass


def build(debug=False, phases=('p1', 'att', 'rwkv', 'p4'), ntiles1=8, rwkv_stop=None):
    nc = bass.Bass("TRN2", target_bir_lowering=False)
    P = Prog()
    stack = contextlib.ExitStack()
    E = Env()

    def din(name, shape, dt=F32):
        return nc.dram_tensor(name, list(shape), dt, kind="ExternalInput").ap()

    def dscr(name, shape, dt=F32, out=False):
        if out:
            return nc.dram_tensor(name, list(shape), dt, kind="ExternalOutput").ap()
        return nc.dram_tensor(name, list(shape), dt).ap()

    x_d = din("x", [S, D])
    wgu_d = [din("wgu%d" % k, [NF, 128, 2, 8, 128]) for k in (1, 2)]
    wdc_d = [din("wdc%d" % k, [8, 128, NF, 128]) for k in (1, 2)]
    win_d = din("win", [31, 128, 8, 128])
    wv_d = din("wv", [128, 8, 512])
    wo_d = din("wo", [8, 128, 8, 128])
    gv_d = din("gv", [128, 6, 8])
    ident_d = din("ident", [128, 128])
    cos_d = din("cosT", [128, S])
    sin_d = din("sinT", [128, S])
    wm_d = din("wm", [20, 128, 512], BF16)
    rep_d = din("rep", [128, 3, 512])
    rwm_d = din("rwmask", [128, 2, 5, 128])
    rwc_d = din("rwconst", [128, 1218])
    rwp_d = din("rwpar", [128, 64])
    lora_d = din("lora", [128, 3, 512])
    out_d = nc.dram_tensor("out", [OWN, D], F32, kind="ExternalOutput").ap()
    h1s_d = dscr("h1s", [8, 128, OWN], out=debug)
    zr_d = dscr("zr", [15, 128, S], out=debug)
    qT_d = dscr("qTs", [4, 128, OWN], BF16, out=debug)
    kT_d = dscr("kTs", [4, 128, NKT * 128], BF16, out=debug)
    v_d = dscr("vs", [NKT * 128, 520], BF16, out=debug)
    yc_d = dscr("yc", [OWN, D], out=debug)
    yf_d = dscr("yf", [OWN, 512], out=debug)
    hsel_d = din("hsel", [128, 2])
    wmf_d = din("wmf", [20, 128, 512], BF16)
    kh_d = dscr("kh", [512, 1024], BF16)
    vh_d = dscr("vh", [1024, 520], BF16)
    zb_d = dscr("zb", [128, 15])
    khg_d = dscr("khg", [1024, 1024], BF16)
    vhg_d = dscr("vhg", [2048, 520], BF16)
    zbg_d = dscr("zbg", [256, 15])
    tf_d = dscr("tf", [128, 256])
    tfg_d = dscr("tfg", [256, 256])
    PAIRS = [[0, 1], [2, 3], [4, 5], [6, 7]]

    def CC(in_ap, out_ap, r, w):
        return P.add('pool', lambda e: e.collective_compute("AllGather", ALU.bypass, replica_groups=PAIRS,
                                                            ins=[in_ap.opt()], outs=[out_ap.opt()]),
                     r=r, w=w, dma=True, cc=True)

    ARENA_BYTES = 204 * 1024
    arena_t = stack.enter_context(nc.sbuf_tensor("arena", [128, ARENA_BYTES // 4], F32))
    A = Arena(arena_t, ARENA_BYTES)
    ps = [stack.enter_context(nc.psum_tensor("ps%d" % i, [128, 512], F32)) for i in range(8)]

    def MM(out, lhsT, rhs, start, stop, r, w):
        rb0, kk_ = lhsT.base_partition(), lhsT.shape[0]
        for (pb0, pk, pw) in P.pe_hist[-1:]:
            if (rb0 + kk_ <= pb0 or pb0 + pk <= rb0) and pw == w[0]:
                P.warn.append(('row-group conflict', w[0], (pb0, pk), (rb0, kk_), len(P.ops)))
        P.pe_hist.append((rb0, kk_, w[0]))
        P.add('pe', lambda e: e.matmul(out, lhsT, rhs, start=start, stop=stop), r=r, w=w)

    def TR(out, in_, idt, r, w):
        P.pe_hist.append((in_.base_partition(), in_.shape[0], w[0]))
        P.add('pe', lambda e: e.transpose(out, in_, idt), r=r, w=w)

    def ACT(out, in_, func, r, w, bias=None, scale=None):
        kw = {}
        if bias is not None:
            kw['bias'] = bias
        if scale is not None:
            kw['scale'] = scale
        P.add('act', lambda e: e.activation(out, in_, func, **kw), r=r, w=w)

    def TTo(eng, out, in0, in1, op, r, w):
        P.add(eng, lambda e: e.tensor_tensor(out=out, in0=in0, in1=in1, op=op), r=r, w=w)

    def TS(eng, out, in0, s1, s2, op0, op1, r, w):
        if op1 is None:
            P.add(eng, lambda e: e.tensor_scalar(out=out, in0=in0, scalar1=s1, scalar2=None, op0=op0), r=r, w=w)
        else:
            P.add(eng, lambda e: e.tensor_scalar(out=out, in0=in0, scalar1=s1, scalar2=s2, op0=op0, op1=op1),
                  r=r, w=w)

    def STT(out, in0, scalar, in1, op0, op1, r, w):
        P.add('dve', lambda e: e.scalar_tensor_tensor(out=out, in0=in0, scalar=scalar, in1=in1, op0=op0, op1=op1),
              r=r, w=w)

    def CP(eng, out, in_, r, w):
        if eng == 'act':
            P.add('act', lambda e: e.copy(out, in_), r=r, w=w)
        else:
            P.add(eng, lambda e: e.tensor_copy(out, in_), r=r, w=w)

    def RECIP(out, in_, r, w):
        P.add('dve', lambda e: e.reciprocal(out, in_), r=r, w=w)

    def DMA(q, out, in_, r, w):
        return P.add(q, lambda e: e.dma_start(out=out, in_=in_), r=r, w=w, dma=True)

    def MEMSET(eng, ap, val, w):
        P.add(eng, lambda e: e.memset(ap, val), w=w)

    ident = A.alloc([128, 128])
    identb = A.alloc([128, 128], BF16)
    ones = A.alloc([128, 128], BF16)
    gv = A.alloc([128, 6, 8])
    eps_t = A.alloc([128, 1])
    gneps_t = A.alloc([128, 1])
    rep = A.alloc([128, 3, 512])
    rwm = A.alloc([128, 2, 5, 128])
    rwc = A.alloc([128, 1218])
    rwp = A.alloc([128, 64])
    lora = A.alloc([128, 3, 512])
    hsel = A.alloc([128, 2])
    DMA('sp', ident, ident_d, [], ['ident'])
    DMA('sp', gv, gv_d, [], ['gv'])
    DMA('sp', rep, rep_d, [], ['rep'])
    DMA('sp', rwm, rwm_d, [], ['rwm'])
    DMA('sp', rwc, rwc_d, [], ['rwc'])
    DMA('sp', rwp, rwp_d, [], ['rwp'])
    DMA('sp', lora, lora_d, [], ['lora'])
    DMA('sp', hsel, hsel_d, [], ['hsel'])
    MEMSET('pool', ones, 1.0, ['ones'])
    MEMSET('pool', eps_t, EPS, ['eps'])
    MEMSET('pool', gneps_t, GN_EPS, ['eps'])
    CP('dve', identb, ident, ['ident'], ['identb'])
    A_MARK = A.off

    cnt = {'ps01': 0, 'x': 0, 'z': 0, 'st': 0}
    for k_, v_ in list(locals().items()):
        setattr(E, k_, v_)

    def alloc_ffn_bufs():
        B = {}
        B['xin'] = [A.alloc([128, D]) for _ in range(2)]
        B['xT'] = A.alloc([128, 8, TT])
        B['hn'] = A.alloc([128, 8, TT], BF16)
        B['sq'] = A.alloc([128, 8, TT], BF16)
        B['fb'] = A.alloc([128, 8, TT])
        B['aT'] = A.alloc([128, NF, TT], BF16)
        B['rstd'] = A.alloc([128, TT])
        B['tmp'] = A.alloc([128, TT])
        B['wgu'] = [A.alloc([128, 2, 8, 128], BF16) for _ in range(3)]
        B['wdc'] = [A.alloc([128, NF, 128], BF16) for _ in range(2)]
        B['sgl'] = [A.alloc([128, TT]) for _ in range(2)]
        return B

    def load_T(B, src_d, tt, dst, dstkey):
        xin = B['xin']
        for sub in range(4):
            b = cnt['x'] % 2
            cnt['x'] += 1
            t0 = tt * TT + sub * 128
            DMA('sp', xin[b], src_d[t0:t0 + 128, :], ['yc'] if src_d is yc_d else [], ['xin%d' % b])
            for half in range(2):
                pb = cnt['ps01'] % 2
                cnt['ps01'] += 1
                for q in range(4):
                    c = half * 4 + q
                    TR(ps[pb][:, q * 128:(q + 1) * 128], xin[b][:, c * 128:(c + 1) * 128], ident,
                       ['xin%d' % b, 'ident'], ['ps%d' % pb])
                src = ps[pb][:].rearrange("p (q t) -> p q t", q=4)
                d_ = dst[:, half * 4:(half + 1) * 4, sub * 128:(sub + 1) * 128]
                CP('act' if half == 0 else 'dve', d_, src, ['ps%d' % pb], [dstkey])

    def rmsnorm_stats(B, src, srckey):
        sq, tmp, rstd = B['sq'], B['tmp'], B['rstd']
        ACT(sq, src, AF.Square, [srckey], ['sq'])
        pb = cnt['ps01'] % 2
        cnt['ps01'] += 1
        for c in range(8):
            MM(ps[pb][:], ones, sq[:, c, :], c == 0, c == 7, ['sq', 'ones'], ['ps%d' % pb])
        ACT(tmp, ps[pb][:], AF.Sqrt, ['ps%d' % pb, 'eps'], ['tmp'], bias=eps_t, scale=1.0 / D)
        RECIP(rstd, tmp, ['tmp'], ['rstd'])

    def prenorm(B, gidx):
        xT, hn, rstd = B['xT'], B['hn'], B['rstd']
        rmsnorm_stats(B, xT, 'xT')
        for c in range(8):
            STT(hn[:, c, :], xT[:, c, :], gv[:, gidx, c:c + 1], rstd, ALU.mult, ALU.mult,
                ['xT', 'gv', 'rstd'], ['hn'])

    def ffn(B, k, gpre, gpost):
        xT, hn, fb, aT, rstd = B['xT'], B['hn'], B['fb'], B['aT'], B['rstd']
        wgu, wdc, sgl = B['wgu'], B['wdc'], B['sgl']
        prenorm(B, gpre)

        def ld_gu(j):
            b = j % 3
            DMA('pool', wgu[b], wgu_d[k][j], [], ['wgu%d' % b])

        def ld_d(c):
            b = c % 2
            DMA('pool', wdc[b], wdc_d[k][c], [], ['wdc%d' % b])
        ld_gu(0)
        ld_gu(1)
        for j in range(NF):
            if j + 2 < NF:
                ld_gu(j + 2)
            if j == NF - 2:
                ld_d(0)
            b = j % 3
            pg = 2 + (j % 2) * 2
            pu = pg + 1
            for which, pbank in ((0, pg), (1, pu)):
                for c in range(8):
                    MM(ps[pbank][:], wgu[b][:, which, c, :], hn[:, c, :], c == 0, c == 7,
                       ['wgu%d' % b, 'hn'], ['ps%d' % pbank])
            sgb = j % 2
            ACT(sgl[sgb], ps[pg][:], AF.Silu, ['ps%d' % pg], ['sgl%d' % sgb])
            TTo('dve', aT[:, j, :], sgl[sgb], ps[pu][:], ALU.mult, ['sgl%d' % sgb, 'ps%d' % pu], [('aT', j)])
        for c in range(8):
            if c + 1 < 8:
                ld_d(c + 1)
            b = c % 2
            pf = 6 + (c % 2)
            for j in range(NF):
                MM(ps[pf][:], wdc[b][:, j, :], aT[:, j, :], j == 0, j == NF - 1,
                   ['wdc%d' % b, ('aT', j)], ['ps%d' % pf])
            CP('act' if c % 2 == 0 else 'dve', fb[:, c, :], ps[pf][:], ['ps%d' % pf], ['fb'])
        rmsnorm_stats(B, fb, 'fb')
        for c in range(8):
            STT(fb[:, c, :], fb[:, c, :], gv[:, gpost, c:c + 1], rstd, ALU.mult, ALU.mult,
                ['fb', 'gv', 'rstd'], ['fb'])
            STT(xT[:, c, :], fb[:, c, :], 0.5, xT[:, c, :], ALU.mult, ALU.add, ['fb', 'xT'], ['xT'])

    if 'p1' in phases:
        B = alloc_ffn_bufs()
        winb = [A.alloc([128, 8, 128], BF16) for _ in range(4)]
        wvb = A.alloc([128, 8, 512], BF16)
        cosb = A.alloc([128, TT])
        sinb = A.alloc([128, TT])
        ra = A.alloc([128, TT])
        rb = A.alloc([128, TT])
        qst = [A.alloc([128, TT], BF16) for _ in range(2)]
        vst = [A.alloc([128, 8, 65], BF16) for _ in range(2)]
        zst = [A.alloc([128, TT]) for _ in range(2)]
        xT, hn = B['xT'], B['hn']
        DMA('pool', wvb, wv_d, [], ['wvb'])
        for b in range(2):
            MEMSET('pool', vst[b], 1.0, ['vst%d' % b])
        zbank = [2, 3, 4, 5]
        for tt in range(min(ntiles1, 4)):
            load_T(B, x_d, tt, xT, 'xT')
            ffn(B, 0, 0, 1)
            if tt < 4:
                DMA('sp', h1s_d[:, :, tt * TT:(tt + 1) * TT].rearrange("c p t -> p c t"), xT, ['xT'], ['h1s'])
            prenorm(B, 2)
            if tt < 6:
                DMA('sp', cosb, cos_d[:, tt * TT:(tt + 1) * TT], [], ['cosb'])
                DMA('sp', sinb, sin_d[:, tt * TT:(tt + 1) * TT], [], ['sinb'])
            jobs = []
            if tt < 4:
                jobs += [('q', g, [2 * g, 2 * g + 1]) for g in range(4)]
            if tt < 6:
                jobs += [('k', g, [8 + 2 * g, 9 + 2 * g]) for g in range(4)]
            jobs += [('z', j, [16 + j]) for j in range(15)]
            loads = [ci for (_, _, cis) in jobs for ci in cis]

            def ldw(n):
                if n < len(loads):
                    b = n % 4
                    DMA('pool', winb[b], win_d[loads[n]], [], ['winb%d' % b])
            for n in range(3):
                ldw(n)
            li = 0
            for (kind, g, cis) in jobs:
                banks = []
                for ci in cis:
                    ldw(li + 3)
                    b = li % 4
                    li += 1
                    zb = zbank[cnt['z'] % 4]
                    cnt['z'] += 1
                    banks.append(zb)
                    for c in range(8):
                        MM(ps[zb][:], winb[b][:, c, :], hn[:, c, :], c == 0, c == 7,
                           ['winb%d' % b, 'hn'], ['ps%d' % zb])
                sb_ = cnt['st'] % 2
                cnt['st'] += 1
                if kind in ('q', 'k'):
                    TTo('dve', ra, ps[banks[0]][:], cosb, ALU.mult, ['ps%d' % banks[0], 'cosb'], ['ra'])
                    TTo('dve', rb, ps[banks[1]][:], sinb, ALU.mult, ['ps%d' % banks[1], 'sinb'], ['rb'])
                    TTo('pool', qst[sb_], ra, rb, ALU.add, ['ra', 'rb'], ['qst%d' % sb_])
                    dst = qT_d if kind == 'q' else kT_d
                    DMA('sp', dst[g, :, tt * TT:(tt + 1) * TT], qst[sb_], ['qst%d' % sb_], ['qk_d'])
                    if kind == 'k' and tt >= 2:
                        DMA('sp', kh_d[g * 128:(g + 1) * 128, (tt - 2) * TT:(tt - 1) * TT], qst[sb_],
                            ['qst%d' % sb_], ['kh_d'])
                else:
                    CP('act' if g % 2 == 0 else 'dve', zst[sb_], ps[banks[0]][:], ['ps%d' % banks[0]],
                       ['zst%d' % sb_])
                    DMA('sp', zr_d[g, :, tt * TT:(tt + 1) * TT], zst[sb_], ['zst%d' % sb_], ['zr'])
                    if tt == 3:
                        P.add('sp', (lambda e, g=g, sb_=sb_: e.dma_start(
                            out=zb_d[:, g:g + 1], in_=zst[sb_][:, TT - 1:TT], allow_slow_non_contiguous=True)),
                            r=['zst%d' % sb_], w=['zb_d'], dma=True)
            if tt < 6:
                for sub in range(4):
                    zb = zbank[cnt['z'] % 4]
                    cnt['z'] += 1
                    for c in range(8):
                        MM(ps[zb][:], hn[:, c, sub * 128:(sub + 1) * 128], wvb[:, c, :], c == 0, c == 7,
                           ['hn', 'wvb'], ['ps%d' % zb])
                    sb_ = cnt['st'] % 2
                    cnt['st'] += 1
                    CP('act' if sub % 2 == 0 else 'dve', vst[sb_][:, :, 0:64],
                       ps[zb][:].rearrange("p (h d) -> p h d", h=8), ['ps%d' % zb], ['vst%d' % sb_])
                    r0 = (tt * 4 + sub) * 128
                    DMA('sp', v_d[r0:r0 + 128, :], vst[sb_].rearrange("p h e -> p (h e)"),
                        ['vst%d' % sb_], ['v_d'])
                    if tt >= 2:
                        DMA('sp', vh_d[r0 - 1024:r0 - 1024 + 128, :], vst[sb_].rearrange("p h e -> p (h e)"),
                            ['vst%d' % sb_], ['vh_d'])
        P.barrier()
        A.off = A_MARK
        CC(kh_d, khg_d, ['kh_d'], ['khg'])
        CC(vh_d, vhg_d, ['vh_d'], ['vhg'])
        CC(zb_d, zbg_d, ['zb_d'], ['zbg'])

    if 'att' in phases:
        qT = A.alloc([128, 4, OWN], BF16)
        kT = A.alloc([128, 4, NKT * 128], BF16)
        va = A.alloc([128, NKT, 520], BF16)
        wm = A.alloc([128, 20, 512], BF16)
        wmf = A.alloc([128, 20, 512], BF16)
        khr = A.alloc([128, 2, 4, 1024], BF16)
        vhr = A.alloc([128, 2, 8, 520], BF16)
        kh = A.alloc([128, 4, 1024], BF16)
        vh = A.alloc([128, 8, 520], BF16)
        oall = A.alloc([128, 16, 512])
        eb = [A.alloc([128, 512], BF16) for _ in range(4)]
        pm = [A.alloc([128, 512], BF16) for _ in range(4)]
        rden = [A.alloc([128, 4]) for _ in range(2)]
        sq32 = A.alloc([128, 512])
        ss = A.alloc([128, 8])
        rs = A.alloc([128, 8])
        DMA('sp', qT, qT_d.rearrange("g p t -> p g t"), ['qk_d'], ['qT'])
        DMA('sp', kT, kT_d.rearrange("g p t -> p g t"), ['qk_d'], ['kT'])
        DMA('sp', va, v_d.rearrange("(n p) f -> p n f", p=128), ['v_d'], ['va'])
        DMA('sp', wm, wm_d.rearrange("j p c -> p j c"), [], ['wm'])
        DMA('sp', wmf, wmf_d.rearrange("j p c -> p j c"), [], ['wmf'])
        DMA('sp', khr, khg_d.rearrange("(r g p) t -> p r g t", r=2, g=4), ['khg'], ['khr'])
        DMA('sp', vhr, vhg_d.rearrange("(r n p) f -> p r n f", r=2, n=8), ['vhg'], ['vhr'])
        for (raw, dst, key, n_) in ((khr, kh, 'kh', 4096), (vhr, vh, 'vh', 4160)):
            r0_ = raw[:, 0].rearrange("p a b -> p (a b)")
            r1_ = raw[:, 1].rearrange("p a b -> p (a b)")
            d_ = dst.rearrange("p a b -> p (a b)")
            TS('dve', d_, r0_, hsel[:, 0:1], None, ALU.mult, None, [key + 'r', 'hsel'], [key])
            STT(d_, r1_, hsel[:, 1:2], d_, ALU.mult, ALU.add, [key + 'r', 'hsel', key], [key])
        items = []
        for hh in range(8):
            for qb in range(4):
                blk_id = hh * 4 + qb
                kts = list(range(max(0, 4 * qb - 8), 4 * qb + 12))
                pvl = [(kt, i) for kt in kts for i in range(4) if abs(kt - 4 * qb - i) <= 8]
                for kt in kts:
                    items.append(dict(hh=hh, qb=qb, kt=kt, J=kt - 4 * qb + 8, ob=6 + blk_id % 2,
                                      first=pvl[0], last=pvl[-1], endblk=(kt == kts[-1]), blk=blk_id))
        NB = 4
        SKEW = 2

        def stage_a(n):
            d = items[n]
            hh, qb, kt, J = d['hh'], d['qb'], d['kt'], d['J']
            g, base = hh // 2, 64 * (hh % 2)
            sbk = 2 + (n % NB)
            b = n % NB
            if kt < NKT:
                kop, kkey, mk, mkey_ = kT[base:base + 64, g, kt * 128:(kt + 1) * 128], 'kT', wm, 'wm'
            else:
                ht = 23 - kt
                kop, kkey, mk, mkey_ = kh[base:base + 64, g, ht * 128:(ht + 1) * 128], 'kh', wmf, 'wmf'
            MM(ps[sbk][:], kop, qT[base:base + 64, g, qb * 512:(qb + 1) * 512], True, True, [kkey, 'qT'],
               ['ps%d' % sbk])
            ACT(eb[b], ps[sbk][:], AF.Exp, ['ps%d' % sbk], ['eb%d' % b], scale=0.125)
            TTo('dve', pm[b], eb[b], mk[:, J, :], ALU.mult, ['eb%d' % b, mkey_], ['pm%d' % b])

        def stage_b(n):
            d = items[n]
            hh, qb, kt, J, ob = d['hh'], d['qb'], d['kt'], d['J'], d['ob']
            b = n % NB
            for i in range(4):
                if abs(J - 8 - i) > 8:
                    continue
                vop, vkey = (va[:, kt, hh * 65:(hh + 1) * 65], 'va') if kt < NKT else \
                    (vh[:, 23 - kt, hh * 65:(hh + 1) * 65], 'vh')
                MM(ps[ob][:, i * 65:(i + 1) * 65], pm[b][:, i * 128:(i + 1) * 128],
                   vop, (kt, i) == d['first'], (kt, i) == d['last'],
                   ['pm%d' % b, vkey], ['ps%d' % ob])
            if d['endblk']:
                o4 = ps[ob][:, 0:260].rearrange("p (i e) -> p i e", e=65)
                rb_ = d['blk'] % 2
                RECIP(rden[rb_].unsqueeze(2), o4[:, :, 64:65], ['ps%d' % ob], ['rden%d' % rb_])
                TTo('dve', oall[:, qb * 4:(qb + 1) * 4, hh * 64:(hh + 1) * 64], o4[:, :, 0:64],
                    rden[rb_].unsqueeze(2).broadcast_to([128, 4, 64]), ALU.mult,
                    ['ps%d' % ob, 'rden%d' % rb_], [('oall', qb)])
        for n in range(len(items) + SKEW):
            if n < len(items):
                stage_a(n)
            if n >= SKEW:
                stage_b(n - SKEW)
        for qt in range(16):
            o = oall[:, qt, :]
            o3 = o.rearrange("p (h d) -> p h d", h=8)
            ACT(sq32, o, AF.Square, [('oall', qt // 4)], ['sq32'])
            P.add('dve', lambda e: e.tensor_reduce(out=ss, in_=sq32.rearrange("p (h d) -> p h d", h=8),
                                                   axis=AX.X, op=ALU.add), r=['sq32'], w=['ss'])
            ACT(rs, ss, AF.Sqrt, ['ss', 'eps'], ['rs'], bias=eps_t, scale=1.0 / 64)
            RECIP(rs, rs, ['rs'], ['rs'])
            TTo('dve', o3, o3, rs.unsqueeze(2).broadcast_to([128, 8, 64]), ALU.mult,
                [('oall', qt // 4), 'rs'], [('oall', qt // 4)])
            TTo('dve', o, o, rep[:, 0, :], ALU.mult, [('oall', qt // 4), 'rep'], [('oall', qt // 4)])
            DMA('sp', yc_d[qt * 128:(qt + 1) * 128, 0:512], o, [('oall', qt // 4)], ['yc'])
        P.barrier()
        A.off = A_MARK

    if 'rwkv' in phases:
        rwkv_phase(E, rwkv_stop)
        P.barrier()
        A.off = A_MARK

    if 'p4' in phases:
        B = alloc_ffn_bufs()
        wob = [A.alloc([128, 8, 128], BF16) for _ in range(2)]
        ost = [A.alloc([128, D]) for _ in range(2)]
        xT, hn, fb, rstd = B['xT'], B['hn'], B['fb'], B['rstd']
        for tt in range(4):
            load_T(B, yc_d, tt, hn, 'hn')
            DMA('pool', wob[0], wo_d[0], [], ['wob0'])
            for dc in range(8):
                if dc + 1 < 8:
                    DMA('pool', wob[(dc + 1) % 2], wo_d[dc + 1], [], ['wob%d' % ((dc + 1) % 2)])
                b = dc % 2
                pf = 6 + (dc % 2)
                for cc in range(8):
                    MM(ps[pf][:], wob[b][:, cc, :], hn[:, cc, :], cc == 0, cc == 7,
                       ['wob%d' % b, 'hn'], ['ps%d' % pf])
                CP('act' if dc % 2 == 0 else 'dve', fb[:, dc, :], ps[pf][:], ['ps%d' % pf], ['fb'])
            rmsnorm_stats(B, fb, 'fb')
            DMA('sp', xT, h1s_d[:, :, tt * TT:(tt + 1) * TT].rearrange("c p t -> p c t"), ['h1s'], ['xT'])
            for c in range(8):
                STT(fb[:, c, :], fb[:, c, :], gv[:, 3, c:c + 1], rstd, ALU.mult, ALU.mult,
                    ['fb', 'gv', 'rstd'], ['fb'])
                TTo('dve', xT[:, c, :], fb[:, c, :], xT[:, c, :], ALU.add, ['fb', 'xT'], ['xT'])
            ffn(B, 1, 4, 5)
            for sub in range(4):
                ob = cnt['st'] % 2
                cnt['st'] += 1
                for half in range(2):
                    pb = cnt['ps01'] % 2
                    cnt['ps01'] += 1
                    for q in range(4):
                        c = half * 4 + q
                        TR(ps[pb][:, q * 128:(q + 1) * 128], xT[:, c, sub * 128:(sub + 1) * 128], ident,
                           ['xT', 'ident'], ['ps%d' % pb])
                    CP('act' if half == 0 else 'dve', ost[ob][:, half * 512:(half + 1) * 512], ps[pb][:],
                       ['ps%d' % pb], ['ost%d' % ob])
                r0 = tt * TT + sub * 128
                DMA('sp', out_d[r0:r0 + 128, :], ost[ob], ['ost%d' % ob], ['out'])

    P.add('sp', None, r=['h1s', 'out', 'yc', 'zr', 'qk_d', 'v_d', 'yf'])
    P.emit(nc, stack)
    stack.close()
    return nc, P


def rwkv_phase(E, debug_stop=None):
    P, A, ps = E.P, E.A, E.ps
    MM, TR, ACT, TTo, TS, STT, CP, RECIP, DMA, MEMSET = (E.MM, E.TR, E.ACT, E.TTo, E.TS, E.STT, E.CP, E.RECIP,
                                                          E.DMA, E.MEMSET)
    ident, identb, rep, rwm, rwc, rwp, lora = E.ident, E.identb, E.rep, E.rwm, E.rwc, E.rwp, E.lora
    eps_t, gneps_t = E.eps_t, E.gneps_t
    zr_d, yc_d, yf_d = E.zr_d, E.yc_d, E.yf_d

    rmask = rwc[:, 0:512]
    blockones = rwc[:, 1024:1152]
    identblk = rwc[:, 1152:1216]
    hsel = rwc[:, 1216:1218]
    mp, mn = rwp[:, 0:15], rwp[:, 15:30]
    k_k, k_a, r_k = rwp[:, 46:50], rwp[:, 50:54], rwp[:, 54:58]
    w2sb, a2sb, g2sb = lora[:, 0, :], lora[:, 1, :], lora[:, 2, :]

    def w0T(di, c):
        return rwp[:, 30 + 4 * di + c:31 + 4 * di + c]

    def a0T(di, c):
        return rwp[:, 38 + 4 * di + c:39 + 4 * di + c]

    c0 = A.alloc([128, 15])
    omka = A.alloc([128, 4])
    rk2 = A.alloc([128, 4])
    GW = 256
    NTG = GW // 128
    NG = 4096 // GW
    NG_OWN = 2048 // GW
    zsb = [A.alloc([128, GW + 2]) for _ in range(3)]
    zcnt = {'z': 0}
    ub = A.alloc([128, 15, GW])
    tw = A.alloc([128, GW])
    T_ = {nm: A.alloc([128, GW]) for nm in
          ('sg', 'av', 'kx', 't1', 't2', 'kkv', 'kd', 'ka', 'cs', 'cs2', 'e0', 'e1', 'e2', 'e3', 'kd0')}
    tot_s = A.alloc([128, GW // 64])
    OB = []
    for _ in range(2):
        d_ = {nm: A.alloc([128, 4, GW], BF16) for nm in ('Rb', 'Kb', 'Kd', 'Ad', 'Kh', 'Ahn', 'vb')}
        d_['gc'] = A.alloc([128, 4, GW // 64])
        d_['bprod'] = A.alloc([128, 4, GW])
        d_['sgd'] = A.alloc([128, GW])
        d_['uvf'] = A.alloc([128, 4, GW])
        OB.append(d_)
    KbT, KhT, AhT, VT = [A.alloc([128, 512], BF16) for _ in range(4)]
    VT32 = A.alloc([128, 512])
    Xs = [[A.alloc([128, 4, 128], BF16) for _ in range(2)] for _ in range(2)]
    Ys = [[A.alloc([128, 4, 128], BF16) for _ in range(2)] for _ in range(2)]
    Rms = [[A.alloc([128, 4, 128], BF16) for _ in range(2)] for _ in range(2)]
    pkks, qrks, qras = [[A.alloc([128, 4, 128], BF16) for _ in range(2)] for _ in range(3)]
    pvs = [A.alloc([128, 4, 64], BF16) for _ in range(2)]
    u0 = A.alloc([128, 8, 64], BF16)
    wt = A.alloc([128, 8, 64], BF16)
    y0 = A.alloc([128, 512])
    rpT = A.alloc([128, 4, 128])
    GT = A.alloc([128, 4, 2, 64])
    Hs = A.alloc([128, 4, 2, 64])
    Tst = [A.alloc([128, 4, 64]) for _ in range(2)]
    ytile = A.alloc([128, 512])
    yfb = A.alloc([128, 512])
    sqb = A.alloc([128, 512])
    tmpv = A.alloc([128, 512])
    s1 = A.alloc([128, 8])
    s2 = A.alloc([128, 8])
    bon = A.alloc([128, 8])
    psb2 = ps[2][:].bitcast(BF16)

    TTo('dve', c0, mp, mn, ALU.add, ['rwp'], ['c0'])
    TS('dve', c0, c0, -1.0, 1.0, ALU.mult, ALU.add, ['c0'], ['c0'])
    TS('dve', omka, k_a, -1.0, 1.0, ALU.mult, ALU.add, ['rwp'], ['omka'])
    TS('dve', rk2, r_k, 0.5, None, ALU.mult, None, ['rwp'], ['rk2'])
    zr2 = A.alloc([128, 2, 15])
    zedge = A.alloc([128, 15])
    zg = E.zbg_d.rearrange("(r p) c -> p r c", r=2)
    DMA('sp', zr2, zg, ['zbg'], ['zr2'])
    DMA('sp', zr2[0:64, :, 12:14], zg[64:128, :, 12:14], ['zbg'], ['zr2'])
    DMA('sp', zr2[64:128, :, 12:14], zg[0:64, :, 12:14], ['zbg'], ['zr2'])
    TS('dve', zedge, zr2[:, 0, :], E.hsel[:, 0:1], None, ALU.mult, None, ['zr2', 'hsel'], ['zedge'])
    STT(zedge, zr2[:, 1, :], E.hsel[:, 1:2], zedge, ALU.mult, ALU.add, ['zr2', 'hsel', 'zedge'], ['zedge'])
    tfr = A.alloc([128, 2, 256])

    pcnt = {'pa': 0}

    def pbank(lo=0):
        b = (6 if lo == 0 else 0) + pcnt['pa'] % 2
        pcnt['pa'] += 1
        return b

    def prep_group(g, di, own, final, pb):
        O_ = OB[pb]
        Rb, Kb, Kd, Ad, Kh, Ahn, vb = (O_[n] for n in ('Rb', 'Kb', 'Kd', 'Ad', 'Kh', 'Ahn', 'vb'))
        gc, bprod, sgd, uvf = O_['gc'], O_['bprod'], O_['sgd'], O_['uvf']
        lo, hi = GW * g - 1, GW * g + GW + 1
        clo, chi = max(lo, 0), min(hi, 2048)
        need = list(range(4, 12)) + [12, 13] + ([0, 1, 2, 3] if own else []) + ([14] if final else [])
        for n_, j in enumerate(need):
            zb = zsb[zcnt['z'] % 3]
            zk = 'zsb%d' % (zcnt['z'] % 3)
            zcnt['z'] += 1
            DMA('sp', zb[:, clo - lo:GW + 2 - (hi - chi)], zr_d[j, :, clo:chi], ['zr'], [zk])
            if g == 0:
                MEMSET('pool', zb[:, 0:1], 0.0, [zk])
            if g == NG_OWN - 1:
                CP('pool', zb[:, GW + 1:GW + 2], zedge[:, j:j + 1], ['zedge'], [zk])
            k = ('ub', j)
            TS('dve', ub[:, j, :], zb[:, 0:GW], mp[:, j:j + 1], None, ALU.mult, None, [zk, 'rwp'], [k])
            STT(ub[:, j, :], zb[:, 2:GW + 2], mn[:, j:j + 1], ub[:, j, :], ALU.mult, ALU.add, [zk, 'rwp', k], [k])
            STT(ub[:, j, :], zb[:, 1:GW + 1], c0[:, j:j + 1], ub[:, j, :], ALU.mult, ALU.add, [zk, 'c0', k], [k])
            yield
        ACT(tw, ub[:, 12, :], AF.Tanh, [('ub', 12)], ['tw'])
        if final:
            ACT(sgd, ub[:, 14, :], AF.Sigmoid, [('ub', 14)], [('sgd', pb)])
        sg, av, kx, t1, t2, kkv, kd, ka = (T_[n] for n in ('sg', 'av', 'kx', 't1', 't2', 'kkv', 'kd', 'ka'))
        cs, cs2, e0, e1, e2, e3, kd0 = (T_[n] for n in ('cs', 'cs2', 'e0', 'e1', 'e2', 'e3', 'kd0'))
        lo64 = 64 * di
        NC64 = GW // 64
        for c in range(4):
            ku = ub[:, 4 + c, :]
            kkey = ('ub', 4 + c)
            cc = slice(c * 128, (c + 1) * 128)
            TS('dve', kx, ku, k_k[:, c:c + 1], None, ALU.mult, None, [kkey, 'rwp'], ['kx'])
            ACT(t1, kx, AF.Square, ['kx'], ['t1'])
            b = pbank()
            MM(ps[b][:, 0:GW], blockones, t1, True, True, ['rwc', 't1'], ['ps%d' % b])
            ACT(t1, ps[b][:, 0:GW], AF.Sqrt, ['ps%d' % b], ['t1'])
            yield
            TS('dve', t1, t1, 1e-12, None, ALU.max, None, ['t1'], ['t1'])
            RECIP(t1, t1, ['t1'], ['t1'])
            TTo('dve', kkv, kx, t1, ALU.mult, ['kx', 't1'], ['kkv'])
            b = pbank(lo64)
            MM(ps[b][:, 0:GW], w2sb[lo64:lo64 + 64, cc], tw[lo64:lo64 + 64, :], True, True, ['lora', 'tw'],
               ['ps%d' % b])
            ACT(sg, ps[b][:, 0:GW], AF.Sigmoid, ['ps%d' % b, 'rwp'], ['sg'], bias=w0T(di, c))
            yield
            b = pbank(lo64)
            MM(ps[b][:, 0:GW], a2sb[lo64:lo64 + 64, cc], ub[lo64:lo64 + 64, 13, :], True, True,
               ['lora', ('ub', 13)], ['ps%d' % b])
            ACT(av, ps[b][:, 0:GW], AF.Sigmoid, ['ps%d' % b, 'rwp'], ['av'], bias=a0T(di, c))
            TS('dve', t2, av, k_a[:, c:c + 1], omka[:, c:c + 1], ALU.mult, ALU.add, ['av', 'rwp', 'omka'], ['t2'])
            TTo('dve', kd, ku, t2, ALU.mult, [kkey, 't2'], ['kd'])
            TTo('pool', ka, kkv, av, ALU.mult, ['kkv', 'av'], ['ka'])
            yield
            if final:
                od = 1 - di
                b = pbank(64 * od)
                MM(ps[b][:, 0:GW], a2sb[64 * od:64 * od + 64, cc], ub[64 * od:64 * od + 64, 13, :], True, True,
                   ['lora', ('ub', 13)], ['ps%d' % b])
                ACT(kd0, ps[b][:, 0:GW], AF.Sigmoid, ['ps%d' % b, 'rwp'], ['kd0'], bias=a0T(od, c))
                TS('dve', kd0, kd0, k_a[:, c:c + 1], omka[:, c:c + 1], ALU.mult, ALU.add,
                   ['kd0', 'rwp', 'omka'], ['kd0'])
                TTo('dve', kd0, kd0, ku, ALU.mult, ['kd0', kkey], ['kd0'])
                yield
                TTo('dve', kd0, kd0, kd, ALU.add, ['kd0', 'kd'], ['kd0'])
                TTo('dve', kd0, kd0, ub[:, c, :], ALU.mult, ['kd0', ('ub', c)], ['kd0'])
                TS('dve', bprod[:, c, :], kd0, rk2[:, c:c + 1], None, ALU.mult, None, ['kd0', 'rk2'],
                   [('bprod', c, pb)])
                CP('pool', uvf[:, c, :], ub[:, 8 + c, :], [('ub', 8 + c)], [('uvf', c, pb)])
                yield
            P.add('dve', lambda e, cs=cs, sg=sg: e.tensor_tensor_scan(out=cs, data0=rmask[:, 0:GW], data1=sg,
                                                                      initial=0.0, op0=ALU.mult, op1=ALU.add),
                  r=['rwc', 'sg'], w=['cs'])
            CP('dve', tot_s, cs[:, 63::64], ['cs'], ['tot'])
            totb = tot_s.unsqueeze(2).broadcast_to([128, NC64, 64])
            v3 = lambda ap: ap.rearrange("p (a b) -> p a b", a=NC64)
            if di == 0:
                csx, cskey = cs, 'cs'
            else:
                TTo('dve', t2, sg, cs, ALU.subtract, ['sg', 'cs'], ['t2'])
                TTo('dve', v3(cs2), v3(t2), totb, ALU.add, ['t2', 'tot'], ['cs2'])
                csx, cskey = cs2, 'cs2'
            yield
            TTo('dve', e0, csx, sg, ALU.subtract, [cskey, 'sg'], ['e0'])
            TTo('dve', v3(e3), totb, v3(csx), ALU.subtract, ['tot', cskey], ['e3'])
            ACT(e1, csx, AF.Exp, [cskey], ['e1'], scale=-CDEC)
            ACT(e2, csx, AF.Exp, [cskey], ['e2'], scale=CDEC)
            yield
            ACT(e0, e0, AF.Exp, ['e0'], ['e0'], scale=-CDEC)
            ACT(e3, e3, AF.Exp, ['e3'], ['e3'], scale=-CDEC)
            ACT(gc[:, c, :], tot_s, AF.Exp, ['tot'], [('gc', c, pb)], scale=-CDEC)
            yield
            if own:
                TTo('pool', Rb[:, c, :], ub[:, c, :], e1, ALU.mult, [('ub', c), 'e1'], [('Rb', c, pb)])
            TTo('pool', Kb[:, c, :], kkv, e0, ALU.mult, ['kkv', 'e0'], [('Kb', c, pb)])
            TTo('pool', Kd[:, c, :], kd, e2, ALU.mult, ['kd', 'e2'], [('Kd', c, pb)])
            yield
            TTo('pool', Ad[:, c, :], ka, e2, ALU.mult, ['ka', 'e2'], [('Ad', c, pb)])
            TTo('pool', Kh[:, c, :], kd, e3, ALU.mult, ['kd', 'e3'], [('Kh', c, pb)])
            STT(Ahn[:, c, :], ka, -1.0, e3, ALU.mult, ALU.mult, ['ka', 'e3'], [('Ahn', c, pb)])
            CP('pool', vb[:, c, :], ub[:, 8 + c, :], [('ub', 8 + c)], [('vb', c, pb)])
            yield

    M = lambda di, k: rwm[:, di, k, :].unsqueeze(1).broadcast_to([128, 4, 128])

    def tile_proc(di, g, tl, outp, final, cur, pb, pump):
        cols = slice(tl * 128, (tl + 1) * 128)
        gt = NTG * g + tl
        O_ = OB[pb]
        Rb, Kb, Kd, Ad, Kh, Ahn, vb = (O_[n] for n in ('Rb', 'Kb', 'Kd', 'Ad', 'Kh', 'Ahn', 'vb'))
        gc, bprod, sgd, uvf = O_['gc'], O_['bprod'], O_['sgd'], O_['uvf']
        allk = lambda nm: [(nm, c, pb) for c in range(4)]
        slot_of = lambda hh: 4 * (hh % 2) + hh // 2
        for (src, dstT, nm, eng) in ((Kb, KbT, 'Kb', 'act'), (Kh, KhT, 'Kh', 'dve'), (Ahn, AhT, 'Ahn', 'act'),
                                      (vb, VT, 'vb', 'dve')):
            for c in range(4):
                TR(psb2[:, c * 128:(c + 1) * 128], src[:, c, cols], identb, [(nm, c, pb), 'identb'], ['ps2'])
            CP(eng, dstT, psb2[:, 0:512], ['ps2'], [nm + 'T'])
        if final:
            for c in range(4):
                TR(ps[2][:, c * 128:(c + 1) * 128], uvf[:, c, cols], ident, [('uvf', c, pb), 'ident'], ['ps2'])
            CP('act', VT32, ps[2][:], ['ps2'], ['VT32'])
        def hg_stages(hg):
            BA, BB, BC = ((4, 5, 3), (1, 0, 2))[hg]
            X, Y, Rm = Xs[hg], Ys[hg], Rms[hg]
            pkk, qrk, qra, pv = pkks[hg], qrks[hg], qras[hg], pvs[hg]
            sx = 'g%d' % hg
            base = 64 * hg
            hl = [(2 * i + hg, i) for i in range(4)]

            def blk(bank, i):
                return ps[bank][:, i * 128:(i + 1) * 128]

            def b3(bank):
                return ps[bank][:].rearrange("p (a b) -> p a b", a=4)
            for i, (hh, c) in enumerate(hl):
                MM(blk(BA, i), Kb[base:base + 64, c, cols], Ad[base:base + 64, c, cols], True, True,
                   [('Kb', c, pb), ('Ad', c, pb)], ['ps%d' % BA])
            TTo('dve', X[0], b3(BA), M(di, 0), ALU.mult, ['ps%d' % BA, 'rwm'], ['X0' + sx])
            yield
            for i, (hh, c) in enumerate(hl):
                MM(blk(BB, i), Ad[base:base + 64, c, cols], Kb[base:base + 64, c, cols], True, True,
                   [('Kb', c, pb), ('Ad', c, pb)], ['ps%d' % BB])
            TTo('dve', Y[0], b3(BB), M(di, 1), ALU.mult, ['ps%d' % BB, 'rwm'], ['Y0' + sx])
            TTo('pool', Rm[0], Y[0], identb.unsqueeze(1).broadcast_to([128, 4, 128]), ALU.add,
                ['Y0' + sx, 'identb'], ['R0' + sx])
            yield
            for j in range(1, 6):
                a, b = (j - 1) % 2, j % 2
                for i in range(4):
                    MM(blk(BA, i), Y[a][:, i, :], X[a][:, i, :], True, True, ['X%d' % a + sx, 'Y%d' % a + sx], ['ps%d' % BA])
                CP('act', X[b], b3(BA), ['ps%d' % BA], ['X%d' % b + sx])
                yield
                if j < 5:
                    for i in range(4):
                        MM(blk(BB, i), X[a][:, i, :], Y[a][:, i, :], True, True, ['X%d' % a + sx, 'Y%d' % a + sx], ['ps%d' % BB])
                    CP('act', Y[b], b3(BB), ['ps%d' % BB], ['Y%d' % b + sx])
                    yield
                for i in range(4):
                    MM(blk(BC, i), X[b][:, i, :], Rm[a][:, i, :], True, True, ['X%d' % b + sx, 'R%d' % a + sx], ['ps%d' % BC])
                TTo('dve', Rm[b], b3(BC), Rm[a], ALU.add, ['ps%d' % BC, 'R%d' % a + sx], ['R%d' % b + sx])
                yield
            MinvT, mkey = Rm[1], 'R1' + sx
            for i, (hh, c) in enumerate(hl):
                MM(blk(BA, i), Kd[base:base + 64, c, cols], Kb[base:base + 64, c, cols], True, True,
                   [('Kd', c, pb), ('Kb', c, pb)], ['ps%d' % BA])
            TTo('dve', pkk, b3(BA), M(di, 2), ALU.mult, ['ps%d' % BA, 'rwm'], ['pkk' + sx])
            yield
            if outp:
                for i, (hh, c) in enumerate(hl):
                    MM(blk(BB, i), Kd[base:base + 64, c, cols], Rb[base:base + 64, c, cols], True, True,
                       [('Kd', c, pb), ('Rb', c, pb)], ['ps%d' % BB])
                TTo('dve', qrk, b3(BB), M(di, 3), ALU.mult, ['ps%d' % BB, 'rwm'], ['qrk' + sx])
                yield
                for i, (hh, c) in enumerate(hl):
                    MM(blk(BC, i), Ad[base:base + 64, c, cols], Rb[base:base + 64, c, cols], True, True,
                       [('Ad', c, pb), ('Rb', c, pb)], ['ps%d' % BC])
                TTo('dve', qra, b3(BC), M(di, 4), ALU.mult, ['ps%d' % BC, 'rwm'], ['qra' + sx])
                yield
            for i, (hh, c) in enumerate(hl):
                MM(ps[BA][:, i * 64:(i + 1) * 64], pkk[:, i, :], VT[:, hh * 64:(hh + 1) * 64], True, True,
                   ['pkk' + sx, 'vbT'], ['ps%d' % BA])
            CP('act', pv, ps[BA][:, 0:256].rearrange("p (a b) -> p a b", a=4), ['ps%d' % BA], ['pv' + sx])
            yield
            for i, (hh, c) in enumerate(hl):
                MM(ps[BB][:, i * 64:(i + 1) * 64], MinvT[:, i, :], pv[:, i, :], True, True, [mkey, 'pv' + sx], ['ps%d' % BB])
            for i, (hh, c) in enumerate(hl):
                MM(ps[BC][:, i * 64:(i + 1) * 64], MinvT[:, i, :], KbT[:, hh * 64:(hh + 1) * 64],
                   True, True, [mkey, 'KbT'], ['ps%d' % BC])
            CP('act', u0[:, 4 * hg:4 * hg + 4, :], ps[BB][:, 0:256].rearrange("p (a b) -> p a b", a=4),
               ['ps%d' % BB], [('u0', hg)])
            CP('dve', wt[:, 4 * hg:4 * hg + 4, :], ps[BC][:, 0:256].rearrange("p (a b) -> p a b", a=4),
               ['ps%d' % BC], [('wt', hg)])
            yield
            if outp:
                for i, (hh, c) in enumerate(hl):
                    MM(ps[BB][:, i * 64:(i + 1) * 64], qrk[:, i, :], VT[:, hh * 64:(hh + 1) * 64], True, False,
                       ['qrk' + sx, 'vbT'], ['ps%d' % BB])
                    MM(ps[BB][:, i * 64:(i + 1) * 64], qra[:, i, :], u0[:, 4 * hg + i, :], False, True,
                       ['qra' + sx, ('u0', hg)], ['ps%d' % BB])
                CP('dve', y0.rearrange("p (i q d) -> p i q d", i=4, q=2)[:, :, hg, :],
                   ps[BB][:, 0:256].rearrange("p (a b) -> p a b", a=4), ['ps%d' % BB], [('y0', hg)])
                yield
                for i, (hh, c) in enumerate(hl):
                    MM(ps[BA][base:base + 64, i * 128:(i + 1) * 128], wt[:, 4 * hg + i, :], qra[:, i, :],
                       True, True, [('wt', hg), 'qra' + sx], ['ps%d' % BA])
                TTo('dve', rpT[base:base + 64, :, :],
                    ps[BA][base:base + 64, :].rearrange("p (a b) -> p a b", a=4),
                    Rb[base:base + 64, :, cols], ALU.add, ['ps%d' % BA] + allk('Rb'), [('rpT', hg)])
            yield
        gens = [hg_stages(0), hg_stages(1)]
        alive = [True, True]
        while any(alive):
            pump(1)
            for k_ in (0, 1):
                if alive[k_]:
                    try:
                        next(gens[k_])
                    except StopIteration:
                        alive[k_] = False
        gbank = {0: 5, 1: 3}
        hbank = {0: 4, 1: 2}
        for ch2 in range(2):
            rows = slice(ch2 * 64, ch2 * 64 + 64)
            gb, hb = gbank[ch2], hbank[ch2]
            for hh in range(8):
                c, base = hh // 2, 64 * (hh % 2)
                sl = slot_of(hh)
                o5 = ps[gb][base:base + 64, c * 64:(c + 1) * 64]
                MM(o5, wt[rows, sl, :], AhT[rows, hh * 64:(hh + 1) * 64], True, True,
                   [('wt', hh % 2), 'AhnT'], ['ps%d' % gb])
                o4 = ps[hb][base:base + 64, c * 64:(c + 1) * 64]
                MM(o4, KhT[rows, hh * 64:(hh + 1) * 64], VT[rows, hh * 64:(hh + 1) * 64], True, False,
                   ['KhT', 'vbT'], ['ps%d' % hb])
                MM(o4, AhT[rows, hh * 64:(hh + 1) * 64], u0[rows, sl, :], False, True,
                   ['AhnT', ('u0', hh % 2)], ['ps%d' % hb])
        for ch2 in range(2):
            gb, hb = gbank[ch2], hbank[ch2]
            for c in range(4):
                slot = 2 * tl + ch2
                STT(GT[:, c, ch2, :], identblk, gc[:, c, slot:slot + 1], ps[gb][:, c * 64:(c + 1) * 64],
                    ALU.mult, ALU.add, ['rwc', ('gc', c, pb), 'ps%d' % gb], ['GT'])
            CP('act', Hs[:, :, ch2, :], ps[hb][:, 0:256].rearrange("p (a b) -> p a b", a=4),
               ['ps%d' % hb], ['Hs'])
        pump(2)
        order = [0, 1] if di == 0 else [1, 0]
        tb = {0: 6, 1: 0}
        yb_ = {0: 7, 1: 1}
        for ch2 in order:
            rows = slice(ch2 * 64, ch2 * 64 + 64)
            Tc, Tn = Tst[cur], Tst[1 - cur]
            kc, kn = 'T%d' % cur, 'T%d' % (1 - cur)
            for par in range(2):
                base = 64 * par
                for c in range(4):
                    hh = 2 * c + par
                    if outp:
                        MM(ps[yb_[par]][rows, hh * 64:(hh + 1) * 64],
                           rpT[base:base + 64, c, ch2 * 64:ch2 * 64 + 64],
                           Tc[base:base + 64, c, :], True, True, [('rpT', par), kc], ['ps%d' % yb_[par]])
                    MM(ps[tb[par]][base:base + 64, c * 64:(c + 1) * 64], GT[base:base + 64, c, ch2, :],
                       Tc[base:base + 64, c, :], True, True, ['GT', kc], ['ps%d' % tb[par]])
            for par in range(2):
                base = 64 * par
                if outp:
                    yv = ytile.rearrange("p (i q d) -> p i q d", i=4, q=2)
                    y0v = y0.rearrange("p (i q d) -> p i q d", i=4, q=2)
                    pyv = ps[yb_[par]][:].rearrange("p (i q d) -> p i q d", i=4, q=2)
                    TTo('dve', yv[rows, :, par, :], pyv[rows, :, par, :], y0v[rows, :, par, :], ALU.add,
                        ['ps%d' % yb_[par], ('y0', 0), ('y0', 1)], ['ytile'])
                TTo('dve', Tn[base:base + 64, :, :],
                    ps[tb[par]][base:base + 64, 0:256].rearrange("p (a b) -> p a b", a=4),
                    Hs[base:base + 64, :, ch2, :], ALU.add, ['ps%d' % tb[par], 'Hs'], [kn])
            cur = 1 - cur
            pump(1)
        if outp and not final:
            DMA('sp', yf_d[gt * 128:(gt + 1) * 128, :], ytile, ['ytile'], ['yf'])
        if outp and final:
            DMA('sp', yfb, yf_d[gt * 128:(gt + 1) * 128, :], ['yf'], ['yfb'])
            y = ytile
            y3 = y.rearrange("p (h d) -> p h d", h=8)
            TTo('dve', y, y, yfb, ALU.add, ['ytile', 'yfb'], ['ytile'])
            P.add('dve', lambda e: e.tensor_reduce(out=s1, in_=y3, axis=AX.X, op=ALU.add), r=['ytile'], w=['s1'])
            TS('dve', s1, s1, -1.0 / 64, None, ALU.mult, None, ['s1'], ['s1'])
            TTo('dve', y3, y3, s1.unsqueeze(2).broadcast_to([128, 8, 64]), ALU.add, ['ytile', 's1'], ['ytile'])
            ACT(sqb, y, AF.Square, ['ytile'], ['sqb'])
            P.add('dve', lambda e: e.tensor_reduce(out=s2, in_=sqb.rearrange("p (h d) -> p h d", h=8), axis=AX.X,
                                                   op=ALU.add), r=['sqb'], w=['s2'])
            ACT(s2, s2, AF.Sqrt, ['s2', 'eps'], ['s2'], bias=gneps_t, scale=1.0 / 64)
            RECIP(s2, s2, ['s2'], ['s2'])
            TTo('dve', y3, y3, s2.unsqueeze(2).broadcast_to([128, 8, 64]), ALU.mult, ['ytile', 's2'], ['ytile'])
            TTo('dve', y, y, rep[:, 1, :], ALU.mult, ['ytile', 'rep'], ['ytile'])
            TTo('dve', y, y, rep[:, 2, :], ALU.add, ['ytile', 'rep'], ['ytile'])
            for c in range(4):
                MM(ps[2][:, 2 * c:2 * c + 2], bprod[:, c, cols], hsel, True, True, [('bprod', c, pb), 'rwc'], ['ps2'])
            CP('dve', bon, ps[2][:, 0:8], ['ps2'], ['bon'])
            TTo('dve', tmpv.rearrange("p (h d) -> p h d", h=8), VT32.rearrange("p (h d) -> p h d", h=8),
                bon.unsqueeze(2).broadcast_to([128, 8, 64]), ALU.mult, ['VT32', 'bon'], ['tmpv'])
            TTo('dve', y, y, tmpv, ALU.add, ['ytile', 'tmpv'], ['ytile'])
            MM(ps[3][:], sgd[:, cols], g2sb, True, True, [('sgd', pb), 'lora'], ['ps3'])
            TTo('dve', y, y, ps[3][:], ALU.mult, ['ytile', 'ps3'], ['ytile'])
            DMA('sp', yc_d[gt * 128:(gt + 1) * 128, 512:1024], y, ['ytile'], ['yc'])
        return cur

    sched = [(g, 0, True, False) for g in range(NG_OWN)]
    if debug_stop != 'F':
        sched += [(g, 1, True, True) for g in range(NG_OWN - 1, -1, -1)]
    cur = 0
    MEMSET('pool', Tst[0], 0.0, ['T0'])
    for _ in prep_group(*sched[0], 0):
        pass
    for idx, (g, di, own, final) in enumerate(sched):
        pb = idx % 2
        pg = prep_group(*sched[idx + 1], (idx + 1) % 2) if idx + 1 < len(sched) else None
        st = {'alive': pg is not None}

        def pump(n, pg=pg, st=st):
            for _ in range(n):
                if st['alive']:
                    try:
                        next(pg)
                    except StopIteration:
                        st['alive'] = False
        if idx > 0 and sched[idx - 1][1] != di:
            kc_ = 'T%d' % cur
            tflat = Tst[cur].rearrange("p a b -> p (a b)")
            DMA('sp', E.tf_d, tflat, [kc_], ['tf_d'])
            E.CC(E.tf_d, E.tfg_d, ['tf_d'], ['tfg'])
            DMA('sp', tfr, E.tfg_d.rearrange("(r p) f -> p r f", r=2), ['tfg'], ['tfr'])
            TS('dve', tflat, tfr[:, 0, :], E.hsel[:, 0:1], None, ALU.mult, None, ['tfr', 'hsel'], [kc_])
            STT(tflat, tfr[:, 1, :], E.hsel[:, 1:2], tflat, ALU.mult, ALU.add, ['tfr', 'hsel', kc_], [kc_])
        tls = range(NTG) if di == 0 else range(NTG - 1, -1, -1)
        for tl in tls:
            cur = tile_proc(di, g, tl, own, final, cur, pb, pump)
        pump(10 ** 6)


def _f(a):
    return np.ascontiguousarray(a, dtype=np.float32)


def _lhs_chunks(W):
    n = W.shape[1] // 128
    return _f(W.reshape(8, 128, n, 128).transpose(2, 1, 0, 3))


def _attn_masks():
    p = np.arange(128)[:, None]
    c = np.arange(128)[None, :]
    cntm = {}
    for j in range(-8, 9):
        d = 128 * j + p - c
        m = (np.abs(d) <= 64).astype(np.float32)
        m += ((d % 4 == 0) & (np.abs(d) <= 256)).astype(np.float32)
        m += ((d % 16 == 0) & (np.abs(d) <= 1024)).astype(np.float32)
        cntm[j] = m
    wm = np.zeros((20, 128, 512), np.float32)
    for J in range(20):
        for i in range(4):
            j = J - 8 - i
            if abs(j) <= 8:
                wm[J, :, i * 128:(i + 1) * 128] = cntm[j]
    return wm.astype(ml_dtypes.bfloat16)


def _rope_tables(rev):
    pos = np.arange(S, dtype=np.float32)
    if rev:
        pos = pos[::-1].copy()
    inv_freq = (np.float32(10000.0) ** (-np.arange(0, 64, 2, dtype=np.float32) / np.float32(64))).astype(np.float32)
    ang = (pos[:, None] * inv_freq[None, :]).astype(np.float32)
    cos, sin = np.cos(ang).astype(np.float32), np.sin(ang).astype(np.float32)
    idx = np.arange(128) % 32
    sign = np.where((np.arange(128) % 64) < 32, -1.0, 1.0).astype(np.float32)
    cosT = cos[:, idx].T
    sinT = (sin[:, idx] * sign[None, :]).T
    return _f(cosT), _f(sinT)


def host_inputs(inputs):
    g = lambda k: np.asarray(inputs[k][0], np.float32)
    x = np.asarray(inputs["x"], dtype=np.float32)
    shared = {}
    ffn_names = {1: ("ffn1_w_gate", "ffn1_w_up", "ffn1_w_down"), 2: ("ffn2_w_gate", "ffn2_w_up", "ffn2_w_down")}
    for k in (1, 2):
        wg, wu, wd = (g(nm) for nm in ffn_names[k])
        gg = wg.reshape(8, 128, NF, 128).transpose(2, 1, 0, 3)
        uu = wu.reshape(8, 128, NF, 128).transpose(2, 1, 0, 3)
        shared["wgu%d" % k] = _f(np.stack([gg, uu], axis=2))
        shared["wdc%d" % k] = _f(wd.reshape(NF, 128, 8, 128).transpose(2, 1, 0, 3))
    gnames = ["ffn1_pre_g", "ffn1_post_g", "mix_pre_g", "mix_post_g", "ffn2_pre_g", "ffn2_post_g"]
    shared["gv"] = _f(np.stack([g(nm).reshape(8, 128).T for nm in gnames], axis=1))
    shared["ident"] = np.eye(128, dtype=np.float32)
    w_in = g("w_in")
    swap = np.concatenate([(np.arange(64) + 32) % 64 + 64 * h for h in range(8)])
    shared["wv"] = _f(w_in[:, 1024:1536].reshape(8, 128, 512).transpose(1, 0, 2))
    shared["wo"] = _lhs_chunks(g("w_out"))
    shared["wm"] = _attn_masks()
    shared["wmf"] = np.ascontiguousarray(shared["wm"][:, ::-1, :])
    shared["rep"] = _f(np.stack([np.broadcast_to(g(nm)[None, :], (128, 512))
                                 for nm in ("attn_out_g", "rwkv_lnx_w", "rwkv_lnx_b")], axis=1))
    maps = []
    for c in range(8):
        b, h = c // 2, c % 2
        m = dict(shared)
        m["x"] = _f(x[b] if h == 0 else x[b, ::-1])
        cosT, sinT = _rope_tables(h == 1)
        m["cosT"], m["sinT"] = cosT, sinT
        m["hsel"] = _f(np.tile(np.array([[float(h), float(1 - h)]], np.float32), (128, 1)))
        m.update(_rwkv_host(inputs, h, w_in, swap))
        maps.append(m)
    return maps


def _rwkv_host(inputs, h, w_in, swap):
    g = lambda k: np.asarray(inputs[k][0], np.float32)
    dirs = [0, 1] if h == 0 else [1, 0]
    wq, wk = w_in[:, 0:512], w_in[:, 512:1024]
    cols = []
    for gi in range(4):
        cols.append(wq[:, gi * 128:(gi + 1) * 128])
        cols.append(wq[:, swap][:, gi * 128:(gi + 1) * 128])
    for gi in range(4):
        cols.append(wk[:, gi * 128:(gi + 1) * 128])
        cols.append(wk[:, swap][:, gi * 128:(gi + 1) * 128])
    wr = w_in[:, 1536:]
    rw = [wr[:, 0:1536]]
    for off in (1536, 1664):
        blk = wr[:, off:off + 128]
        rw.append(np.concatenate([blk[:, 64 * dirs[0]:64 * dirs[0] + 64], blk[:, 64 * dirs[1]:64 * dirs[1] + 64]], axis=1))
    rw.append(wr[:, 1792:1920])
    Wall = np.concatenate(cols + rw, axis=1)
    out = {"win": _lhs_chunks(Wall)}
    mp, mn = g("rwkv_mu_prev"), g("rwkv_mu_next")
    if h == 1:
        mp, mn = mn, mp

    def fix(v):
        v = v.copy()
        for off in (1536, 1664):
            blk = v[off:off + 128].copy()
            v[off:off + 128] = np.concatenate([blk[64 * dirs[0]:64 * dirs[0] + 64], blk[64 * dirs[1]:64 * dirs[1] + 64]])
        return v
    mp, mn = fix(mp), fix(mn)
    rwp = np.zeros((128, 64), np.float32)
    rwp[:, 0:15] = mp.reshape(15, 128).T
    rwp[:, 15:30] = mn.reshape(15, 128).T
    w0, a0 = g("rwkv_w0"), g("rwkv_a0")
    for di, d in enumerate(dirs):
        rwp[:, 30 + 4 * di:34 + 4 * di] = w0[d].reshape(4, 128).T
        rwp[:, 38 + 4 * di:42 + 4 * di] = a0[d].reshape(4, 128).T
    rwp[:, 46:50] = g("rwkv_k_k").reshape(4, 128).T
    rwp[:, 50:54] = g("rwkv_k_a").reshape(4, 128).T
    rwp[:, 54:58] = g("rwkv_r_k").reshape(4, 128).T
    out["rwpar"] = rwp
    w2, a2 = g("rwkv_w2"), g("rwkv_a2")
    lora = np.zeros((128, 3, 512), np.float32)
    lora[:, 0, :] = np.concatenate([w2[dirs[0]], w2[dirs[1]]], axis=0)
    lora[:, 1, :] = np.concatenate([a2[dirs[0]], a2[dirs[1]]], axis=0)
    lora[:, 2, :] = g("rwkv_g2")
    out["lora"] = lora
    idx = np.arange(128)
    same = (idx[:, None] // 64) == (idx[None, :] // 64)
    B0 = ((idx[None, :] < idx[:, None]) & same).astype(np.float32)
    I = np.eye(128, dtype=np.float32)
    rwm = np.zeros((128, 2, 5, 128), np.float32)
    for d, Bd in enumerate((B0, B0.T)):
        rwm[:, d, 0] = -Bd
        rwm[:, d, 1] = -Bd.T
        rwm[:, d, 2] = Bd.T
        rwm[:, d, 3] = Bd.T + I
        rwm[:, d, 4] = -(Bd.T + I)
    out["rwmask"] = rwm
    rwc = np.zeros((128, 1218), np.float32)
    rwc[:, 0:512] = (np.arange(512) % 64 != 0).astype(np.float32)[None, :]
    rwc[:, 1024:1152] = same.astype(np.float32)
    rwc[:, 1152:1216] = (idx[:, None] % 64 == np.arange(64)[None, :]).astype(np.float32)
    rwc[:, 1216] = (idx < 64)
    rwc[:, 1217] = (idx >= 64)
    out["rwconst"] = rwc
    return out


_CACHE = {}


def kernel(**inputs):
    if 'nc' not in _CACHE:
        _CACHE['nc'] = build()[0]
    nc = _CACHE['nc']
    maps = host_inputs(inputs)
    res = run_bass_kernel_spmd(nc, maps, core_ids=list(range(8)))
    out = np.zeros((4, S, D), np.float32)
    for c in range(8):
        b, h = c // 2, c % 2
        o = np.asarray(res.results[c]["out"])
        if h == 0:
            out[b, :OWN] = o
        else:
            out[b, OWN:] = o[::-1]
    return out
```

```python
import contextlib
import numpy as np
import ml_dtypes
import concourse.bass as bass
import concourse.mybir as mybir
from concourse.bass_utils import run_bass_kernel_spmd

F32 = mybir.dt.float32
BF16 = mybir.dt.bfloat16
AF = mybir.ActivationFunctionType
ALU = mybir.AluOpType
AX = mybir.AxisListType

D = 1024
DFF = 2816
NF = DFF // 128
S = 4096
OWN = 2048
TT = 512
EPS = 1e-6
GN_EPS = 64e-5
CDEC = float(np.exp(-0.5))
NKT = 16


class Prog:
    NDMA = 16

    def __init__(self):
        self.ops = []
        self.last_w = {}
        self.readers = {}
        self.last_barrier = 0
        self.warn = []
        self.pe_hist = []

    def add(self, eng, fn, r=(), w=(), dma=False, cc=False):
        i = len(self.ops)
        deps = set()
        for k in r:
            j = self.last_w.get(k)
            if j is not None:
                deps.add((j, 'raw'))
        for k in w:
            j = self.last_w.get(k)
            if j is not None:
                deps.add((j, 'waw'))
            for j in self.readers.get(k, ()):
                deps.add((j, 'war'))
        self.ops.append(dict(eng=eng, fn=fn, deps=deps, dma=dma, cc=cc))
        for k in w:
            if isinstance(k, str) and k.startswith('ps'):
                engs = set(self.ops[j]['eng'] for j in self.readers.get(k, ()))
                if len(engs) > 1:
                    self.warn.append(('multi-engine psum readers', k, sorted(engs), i))
        for k in r:
            self.readers.setdefault(k, []).append(i)
        for k in w:
            self.last_w[k] = i
            self.readers[k] = []
        return i

    def barrier(self):
        n = len(self.ops)
        deps = set()
        last = {}
        for i in range(n):
            op = self.ops[i]
            if op['dma']:
                if i >= self.last_barrier:
                    deps.add((i, 'raw'))
            elif op['fn'] is not None:
                last[op['eng']] = i
        for e, i in last.items():
            deps.add((i, 'raw'))
        for e in ['pe', 'act', 'dve', 'pool', 'sp']:
            self.ops.append(dict(eng=e, fn=None, deps=set(deps), dma=False, cc=False))
        self.last_barrier = n

    def emit(self, nc, stack):
        ops = self.ops
        n = len(ops)
        need = [set() for _ in range(n)]
        signaled = [False] * n
        for i, op in enumerate(ops):
            for (j, kind) in op['deps']:
                if j == i:
                    continue
                pj = ops[j]
                same = (pj['eng'] == op['eng'])
                if same and op['eng'] == 'pe' and not pj['dma'] and not op['dma'] and op['fn'] is not None:
                    if kind != 'raw':
                        continue
                need[i].add(j)
            latest = {}
            for j in need[i]:
                pj = ops[j]
                if pj['dma']:
                    continue
                e2 = pj['eng']
                if e2 not in latest or j > latest[e2]:
                    latest[e2] = j
            need[i] = set(j for j in need[i] if ops[j]['dma'] or latest[ops[j]['eng']] == j)
            for j in need[i]:
                signaled[j] = True
        engs = ['pe', 'act', 'dve', 'pool', 'sp']
        csem = {e: stack.enter_context(nc.semaphore("s_" + e)) for e in engs[:4]}
        dsem = {e: [stack.enter_context(nc.semaphore("d_%s%d" % (e, k))) for k in range(self.NDMA)]
                for e in ['sp', 'pool']}
        sig = [None] * n
        ccount = {e: 0 for e in engs}
        dcount = {e: 0 for e in engs}
        prevuse = [None] * n
        for i, op in enumerate(ops):
            e = op['eng']
            if op.get('cc'):
                sig[i] = (stack.enter_context(nc.semaphore("cc_%d" % i)), 1)
            elif op['dma']:
                k = dcount[e]
                dcount[e] += 1
                s = dsem[e][k % self.NDMA]
                sig[i] = (s, 16 * (k // self.NDMA + 1))
                if k >= self.NDMA:
                    prevuse[i] = (s, 16 * (k // self.NDMA))
            elif signaled[i]:
                ccount[e] += 1
                sig[i] = (csem[e], ccount[e])
        per = {e: [i for i in range(n) if ops[i]['eng'] == e] for e in engs}
        self.stats = {e: len(per[e]) for e in engs}
        self.stats['sem'] = dict(ccount)

        def run(e, eng):
            waited = {}
            for i in per[e]:
                op = ops[i]
                ws = []
                if prevuse[i] is not None:
                    ws.append(prevuse[i])
                for j in need[i]:
                    ws.append(sig[j])
                best = {}
                for (s, v) in ws:
                    key = id(s)
                    if v > best.get(key, (None, -1))[1]:
                        best[key] = (s, v)
                for key, (s, v) in best.items():
                    if waited.get(key, -1) >= v:
                        continue
                    eng.wait_ge(s, v)
                    waited[key] = v
                if op['fn'] is None:
                    continue
                ins = op['fn'](eng)
                if sig[i] is not None:
                    if op.get('cc'):
                        ins.then_inc(sig[i][0])
                    else:
                        ins.then_inc(sig[i][0], 16 if op['dma'] else 1)

        with nc.Block() as block:
            @block.tensor
            def _(eng):
                run('pe', eng)

            @block.scalar
            def _(eng):
                run('act', eng)

            @block.vector
            def _(eng):
                run('dve', eng)

            @block.gpsimd
            def _(eng):
                run('pool', eng)

            @block.sync
            def _(eng):
                run('sp', eng)


class Arena:
    def __init__(self, tensor, nbytes):
        self.t = tensor
        self.n = nbytes
        self.off = 0

    def alloc(self, shape, dt=F32):
        esz = mybir.dt.size(dt)
        nel = int(np.prod(shape[1:]))
        nb = (nel * esz + 31) // 32 * 32
        assert self.off + nb <= self.n, ("arena overflow", self.off, nb, self.n)
        ap = self.t[:, self.off // 4:(self.off + nb) // 4]
        self.off += nb
        if dt != F32:
            ap = ap.bitcast(dt)
        ap = ap[:, 0:nel]
        if len(shape) == 3:
            ap = ap.rearrange("p (a b) -> p a b", a=shape[1])
        elif len(shape) == 4:
            ap = ap.rearrange("p (a b c) -> p a b c", a=shape[1], b=shape[2])
        return ap


class Env:
    pass


def build(debug=False, phases=('p1', 'att', 'rwkv', 'p4'), ntiles1=8, rwkv_stop=None):
    nc = bass.Bass("TRN2", target_bir_lowering=False)
    P = Prog()
    stack = contextlib.ExitStack()
    E = Env()

    def din(name, shape, dt=F32):
        return nc.dram_tensor(name, list(shape), dt, kind="ExternalInput").ap()

    def dscr(name, shape, dt=F32, out=False):
        if out:
            return nc.dram_tensor(name, list(shape), dt, kind="ExternalOutput").ap()
        return nc.dram_tensor(name, list(shape), dt).ap()

    x_d = din("x", [S, D])
    wgu_d = [din("wgu%d" % k, [NF, 128, 2, 8, 128]) for k in (1, 2)]
    wdc_d = [din("wdc%d" % k, [8, 128, NF, 128]) for k in (1, 2)]
    win_d = din("win", [31, 128, 8, 128])
    wv_d = din("wv", [128, 8, 512])
    wo_d = din("wo", [8, 128, 8, 128])
    gv_d = din("gv", [128, 6, 8])
    ident_d = din("ident", [128, 128])
    cos_d = din("cosT", [128, S])
    sin_d = din("sinT", [128, S])
    wm_d = din("wm", [20, 128, 512], BF16)
    rep_d = din("rep", [128, 3, 512])
    rwm_d = din("rwmask", [128, 2, 5, 128])
    rwc_d = din("rwconst", [128, 1218])
    rwp_d = din("rwpar", [128, 64])
    lora_d = din("lora", [128, 3, 512])
    out_d = nc.dram_tensor("out", [OWN, D], F32, kind="ExternalOutput").ap()
    h1s_d = dscr("h1s", [8, 128, OWN], out=debug)
    zr_d = dscr("zr", [15, 128, S], out=debug)
    qT_d = dscr("qTs", [4, 128, OWN], BF16, out=debug)
    kT_d = dscr("kTs", [4, 128, NKT * 128], BF16, out=debug)
    v_d = dscr("vs", [NKT * 128, 520], BF16, out=debug)
    yc_d = dscr("yc", [OWN, D], out=debug)
    yf_d = dscr("yf", [OWN, 512], out=debug)
    hsel_d = din("hsel", [128, 2])
    wmf_d = din("wmf", [20, 128, 512], BF16)
    kh_d = dscr("kh", [512, 1024], BF16)
    vh_d = dscr("vh", [1024, 520], BF16)
    zb_d = dscr("zb", [128, 15])
    khg_d = dscr("khg", [1024, 1024], BF16)
    vhg_d = dscr("vhg", [2048, 520], BF16)
    zbg_d = dscr("zbg", [256, 15])
    tf_d = dscr("tf", [128, 256])
    tfg_d = dscr("tfg", [256, 256])
    wgu16_d = [dscr("wgu16_%d" % k, [NF, 128, 2 * 8 * 128], BF16) for k in (1, 2)]
    wdc16_d = [dscr("wdc16_%d" % k, [8, 128, NF * 128], BF16) for k in (1, 2)]
    win16_d = dscr("win16", [31, 128, 8 * 128], BF16)
    wo16_d = dscr("wo16", [8, 128, 8 * 128], BF16)
    PAIRS = [[0, 1], [2, 3], [4, 5], [6, 7]]

    def CC(in_ap, out_ap, r, w):
        return P.add('pool', lambda e: e.collective_compute("AllGather", ALU.bypass, replica_groups=PAIRS,
                                                            ins=[in_ap.opt()], outs=[out_ap.opt()]),
                     r=r, w=w, dma=True, cc=True)

    ARENA_BYTES = 204 * 1024
    arena_t = stack.enter_context(nc.sbuf_tensor("arena", [128, ARENA_BYTES // 4], F32))
    A = Arena(arena_t, ARENA_BYTES)
    ps = [stack.enter_context(nc.psum_tensor("ps%d" % i, [128, 512], F32)) for i in range(8)]

    def MM(out, lhsT, rhs, start, stop, r, w):
        rb0, kk_ = lhsT.base_partition(), lhsT.shape[0]
        for (pb0, pk, pw) in P.pe_hist[-1:]:
            if (rb0 + kk_ <= pb0 or pb0 + pk <= rb0) and pw == w[0]:
                P.warn.append(('row-group conflict', w[0], (pb0, pk), (rb0, kk_), len(P.ops)))
        P.pe_hist.append((rb0, kk_, w[0]))
        P.add('pe', lambda e: e.matmul(out, lhsT, rhs, start=start, stop=stop), r=r, w=w)

    def TR(out, in_, idt, r, w):
        P.pe_hist.append((in_.base_partition(), in_.shape[0], w[0]))
        P.add('pe', lambda e: e.transpose(out, in_, idt), r=r, w=w)

    def ACT(out, in_, func, r, w, bias=None, scale=None):
        kw = {}
        if bias is not None:
            kw['bias'] = bias
        if scale is not None:
            kw['scale'] = scale
        P.add('act', lambda e: e.activation(out, in_, func, **kw), r=r, w=w)

    def TTo(eng, out, in0, in1, op, r, w):
        P.add(eng, lambda e: e.tensor_tensor(out=out, in0=in0, in1=in1, op=op), r=r, w=w)

    def TS(eng, out, in0, s1, s2, op0, op1, r, w):
        if op1 is None:
            P.add(eng, lambda e: e.tensor_scalar(out=out, in0=in0, scalar1=s1, scalar2=None, op0=op0), r=r, w=w)
        else:
            P.add(eng, lambda e: e.tensor_scalar(out=out, in0=in0, scalar1=s1, scalar2=s2, op0=op0, op1=op1),
                  r=r, w=w)

    def STT(out, in0, scalar, in1, op0, op1, r, w):
        P.add('dve', lambda e: e.scalar_tensor_tensor(out=out, in0=in0, scalar=scalar, in1=in1, op0=op0, op1=op1),
              r=r, w=w)

    def CP(eng, out, in_, r, w):
        if eng == 'act':
            P.add('act', lambda e: e.copy(out, in_), r=r, w=w)
        else:
            P.add(eng, lambda e: e.tensor_copy(out, in_), r=r, w=w)

    def RECIP(out, in_, r, w):
        P.add('dve', lambda e: e.reciprocal(out, in_), r=r, w=w)

    def DMA(q, out, in_, r, w):
        return P.add(q, lambda e: e.dma_start(out=out, in_=in_), r=r, w=w, dma=True)

    def MEMSET(eng, ap, val, w):
        P.add(eng, lambda e: e.memset(ap, val), w=w)

    ident = A.alloc([128, 128])
    identb = A.alloc([128, 128], BF16)
    ones = A.alloc([128, 128], BF16)
    gv = A.alloc([128, 6, 8])
    eps_t = A.alloc([128, 1])
    gneps_t = A.alloc([128, 1])
    rep = A.alloc([128, 3, 512])
    rwm = A.alloc([128, 2, 5, 128])
    rwc = A.alloc([128, 1218])
    rwp = A.alloc([128, 64])
    lora = A.alloc([128, 3, 512])
    hsel = A.alloc([128, 2])
    DMA('sp', ident, ident_d, [], ['ident'])
    DMA('sp', gv, gv_d, [], ['gv'])
    DMA('sp', rep, rep_d, [], ['rep'])
    DMA('sp', rwm, rwm_d, [], ['rwm'])
    DMA('sp', rwc, rwc_d, [], ['rwc'])
    DMA('sp', rwp, rwp_d, [], ['rwp'])
    DMA('sp', lora, lora_d, [], ['lora'])
    DMA('sp', hsel, hsel_d, [], ['hsel'])
    MEMSET('pool', ones, 1.0, ['ones'])
    MEMSET('pool', eps_t, EPS, ['eps'])
    MEMSET('pool', gneps_t, GN_EPS, ['eps'])
    CP('dve', identb, ident, ['ident'], ['identb'])
    A_MARK = A.off

    cnt = {'ps01': 0, 'x': 0, 'z': 0, 'st': 0}
    for k_, v_ in list(locals().items()):
        setattr(E, k_, v_)

    def alloc_ffn_bufs():
        B = {}
        B['xin'] = [A.alloc([128, D]) for _ in range(2)]
        B['xT'] = A.alloc([128, 8, TT])
        B['hn'] = A.alloc([128, 8, TT], BF16)
        B['sq'] = A.alloc([128, 8, TT], BF16)
        B['fb'] = A.alloc([128, 8, TT])
        B['aT'] = A.alloc([128, NF, TT], BF16)
        B['rstd'] = A.alloc([128, TT])
        B['tmp'] = A.alloc([128, TT])
        B['wgu'] = [A.alloc([128, 2, 8, 128], BF16) for _ in range(3)]
        B['wdc'] = [A.alloc([128, NF, 128], BF16) for _ in range(3)]
        B['sgl'] = [A.alloc([128, TT]) for _ in range(2)]
        return B

    def load_T(B, src_d, tt, dst, dstkey):
        xin = B['xin']
        for sub in range(4):
            b = cnt['x'] % 2
            cnt['x'] += 1
            t0 = tt * TT + sub * 128
            DMA('sp', xin[b], src_d[t0:t0 + 128, :], ['yc'] if src_d is yc_d else [], ['xin%d' % b])
            for half in range(2):
                pb = cnt['ps01'] % 2
                cnt['ps01'] += 1
                for q in range(4):
                    c = half * 4 + q
                    TR(ps[pb][:, q * 128:(q + 1) * 128], xin[b][:, c * 128:(c + 1) * 128], ident,
                       ['xin%d' % b, 'ident'], ['ps%d' % pb])
                src = ps[pb][:].rearrange("p (q t) -> p q t", q=4)
                d_ = dst[:, half * 4:(half + 1) * 4, sub * 128:(sub + 1) * 128]
                CP('act' if half == 0 else 'dve', d_, src, ['ps%d' % pb], [dstkey])

    def rmsnorm_stats(B, src, srckey):
        sq, tmp, rstd = B['sq'], B['tmp'], B['rstd']
        ACT(sq, src, AF.Square, [srckey], ['sq'])
        pb = cnt['ps01'] % 2
        cnt['ps01'] += 1
        for c in range(8):
            MM(ps[pb][:], ones, sq[:, c, :], c == 0, c == 7, ['sq', 'ones'], ['ps%d' % pb])
        ACT(tmp, ps[pb][:], AF.Sqrt, ['ps%d' % pb, 'eps'], ['tmp'], bias=eps_t, scale=1.0 / D)
        RECIP(rstd, tmp, ['tmp'], ['rstd'])

    def prenorm(B, gidx):
        xT, hn, rstd = B['xT'], B['hn'], B['rstd']
        rmsnorm_stats(B, xT, 'xT')
        for c in range(8):
            STT(hn[:, c, :], xT[:, c, :], gv[:, gidx, c:c + 1], rstd, ALU.mult, ALU.mult,
                ['xT', 'gv', 'rstd'], ['hn'])

    def ffn(B, k, gpre, gpost, first):
        xT, hn, fb, aT, rstd = B['xT'], B['hn'], B['fb'], B['aT'], B['rstd']
        wgu, wdc, sgl = B['wgu'], B['wdc'], B['sgl']
        prenorm(B, gpre)

        def ld_gu(j):
            b = j % 3
            flat = wgu[b].rearrange("p a c f -> p (a c f)")
            if first:
                DMA('pool', wgu[b], wgu_d[k][j], [], ['wgu%d' % b])
                DMA('sp', wgu16_d[k][j], flat, ['wgu%d' % b], [('wgu16', k, j)])
            else:
                DMA('sp', flat, wgu16_d[k][j], [('wgu16', k, j)], ['wgu%d' % b])

        def ld_d(c):
            b = c % 3
            flat = wdc[b].rearrange("p j f -> p (j f)")
            if first:
                DMA('pool', wdc[b], wdc_d[k][c], [], ['wdc%d' % b])
                DMA('sp', wdc16_d[k][c], flat, ['wdc%d' % b], [('wdc16', k, c)])
            else:
                DMA('sp', flat, wdc16_d[k][c], [('wdc16', k, c)], ['wdc%d' % b])
        ld_gu(0)
        ld_gu(1)
        for j in range(NF):
            if j + 2 < NF:
                ld_gu(j + 2)
            if j == NF - 3:
                ld_d(0)
            if j == NF - 1:
                ld_d(1)
            b = j % 3
            pg = 2 + (j % 2) * 2
            pu = pg + 1
            for which, pbank in ((0, pg), (1, pu)):
                for c in range(8):
                    MM(ps[pbank][:], wgu[b][:, which, c, :], hn[:, c, :], c == 0, c == 7,
                       ['wgu%d' % b, 'hn'], ['ps%d' % pbank])
            sgb = j % 2
            ACT(sgl[sgb], ps[pg][:], AF.Silu, ['ps%d' % pg], ['sgl%d' % sgb])
            TTo('dve', aT[:, j, :], sgl[sgb], ps[pu][:], ALU.mult, ['sgl%d' % sgb, 'ps%d' % pu], [('aT', j)])
        for c in range(8):
            if c + 2 < 8:
                ld_d(c + 2)
            b = c % 3
            pf = 6 + (c % 2)
            for j in range(NF):
                MM(ps[pf][:], wdc[b][:, j, :], aT[:, j, :], j == 0, j == NF - 1,
                   ['wdc%d' % b, ('aT', j)], ['ps%d' % pf])
            CP('act' if c % 2 == 0 else 'dve', fb[:, c, :], ps[pf][:], ['ps%d' % pf], ['fb'])
        rmsnorm_stats(B, fb, 'fb')
        for c in range(8):
            STT(fb[:, c, :], fb[:, c, :], gv[:, gpost, c:c + 1], rstd, ALU.mult, ALU.mult,
                ['fb', 'gv', 'rstd'], ['fb'])
            STT(xT[:, c, :], fb[:, c, :], 0.5, xT[:, c, :], ALU.mult, ALU.add, ['fb', 'xT'], ['xT'])

    if 'p1' in phases:
        B = alloc_ffn_bufs()
        winb = [A.alloc([128, 8, 128], BF16) for _ in range(4)]
        wvb = A.alloc([128, 8, 512], BF16)
        cosb = A.alloc([128, TT])
        sinb = A.alloc([128, TT])
        ra = A.alloc([128, TT])
        rb = A.alloc([128, TT])
        qst = [A.alloc([128, TT], BF16) for _ in range(2)]
        vst = [A.alloc([128, 8, 65], BF16) for _ in range(2)]
        zst = [A.alloc([128, TT]) for _ in range(2)]
        xT, hn = B['xT'], B['hn']
        DMA('pool', wvb, wv_d, [], ['wvb'])
        for b in range(2):
            MEMSET('pool', vst[b], 1.0, ['vst%d' % b])
        zbank = [2, 3, 4, 5]
        for tt in range(min(ntiles1, 4)):
            load_T(B, x_d, tt, xT, 'xT')
            ffn(B, 0, 0, 1, tt == 0)
            if tt < 4:
                DMA('sp', h1s_d[:, :, tt * TT:(tt + 1) * TT].rearrange("c p t -> p c t"), xT, ['xT'], ['h1s'])
            prenorm(B, 2)
            if tt < 6:
                DMA('sp', cosb, cos_d[:, tt * TT:(tt + 1) * TT], [], ['cosb'])
                DMA('sp', sinb, sin_d[:, tt * TT:(tt + 1) * TT], [], ['sinb'])
            jobs = []
            if tt < 4:
                jobs += [('q', g, [2 * g, 2 * g + 1]) for g in range(4)]
            if tt < 6:
                jobs += [('k', g, [8 + 2 * g, 9 + 2 * g]) for g in range(4)]
            jobs += [('z', j, [16 + j]) for j in range(15)]
            loads = [ci for (_, _, cis) in jobs for ci in cis]

            def ldw(n):
                if n < len(loads):
                    b = n % 4
                    flat = winb[b].rearrange("p c f -> p (c f)")
                    if tt == 0:
                        DMA('pool', winb[b], win_d[loads[n]], [], ['winb%d' % b])
                        DMA('sp', win16_d[loads[n]], flat, ['winb%d' % b], [('win16', loads[n])])
                    else:
                        DMA('sp', flat, win16_d[loads[n]], [('win16', loads[n])], ['winb%d' % b])
            for n in range(3):
                ldw(n)
            li = 0
            for (kind, g, cis) in jobs:
                banks = []
                for ci in cis:
                    ldw(li + 3)
                    b = li % 4
                    li += 1
                    zb = zbank[cnt['z'] % 4]
                    cnt['z'] += 1
                    banks.append(zb)
                    for c in range(8):
                        MM(ps[zb][:], winb[b][:, c, :], hn[:, c, :], c == 0, c == 7,
                           ['winb%d' % b, 'hn'], ['ps%d' % zb])
                sb_ = cnt['st'] % 2
                cnt['st'] += 1
                if kind in ('q', 'k'):
                    TTo('dve', ra, ps[banks[0]][:], cosb, ALU.mult, ['ps%d' % banks[0], 'cosb'], ['ra'])
                    TTo('dve', rb, ps[banks[1]][:], sinb, ALU.mult, ['ps%d' % banks[1], 'sinb'], ['rb'])
                    TTo('pool', qst[sb_], ra, rb, ALU.add, ['ra', 'rb'], ['qst%d' % sb_])
                    dst = qT_d if kind == 'q' else kT_d
                    DMA('sp', dst[g, :, tt * TT:(tt + 1) * TT], qst[sb_], ['qst%d' % sb_], ['qk_d'])
                    if kind == 'k' and tt >= 2:
                        DMA('sp', kh_d[g * 128:(g + 1) * 128, (tt - 2) * TT:(tt - 1) * TT], qst[sb_],
                            ['qst%d' % sb_], ['kh_d'])
                else:
                    CP('act' if g % 2 == 0 else 'dve', zst[sb_], ps[banks[0]][:], ['ps%d' % banks[0]],
                       ['zst%d' % sb_])
                    DMA('sp', zr_d[g, :, tt * TT:(tt + 1) * TT], zst[sb_], ['zst%d' % sb_], ['zr'])
                    if tt == 3:
                        P.add('sp', (lambda e, g=g, sb_=sb_: e.dma_start(
                            out=zb_d[:, g:g + 1], in_=zst[sb_][:, TT - 1:TT], allow_slow_non_contiguous=True)),
                            r=['zst%d' % sb_], w=['zb_d'], dma=True)
            if tt < 6:
                for sub in range(4):
                    zb = zbank[cnt['z'] % 4]
                    cnt['z'] += 1
                    for c in range(8):
                        MM(ps[zb][:], hn[:, c, sub * 128:(sub + 1) * 128], wvb[:, c, :], c == 0, c == 7,
                           ['hn', 'wvb'], ['ps%d' % zb])
                    sb_ = cnt['st'] % 2
                    cnt['st'] += 1
                    CP('act' if sub % 2 == 0 else 'dve', vst[sb_][:, :, 0:64],
                       ps[zb][:].rearrange("p (h d) -> p h d", h=8), ['ps%d' % zb], ['vst%d' % sb_])
                    r0 = (tt * 4 + sub) * 128
                    DMA('sp', v_d[r0:r0 + 128, :], vst[sb_].rearrange("p h e -> p (h e)"),
                        ['vst%d' % sb_], ['v_d'])
                    if tt >= 2:
                        DMA('sp', vh_d[r0 - 1024:r0 - 1024 + 128, :], vst[sb_].rearrange("p h e -> p (h e)"),
                            ['vst%d' % sb_], ['vh_d'])
        P.barrier()
        A.off = A_MARK
        CC(kh_d, khg_d, ['kh_d'], ['khg'])
        CC(vh_d, vhg_d, ['vh_d'], ['vhg'])
        CC(zb_d, zbg_d, ['zb_d'], ['zbg'])

    if 'att' in phases:
        qT = A.alloc([128, 4, OWN], BF16)
        kT = A.alloc([128, 4, NKT * 128], BF16)
        va = A.alloc([128, NKT, 520], BF16)
        wm = A.alloc([128, 20, 512], BF16)
        wmf = A.alloc([128, 20, 512], BF16)
        khr = A.alloc([128, 2, 4, 1024], BF16)
        vhr = A.alloc([128, 2, 8, 520], BF16)
        kh = A.alloc([128, 4, 1024], BF16)
        vh = A.alloc([128, 8, 520], BF16)
        oall = A.alloc([128, 16, 512])
        eb = [A.alloc([128, 512], BF16) for _ in range(4)]
        pm = [A.alloc([128, 512], BF16) for _ in range(4)]
        rden = [A.alloc([128, 4]) for _ in range(2)]
        sq32 = A.alloc([128, 512])
        ss = A.alloc([128, 8])
        rs = A.alloc([128, 8])
        DMA('sp', qT, qT_d.rearrange("g p t -> p g t"), ['qk_d'], ['qT'])
        DMA('sp', kT, kT_d.rearrange("g p t -> p g t"), ['qk_d'], ['kT'])
        DMA('sp', va, v_d.rearrange("(n p) f -> p n f", p=128), ['v_d'], ['va'])
        DMA('sp', wm, wm_d.rearrange("j p c -> p j c"), [], ['wm'])
        DMA('sp', wmf, wmf_d.rearrange("j p c -> p j c"), [], ['wmf'])
        DMA('sp', khr, khg_d.rearrange("(r g p) t -> p r g t", r=2, g=4), ['khg'], ['khr'])
        DMA('sp', vhr, vhg_d.rearrange("(r n p) f -> p r n f", r=2, n=8), ['vhg'], ['vhr'])
        for (raw, dst, key, n_) in ((khr, kh, 'kh', 4096), (vhr, vh, 'vh', 4160)):
            r0_ = raw[:, 0].rearrange("p a b -> p (a b)")
            r1_ = raw[:, 1].rearrange("p a b -> p (a b)")
            d_ = dst.rearrange("p a b -> p (a b)")
            TS('dve', d_, r0_, hsel[:, 0:1], None, ALU.mult, None, [key + 'r', 'hsel'], [key])
            STT(d_, r1_, hsel[:, 1:2], d_, ALU.mult, ALU.add, [key + 'r', 'hsel', key], [key])
        items = []
        for hh in range(8):
            for qb in range(4):
                blk_id = hh * 4 + qb
                kts = list(range(max(0, 4 * qb - 8), 4 * qb + 12))
                pvl = [(kt, i) for kt in kts for i in range(4) if abs(kt - 4 * qb - i) <= 8]
                for kt in kts:
                    items.append(dict(hh=hh, qb=qb, kt=kt, J=kt - 4 * qb + 8, ob=6 + blk_id % 2,
                                      first=pvl[0], last=pvl[-1], endblk=(kt == kts[-1]), blk=blk_id))
        NB = 4
        SKEW = 2

        def stage_a(n):
            d = items[n]
            hh, qb, kt, J = d['hh'], d['qb'], d['kt'], d['J']
            g, base = hh // 2, 64 * (hh % 2)
            sbk = 2 + (n % NB)
            b = n % NB
            if kt < NKT:
                kop, kkey, mk, mkey_ = kT[base:base + 64, g, kt * 128:(kt + 1) * 128], 'kT', wm, 'wm'
            else:
                ht = 23 - kt
                kop, kkey, mk, mkey_ = kh[base:base + 64, g, ht * 128:(ht + 1) * 128], 'kh', wmf, 'wmf'
            MM(ps[sbk][:], kop, qT[base:base + 64, g, qb * 512:(qb + 1) * 512], True, True, [kkey, 'qT'],
               ['ps%d' % sbk])
            ACT(eb[b], ps[sbk][:], AF.Exp, ['ps%d' % sbk], ['eb%d' % b], scale=0.125)
            TTo('dve', pm[b], eb[b], mk[:, J, :], ALU.mult, ['eb%d' % b, mkey_], ['pm%d' % b])

        def stage_b(n):
            d = items[n]
            hh, qb, kt, J, ob = d['hh'], d['qb'], d['kt'], d['J'], d['ob']
            b = n % NB
            for i in range(4):
                if abs(J - 8 - i) > 8:
                    continue
                vop, vkey = (va[:, kt, hh * 65:(hh + 1) * 65], 'va') if kt < NKT else \
                    (vh[:, 23 - kt, hh * 65:(hh + 1) * 65], 'vh')
                MM(ps[ob][:, i * 65:(i + 1) * 65], pm[b][:, i * 128:(i + 1) * 128],
                   vop, (kt, i) == d['first'], (kt, i) == d['last'],
                   ['pm%d' % b, vkey], ['ps%d' % ob])
            if d['endblk']:
                o4 = ps[ob][:, 0:260].rearrange("p (i e) -> p i e", e=65)
                rb_ = d['blk'] % 2
                RECIP(rden[rb_].unsqueeze(2), o4[:, :, 64:65], ['ps%d' % ob], ['rden%d' % rb_])
                TTo('dve', oall[:, qb * 4:(qb + 1) * 4, hh * 64:(hh + 1) * 64], o4[:, :, 0:64],
                    rden[rb_].unsqueeze(2).broadcast_to([128, 4, 64]), ALU.mult,
                    ['ps%d' % ob, 'rden%d' % rb_], [('oall', qb)])
        for n in range(len(items) + SKEW):
            if n < len(items):
                stage_a(n)
            if n >= SKEW:
                stage_b(n - SKEW)
        for qt in range(16):
            o = oall[:, qt, :]
            o3 = o.rearrange("p (h d) -> p h d", h=8)
            ACT(sq32, o, AF.Square, [('oall', qt // 4)], ['sq32'])
            P.add('dve', lambda e: e.tensor_reduce(out=ss, in_=sq32.rearrange("p (h d) -> p h d", h=8),
                                                   axis=AX.X, op=ALU.add), r=['sq32'], w=['ss'])
            ACT(rs, ss, AF.Sqrt, ['ss', 'eps'], ['rs'], bias=eps_t, scale=1.0 / 64)
            RECIP(rs, rs, ['rs'], ['rs'])
            TTo('dve', o3, o3, rs.unsqueeze(2).broadcast_to([128, 8, 64]), ALU.mult,
                [('oall', qt // 4), 'rs'], [('oall', qt // 4)])
            TTo('dve', o, o, rep[:, 0, :], ALU.mult, [('oall', qt // 4), 'rep'], [('oall', qt // 4)])
            DMA('sp', yc_d[qt * 128:(qt + 1) * 128, 0:512], o, [('oall', qt // 4)], ['yc'])
        P.barrier()
        A.off = A_MARK

    if 'rwkv' in phases:
        rwkv_phase(E, rwkv_stop)
        P.barrier()
        A.off = A_MARK

    if 'p4' in phases:
        B = alloc_ffn_bufs()
        wob = [A.alloc([128, 8, 128], BF16) for _ in range(2)]
        ost = [A.alloc([128, D]) for _ in range(2)]
        xT, hn, fb, rstd = B['xT'], B['hn'], B['fb'], B['rstd']
        for tt in range(4):
            load_T(B, yc_d, tt, hn, 'hn')
            def ld_o(dc):
                b_ = dc % 2
                flat = wob[b_].rearrange("p c f -> p (c f)")
                if tt == 0:
                    DMA('pool', wob[b_], wo_d[dc], [], ['wob%d' % b_])
                    DMA('sp', wo16_d[dc], flat, ['wob%d' % b_], [('wo16', dc)])
                else:
                    DMA('sp', flat, wo16_d[dc], [('wo16', dc)], ['wob%d' % b_])
            ld_o(0)
            for dc in range(8):
                if dc + 1 < 8:
                    ld_o(dc + 1)
                b = dc % 2
                pf = 6 + (dc % 2)
                for cc in range(8):
                    MM(ps[pf][:], wob[b][:, cc, :], hn[:, cc, :], cc == 0, cc == 7,
                       ['wob%d' % b, 'hn'], ['ps%d' % pf])
                CP('act' if dc % 2 == 0 else 'dve', fb[:, dc, :], ps[pf][:], ['ps%d' % pf], ['fb'])
            rmsnorm_stats(B, fb, 'fb')
            DMA('sp', xT, h1s_d[:, :, tt * TT:(tt + 1) * TT].rearrange("c p t -> p c t"), ['h1s'], ['xT'])
            for c in range(8):
                STT(fb[:, c, :], fb[:, c, :], gv[:, 3, c:c + 1], rstd, ALU.mult, ALU.mult,
                    ['fb', 'gv', 'rstd'], ['fb'])
                TTo('dve', xT[:, c, :], fb[:, c, :], xT[:, c, :], ALU.add, ['fb', 'xT'], ['xT'])
            ffn(B, 1, 4, 5, tt == 0)
            for sub in range(4):
                ob = cnt['st'] % 2
                cnt['st'] += 1
                for half in range(2):
                    pb = cnt['ps01'] % 2
                    cnt['ps01'] += 1
                    for q in range(4):
                        c = half * 4 + q
                        TR(ps[pb][:, q * 128:(q + 1) * 128], xT[:, c, sub * 128:(sub + 1) * 128], ident,
                           ['xT', 'ident'], ['ps%d' % pb])
                    CP('act' if half == 0 else 'dve', ost[ob][:, half * 512:(half + 1) * 512], ps[pb][:],
                       ['ps%d' % pb], ['ost%d' % ob])
                r0 = tt * TT + sub * 128
                DMA('sp', out_d[r0:r0 + 128, :], ost[ob], ['ost%d' % ob], ['out'])

    P.add('sp', None, r=['h1s', 'out', 'yc', 'zr', 'qk_d', 'v_d', 'yf'])
    P.emit(nc, stack)
    stack.close()
    return nc, P


def rwkv_phase(E, debug_stop=None):
    P, A, ps = E.P, E.A, E.ps
    MM, TR, ACT, TTo, TS, STT, CP, RECIP, DMA, MEMSET = (E.MM, E.TR, E.ACT, E.TTo, E.TS, E.STT, E.CP, E.RECIP,
                                                          E.DMA, E.MEMSET)
    ident, identb, rep, rwm, rwc, rwp, lora = E.ident, E.identb, E.rep, E.rwm, E.rwc, E.rwp, E.lora
    eps_t, gneps_t = E.eps_t, E.gneps_t
    zr_d, yc_d, yf_d = E.zr_d, E.yc_d, E.yf_d

    rmask = rwc[:, 0:512]
    blockones = rwc[:, 1024:1152]
    identblk = rwc[:, 1152:1216]
    hsel = rwc[:, 1216:1218]
    mp, mn = rwp[:, 0:15], rwp[:, 15:30]
    k_k, k_a, r_k = rwp[:, 46:50], rwp[:, 50:54], rwp[:, 54:58]
    w2sb, a2sb, g2sb = lora[:, 0, :], lora[:, 1, :], lora[:, 2, :]

    def w0T(di, c):
        return rwp[:, 30 + 4 * di + c:31 + 4 * di + c]

    def a0T(di, c):
        return rwp[:, 38 + 4 * di + c:39 + 4 * di + c]

    c0 = A.alloc([128, 15])
    omka = A.alloc([128, 4])
    rk2 = A.alloc([128, 4])
    GW = 256
    NTG = GW // 128
    NG = 4096 // GW
    NG_OWN = 2048 // GW
    zsb = [A.alloc([128, GW + 2]) for _ in range(3)]
    zcnt = {'z': 0}
    ub = A.alloc([128, 15, GW])
    tw = A.alloc([128, GW])
    T_ = {nm: A.alloc([128, GW]) for nm in
          ('sg', 'av', 'kx', 't1', 't2', 'kkv', 'kd', 'ka', 'cs', 'cs2', 'e0', 'e1', 'e2', 'e3', 'kd0')}
    tot_s = A.alloc([128, GW // 64])
    OB = []
    for _ in range(2):
        d_ = {nm: A.alloc([128, 4, GW], BF16) for nm in ('Rb', 'Kb', 'Kd', 'Ad', 'Kh', 'Ahn', 'vb')}
        d_['gc'] = A.alloc([128, 4, GW // 64])
        d_['bprod'] = A.alloc([128, 4, GW])
        d_['sgd'] = A.alloc([128, GW])
        d_['uvf'] = A.alloc([128, 4, GW])
        OB.append(d_)
    KbT, KhT, AhT, VT = [A.alloc([128, 512], BF16) for _ in range(4)]
    VT32 = A.alloc([128, 512])
    Xs = [[A.alloc([128, 4, 128], BF16) for _ in range(2)] for _ in range(2)]
    Ys = [[A.alloc([128, 4, 128], BF16) for _ in range(2)] for _ in range(2)]
    Rms = [[A.alloc([128, 4, 128], BF16) for _ in range(2)] for _ in range(2)]
    pkks, qrks, qras = [[A.alloc([128, 4, 128], BF16) for _ in range(2)] for _ in range(3)]
    pvs = [A.alloc([128, 4, 64], BF16) for _ in range(2)]
    u0 = A.alloc([128, 8, 64], BF16)
    wt = A.alloc([128, 8, 64], BF16)
    y0 = A.alloc([128, 512])
    rpT = A.alloc([128, 4, 128])
    GT = A.alloc([128, 4, 2, 64])
    Hs = A.alloc([128, 4, 2, 64])
    Tst = [A.alloc([128, 4, 64]) for _ in range(2)]
    ytile = A.alloc([128, 512])
    yfb = A.alloc([128, 512])
    sqb = A.alloc([128, 512])
    tmpv = A.alloc([128, 512])
    s1 = A.alloc([128, 8])
    s2 = A.alloc([128, 8])
    bon = A.alloc([128, 8])
    psb2 = ps[2][:].bitcast(BF16)

    TTo('dve', c0, mp, mn, ALU.add, ['rwp'], ['c0'])
    TS('dve', c0, c0, -1.0, 1.0, ALU.mult, ALU.add, ['c0'], ['c0'])
    TS('dve', omka, k_a, -1.0, 1.0, ALU.mult, ALU.add, ['rwp'], ['omka'])
    TS('dve', rk2, r_k, 0.5, None, ALU.mult, None, ['rwp'], ['rk2'])
    zr2 = A.alloc([128, 2, 15])
    zedge = A.alloc([128, 15])
    zg = E.zbg_d.rearrange("(r p) c -> p r c", r=2)
    DMA('sp', zr2, zg, ['zbg'], ['zr2'])
    DMA('sp', zr2[0:64, :, 12:14], zg[64:128, :, 12:14], ['zbg'], ['zr2'])
    DMA('sp', zr2[64:128, :, 12:14], zg[0:64, :, 12:14], ['zbg'], ['zr2'])
    TS('dve', zedge, zr2[:, 0, :], E.hsel[:, 0:1], None, ALU.mult, None, ['zr2', 'hsel'], ['zedge'])
    STT(zedge, zr2[:, 1, :], E.hsel[:, 1:2], zedge, ALU.mult, ALU.add, ['zr2', 'hsel', 'zedge'], ['zedge'])
    tfr = A.alloc([128, 2, 256])

    pcnt = {'pa': 0}

    def pbank(lo=0):
        b = (6 if lo == 0 else 0) + pcnt['pa'] % 2
        pcnt['pa'] += 1
        return b

    def prep_group(g, di, own, final, pb):
        O_ = OB[pb]
        Rb, Kb, Kd, Ad, Kh, Ahn, vb = (O_[n] for n in ('Rb', 'Kb', 'Kd', 'Ad', 'Kh', 'Ahn', 'vb'))
        gc, bprod, sgd, uvf = O_['gc'], O_['bprod'], O_['sgd'], O_['uvf']
        lo, hi = GW * g - 1, GW * g + GW + 1
        clo, chi = max(lo, 0), min(hi, 2048)
        need = list(range(4, 12)) + [12, 13] + ([0, 1, 2, 3] if own else []) + ([14] if final else [])
        for n_, j in enumerate(need):
            zb = zsb[zcnt['z'] % 3]
            zk = 'zsb%d' % (zcnt['z'] % 3)
            zcnt['z'] += 1
            DMA('sp', zb[:, clo - lo:GW + 2 - (hi - chi)], zr_d[j, :, clo:chi], ['zr'], [zk])
            if g == 0:
                MEMSET('pool', zb[:, 0:1], 0.0, [zk])
            if g == NG_OWN - 1:
                CP('pool', zb[:, GW + 1:GW + 2], zedge[:, j:j + 1], ['zedge'], [zk])
            k = ('ub', j)
            ACT(ub[:, j, :], zb[:, 0:GW], AF.Copy, [zk, 'rwp'], [k], scale=mp[:, j:j + 1])
            STT(ub[:, j, :], zb[:, 2:GW + 2], mn[:, j:j + 1], ub[:, j, :], ALU.mult, ALU.add, [zk, 'rwp', k], [k])
            STT(ub[:, j, :], zb[:, 1:GW + 1], c0[:, j:j + 1], ub[:, j, :], ALU.mult, ALU.add, [zk, 'c0', k], [k])
            yield
        ACT(tw, ub[:, 12, :], AF.Tanh, [('ub', 12)], ['tw'])
        if final:
            ACT(sgd, ub[:, 14, :], AF.Sigmoid, [('ub', 14)], [('sgd', pb)])
        sg, av, kx, t1, t2, kkv, kd, ka = (T_[n] for n in ('sg', 'av', 'kx', 't1', 't2', 'kkv', 'kd', 'ka'))
        cs, cs2, e0, e1, e2, e3, kd0 = (T_[n] for n in ('cs', 'cs2', 'e0', 'e1', 'e2', 'e3', 'kd0'))
        lo64 = 64 * di
        NC64 = GW // 64
        for c in range(4):
            ku = ub[:, 4 + c, :]
            kkey = ('ub', 4 + c)
            cc = slice(c * 128, (c + 1) * 128)
            ACT(kx, ku, AF.Copy, [kkey, 'rwp'], ['kx'], scale=k_k[:, c:c + 1])
            ACT(t1, kx, AF.Square, ['kx'], ['t1'])
            b = pbank()
            MM(ps[b][:, 0:GW], blockones, t1, True, True, ['rwc', 't1'], ['ps%d' % b])
            ACT(t1, ps[b][:, 0:GW], AF.Sqrt, ['ps%d' % b], ['t1'])
            yield
            TS('dve', t1, t1, 1e-12, None, ALU.max, None, ['t1'], ['t1'])
            RECIP(t1, t1, ['t1'], ['t1'])
            TTo('dve', kkv, kx, t1, ALU.mult, ['kx', 't1'], ['kkv'])
            b = pbank(lo64)
            MM(ps[b][:, 0:GW], w2sb[lo64:lo64 + 64, cc], tw[lo64:lo64 + 64, :], True, True, ['lora', 'tw'],
               ['ps%d' % b])
            ACT(sg, ps[b][:, 0:GW], AF.Sigmoid, ['ps%d' % b, 'rwp'], ['sg'], bias=w0T(di, c))
            yield
            b = pbank(lo64)
            MM(ps[b][:, 0:GW], a2sb[lo64:lo64 + 64, cc], ub[lo64:lo64 + 64, 13, :], True, True,
               ['lora', ('ub', 13)], ['ps%d' % b])
            ACT(av, ps[b][:, 0:GW], AF.Sigmoid, ['ps%d' % b, 'rwp'], ['av'], bias=a0T(di, c))
            ACT(t2, av, AF.Identity, ['av', 'rwp', 'omka'], ['t2'], bias=omka[:, c:c + 1], scale=k_a[:, c:c + 1])
            TTo('dve', kd, ku, t2, ALU.mult, [kkey, 't2'], ['kd'])
            TTo('pool', ka, kkv, av, ALU.mult, ['kkv', 'av'], ['ka'])
            yield
            if final:
                od = 1 - di
                b = pbank(64 * od)
                MM(ps[b][:, 0:GW], a2sb[64 * od:64 * od + 64, cc], ub[64 * od:64 * od + 64, 13, :], True, True,
                   ['lora', ('ub', 13)], ['ps%d' % b])
                ACT(kd0, ps[b][:, 0:GW], AF.Sigmoid, ['ps%d' % b, 'rwp'], ['kd0'], bias=a0T(od, c))
                TS('dve', kd0, kd0, k_a[:, c:c + 1], omka[:, c:c + 1], ALU.mult, ALU.add,
                   ['kd0', 'rwp', 'omka'], ['kd0'])
                TTo('dve', kd0, kd0, ku, ALU.mult, ['kd0', kkey], ['kd0'])
                yield
                TTo('dve', kd0, kd0, kd, ALU.add, ['kd0', 'kd'], ['kd0'])
                TTo('dve', kd0, kd0, ub[:, c, :], ALU.mult, ['kd0', ('ub', c)], ['kd0'])
                TS('dve', bprod[:, c, :], kd0, rk2[:, c:c + 1], None, ALU.mult, None, ['kd0', 'rk2'],
                   [('bprod', c, pb)])
                CP('pool', uvf[:, c, :], ub[:, 8 + c, :], [('ub', 8 + c)], [('uvf', c, pb)])
                yield
            P.add('dve', lambda e, cs=cs, sg=sg: e.tensor_tensor_scan(out=cs, data0=rmask[:, 0:GW], data1=sg,
                                                                      initial=0.0, op0=ALU.mult, op1=ALU.add),
                  r=['rwc', 'sg'], w=['cs'])
            CP('dve', tot_s, cs[:, 63::64], ['cs'], ['tot'])
            totb = tot_s.unsqueeze(2).broadcast_to([128, NC64, 64])
            v3 = lambda ap: ap.rearrange("p (a b) -> p a b", a=NC64)
            if di == 0:
                csx, cskey = cs, 'cs'
            else:
                TTo('dve', t2, sg, cs, ALU.subtract, ['sg', 'cs'], ['t2'])
                TTo('dve', v3(cs2), v3(t2), totb, ALU.add, ['t2', 'tot'], ['cs2'])
                csx, cskey = cs2, 'cs2'
            yield
            TTo('dve', e0, csx, sg, ALU.subtract, [cskey, 'sg'], ['e0'])
            TTo('dve', v3(e3), totb, v3(csx), ALU.subtract, ['tot', cskey], ['e3'])
            ACT(e1, csx, AF.Exp, [cskey], ['e1'], scale=-CDEC)
            ACT(e2, csx, AF.Exp, [cskey], ['e2'], scale=CDEC)
            yield
            ACT(e0, e0, AF.Exp, ['e0'], ['e0'], scale=-CDEC)
            ACT(e3, e3, AF.Exp, ['e3'], ['e3'], scale=-CDEC)
            ACT(gc[:, c, :], tot_s, AF.Exp, ['tot'], [('gc', c, pb)], scale=-CDEC)
            yield
            if own:
                TTo('pool', Rb[:, c, :], ub[:, c, :], e1, ALU.mult, [('ub', c), 'e1'], [('Rb', c, pb)])
            TTo('pool', Kb[:, c, :], kkv, e0, ALU.mult, ['kkv', 'e0'], [('Kb', c, pb)])
            TTo('pool', Kd[:, c, :], kd, e2, ALU.mult, ['kd', 'e2'], [('Kd', c, pb)])
            yield
            TTo('pool', Ad[:, c, :], ka, e2, ALU.mult, ['ka', 'e2'], [('Ad', c, pb)])
            TTo('pool', Kh[:, c, :], kd, e3, ALU.mult, ['kd', 'e3'], [('Kh', c, pb)])
            TS('pool', ka, ka, -1.0, 0.0, ALU.mult, ALU.add, ['ka'], ['ka'])
            TTo('pool', Ahn[:, c, :], ka, e3, ALU.mult, ['ka', 'e3'], [('Ahn', c, pb)])
            CP('pool', vb[:, c, :], ub[:, 8 + c, :], [('ub', 8 + c)], [('vb', c, pb)])
            yield

    M = lambda di, k: rwm[:, di, k, :].unsqueeze(1).broadcast_to([128, 4, 128])

    def tile_proc(di, g, tl, outp, final, cur, pb, pump):
        cols = slice(tl * 128, (tl + 1) * 128)
        gt = NTG * g + tl
        O_ = OB[pb]
        Rb, Kb, Kd, Ad, Kh, Ahn, vb = (O_[n] for n in ('Rb', 'Kb', 'Kd', 'Ad', 'Kh', 'Ahn', 'vb'))
        gc, bprod, sgd, uvf = O_['gc'], O_['bprod'], O_['sgd'], O_['uvf']
        allk = lambda nm: [(nm, c, pb) for c in range(4)]
        slot_of = lambda hh: 4 * (hh % 2) + hh // 2
        for (src, dstT, nm, eng) in ((Kb, KbT, 'Kb', 'act'), (Kh, KhT, 'Kh', 'dve'), (Ahn, AhT, 'Ahn', 'act'),
                                      (vb, VT, 'vb', 'dve')):
            for c in range(4):
                TR(psb2[:, c * 128:(c + 1) * 128], src[:, c, cols], identb, [(nm, c, pb), 'identb'], ['ps2'])
            CP(eng, dstT, psb2[:, 0:512], ['ps2'], [nm + 'T'])
        if final:
            for c in range(4):
                TR(ps[2][:, c * 128:(c + 1) * 128], uvf[:, c, cols], ident, [('uvf', c, pb), 'ident'], ['ps2'])
            CP('act', VT32, ps[2][:], ['ps2'], ['VT32'])
        def hg_stages(hg):
            BA, BB, BC = ((4, 5, 3), (1, 0, 2))[hg]
            X, Y, Rm = Xs[hg], Ys[hg], Rms[hg]
            pkk, qrk, qra, pv = pkks[hg], qrks[hg], qras[hg], pvs[hg]
            sx = 'g%d' % hg
            base = 64 * hg
            hl = [(2 * i + hg, i) for i in range(4)]

            def blk(bank, i):
                return ps[bank][:, i * 128:(i + 1) * 128]

            def b3(bank):
                return ps[bank][:].rearrange("p (a b) -> p a b", a=4)
            for i, (hh, c) in enumerate(hl):
                MM(blk(BA, i), Kb[base:base + 64, c, cols], Ad[base:base + 64, c, cols], True, True,
                   [('Kb', c, pb), ('Ad', c, pb)], ['ps%d' % BA])
            TTo('dve', X[0], b3(BA), M(di, 0), ALU.mult, ['ps%d' % BA, 'rwm'], ['X0' + sx])
            yield
            for i, (hh, c) in enumerate(hl):
                MM(blk(BB, i), Ad[base:base + 64, c, cols], Kb[base:base + 64, c, cols], True, True,
                   [('Kb', c, pb), ('Ad', c, pb)], ['ps%d' % BB])
            TTo('dve', Y[0], b3(BB), M(di, 1), ALU.mult, ['ps%d' % BB, 'rwm'], ['Y0' + sx])
            TTo('pool', Rm[0], Y[0], identb.unsqueeze(1).broadcast_to([128, 4, 128]), ALU.add,
                ['Y0' + sx, 'identb'], ['R0' + sx])
            yield
            for j in range(1, 6):
                a, b = (j - 1) % 2, j % 2
                for i in range(4):
                    MM(blk(BA, i), Y[a][:, i, :], X[a][:, i, :], True, True, ['X%d' % a + sx, 'Y%d' % a + sx], ['ps%d' % BA])
                CP('act', X[b], b3(BA), ['ps%d' % BA], ['X%d' % b + sx])
                yield
                if j < 5:
                    for i in range(4):
                        MM(blk(BB, i), X[a][:, i, :], Y[a][:, i, :], True, True, ['X%d' % a + sx, 'Y%d' % a + sx], ['ps%d' % BB])
                    CP('act', Y[b], b3(BB), ['ps%d' % BB], ['Y%d' % b + sx])
                    yield
                for i in range(4):
                    MM(blk(BC, i), identb, Rm[a][:, i, :], True, False, ['identb', 'R%d' % a + sx], ['ps%d' % BC])
                    MM(blk(BC, i), X[b][:, i, :], Rm[a][:, i, :], False, True, ['X%d' % b + sx, 'R%d' % a + sx], ['ps%d' % BC])
                CP('act', Rm[b], b3(BC), ['ps%d' % BC], ['R%d' % b + sx])
                yield
            MinvT, mkey = Rm[1], 'R1' + sx
            for i, (hh, c) in enumerate(hl):
                MM(blk(BA, i), Kd[base:base + 64, c, cols], Kb[base:base + 64, c, cols], True, True,
                   [('Kd', c, pb), ('Kb', c, pb)], ['ps%d' % BA])
            TTo('dve', pkk, b3(BA), M(di, 2), ALU.mult, ['ps%d' % BA, 'rwm'], ['pkk' + sx])
            yield
            if outp:
                for i, (hh, c) in enumerate(hl):
                    MM(blk(BB, i), Kd[base:base + 64, c, cols], Rb[base:base + 64, c, cols], True, True,
                       [('Kd', c, pb), ('Rb', c, pb)], ['ps%d' % BB])
                TTo('dve', qrk, b3(BB), M(di, 3), ALU.mult, ['ps%d' % BB, 'rwm'], ['qrk' + sx])
                yield
                for i, (hh, c) in enumerate(hl):
                    MM(blk(BC, i), Ad[base:base + 64, c, cols], Rb[base:base + 64, c, cols], True, True,
                       [('Ad', c, pb), ('Rb', c, pb)], ['ps%d' % BC])
                TTo('dve', qra, b3(BC), M(di, 4), ALU.mult, ['ps%d' % BC, 'rwm'], ['qra' + sx])
                yield
            for i, (hh, c) in enumerate(hl):
                MM(ps[BA][:, i * 64:(i + 1) * 64], pkk[:, i, :], VT[:, hh * 64:(hh + 1) * 64], True, True,
                   ['pkk' + sx, 'vbT'], ['ps%d' % BA])
            CP('act', pv, ps[BA][:, 0:256].rearrange("p (a b) -> p a b", a=4), ['ps%d' % BA], ['pv' + sx])
            yield
            for i, (hh, c) in enumerate(hl):
                MM(ps[BB][:, i * 64:(i + 1) * 64], MinvT[:, i, :], pv[:, i, :], True, True, [mkey, 'pv' + sx], ['ps%d' % BB])
            for i, (hh, c) in enumerate(hl):
                MM(ps[BC][:, i * 64:(i + 1) * 64], MinvT[:, i, :], KbT[:, hh * 64:(hh + 1) * 64],
                   True, True, [mkey, 'KbT'], ['ps%d' % BC])
            CP('act', u0[:, 4 * hg:4 * hg + 4, :], ps[BB][:, 0:256].rearrange("p (a b) -> p a b", a=4),
               ['ps%d' % BB], [('u0', hg)])
            CP('dve', wt[:, 4 * hg:4 * hg + 4, :], ps[BC][:, 0:256].rearrange("p (a b) -> p a b", a=4),
               ['ps%d' % BC], [('wt', hg)])
            yield
            if outp:
                for i, (hh, c) in enumerate(hl):
                    MM(ps[BB][:, i * 64:(i + 1) * 64], qrk[:, i, :], VT[:, hh * 64:(hh + 1) * 64], True, False,
                       ['qrk' + sx, 'vbT'], ['ps%d' % BB])
                    MM(ps[BB][:, i * 64:(i + 1) * 64], qra[:, i, :], u0[:, 4 * hg + i, :], False, True,
                       ['qra' + sx, ('u0', hg)], ['ps%d' % BB])
                CP('act', y0.rearrange("p (i q d) -> p i q d", i=4, q=2)[:, :, hg, :],
                   ps[BB][:, 0:256].rearrange("p (a b) -> p a b", a=4), ['ps%d' % BB], [('y0', hg)])
                yield
                for i, (hh, c) in enumerate(hl):
                    MM(ps[BA][base:base + 64, i * 128:(i + 1) * 128], wt[:, 4 * hg + i, :], qra[:, i, :],
                       True, True, [('wt', hg), 'qra' + sx], ['ps%d' % BA])
                TTo('dve', rpT[base:base + 64, :, :],
                    ps[BA][base:base + 64, :].rearrange("p (a b) -> p a b", a=4),
                    Rb[base:base + 64, :, cols], ALU.add, ['ps%d' % BA] + allk('Rb'), [('rpT', hg)])
            yield
        gens = [hg_stages(0), hg_stages(1)]
        alive = [True, True]
        while any(alive):
            pump(1)
            for k_ in (0, 1):
                if alive[k_]:
                    try:
                        next(gens[k_])
                    except StopIteration:
                        alive[k_] = False
        gbank = {0: 5, 1: 3}
        hbank = {0: 4, 1: 2}
        for ch2 in range(2):
            rows = slice(ch2 * 64, ch2 * 64 + 64)
            gb, hb = gbank[ch2], hbank[ch2]
            for hh in range(8):
                c, base = hh // 2, 64 * (hh % 2)
                sl = slot_of(hh)
                o5 = ps[gb][base:base + 64, c * 64:(c + 1) * 64]
                MM(o5, wt[rows, sl, :], AhT[rows, hh * 64:(hh + 1) * 64], True, True,
                   [('wt', hh % 2), 'AhnT'], ['ps%d' % gb])
                o4 = ps[hb][base:base + 64, c * 64:(c + 1) * 64]
                MM(o4, KhT[rows, hh * 64:(hh + 1) * 64], VT[rows, hh * 64:(hh + 1) * 64], True, False,
                   ['KhT', 'vbT'], ['ps%d' % hb])
                MM(o4, AhT[rows, hh * 64:(hh + 1) * 64], u0[rows, sl, :], False, True,
                   ['AhnT', ('u0', hh % 2)], ['ps%d' % hb])
        for ch2 in range(2):
            gb, hb = gbank[ch2], hbank[ch2]
            for c in range(4):
                slot = 2 * tl + ch2
                STT(GT[:, c, ch2, :], identblk, gc[:, c, slot:slot + 1], ps[gb][:, c * 64:(c + 1) * 64],
                    ALU.mult, ALU.add, ['rwc', ('gc', c, pb), 'ps%d' % gb], ['GT'])
            CP('act', Hs[:, :, ch2, :], ps[hb][:, 0:256].rearrange("p (a b) -> p a b", a=4),
               ['ps%d' % hb], ['Hs'])
        pump(2)
        order = [0, 1] if di == 0 else [1, 0]
        tb = {0: 6, 1: 0}
        yb_ = {0: 7, 1: 1}
        for ch2 in order:
            rows = slice(ch2 * 64, ch2 * 64 + 64)
            Tc, Tn = Tst[cur], Tst[1 - cur]
            kc, kn = 'T%d' % cur, 'T%d' % (1 - cur)
            for par in range(2):
                base = 64 * par
                for c in range(4):
                    hh = 2 * c + par
                    if outp:
                        MM(ps[yb_[par]][rows, hh * 64:(hh + 1) * 64],
                           rpT[base:base + 64, c, ch2 * 64:ch2 * 64 + 64],
                           Tc[base:base + 64, c, :], True, True, [('rpT', par), kc], ['ps%d' % yb_[par]])
                    MM(ps[tb[par]][base:base + 64, c * 64:(c + 1) * 64], GT[base:base + 64, c, ch2, :],
                       Tc[base:base + 64, c, :], True, True, ['GT', kc], ['ps%d' % tb[par]])
            for par in range(2):
                base = 64 * par
                if outp:
                    yv = ytile.rearrange("p (i q d) -> p i q d", i=4, q=2)
                    y0v = y0.rearrange("p (i q d) -> p i q d", i=4, q=2)
                    pyv = ps[yb_[par]][:].rearrange("p (i q d) -> p i q d", i=4, q=2)
                    TTo('dve', yv[rows, :, par, :], pyv[rows, :, par, :], y0v[rows, :, par, :], ALU.add,
                        ['ps%d' % yb_[par], ('y0', 0), ('y0', 1)], ['ytile'])
                TTo('dve', Tn[base:base + 64, :, :],
                    ps[tb[par]][base:base + 64, 0:256].rearrange("p (a b) -> p a b", a=4),
                    Hs[base:base + 64, :, ch2, :], ALU.add, ['ps%d' % tb[par], 'Hs'], [kn])
            cur = 1 - cur
            pump(1)
        if outp and not final:
            DMA('sp', yf_d[gt * 128:(gt + 1) * 128, :], ytile, ['ytile'], ['yf'])
        if outp and final:
            DMA('sp', yfb, yf_d[gt * 128:(gt + 1) * 128, :], ['yf'], ['yfb'])
            y = ytile
            y3 = y.rearrange("p (h d) -> p h d", h=8)
            TTo('dve', y, y, yfb, ALU.add, ['ytile', 'yfb'], ['ytile'])
            P.add('dve', lambda e: e.tensor_reduce(out=s1, in_=y3, axis=AX.X, op=ALU.add), r=['ytile'], w=['s1'])
            TS('dve', s1, s1, -1.0 / 64, None, ALU.mult, None, ['s1'], ['s1'])
            TTo('dve', y3, y3, s1.unsqueeze(2).broadcast_to([128, 8, 64]), ALU.add, ['ytile', 's1'], ['ytile'])
            ACT(sqb, y, AF.Square, ['ytile'], ['sqb'])
            P.add('dve', lambda e: e.tensor_reduce(out=s2, in_=sqb.rearrange("p (h d) -> p h d", h=8), axis=AX.X,
                                                   op=ALU.add), r=['sqb'], w=['s2'])
            ACT(s2, s2, AF.Sqrt, ['s2', 'eps'], ['s2'], bias=gneps_t, scale=1.0 / 64)
            RECIP(s2, s2, ['s2'], ['s2'])
            TTo('dve', y3, y3, s2.unsqueeze(2).broadcast_to([128, 8, 64]), ALU.mult, ['ytile', 's2'], ['ytile'])
            TTo('dve', y, y, rep[:, 1, :], ALU.mult, ['ytile', 'rep'], ['ytile'])
            TTo('dve', y, y, rep[:, 2, :], ALU.add, ['ytile', 'rep'], ['ytile'])
            for c in range(4):
                MM(ps[2][:, 2 * c:2 * c + 2], bprod[:, c, cols], hsel, True, True, [('bprod', c, pb), 'rwc'], ['ps2'])
            CP('dve', bon, ps[2][:, 0:8], ['ps2'], ['bon'])
            TTo('dve', tmpv.rearrange("p (h d) -> p h d", h=8), VT32.rearrange("p (h d) -> p h d", h=8),
                bon.unsqueeze(2).broadcast_to([128, 8, 64]), ALU.mult, ['VT32', 'bon'], ['tmpv'])
            TTo('dve', y, y, tmpv, ALU.add, ['ytile', 'tmpv'], ['ytile'])
            MM(ps[3][:], sgd[:, cols], g2sb, True, True, [('sgd', pb), 'lora'], ['ps3'])
            TTo('dve', y, y, ps[3][:], ALU.mult, ['ytile', 'ps3'], ['ytile'])
            DMA('sp', yc_d[gt * 128:(gt + 1) * 128, 512:1024], y, ['ytile'], ['yc'])
        return cur

    sched = [(g, 0, True, False) for g in range(NG_OWN)]
    if debug_stop != 'F':
        sched += [(g, 1, True, True) for g in range(NG_OWN - 1, -1, -1)]
    cur = 0
    MEMSET('pool', Tst[0], 0.0, ['T0'])
    for _ in prep_group(*sched[0], 0):
        pass
    for idx, (g, di, own, final) in enumerate(sched):
        pb = idx % 2
        pg = prep_group(*sched[idx + 1], (idx + 1) % 2) if idx + 1 < len(sched) else None
        st = {'alive': pg is not None}

        def pump(n, pg=pg, st=st):
            for _ in range(n):
                if st['alive']:
                    try:
                        next(pg)
                    except StopIteration:
                        st['alive'] = False
        if idx > 0 and sched[idx - 1][1] != di:
            kc_ = 'T%d' % cur
            tflat = Tst[cur].rearrange("p a b -> p (a b)")
            DMA('sp', E.tf_d, tflat, [kc_], ['tf_d'])
            E.CC(E.tf_d, E.tfg_d, ['tf_d'], ['tfg'])
            DMA('sp', tfr, E.tfg_d.rearrange("(r p) f -> p r f", r=2), ['tfg'], ['tfr'])
            TS('dve', tflat, tfr[:, 0, :], E.hsel[:, 0:1], None, ALU.mult, None, ['tfr', 'hsel'], [kc_])
            STT(tflat, tfr[:, 1, :], E.hsel[:, 1:2], tflat, ALU.mult, ALU.add, ['tfr', 'hsel', kc_], [kc_])
        tls = range(NTG) if di == 0 else range(NTG - 1, -1, -1)
        for tl in tls:
            cur = tile_proc(di, g, tl, own, final, cur, pb, pump)
        pump(10 ** 6)


def _f(a):
    return np.ascontiguousarray(a, dtype=np.float32)


def _lhs_chunks(W):
    n = W.shape[1] // 128
    return _f(W.reshape(8, 128, n, 128).transpose(2, 1, 0, 3))


def _attn_masks():
    p = np.arange(128)[:, None]
    c = np.arange(128)[None, :]
    cntm = {}
    for j in range(-8, 9):
        d = 128 * j + p - c
        m = (np.abs(d) <= 64).astype(np.float32)
        m += ((d % 4 == 0) & (np.abs(d) <= 256)).astype(np.float32)
        m += ((d % 16 == 0) & (np.abs(d) <= 1024)).astype(np.float32)
        cntm[j] = m
    wm = np.zeros((20, 128, 512), np.float32)
    for J in range(20):
        for i in range(4):
            j = J - 8 - i
            if abs(j) <= 8:
                wm[J, :, i * 128:(i + 1) * 128] = cntm[j]
    return wm.astype(ml_dtypes.bfloat16)


def _rope_tables(rev):
    pos = np.arange(S, dtype=np.float32)
    if rev:
        pos = pos[::-1].copy()
    inv_freq = (np.float32(10000.0) ** (-np.arange(0, 64, 2, dtype=np.float32) / np.float32(64))).astype(np.float32)
    ang = (pos[:, None] * inv_freq[None, :]).astype(np.float32)
    cos, sin = np.cos(ang).astype(np.float32), np.sin(ang).astype(np.float32)
    idx = np.arange(128) % 32
    sign = np.where((np.arange(128) % 64) < 32, -1.0, 1.0).astype(np.float32)
    cosT = cos[:, idx].T
    sinT = (sin[:, idx] * sign[None, :]).T
    return _f(cosT), _f(sinT)


def host_inputs(inputs):
    g = lambda k: np.asarray(inputs[k][0], np.float32)
    x = np.asarray(inputs["x"], dtype=np.float32)
    shared = {}
    ffn_names = {1: ("ffn1_w_gate", "ffn1_w_up", "ffn1_w_down"), 2: ("ffn2_w_gate", "ffn2_w_up", "ffn2_w_down")}
    for k in (1, 2):
        wg, wu, wd = (g(nm) for nm in ffn_names[k])
        gg = wg.reshape(8, 128, NF, 128).transpose(2, 1, 0, 3)
        uu = wu.reshape(8, 128, NF, 128).transpose(2, 1, 0, 3)
        shared["wgu%d" % k] = _f(np.stack([gg, uu], axis=2))
        shared["wdc%d" % k] = _f(wd.reshape(NF, 128, 8, 128).transpose(2, 1, 0, 3))
    gnames = ["ffn1_pre_g", "ffn1_post_g", "mix_pre_g", "mix_post_g", "ffn2_pre_g", "ffn2_post_g"]
    shared["gv"] = _f(np.stack([g(nm).reshape(8, 128).T for nm in gnames], axis=1))
    shared["ident"] = np.eye(128, dtype=np.float32)
    w_in = g("w_in")
    swap = np.concatenate([(np.arange(64) + 32) % 64 + 64 * h for h in range(8)])
    shared["wv"] = _f(w_in[:, 1024:1536].reshape(8, 128, 512).transpose(1, 0, 2))
    shared["wo"] = _lhs_chunks(g("w_out"))
    shared["wm"] = _attn_masks()
    shared["wmf"] = np.ascontiguousarray(shared["wm"][:, ::-1, :])
    shared["rep"] = _f(np.stack([np.broadcast_to(g(nm)[None, :], (128, 512))
                                 for nm in ("attn_out_g", "rwkv_lnx_w", "rwkv_lnx_b")], axis=1))
    maps = []
    for c in range(8):
        b, h = c // 2, c % 2
        m = dict(shared)
        m["x"] = _f(x[b] if h == 0 else x[b, ::-1])
        cosT, sinT = _rope_tables(h == 1)
        m["cosT"], m["sinT"] = cosT, sinT
        m["hsel"] = _f(np.tile(np.array([[float(h), float(1 - h)]], np.float32), (128, 1)))
        m.update(_rwkv_host(inputs, h, w_in, swap))
        maps.append(m)
    return maps


def _rwkv_host(inputs, h, w_in, swap):
    g = lambda k: np.asarray(inputs[k][0], np.float32)
    dirs = [0, 1] if h == 0 else [1, 0]
    wq, wk = w_in[:, 0:512], w_in[:, 512:1024]
    cols = []
    for gi in range(4):
        cols.append(wq[:, gi * 128:(gi + 1) * 128])
        cols.append(wq[:, swap][:, gi * 128:(gi + 1) * 128])
    for gi in range(4):
        cols.append(wk[:, gi * 128:(gi + 1) * 128])
        cols.append(wk[:, swap][:, gi * 128:(gi + 1) * 128])
    wr = w_in[:, 1536:]
    rw = [wr[:, 0:1536]]
    for off in (1536, 1664):
        blk = wr[:, off:off + 128]
        rw.append(np.concatenate([blk[:, 64 * dirs[0]:64 * dirs[0] + 64], blk[:, 64 * dirs[1]:64 * dirs[1] + 64]], axis=1))
    rw.append(wr[:, 1792:1920])
    Wall = np.concatenate(cols + rw, axis=1)
    out = {"win": _lhs_chunks(Wall)}
    mp, mn = g("rwkv_mu_prev"), g("rwkv_mu_next")
    if h == 1:
        mp, mn = mn, mp

    def fix(v):
        v = v.copy()
        for off in (1536, 1664):
            blk = v[off:off + 128].copy()
            v[off:off + 128] = np.concatenate([blk[64 * dirs[0]:64 * dirs[0] + 64], blk[64 * dirs[1]:64 * dirs[1] + 64]])
        return v
    mp, mn = fix(mp), fix(mn)
    rwp = np.zeros((128, 64), np.float32)
    rwp[:, 0:15] = mp.reshape(15, 128).T
    rwp[:, 15:30] = mn.reshape(15, 128).T
    w0, a0 = g("rwkv_w0"), g("rwkv_a0")
    for di, d in enumerate(dirs):
        rwp[:, 30 + 4 * di:34 + 4 * di] = w0[d].reshape(4, 128).T
        rwp[:, 38 + 4 * di:42 + 4 * di] = a0[d].reshape(4, 128).T
    rwp[:, 46:50] = g("rwkv_k_k").reshape(4, 128).T
    rwp[:, 50:54] = g("rwkv_k_a").reshape(4, 128).T
    rwp[:, 54:58] = g("rwkv_r_k").reshape(4, 128).T
    out["rwpar"] = rwp
    w2, a2 = g("rwkv_w2"), g("rwkv_a2")
    lora = np.zeros((128, 3, 512), np.float32)
    lora[:, 0, :] = np.concatenate([w2[dirs[0]], w2[dirs[1]]], axis=0)
    lora[:, 1, :] = np.concatenate([a2[dirs[0]], a2[dirs[1]]], axis=0)
    lora[:, 2, :] = g("rwkv_g2")
    out["lora"] = lora
    idx = np.arange(128)
    same = (idx[:, None] // 64) == (idx[None, :] // 64)
    B0 = ((idx[None, :] < idx[:, None]) & same).astype(np.float32)
    I = np.eye(128, dtype=np.float32)
    rwm = np.zeros((128, 2, 5, 128), np.float32)
    for d, Bd in enumerate((B0, B0.T)):
        rwm[:, d, 0] = -Bd
        rwm[:, d, 1] = -Bd.T
        rwm[:, d, 2] = Bd.T
        rwm[:, d, 3] = Bd.T + I
        rwm[:, d, 4] = -(Bd.T + I)
    out["rwmask"] = rwm
    rwc = np.zeros((128, 1218), np.float32)
    rwc[:, 0:512] = (np.arange(512) % 64 != 0).astype(np.float32)[None, :]
    rwc[:, 1024:1152] = same.astype(np.float32)
    rwc[:, 1152:1216] = (idx[:, None] % 64 == np.arange(64)[None, :]).astype(np.float32)
    rwc[:, 1216] = (idx < 64)
    rwc[:, 1217] = (idx >= 64)
    out["rwconst"] = rwc
    return out


_CACHE = {}


def kernel(**inputs):
    if 'nc' not in _CACHE:
        _CACHE['nc'] = build()[0]
    nc = _CACHE['nc']
    maps = host_inputs(inputs)
    res = run_bass_kernel_spmd(nc, maps, core_ids=list(range(8)))
    out = np.zeros((4, S, D), np.float32)
    for c in range(8):
        b, h = c // 2, c % 2
        o = np.asarray(res.results[c]["out"])
        if h == 0:
            out[b, :OWN] = o
        else:
            out[b, OWN:] = o[::-1]
    return out
```

```python
import contextlib
import numpy as np
import ml_dtypes
import concourse.bass as bass
import concourse.mybir as mybir
from concourse.bass_utils import run_bass_kernel_spmd

F32 = mybir.dt.float32
BF16 = mybir.dt.bfloat16
AF = mybir.ActivationFunctionType
ALU = mybir.AluOpType
AX = mybir.AxisListType

D = 1024
DFF = 2816
NF = DFF // 128
S = 4096
OWN = 2048
TT = 512
EPS = 1e-6
GN_EPS = 64e-5
CDEC = float(np.exp(-0.5))
NKT = 16


class Prog:
    NDMA = 16

    def __init__(self):
        self.ops = []
        self.last_w = {}
        self.readers = {}
        self.last_barrier = 0
        self.warn = []
        self.pe_hist = []

    def add(self, eng, fn, r=(), w=(), dma=False, cc=False):
        i = len(self.ops)
        deps = set()
        for k in r:
            j = self.last_w.get(k)
            if j is not None:
                deps.add((j, 'raw'))
        for k in w:
            j = self.last_w.get(k)
            if j is not None:
                deps.add((j, 'waw'))
            for j in self.readers.get(k, ()):
                deps.add((j, 'war'))
        self.ops.append(dict(eng=eng, fn=fn, deps=deps, dma=dma, cc=cc))
        for k in w:
            if isinstance(k, str) and k.startswith('ps'):
                engs = set(self.ops[j]['eng'] for j in self.readers.get(k, ()))
                if len(engs) > 1:
                    self.warn.append(('multi-engine psum readers', k, sorted(engs), i))
        for k in r:
            self.readers.setdefault(k, []).append(i)
        for k in w:
            self.last_w[k] = i
            self.readers[k] = []
        return i

    def barrier(self):
        n = len(self.ops)
        deps = set()
        last = {}
        for i in range(n):
            op = self.ops[i]
            if op['dma']:
                if i >= self.last_barrier:
                    deps.add((i, 'raw'))
            elif op['fn'] is not None:
                last[op['eng']] = i
        for e, i in last.items():
            deps.add((i, 'raw'))
        for e in ['pe', 'act', 'dve', 'pool', 'sp']:
            self.ops.append(dict(eng=e, fn=None, deps=set(deps), dma=False, cc=False))
        self.last_barrier = n

    def emit(self, nc, stack):
        ops = self.ops
        n = len(ops)
        need = [set() for _ in range(n)]
        signaled = [False] * n
        for i, op in enumerate(ops):
            for (j, kind) in op['deps']:
                if j == i:
                    continue
                pj = ops[j]
                same = (pj['eng'] == op['eng'])
                if same and op['eng'] == 'pe' and not pj['dma'] and not op['dma'] and op['fn'] is not None:
                    if kind != 'raw':
                        continue
                need[i].add(j)
            latest = {}
            for j in need[i]:
                pj = ops[j]
                if pj['dma']:
                    continue
                e2 = pj['eng']
                if e2 not in latest or j > latest[e2]:
                    latest[e2] = j
            need[i] = set(j for j in need[i] if ops[j]['dma'] or latest[ops[j]['eng']] == j)
            for j in need[i]:
                signaled[j] = True
        engs = ['pe', 'act', 'dve', 'pool', 'sp']
        csem = {e: stack.enter_context(nc.semaphore("s_" + e)) for e in engs[:4]}
        dsem = {e: [stack.enter_context(nc.semaphore("d_%s%d" % (e, k))) for k in range(self.NDMA)]
                for e in ['sp', 'pool']}
        sig = [None] * n
        ccount = {e: 0 for e in engs}
        dcount = {e: 0 for e in engs}
        prevuse = [None] * n
        for i, op in enumerate(ops):
            e = op['eng']
            if op.get('cc'):
                sig[i] = (stack.enter_context(nc.semaphore("cc_%d" % i)), 1)
            elif op['dma']:
                k = dcount[e]
                dcount[e] += 1
                s = dsem[e][k % self.NDMA]
                sig[i] = (s, 16 * (k // self.NDMA + 1))
                if k >= self.NDMA:
                    prevuse[i] = (s, 16 * (k // self.NDMA))
            elif signaled[i]:
                ccount[e] += 1
                sig[i] = (csem[e], ccount[e])
        per = {e: [i for i in range(n) if ops[i]['eng'] == e] for e in engs}
        self.stats = {e: len(per[e]) for e in engs}
        self.stats['sem'] = dict(ccount)

        def run(e, eng):
            waited = {}
            for i in per[e]:
                op = ops[i]
                ws = []
                if prevuse[i] is not None:
                    ws.append(prevuse[i])
                for j in need[i]:
                    ws.append(sig[j])
                best = {}
                for (s, v) in ws:
                    key = id(s)
                    if v > best.get(key, (None, -1))[1]:
                        best[key] = (s, v)
                for key, (s, v) in best.items():
                    if waited.get(key, -1) >= v:
                        continue
                    eng.wait_ge(s, v)
                    waited[key] = v
                if op['fn'] is None:
                    continue
                ins = op['fn'](eng)
                if sig[i] is not None:
                    if op.get('cc'):
                        ins.then_inc(sig[i][0])
                    else:
                        ins.then_inc(sig[i][0], 16 if op['dma'] else 1)

        with nc.Block() as block:
            @block.tensor
            def _(eng):
                run('pe', eng)

            @block.scalar
            def _(eng):
                run('act', eng)

            @block.vector
            def _(eng):
                run('dve', eng)

            @block.gpsimd
            def _(eng):
                run('pool', eng)

            @block.sync
            def _(eng):
                run('sp', eng)


class Arena:
    def __init__(self, tensor, nbytes):
        self.t = tensor
        self.n = nbytes
        self.off = 0

    def alloc(self, shape, dt=F32):
        esz = mybir.dt.size(dt)
        nel = int(np.prod(shape[1:]))
        nb = (nel * esz + 31) // 32 * 32
        assert self.off + nb <= self.n, ("arena overflow", self.off, nb, self.n)
        ap = self.t[:, self.off // 4:(self.off + nb) // 4]
        self.off += nb
        if dt != F32:
            ap = ap.bitcast(dt)
        ap = ap[:, 0:nel]
        if len(shape) == 3:
            ap = ap.rearrange("p (a b) -> p a b", a=shape[1])
        elif len(shape) == 4:
            ap = ap.rearrange("p (a b c) -> p a b c", a=shape[1], b=shape[2])
        return ap


class Env:
    pass


def build(debug=False, phases=('p1', 'att', 'rwkv', 'p4'), ntiles1=8, rwkv_stop=None):
    nc = bass.Bass("TRN2", target_bir_lowering=False)
    P = Prog()
    stack = contextlib.ExitStack()
    E = Env()

    def din(name, shape, dt=F32):
        return nc.dram_tensor(name, list(shape), dt, kind="ExternalInput").ap()

    def dscr(name, shape, dt=F32, out=False):
        if out:
            return nc.dram_tensor(name, list(shape), dt, kind="ExternalOutput").ap()
        return nc.dram_tensor(name, list(shape), dt).ap()

    x_d = din("x", [S, D])
    wgu_d = [din("wgu%d" % k, [NF, 128, 2, 8, 128]) for k in (1, 2)]
    wdc_d = [din("wdc%d" % k, [8, 128, NF, 128]) for k in (1, 2)]
    win_d = din("win", [31, 128, 8, 128])
    wv_d = din("wv", [128, 8, 512])
    wo_d = din("wo", [8, 128, 8, 128])
    gv_d = din("gv", [128, 6, 8])
    ident_d = din("ident", [128, 128])
    cos_d = din("cosT", [128, S])
    sin_d = din("sinT", [128, S])
    wm_d = din("wm", [20, 128, 512], BF16)
    rep_d = din("rep", [128, 3, 512])
    rwm_d = din("rwmask", [128, 2, 5, 128])
    rwc_d = din("rwconst", [128, 1218])
    rwp_d = din("rwpar", [128, 64])
    lora_d = din("lora", [128, 3, 512])
    out_d = nc.dram_tensor("out", [OWN, D], F32, kind="ExternalOutput").ap()
    h1s_d = dscr("h1s", [8, 128, OWN], out=debug)
    zr_d = dscr("zr", [15, 128, S], out=debug)
    qT_d = dscr("qTs", [4, 128, OWN], BF16, out=debug)
    kT_d = dscr("kTs", [4, 128, NKT * 128], BF16, out=debug)
    v_d = dscr("vs", [NKT * 128, 520], BF16, out=debug)
    yc_d = dscr("yc", [OWN, D], out=debug)
    yf_d = dscr("yf", [OWN, 512], out=debug)
    hsel_d = din("hsel", [128, 2])
    wmf_d = din("wmf", [20, 128, 512], BF16)
    kh_d = dscr("kh", [512, 1024], BF16)
    vh_d = dscr("vh", [1024, 520], BF16)
    zb_d = dscr("zb", [128, 15])
    khg_d = dscr("khg", [1024, 1024], BF16)
    vhg_d = dscr("vhg", [2048, 520], BF16)
    zbg_d = dscr("zbg", [256, 15])
    tf_d = dscr("tf", [128, 256])
    tfg_d = dscr("tfg", [256, 256])
    wgu16_d = [dscr("wgu16_%d" % k, [NF, 128, 2 * 8 * 128], BF16) for k in (1, 2)]
    wdc16_d = [dscr("wdc16_%d" % k, [8, 128, NF * 128], BF16) for k in (1, 2)]
    win16_d = dscr("win16", [31, 128, 8 * 128], BF16)
    wo16_d = dscr("wo16", [8, 128, 8 * 128], BF16)
    PAIRS = [[0, 1], [2, 3], [4, 5], [6, 7]]

    def CC(in_ap, out_ap, r, w):
        return P.add('pool', lambda e: e.collective_compute("AllGather", ALU.bypass, replica_groups=PAIRS,
                                                            ins=[in_ap.opt()], outs=[out_ap.opt()]),
                     r=r, w=w, dma=True, cc=True)

    ARENA_BYTES = 204 * 1024
    arena_t = stack.enter_context(nc.sbuf_tensor("arena", [128, ARENA_BYTES // 4], F32))
    A = Arena(arena_t, ARENA_BYTES)
    ps = [stack.enter_context(nc.psum_tensor("ps%d" % i, [128, 512], F32)) for i in range(8)]

    def MM(out, lhsT, rhs, start, stop, r, w):
        rb0, kk_ = lhsT.base_partition(), lhsT.shape[0]
        for (pb0, pk, pw) in P.pe_hist[-1:]:
            if (rb0 + kk_ <= pb0 or pb0 + pk <= rb0) and pw == w[0]:
                P.warn.append(('row-group conflict', w[0], (pb0, pk), (rb0, kk_), len(P.ops)))
        P.pe_hist.append((rb0, kk_, w[0]))
        P.add('pe', lambda e: e.matmul(out, lhsT, rhs, start=start, stop=stop), r=r, w=w)

    def TR(out, in_, idt, r, w):
        P.pe_hist.append((in_.base_partition(), in_.shape[0], w[0]))
        P.add('pe', lambda e: e.transpose(out, in_, idt), r=r, w=w)

    def ACT(out, in_, func, r, w, bias=None, scale=None):
        kw = {}
        if bias is not None:
            kw['bias'] = bias
        if scale is not None:
            kw['scale'] = scale
        P.add('act', lambda e: e.activation(out, in_, func, **kw), r=r, w=w)

    def TTo(eng, out, in0, in1, op, r, w):
        P.add(eng, lambda e: e.tensor_tensor(out=out, in0=in0, in1=in1, op=op), r=r, w=w)

    def TS(eng, out, in0, s1, s2, op0, op1, r, w):
        if op1 is None:
            P.add(eng, lambda e: e.tensor_scalar(out=out, in0=in0, scalar1=s1, scalar2=None, op0=op0), r=r, w=w)
        else:
            P.add(eng, lambda e: e.tensor_scalar(out=out, in0=in0, scalar1=s1, scalar2=s2, op0=op0, op1=op1),
                  r=r, w=w)

    def STT(out, in0, scalar, in1, op0, op1, r, w):
        P.add('dve', lambda e: e.scalar_tensor_tensor(out=out, in0=in0, scalar=scalar, in1=in1, op0=op0, op1=op1),
              r=r, w=w)

    def CP(eng, out, in_, r, w):
        if eng == 'act':
            P.add('act', lambda e: e.copy(out, in_), r=r, w=w)
        else:
            P.add(eng, lambda e: e.tensor_copy(out, in_), r=r, w=w)

    def RECIP(out, in_, r, w):
        P.add('dve', lambda e: e.reciprocal(out, in_), r=r, w=w)

    def DMA(q, out, in_, r, w):
        return P.add(q, lambda e: e.dma_start(out=out, in_=in_), r=r, w=w, dma=True)

    def MEMSET(eng, ap, val, w):
        P.add(eng, lambda e: e.memset(ap, val), w=w)

    ident = A.alloc([128, 128])
    identb = A.alloc([128, 128], BF16)
    ones = A.alloc([128, 128], BF16)
    gv = A.alloc([128, 6, 8])
    gvh = A.alloc([128, 6, 8])
    eps_t = A.alloc([128, 1])
    gneps_t = A.alloc([128, 1])
    rep = A.alloc([128, 3, 512])
    rwm = A.alloc([128, 2, 5, 128])
    rwc = A.alloc([128, 1218])
    rwp = A.alloc([128, 64])
    lora = A.alloc([128, 3, 512])
    hsel = A.alloc([128, 2])
    DMA('sp', ident, ident_d, [], ['ident'])
    DMA('sp', gv, gv_d, [], ['gv'])
    DMA('sp', rep, rep_d, [], ['rep'])
    DMA('sp', rwm, rwm_d, [], ['rwm'])
    DMA('sp', rwc, rwc_d, [], ['rwc'])
    DMA('sp', rwp, rwp_d, [], ['rwp'])
    DMA('sp', lora, lora_d, [], ['lora'])
    DMA('sp', hsel, hsel_d, [], ['hsel'])
    MEMSET('pool', ones, 1.0, ['ones'])
    MEMSET('pool', eps_t, EPS, ['eps'])
    MEMSET('pool', gneps_t, GN_EPS, ['eps'])
    CP('dve', identb, ident, ['ident'], ['identb'])
    TS('dve', gvh, gv, 0.5, None, ALU.mult, None, ['gv'], ['gvh'])
    A_MARK = A.off

    cnt = {'ps01': 0, 'x': 0, 'z': 0, 'st': 0}
    for k_, v_ in list(locals().items()):
        setattr(E, k_, v_)

    def alloc_ffn_bufs():
        B = {}
        B['xin'] = [A.alloc([128, D]) for _ in range(4)]
        B['xT'] = A.alloc([128, 8, TT])
        B['hn'] = A.alloc([128, 8, TT], BF16)
        B['sq'] = A.alloc([128, 8, TT], BF16)
        B['fb'] = A.alloc([128, 8, TT])
        B['aT'] = A.alloc([128, NF, TT], BF16)
        B['rstd'] = A.alloc([128, TT])
        B['tmp'] = A.alloc([128, TT])
        B['wgu'] = [A.alloc([128, 2, 8, 128], BF16) for _ in range(3)]
        B['wdc'] = [A.alloc([128, NF, 128], BF16) for _ in range(3)]
        B['sgl'] = [A.alloc([128, TT]) for _ in range(2)]
        return B

    def issue_x_loads(B, src_d, tt):
        xin = B['xin']
        for sub in range(4):
            t0 = tt * TT + sub * 128
            DMA('sp', xin[sub], src_d[t0:t0 + 128, :], ['yc'] if src_d is yc_d else [], ['xin%d' % sub])

    def load_T(B, src_d, tt, dst, dstkey, loads_issued=False):
        xin = B['xin']
        if not loads_issued:
            issue_x_loads(B, src_d, tt)
        for sub in range(4):
            b = sub
            for half in range(2):
                pb = cnt['ps01'] % 2
                cnt['ps01'] += 1
                for q in range(4):
                    c = half * 4 + q
                    TR(ps[pb][:, q * 128:(q + 1) * 128], xin[b][:, c * 128:(c + 1) * 128], ident,
                       ['xin%d' % b, 'ident'], ['ps%d' % pb])
                src = ps[pb][:].rearrange("p (q t) -> p q t", q=4)
                d_ = dst[:, half * 4:(half + 1) * 4, sub * 128:(sub + 1) * 128]
                CP('act' if half == 0 else 'dve', d_, src, ['ps%d' % pb],
                   [(dstkey, half * 4 + q) for q in range(4)])

    def rmsnorm_stats(B, src, srckey):
        sq, tmp, rstd = B['sq'], B['tmp'], B['rstd']
        pb = cnt['ps01'] % 2
        cnt['ps01'] += 1
        for c in range(8):
            ACT(sq[:, c, :], src[:, c, :], AF.Square, [(srckey, c)], [('sq', c)])
            MM(ps[pb][:], ones, sq[:, c, :], c == 0, c == 7, [('sq', c), 'ones'], ['ps%d' % pb])
        ACT(tmp, ps[pb][:], AF.Sqrt, ['ps%d' % pb, 'eps'], ['tmp'], bias=eps_t, scale=1.0 / D)
        RECIP(rstd, tmp, ['tmp'], ['rstd'])

    def prenorm(B, gidx):
        xT, hn, rstd = B['xT'], B['hn'], B['rstd']
        rmsnorm_stats(B, xT, 'xT')
        for c in range(8):
            STT(hn[:, c, :], xT[:, c, :], gv[:, gidx, c:c + 1], rstd, ALU.mult, ALU.mult,
                [('xT', c), 'gv', 'rstd'], [('hn', c)])

    def ffn(B, k, gpre, gpost, first):
        xT, hn, fb, aT, rstd = B['xT'], B['hn'], B['fb'], B['aT'], B['rstd']
        wgu, wdc, sgl = B['wgu'], B['wdc'], B['sgl']
        prenorm(B, gpre)

        def ld_gu(j):
            b = j % 3
            flat = wgu[b].rearrange("p a c f -> p (a c f)")
            if first:
                DMA('pool', wgu[b], wgu_d[k][j], [], ['wgu%d' % b])
                DMA('sp', wgu16_d[k][j], flat, ['wgu%d' % b], [('wgu16', k, j)])
            else:
                DMA('sp', flat, wgu16_d[k][j], [('wgu16', k, j)], ['wgu%d' % b])

        def ld_d(c):
            b = c % 3
            flat = wdc[b].rearrange("p j f -> p (j f)")
            if first:
                DMA('pool', wdc[b], wdc_d[k][c], [], ['wdc%d' % b])
                DMA('sp', wdc16_d[k][c], flat, ['wdc%d' % b], [('wdc16', k, c)])
            else:
                DMA('sp', flat, wdc16_d[k][c], [('wdc16', k, c)], ['wdc%d' % b])
        ld_gu(0)
        ld_gu(1)
        for j in range(NF):
            if j + 2 < NF:
                ld_gu(j + 2)
            if j == NF - 3:
                ld_d(0)
            if j == NF - 1:
                ld_d(1)
            b = j % 3
            pg = 2 + (j % 2) * 2
            pu = pg + 1
            for which, pbank in ((0, pg), (1, pu)):
                for c in range(8):
                    MM(ps[pbank][:], wgu[b][:, which, c, :], hn[:, c, :], c == 0, c == 7,
                       ['wgu%d' % b, ('hn', c)], ['ps%d' % pbank])
            sgb = j % 2
            ACT(sgl[sgb], ps[pg][:], AF.Silu, ['ps%d' % pg], ['sgl%d' % sgb])
            TTo('dve', aT[:, j, :], sgl[sgb], ps[pu][:], ALU.mult, ['sgl%d' % sgb, 'ps%d' % pu], [('aT', j)])
        for c in range(8):
            if c + 2 < 8:
                ld_d(c + 2)
            b = c % 3
            pf = 6 + (c % 2)
            for j in range(NF):
                MM(ps[pf][:], wdc[b][:, j, :], aT[:, j, :], j == 0, j == NF - 1,
                   ['wdc%d' % b, ('aT', j)], ['ps%d' % pf])
            CP('act' if c % 2 == 0 else 'dve', fb[:, c, :], ps[pf][:], ['ps%d' % pf], [('fb', c)])
        rmsnorm_stats(B, fb, 'fb')
        for c in range(8):
            ACT(fb[:, c, :], fb[:, c, :], AF.Copy, [('fb', c), 'gvh'], [('fb', c)], scale=gvh[:, gpost, c:c + 1])
            TTo('pool', fb[:, c, :], fb[:, c, :], rstd, ALU.mult, [('fb', c), 'rstd'], [('fb', c)])
            TTo('dve', xT[:, c, :], fb[:, c, :], xT[:, c, :], ALU.add, [('fb', c), ('xT', c)], [('xT', c)])

    if 'p1' in phases:
        B = alloc_ffn_bufs()
        winb = [A.alloc([128, 8, 128], BF16) for _ in range(4)]
        wvb = A.alloc([128, 8, 512], BF16)
        cosb = A.alloc([128, TT])
        sinb = A.alloc([128, TT])
        ra = A.alloc([128, TT])
        rb = A.alloc([128, TT])
        qst = [A.alloc([128, TT], BF16) for _ in range(2)]
        vst = [A.alloc([128, 8, 65], BF16) for _ in range(2)]
        zst = [A.alloc([128, TT]) for _ in range(2)]
        xT, hn = B['xT'], B['hn']
        DMA('pool', wvb, wv_d, [], ['wvb'])
        for b in range(2):
            MEMSET('pool', vst[b], 1.0, ['vst%d' % b])
        zbank = [2, 3, 4, 5]
        for tt in range(min(ntiles1, 4)):
            load_T(B, x_d, tt, xT, 'xT', loads_issued=(tt > 0))
            ffn(B, 0, 0, 1, tt == 0)
            if tt < 4:
                DMA('sp', h1s_d[:, :, tt * TT:(tt + 1) * TT].rearrange("c p t -> p c t"), xT, [('xT', c_) for c_ in range(8)], ['h1s'])
            prenorm(B, 2)
            if tt + 1 < min(ntiles1, 4):
                issue_x_loads(B, x_d, tt + 1)
            if tt < 6:
                DMA('sp', cosb, cos_d[:, tt * TT:(tt + 1) * TT], [], ['cosb'])
                DMA('sp', sinb, sin_d[:, tt * TT:(tt + 1) * TT], [], ['sinb'])
            jobs = []
            if tt < 4:
                jobs += [('q', g, [2 * g, 2 * g + 1]) for g in range(4)]
            if tt < 6:
                jobs += [('k', g, [8 + 2 * g, 9 + 2 * g]) for g in range(4)]
            jobs += [('z', j, [16 + j]) for j in range(15)]
            loads = [ci for (_, _, cis) in jobs for ci in cis]

            def ldw(n):
                if n < len(loads):
                    b = n % 4
                    flat = winb[b].rearrange("p c f -> p (c f)")
                    if tt == 0:
                        DMA('pool', winb[b], win_d[loads[n]], [], ['winb%d' % b])
                        DMA('sp', win16_d[loads[n]], flat, ['winb%d' % b], [('win16', loads[n])])
                    else:
                        DMA('sp', flat, win16_d[loads[n]], [('win16', loads[n])], ['winb%d' % b])
            for n in range(3):
                ldw(n)
            li = 0
            for (kind, g, cis) in jobs:
                banks = []
                for ci in cis:
                    ldw(li + 3)
                    b = li % 4
                    li += 1
                    zb = zbank[cnt['z'] % 4]
                    cnt['z'] += 1
                    banks.append(zb)
                    for c in range(8):
                        MM(ps[zb][:], winb[b][:, c, :], hn[:, c, :], c == 0, c == 7,
                           ['winb%d' % b, ('hn', c)], ['ps%d' % zb])
                sb_ = cnt['st'] % 2
                cnt['st'] += 1
                if kind in ('q', 'k'):
                    TTo('dve', ra, ps[banks[0]][:], cosb, ALU.mult, ['ps%d' % banks[0], 'cosb'], ['ra'])
                    TTo('dve', rb, ps[banks[1]][:], sinb, ALU.mult, ['ps%d' % banks[1], 'sinb'], ['rb'])
                    TTo('pool', qst[sb_], ra, rb, ALU.add, ['ra', 'rb'], ['qst%d' % sb_])
                    dst = qT_d if kind == 'q' else kT_d
                    DMA('sp', dst[g, :, tt * TT:(tt + 1) * TT], qst[sb_], ['qst%d' % sb_], ['qk_d'])
                    if kind == 'k' and tt >= 2:
                        DMA('sp', kh_d[g * 128:(g + 1) * 128, (tt - 2) * TT:(tt - 1) * TT], qst[sb_],
                            ['qst%d' % sb_], ['kh_d'])
                else:
                    CP('act' if g % 2 == 0 else 'dve', zst[sb_], ps[banks[0]][:], ['ps%d' % banks[0]],
                       ['zst%d' % sb_])
                    DMA('sp', zr_d[g, :, tt * TT:(tt + 1) * TT], zst[sb_], ['zst%d' % sb_], ['zr'])
                    if tt == 3:
                        P.add('sp', (lambda e, g=g, sb_=sb_: e.dma_start(
                            out=zb_d[:, g:g + 1], in_=zst[sb_][:, TT - 1:TT], allow_slow_non_contiguous=True)),
                            r=['zst%d' % sb_], w=['zb_d'], dma=True)
            if tt < 6:
                for sub in range(4):
                    zb = zbank[cnt['z'] % 4]
                    cnt['z'] += 1
                    for c in range(8):
                        MM(ps[zb][:], hn[:, c, sub * 128:(sub + 1) * 128], wvb[:, c, :], c == 0, c == 7,
                           [('hn', c), 'wvb'], ['ps%d' % zb])
                    sb_ = cnt['st'] % 2
                    cnt['st'] += 1
                    CP('act' if sub % 2 == 0 else 'dve', vst[sb_][:, :, 0:64],
                       ps[zb][:].rearrange("p (h d) -> p h d", h=8), ['ps%d' % zb], ['vst%d' % sb_])
                    r0 = (tt * 4 + sub) * 128
                    DMA('sp', v_d[r0:r0 + 128, :], vst[sb_].rearrange("p h e -> p (h e)"),
                        ['vst%d' % sb_], ['v_d'])
                    if tt >= 2:
                        DMA('sp', vh_d[r0 - 1024:r0 - 1024 + 128, :], vst[sb_].rearrange("p h e -> p (h e)"),
                            ['vst%d' % sb_], ['vh_d'])
        P.barrier()
        A.off = A_MARK
        CC(kh_d, khg_d, ['kh_d'], ['khg'])
        CC(vh_d, vhg_d, ['vh_d'], ['vhg'])
        CC(zb_d, zbg_d, ['zb_d'], ['zbg'])

    if 'att' in phases:
        qT = A.alloc([128, 4, OWN], BF16)
        kT = A.alloc([128, 4, NKT * 128], BF16)
        va = A.alloc([128, NKT, 520], BF16)
        wm = A.alloc([128, 20, 512], BF16)
        wmf = A.alloc([128, 20, 512], BF16)
        khr = A.alloc([128, 2, 4, 1024], BF16)
        vhr = A.alloc([128, 2, 8, 520], BF16)
        kh = A.alloc([128, 4, 1024], BF16)
        vh = A.alloc([128, 8, 520], BF16)
        oall = A.alloc([128, 16, 512])
        eb = [A.alloc([128, 512], BF16) for _ in range(4)]
        pm = [A.alloc([128, 512], BF16) for _ in range(4)]
        rden = [A.alloc([128, 4]) for _ in range(2)]
        sq32 = A.alloc([128, 512])
        ss = A.alloc([128, 8])
        rs = A.alloc([128, 8])
        DMA('sp', qT, qT_d.rearrange("g p t -> p g t"), ['qk_d'], ['qT'])
        DMA('sp', kT, kT_d.rearrange("g p t -> p g t"), ['qk_d'], ['kT'])
        DMA('sp', va, v_d.rearrange("(n p) f -> p n f", p=128), ['v_d'], ['va'])
        DMA('sp', wm, wm_d.rearrange("j p c -> p j c"), [], ['wm'])
        DMA('sp', wmf, wmf_d.rearrange("j p c -> p j c"), [], ['wmf'])
        DMA('sp', khr, khg_d.rearrange("(r g p) t -> p r g t", r=2, g=4), ['khg'], ['khr'])
        DMA('sp', vhr, vhg_d.rearrange("(r n p) f -> p r n f", r=2, n=8), ['vhg'], ['vhr'])
        for (raw, dst, key, n_) in ((khr, kh, 'kh', 4096), (vhr, vh, 'vh', 4160)):
            r0_ = raw[:, 0].rearrange("p a b -> p (a b)")
            r1_ = raw[:, 1].rearrange("p a b -> p (a b)")
            d_ = dst.rearrange("p a b -> p (a b)")
            TS('dve', d_, r0_, hsel[:, 0:1], None, ALU.mult, None, [key + 'r', 'hsel'], [key])
            STT(d_, r1_, hsel[:, 1:2], d_, ALU.mult, ALU.add, [key + 'r', 'hsel', key], [key])
        items = []
        for hh in range(8):
            for qb in range(4):
                blk_id = hh * 4 + qb
                kts = list(range(max(0, 4 * qb - 8), 4 * qb + 12))
                pvl = [(kt, i) for kt in kts for i in range(4) if abs(kt - 4 * qb - i) <= 8]
                for kt in kts:
                    items.append(dict(hh=hh, qb=qb, kt=kt, J=kt - 4 * qb + 8, ob=6 + blk_id % 2,
                                      first=pvl[0], last=pvl[-1], endblk=(kt == kts[-1]), blk=blk_id))
        NB = 4
        SKEW = 2

        def stage_a(n):
            d = items[n]
            hh, qb, kt, J = d['hh'], d['qb'], d['kt'], d['J']
            g, base = hh // 2, 64 * (hh % 2)
            sbk = 2 + (n % NB)
            b = n % NB
            if kt < NKT:
                kop, kkey, mk, mkey_ = kT[base:base + 64, g, kt * 128:(kt + 1) * 128], 'kT', wm, 'wm'
            else:
                ht = 23 - kt
                kop, kkey, mk, mkey_ = kh[base:base + 64, g, ht * 128:(ht + 1) * 128], 'kh', wmf, 'wmf'
            MM(ps[sbk][:], kop, qT[base:base + 64, g, qb * 512:(qb + 1) * 512], True, True, [kkey, 'qT'],
               ['ps%d' % sbk])
            ACT(eb[b], ps[sbk][:], AF.Exp, ['ps%d' % sbk], ['eb%d' % b], scale=0.125)
            TTo('dve', pm[b], eb[b], mk[:, J, :], ALU.mult, ['eb%d' % b, mkey_], ['pm%d' % b])

        def stage_b(n):
            d = items[n]
            hh, qb, kt, J, ob = d['hh'], d['qb'], d['kt'], d['J'], d['ob']
            b = n % NB
            for i in range(4):
                if abs(J - 8 - i) > 8:
                    continue
                vop, vkey = (va[:, kt, hh * 65:(hh + 1) * 65], 'va') if kt < NKT else \
                    (vh[:, 23 - kt, hh * 65:(hh + 1) * 65], 'vh')
                MM(ps[ob][:, i * 65:(i + 1) * 65], pm[b][:, i * 128:(i + 1) * 128],
                   vop, (kt, i) == d['first'], (kt, i) == d['last'],
                   ['pm%d' % b, vkey], ['ps%d' % ob])
            if d['endblk']:
                o4 = ps[ob][:, 0:260].rearrange("p (i e) -> p i e", e=65)
                rb_ = d['blk'] % 2
                RECIP(rden[rb_].unsqueeze(2), o4[:, :, 64:65], ['ps%d' % ob], ['rden%d' % rb_])
                TTo('dve', oall[:, qb * 4:(qb + 1) * 4, hh * 64:(hh + 1) * 64], o4[:, :, 0:64],
                    rden[rb_].unsqueeze(2).broadcast_to([128, 4, 64]), ALU.mult,
                    ['ps%d' % ob, 'rden%d' % rb_], [('oall', qb)])
        for n in range(len(items) + SKEW):
            if n < len(items):
                stage_a(n)
            if n >= SKEW:
                stage_b(n - SKEW)
        for qt in range(16):
            o = oall[:, qt, :]
            o3 = o.rearrange("p (h d) -> p h d", h=8)
            ACT(sq32, o, AF.Square, [('oall', qt // 4)], ['sq32'])
            P.add('dve', lambda e: e.tensor_reduce(out=ss, in_=sq32.rearrange("p (h d) -> p h d", h=8),
                                                   axis=AX.X, op=ALU.add), r=['sq32'], w=['ss'])
            ACT(rs, ss, AF.Sqrt, ['ss', 'eps'], ['rs'], bias=eps_t, scale=1.0 / 64)
            RECIP(rs, rs, ['rs'], ['rs'])
            TTo('dve', o3, o3, rs.unsqueeze(2).broadcast_to([128, 8, 64]), ALU.mult,
                [('oall', qt // 4), 'rs'], [('oall', qt // 4)])
            TTo('dve', o, o, rep[:, 0, :], ALU.mult, [('oall', qt // 4), 'rep'], [('oall', qt // 4)])
            DMA('sp', yc_d[qt * 128:(qt + 1) * 128, 0:512], o, [('oall', qt // 4)], ['yc'])
        P.barrier()
        A.off = A_MARK

    if 'rwkv' in phases:
        rwkv_phase(E, rwkv_stop)
        P.barrier()
        A.off = A_MARK

    if 'p4' in phases:
        B = alloc_ffn_bufs()
        wob = [A.alloc([128, 8, 128], BF16) for _ in range(2)]
        ost = [A.alloc([128, D]) for _ in range(2)]
        xT, hn, fb, rstd = B['xT'], B['hn'], B['fb'], B['rstd']
        for tt in range(4):
            load_T(B, yc_d, tt, hn, 'hn')
            def ld_o(dc):
                b_ = dc % 2
                flat = wob[b_].rearrange("p c f -> p (c f)")
                if tt == 0:
                    DMA('pool', wob[b_], wo_d[dc], [], ['wob%d' % b_])
                    DMA('sp', wo16_d[dc], flat, ['wob%d' % b_], [('wo16', dc)])
                else:
                    DMA('sp', flat, wo16_d[dc], [('wo16', dc)], ['wob%d' % b_])
            ld_o(0)
            for dc in range(8):
                if dc + 1 < 8:
                    ld_o(dc + 1)
                b = dc % 2
                pf = 6 + (dc % 2)
                for cc in range(8):
                    MM(ps[pf][:], wob[b][:, cc, :], hn[:, cc, :], cc == 0, cc == 7,
                       ['wob%d' % b, ('hn', cc)], ['ps%d' % pf])
                CP('act' if dc % 2 == 0 else 'dve', fb[:, dc, :], ps[pf][:], ['ps%d' % pf], [('fb', dc)])
            rmsnorm_stats(B, fb, 'fb')
            DMA('sp', xT, h1s_d[:, :, tt * TT:(tt + 1) * TT].rearrange("c p t -> p c t"), ['h1s'], [('xT', c_) for c_ in range(8)])
            for c in range(8):
                ACT(fb[:, c, :], fb[:, c, :], AF.Copy, [('fb', c), 'gv'], [('fb', c)], scale=gv[:, 3, c:c + 1])
                TTo('pool', fb[:, c, :], fb[:, c, :], rstd, ALU.mult, [('fb', c), 'rstd'], [('fb', c)])
                TTo('dve', xT[:, c, :], fb[:, c, :], xT[:, c, :], ALU.add, [('fb', c), ('xT', c)], [('xT', c)])
            ffn(B, 1, 4, 5, tt == 0)
            for sub in range(4):
                ob = cnt['st'] % 2
                cnt['st'] += 1
                for half in range(2):
                    pb = cnt['ps01'] % 2
                    cnt['ps01'] += 1
                    for q in range(4):
                        c = half * 4 + q
                        TR(ps[pb][:, q * 128:(q + 1) * 128], xT[:, c, sub * 128:(sub + 1) * 128], ident,
                           [('xT', c), 'ident'], ['ps%d' % pb])
                    CP('act' if half == 0 else 'dve', ost[ob][:, half * 512:(half + 1) * 512], ps[pb][:],
                       ['ps%d' % pb], ['ost%d' % ob])
                r0 = tt * TT + sub * 128
                DMA('sp', out_d[r0:r0 + 128, :], ost[ob], ['ost%d' % ob], ['out'])

    P.add('sp', None, r=['h1s', 'out', 'yc', 'zr', 'qk_d', 'v_d', 'yf'])
    P.emit(nc, stack)
    stack.close()
    return nc, P


def rwkv_phase(E, debug_stop=None):
    P, A, ps = E.P, E.A, E.ps
    MM, TR, ACT, TTo, TS, STT, CP, RECIP, DMA, MEMSET = (E.MM, E.TR, E.ACT, E.TTo, E.TS, E.STT, E.CP, E.RECIP,
                                                          E.DMA, E.MEMSET)
    ident, identb, rep, rwm, rwc, rwp, lora = E.ident, E.identb, E.rep, E.rwm, E.rwc, E.rwp, E.lora
    eps_t, gneps_t = E.eps_t, E.gneps_t
    zr_d, yc_d, yf_d = E.zr_d, E.yc_d, E.yf_d

    rmask = rwc[:, 0:512]
    blockones = rwc[:, 1024:1152]
    identblk = rwc[:, 1152:1216]
    hsel = rwc[:, 1216:1218]
    mp, mn = rwp[:, 0:15], rwp[:, 15:30]
    k_k, k_a, r_k = rwp[:, 46:50], rwp[:, 50:54], rwp[:, 54:58]
    w2sb, a2sb, g2sb = lora[:, 0, :], lora[:, 1, :], lora[:, 2, :]

    def w0T(di, c):
        return rwp[:, 30 + 4 * di + c:31 + 4 * di + c]

    def a0T(di, c):
        return rwp[:, 38 + 4 * di + c:39 + 4 * di + c]

    c0 = A.alloc([128, 15])
    omka = A.alloc([128, 4])
    rk2 = A.alloc([128, 4])
    GW = 256
    NTG = GW // 128
    NG = 4096 // GW
    NG_OWN = 2048 // GW
    zsb = [A.alloc([128, GW + 2]) for _ in range(3)]
    zcnt = {'z': 0}
    ub = A.alloc([128, 15, GW])
    tw = A.alloc([128, GW])
    T_ = {nm: A.alloc([128, GW]) for nm in
          ('sg', 'av', 'kx', 't1', 't2', 'kkv', 'kd', 'ka', 'cs', 'cs2', 'e0', 'e1', 'e2', 'e3', 'kd0')}
    tot_s = A.alloc([128, GW // 64])
    OB = []
    for _ in range(2):
        d_ = {nm: A.alloc([128, 4, GW], BF16) for nm in ('Rb', 'Kb', 'Kd', 'Ad', 'Kh', 'Ahn', 'vb')}
        d_['gc'] = A.alloc([128, 4, GW // 64])
        d_['bprod'] = A.alloc([128, 4, GW])
        d_['sgd'] = A.alloc([128, GW])
        d_['uvf'] = A.alloc([128, 4, GW])
        OB.append(d_)
    KbT, KhT, AhT, VT = [A.alloc([128, 512], BF16) for _ in range(4)]
    VT32 = A.alloc([128, 512])
    Xs = [[A.alloc([128, 4, 128], BF16) for _ in range(2)] for _ in range(2)]
    Ys = [[A.alloc([128, 4, 128], BF16) for _ in range(2)] for _ in range(2)]
    Rms = [[A.alloc([128, 4, 128], BF16) for _ in range(2)] for _ in range(2)]
    pkks, qrks, qras = [[A.alloc([128, 4, 128], BF16) for _ in range(2)] for _ in range(3)]
    pvs = [A.alloc([128, 4, 64], BF16) for _ in range(2)]
    u0 = A.alloc([128, 8, 64], BF16)
    wt = A.alloc([128, 8, 64], BF16)
    y0 = A.alloc([128, 512])
    rpT = A.alloc([128, 4, 128])
    GT = A.alloc([128, 4, 2, 64])
    Hs = A.alloc([128, 4, 2, 64])
    Tst = [A.alloc([128, 4, 64]) for _ in range(2)]
    ytile = A.alloc([128, 512])
    yfb = A.alloc([128, 512])
    sqb = A.alloc([128, 512])
    tmpv = A.alloc([128, 512])
    s1 = A.alloc([128, 8])
    s2 = A.alloc([128, 8])
    bon = A.alloc([128, 8])
    psb2 = ps[2][:].bitcast(BF16)

    TTo('dve', c0, mp, mn, ALU.add, ['rwp'], ['c0'])
    TS('dve', c0, c0, -1.0, 1.0, ALU.mult, ALU.add, ['c0'], ['c0'])
    TS('dve', omka, k_a, -1.0, 1.0, ALU.mult, ALU.add, ['rwp'], ['omka'])
    TS('dve', rk2, r_k, 0.5, None, ALU.mult, None, ['rwp'], ['rk2'])
    zr2 = A.alloc([128, 2, 15])
    zedge = A.alloc([128, 15])
    zg = E.zbg_d.rearrange("(r p) c -> p r c", r=2)
    DMA('sp', zr2, zg, ['zbg'], ['zr2'])
    DMA('sp', zr2[0:64, :, 12:14], zg[64:128, :, 12:14], ['zbg'], ['zr2'])
    DMA('sp', zr2[64:128, :, 12:14], zg[0:64, :, 12:14], ['zbg'], ['zr2'])
    TS('dve', zedge, zr2[:, 0, :], E.hsel[:, 0:1], None, ALU.mult, None, ['zr2', 'hsel'], ['zedge'])
    STT(zedge, zr2[:, 1, :], E.hsel[:, 1:2], zedge, ALU.mult, ALU.add, ['zr2', 'hsel', 'zedge'], ['zedge'])
    tfr = A.alloc([128, 2, 256])

    pcnt = {'pa': 0}

    def pbank(lo=0):
        b = (6 if lo == 0 else 0) + pcnt['pa'] % 2
        pcnt['pa'] += 1
        return b

    def prep_group(g, di, own, final, pb):
        O_ = OB[pb]
        Rb, Kb, Kd, Ad, Kh, Ahn, vb = (O_[n] for n in ('Rb', 'Kb', 'Kd', 'Ad', 'Kh', 'Ahn', 'vb'))
        gc, bprod, sgd, uvf = O_['gc'], O_['bprod'], O_['sgd'], O_['uvf']
        lo, hi = GW * g - 1, GW * g + GW + 1
        clo, chi = max(lo, 0), min(hi, 2048)
        need = list(range(4, 12)) + [12, 13] + ([0, 1, 2, 3] if own else []) + ([14] if final else [])
        for n_, j in enumerate(need):
            zb = zsb[zcnt['z'] % 3]
            zk = 'zsb%d' % (zcnt['z'] % 3)
            zcnt['z'] += 1
            DMA('sp', zb[:, clo - lo:GW + 2 - (hi - chi)], zr_d[j, :, clo:chi], ['zr'], [zk])
            if g == 0:
                MEMSET('pool', zb[:, 0:1], 0.0, [zk])
            if g == NG_OWN - 1:
                CP('pool', zb[:, GW + 1:GW + 2], zedge[:, j:j + 1], ['zedge'], [zk])
            k = ('ub', j)
            ACT(ub[:, j, :], zb[:, 0:GW], AF.Copy, [zk, 'rwp'], [k], scale=mp[:, j:j + 1])
            STT(ub[:, j, :], zb[:, 2:GW + 2], mn[:, j:j + 1], ub[:, j, :], ALU.mult, ALU.add, [zk, 'rwp', k], [k])
            STT(ub[:, j, :], zb[:, 1:GW + 1], c0[:, j:j + 1], ub[:, j, :], ALU.mult, ALU.add, [zk, 'c0', k], [k])
            yield
        ACT(tw, ub[:, 12, :], AF.Tanh, [('ub', 12)], ['tw'])
        if final:
            ACT(sgd, ub[:, 14, :], AF.Sigmoid, [('ub', 14)], [('sgd', pb)])
        sg, av, kx, t1, t2, kkv, kd, ka = (T_[n] for n in ('sg', 'av', 'kx', 't1', 't2', 'kkv', 'kd', 'ka'))
        cs, cs2, e0, e1, e2, e3, kd0 = (T_[n] for n in ('cs', 'cs2', 'e0', 'e1', 'e2', 'e3', 'kd0'))
        lo64 = 64 * di
        NC64 = GW // 64
        for c in range(4):
            ku = ub[:, 4 + c, :]
            kkey = ('ub', 4 + c)
            cc = slice(c * 128, (c + 1) * 128)
            ACT(kx, ku, AF.Copy, [kkey, 'rwp'], ['kx'], scale=k_k[:, c:c + 1])
            ACT(t1, kx, AF.Square, ['kx'], ['t1'])
            b = pbank()
            MM(ps[b][:, 0:GW], blockones, t1, True, True, ['rwc', 't1'], ['ps%d' % b])
            ACT(t1, ps[b][:, 0:GW], AF.Sqrt, ['ps%d' % b], ['t1'])
            yield
            TS('dve', t1, t1, 1e-12, None, ALU.max, None, ['t1'], ['t1'])
            RECIP(t1, t1, ['t1'], ['t1'])
            TTo('dve', kkv, kx, t1, ALU.mult, ['kx', 't1'], ['kkv'])
            b = pbank(lo64)
            MM(ps[b][:, 0:GW], w2sb[lo64:lo64 + 64, cc], tw[lo64:lo64 + 64, :], True, True, ['lora', 'tw'],
               ['ps%d' % b])
            ACT(sg, ps[b][:, 0:GW], AF.Sigmoid, ['ps%d' % b, 'rwp'], ['sg'], bias=w0T(di, c))
            yield
            b = pbank(lo64)
            MM(ps[b][:, 0:GW], a2sb[lo64:lo64 + 64, cc], ub[lo64:lo64 + 64, 13, :], True, True,
               ['lora', ('ub', 13)], ['ps%d' % b])
            ACT(av, ps[b][:, 0:GW], AF.Sigmoid, ['ps%d' % b, 'rwp'], ['av'], bias=a0T(di, c))
            ACT(t2, av, AF.Identity, ['av', 'rwp', 'omka'], ['t2'], bias=omka[:, c:c + 1], scale=k_a[:, c:c + 1])
            TTo('dve', kd, ku, t2, ALU.mult, [kkey, 't2'], ['kd'])
            TTo('pool', ka, kkv, av, ALU.mult, ['kkv', 'av'], ['ka'])
            yield
            if final:
                od = 1 - di
                b = pbank(64 * od)
                MM(ps[b][:, 0:GW], a2sb[64 * od:64 * od + 64, cc], ub[64 * od:64 * od + 64, 13, :], True, True,
                   ['lora', ('ub', 13)], ['ps%d' % b])
                ACT(kd0, ps[b][:, 0:GW], AF.Sigmoid, ['ps%d' % b, 'rwp'], ['kd0'], bias=a0T(od, c))
                TS('dve', kd0, kd0, k_a[:, c:c + 1], omka[:, c:c + 1], ALU.mult, ALU.add,
                   ['kd0', 'rwp', 'omka'], ['kd0'])
                TTo('dve', kd0, kd0, ku, ALU.mult, ['kd0', kkey], ['kd0'])
                yield
                TTo('dve', kd0, kd0, kd, ALU.add, ['kd0', 'kd'], ['kd0'])
                TTo('dve', kd0, kd0, ub[:, c, :], ALU.mult, ['kd0', ('ub', c)], ['kd0'])
                TS('dve', bprod[:, c, :], kd0, rk2[:, c:c + 1], None, ALU.mult, None, ['kd0', 'rk2'],
                   [('bprod', c, pb)])
                CP('pool', uvf[:, c, :], ub[:, 8 + c, :], [('ub', 8 + c)], [('uvf', c, pb)])
                yield
            P.add('dve', lambda e, cs=cs, sg=sg: e.tensor_tensor_scan(out=cs, data0=rmask[:, 0:GW], data1=sg,
                                                                      initial=0.0, op0=ALU.mult, op1=ALU.add),
                  r=['rwc', 'sg'], w=['cs'])
            CP('dve', tot_s, cs[:, 63::64], ['cs'], ['tot'])
            totb = tot_s.unsqueeze(2).broadcast_to([128, NC64, 64])
            v3 = lambda ap: ap.rearrange("p (a b) -> p a b", a=NC64)
            if di == 0:
                csx, cskey = cs, 'cs'
            else:
                TTo('dve', t2, sg, cs, ALU.subtract, ['sg', 'cs'], ['t2'])
                TTo('dve', v3(cs2), v3(t2), totb, ALU.add, ['t2', 'tot'], ['cs2'])
                csx, cskey = cs2, 'cs2'
            yield
            TTo('dve', e0, csx, sg, ALU.subtract, [cskey, 'sg'], ['e0'])
            TTo('dve', v3(e3), totb, v3(csx), ALU.subtract, ['tot', cskey], ['e3'])
            ACT(e1, csx, AF.Exp, [cskey], ['e1'], scale=-CDEC)
            ACT(e2, csx, AF.Exp, [cskey], ['e2'], scale=CDEC)
            yield
            ACT(e0, e0, AF.Exp, ['e0'], ['e0'], scale=-CDEC)
            ACT(e3, e3, AF.Exp, ['e3'], ['e3'], scale=-CDEC)
            ACT(gc[:, c, :], tot_s, AF.Exp, ['tot'], [('gc', c, pb)], scale=-CDEC)
            yield
            if own:
                TTo('pool', Rb[:, c, :], ub[:, c, :], e1, ALU.mult, [('ub', c), 'e1'], [('Rb', c, pb)])
            TTo('pool', Kb[:, c, :], kkv, e0, ALU.mult, ['kkv', 'e0'], [('Kb', c, pb)])
            TTo('pool', Kd[:, c, :], kd, e2, ALU.mult, ['kd', 'e2'], [('Kd', c, pb)])
            yield
            TTo('pool', Ad[:, c, :], ka, e2, ALU.mult, ['ka', 'e2'], [('Ad', c, pb)])
            TTo('pool', Kh[:, c, :], kd, e3, ALU.mult, ['kd', 'e3'], [('Kh', c, pb)])
            TS('pool', ka, ka, -1.0, 0.0, ALU.mult, ALU.add, ['ka'], ['ka'])
            TTo('pool', Ahn[:, c, :], ka, e3, ALU.mult, ['ka', 'e3'], [('Ahn', c, pb)])
            CP('pool', vb[:, c, :], ub[:, 8 + c, :], [('ub', 8 + c)], [('vb', c, pb)])
            yield

    M = lambda di, k: rwm[:, di, k, :].unsqueeze(1).broadcast_to([128, 4, 128])

    def tile_proc(di, g, tl, outp, final, cur, pb, pump):
        cols = slice(tl * 128, (tl + 1) * 128)
        gt = NTG * g + tl
        O_ = OB[pb]
        Rb, Kb, Kd, Ad, Kh, Ahn, vb = (O_[n] for n in ('Rb', 'Kb', 'Kd', 'Ad', 'Kh', 'Ahn', 'vb'))
        gc, bprod, sgd, uvf = O_['gc'], O_['bprod'], O_['sgd'], O_['uvf']
        allk = lambda nm: [(nm, c, pb) for c in range(4)]
        slot_of = lambda hh: 4 * (hh % 2) + hh // 2
        for (src, dstT, nm, eng) in ((Kb, KbT, 'Kb', 'act'), (Kh, KhT, 'Kh', 'dve'), (Ahn, AhT, 'Ahn', 'act'),
                                      (vb, VT, 'vb', 'dve')):
            for c in range(4):
                TR(psb2[:, c * 128:(c + 1) * 128], src[:, c, cols], identb, [(nm, c, pb), 'identb'], ['ps2'])
            CP(eng, dstT, psb2[:, 0:512], ['ps2'], [nm + 'T'])
        if final:
            for c in range(4):
                TR(ps[2][:, c * 128:(c + 1) * 128], uvf[:, c, cols], ident, [('uvf', c, pb), 'ident'], ['ps2'])
            CP('act', VT32, ps[2][:], ['ps2'], ['VT32'])
        def hg_stages(hg):
            BA, BB, BC = ((4, 5, 3), (1, 0, 2))[hg]
            X, Y, Rm = Xs[hg], Ys[hg], Rms[hg]
            pkk, qrk, qra, pv = pkks[hg], qrks[hg], qras[hg], pvs[hg]
            sx = 'g%d' % hg
            base = 64 * hg
            hl = [(2 * i + hg, i) for i in range(4)]

            def blk(bank, i):
                return ps[bank][:, i * 128:(i + 1) * 128]

            def b3(bank):
                return ps[bank][:].rearrange("p (a b) -> p a b", a=4)
            for i, (hh, c) in enumerate(hl):
                MM(blk(BA, i), Kb[base:base + 64, c, cols], Ad[base:base + 64, c, cols], True, True,
                   [('Kb', c, pb), ('Ad', c, pb)], ['ps%d' % BA])
            TTo('dve', X[0], b3(BA), M(di, 0), ALU.mult, ['ps%d' % BA, 'rwm'], ['X0' + sx])
            yield
            for i, (hh, c) in enumerate(hl):
                MM(blk(BB, i), Ad[base:base + 64, c, cols], Kb[base:base + 64, c, cols], True, True,
                   [('Kb', c, pb), ('Ad', c, pb)], ['ps%d' % BB])
            TTo('dve', Y[0], b3(BB), M(di, 1), ALU.mult, ['ps%d' % BB, 'rwm'], ['Y0' + sx])
            TTo('pool', Rm[0], Y[0], identb.unsqueeze(1).broadcast_to([128, 4, 128]), ALU.add,
                ['Y0' + sx, 'identb'], ['R0' + sx])
            yield
            for j in range(1, 6):
                a, b = (j - 1) % 2, j % 2
                for i in range(4):
                    MM(blk(BA, i), Y[a][:, i, :], X[a][:, i, :], True, True, ['X%d' % a + sx, 'Y%d' % a + sx], ['ps%d' % BA])
                CP('act', X[b], b3(BA), ['ps%d' % BA], ['X%d' % b + sx])
                yield
                if j < 5:
                    for i in range(4):
                        MM(blk(BB, i), X[a][:, i, :], Y[a][:, i, :], True, True, ['X%d' % a + sx, 'Y%d' % a + sx], ['ps%d' % BB])
                    CP('act', Y[b], b3(BB), ['ps%d' % BB], ['Y%d' % b + sx])
                    yield
                for i in range(4):
                    MM(blk(BC, i), identb, Rm[a][:, i, :], True, False, ['identb', 'R%d' % a + sx], ['ps%d' % BC])
                    MM(blk(BC, i), X[b][:, i, :], Rm[a][:, i, :], False, True, ['X%d' % b + sx, 'R%d' % a + sx], ['ps%d' % BC])
                CP('act', Rm[b], b3(BC), ['ps%d' % BC], ['R%d' % b + sx])
                yield
            MinvT, mkey = Rm[1], 'R1' + sx
            for i, (hh, c) in enumerate(hl):
                MM(blk(BA, i), Kd[base:base + 64, c, cols], Kb[base:base + 64, c, cols], True, True,
                   [('Kd', c, pb), ('Kb', c, pb)], ['ps%d' % BA])
            TTo('dve', pkk, b3(BA), M(di, 2), ALU.mult, ['ps%d' % BA, 'rwm'], ['pkk' + sx])
            yield
            if outp:
                for i, (hh, c) in enumerate(hl):
                    MM(blk(BB, i), Kd[base:base + 64, c, cols], Rb[base:base + 64, c, cols], True, True,
                       [('Kd', c, pb), ('Rb', c, pb)], ['ps%d' % BB])
                TTo('dve', qrk, b3(BB), M(di, 3), ALU.mult, ['ps%d' % BB, 'rwm'], ['qrk' + sx])
                yield
                for i, (hh, c) in enumerate(hl):
                    MM(blk(BC, i), Ad[base:base + 64, c, cols], Rb[base:base + 64, c, cols], True, True,
                       [('Ad', c, pb), ('Rb', c, pb)], ['ps%d' % BC])
                TTo('dve', qra, b3(BC), M(di, 4), ALU.mult, ['ps%d' % BC, 'rwm'], ['qra' + sx])
                yield
            for i, (hh, c) in enumerate(hl):
                MM(ps[BA][:, i * 64:(i + 1) * 64], pkk[:, i, :], VT[:, hh * 64:(hh + 1) * 64], True, True,
                   ['pkk' + sx, 'vbT'], ['ps%d' % BA])
            CP('act', pv, ps[BA][:, 0:256].rearrange("p (a b) -> p a b", a=4), ['ps%d' % BA], ['pv' + sx])
            yield
            for i, (hh, c) in enumerate(hl):
                MM(ps[BB][:, i * 64:(i + 1) * 64], MinvT[:, i, :], pv[:, i, :], True, True, [mkey, 'pv' + sx], ['ps%d' % BB])
            for i, (hh, c) in enumerate(hl):
                MM(ps[BC][:, i * 64:(i + 1) * 64], MinvT[:, i, :], KbT[:, hh * 64:(hh + 1) * 64],
                   True, True, [mkey, 'KbT'], ['ps%d' % BC])
            CP('act', u0[:, 4 * hg:4 * hg + 4, :], ps[BB][:, 0:256].rearrange("p (a b) -> p a b", a=4),
               ['ps%d' % BB], [('u0', hg)])
            CP('dve', wt[:, 4 * hg:4 * hg + 4, :], ps[BC][:, 0:256].rearrange("p (a b) -> p a b", a=4),
               ['ps%d' % BC], [('wt', hg)])
            yield
            if outp:
                for i, (hh, c) in enumerate(hl):
                    MM(ps[BB][:, i * 64:(i + 1) * 64], qrk[:, i, :], VT[:, hh * 64:(hh + 1) * 64], True, False,
                       ['qrk' + sx, 'vbT'], ['ps%d' % BB])
                    MM(ps[BB][:, i * 64:(i + 1) * 64], qra[:, i, :], u0[:, 4 * hg + i, :], False, True,
                       ['qra' + sx, ('u0', hg)], ['ps%d' % BB])
                CP('act', y0.rearrange("p (i q d) -> p i q d", i=4, q=2)[:, :, hg, :],
                   ps[BB][:, 0:256].rearrange("p (a b) -> p a b", a=4), ['ps%d' % BB], [('y0', hg)])
                yield
                for i, (hh, c) in enumerate(hl):
                    MM(ps[BA][base:base + 64, i * 128:(i + 1) * 128], wt[:, 4 * hg + i, :], qra[:, i, :],
                       True, True, [('wt', hg), 'qra' + sx], ['ps%d' % BA])
                TTo('dve', rpT[base:base + 64, :, :],
                    ps[BA][base:base + 64, :].rearrange("p (a b) -> p a b", a=4),
                    Rb[base:base + 64, :, cols], ALU.add, ['ps%d' % BA] + allk('Rb'), [('rpT', hg)])
            yield
        gens = [hg_stages(0), hg_stages(1)]
        alive = [True, True]
        while any(alive):
            pump(1)
            for k_ in (0, 1):
                if alive[k_]:
                    try:
                        next(gens[k_])
                    except StopIteration:
                        alive[k_] = False
        gbank = {0: 5, 1: 3}
        hbank = {0: 4, 1: 2}
        for ch2 in range(2):
            rows = slice(ch2 * 64, ch2 * 64 + 64)
            gb, hb = gbank[ch2], hbank[ch2]
            for hh in range(8):
                c, base = hh // 2, 64 * (hh % 2)
                sl = slot_of(hh)
                o5 = ps[gb][base:base + 64, c * 64:(c + 1) * 64]
                MM(o5, wt[rows, sl, :], AhT[rows, hh * 64:(hh + 1) * 64], True, True,
                   [('wt', hh % 2), 'AhnT'], ['ps%d' % gb])
                o4 = ps[hb][base:base + 64, c * 64:(c + 1) * 64]
                MM(o4, KhT[rows, hh * 64:(hh + 1) * 64], VT[rows, hh * 64:(hh + 1) * 64], True, False,
                   ['KhT', 'vbT'], ['ps%d' % hb])
                MM(o4, AhT[rows, hh * 64:(hh + 1) * 64], u0[rows, sl, :], False, True,
                   ['AhnT', ('u0', hh % 2)], ['ps%d' % hb])
        for ch2 in range(2):
            gb, hb = gbank[ch2], hbank[ch2]
            for c in range(4):
                slot = 2 * tl + ch2
                STT(GT[:, c, ch2, :], identblk, gc[:, c, slot:slot + 1], ps[gb][:, c * 64:(c + 1) * 64],
                    ALU.mult, ALU.add, ['rwc', ('gc', c, pb), 'ps%d' % gb], ['GT'])
            CP('act', Hs[:, :, ch2, :], ps[hb][:, 0:256].rearrange("p (a b) -> p a b", a=4),
               ['ps%d' % hb], ['Hs'])
        pump(2)
        order = [0, 1] if di == 0 else [1, 0]
        tb = {0: 6, 1: 0}
        yb_ = {0: 7, 1: 1}
        for ch2 in order:
            rows = slice(ch2 * 64, ch2 * 64 + 64)
            Tc, Tn = Tst[cur], Tst[1 - cur]
            kc, kn = 'T%d' % cur, 'T%d' % (1 - cur)
            for par in range(2):
                base = 64 * par
                for c in range(4):
                    hh = 2 * c + par
                    if outp:
                        MM(ps[yb_[par]][rows, hh * 64:(hh + 1) * 64],
                           rpT[base:base + 64, c, ch2 * 64:ch2 * 64 + 64],
                           Tc[base:base + 64, c, :], True, True, [('rpT', par), kc], ['ps%d' % yb_[par]])
                    MM(ps[tb[par]][base:base + 64, c * 64:(c + 1) * 64], GT[base:base + 64, c, ch2, :],
                       Tc[base:base + 64, c, :], True, True, ['GT', kc], ['ps%d' % tb[par]])
            for par in range(2):
                base = 64 * par
                if outp:
                    yv = ytile.rearrange("p (i q d) -> p i q d", i=4, q=2)
                    y0v = y0.rearrange("p (i q d) -> p i q d", i=4, q=2)
                    pyv = ps[yb_[par]][:].rearrange("p (i q d) -> p i q d", i=4, q=2)
                    TTo('dve', yv[rows, :, par, :], pyv[rows, :, par, :], y0v[rows, :, par, :], ALU.add,
                        ['ps%d' % yb_[par], ('y0', 0), ('y0', 1)], ['ytile'])
                TTo('dve', Tn[base:base + 64, :, :],
                    ps[tb[par]][base:base + 64, 0:256].rearrange("p (a b) -> p a b", a=4),
                    Hs[base:base + 64, :, ch2, :], ALU.add, ['ps%d' % tb[par], 'Hs'], [kn])
            cur = 1 - cur
            pump(1)
        if outp and not final:
            DMA('sp', yf_d[gt * 128:(gt + 1) * 128, :], ytile, ['ytile'], ['yf'])
        if outp and final:
            DMA('sp', yfb, yf_d[gt * 128:(gt + 1) * 128, :], ['yf'], ['yfb'])
            y = ytile
            y3 = y.rearrange("p (h d) -> p h d", h=8)
            TTo('dve', y, y, yfb, ALU.add, ['ytile', 'yfb'], ['ytile'])
            P.add('dve', lambda e: e.tensor_reduce(out=s1, in_=y3, axis=AX.X, op=ALU.add), r=['ytile'], w=['s1'])
            TS('dve', s1, s1, -1.0 / 64, None, ALU.mult, None, ['s1'], ['s1'])
            TTo('dve', y3, y3, s1.unsqueeze(2).broadcast_to([128, 8, 64]), ALU.add, ['ytile', 's1'], ['ytile'])
            ACT(sqb, y, AF.Square, ['ytile'], ['sqb'])
            P.add('dve', lambda e: e.tensor_reduce(out=s2, in_=sqb.rearrange("p (h d) -> p h d", h=8), axis=AX.X,
                                                   op=ALU.add), r=['sqb'], w=['s2'])
            ACT(s2, s2, AF.Sqrt, ['s2', 'eps'], ['s2'], bias=gneps_t, scale=1.0 / 64)
            RECIP(s2, s2, ['s2'], ['s2'])
            TTo('dve', y3, y3, s2.unsqueeze(2).broadcast_to([128, 8, 64]), ALU.mult, ['ytile', 's2'], ['ytile'])
            TTo('dve', y, y, rep[:, 1, :], ALU.mult, ['ytile', 'rep'], ['ytile'])
            TTo('dve', y, y, rep[:, 2, :], ALU.add, ['ytile', 'rep'], ['ytile'])
            for c in range(4):
                MM(ps[2][:, 2 * c:2 * c + 2], bprod[:, c, cols], hsel, True, True, [('bprod', c, pb), 'rwc'], ['ps2'])
            CP('dve', bon, ps[2][:, 0:8], ['ps2'], ['bon'])
            TTo('dve', tmpv.rearrange("p (h d) -> p h d", h=8), VT32.rearrange("p (h d) -> p h d", h=8),
                bon.unsqueeze(2).broadcast_to([128, 8, 64]), ALU.mult, ['VT32', 'bon'], ['tmpv'])
            TTo('dve', y, y, tmpv, ALU.add, ['ytile', 'tmpv'], ['ytile'])
            MM(ps[3][:], sgd[:, cols], g2sb, True, True, [('sgd', pb), 'lora'], ['ps3'])
            TTo('dve', y, y, ps[3][:], ALU.mult, ['ytile', 'ps3'], ['ytile'])
            DMA('sp', yc_d[gt * 128:(gt + 1) * 128, 512:1024], y, ['ytile'], ['yc'])
        return cur

    sched = [(g, 0, True, False) for g in range(NG_OWN)]
    if debug_stop != 'F':
        sched += [(g, 1, True, True) for g in range(NG_OWN - 1, -1, -1)]
    cur = 0
    MEMSET('pool', Tst[0], 0.0, ['T0'])
    for _ in prep_group(*sched[0], 0):
        pass
    for idx, (g, di, own, final) in enumerate(sched):
        pb = idx % 2
        pg = prep_group(*sched[idx + 1], (idx + 1) % 2) if idx + 1 < len(sched) else None
        st = {'alive': pg is not None}

        def pump(n, pg=pg, st=st):
            for _ in range(n):
                if st['alive']:
                    try:
                        next(pg)
                    except StopIteration:
                        st['alive'] = False
        if idx > 0 and sched[idx - 1][1] != di:
            kc_ = 'T%d' % cur
            tflat = Tst[cur].rearrange("p a b -> p (a b)")
            DMA('sp', E.tf_d, tflat, [kc_], ['tf_d'])
            E.CC(E.tf_d, E.tfg_d, ['tf_d'], ['tfg'])
            DMA('sp', tfr, E.tfg_d.rearrange("(r p) f -> p r f", r=2), ['tfg'], ['tfr'])
            TS('dve', tflat, tfr[:, 0, :], E.hsel[:, 0:1], None, ALU.mult, None, ['tfr', 'hsel'], [kc_])
            STT(tflat, tfr[:, 1, :], E.hsel[:, 1:2], tflat, ALU.mult, ALU.add, ['tfr', 'hsel', kc_], [kc_])
        tls = range(NTG) if di == 0 else range(NTG - 1, -1, -1)
        for tl in tls:
            cur = tile_proc(di, g, tl, own, final, cur, pb, pump)
        pump(10 ** 6)


def _f(a):
    return np.ascontiguousarray(a, dtype=np.float32)


def _lhs_chunks(W):
    n = W.shape[1] // 128
    return _f(W.reshape(8, 128, n, 128).transpose(2, 1, 0, 3))


def _attn_masks():
    p = np.arange(128)[:, None]
    c = np.arange(128)[None, :]
    cntm = {}
    for j in range(-8, 9):
        d = 128 * j + p - c
        m = (np.abs(d) <= 64).astype(np.float32)
        m += ((d % 4 == 0) & (np.abs(d) <= 256)).astype(np.float32)
        m += ((d % 16 == 0) & (np.abs(d) <= 1024)).astype(np.float32)
        cntm[j] = m
    wm = np.zeros((20, 128, 512), np.float32)
    for J in range(20):
        for i in range(4):
            j = J - 8 - i
            if abs(j) <= 8:
                wm[J, :, i * 128:(i + 1) * 128] = cntm[j]
    return wm.astype(ml_dtypes.bfloat16)


def _rope_tables(rev):
    pos = np.arange(S, dtype=np.float32)
    if rev:
        pos = pos[::-1].copy()
    inv_freq = (np.float32(10000.0) ** (-np.arange(0, 64, 2, dtype=np.float32) / np.float32(64))).astype(np.float32)
    ang = (pos[:, None] * inv_freq[None, :]).astype(np.float32)
    cos, sin = np.cos(ang).astype(np.float32), np.sin(ang).astype(np.float32)
    idx = np.arange(128) % 32
    sign = np.where((np.arange(128) % 64) < 32, -1.0, 1.0).astype(np.float32)
    cosT = cos[:, idx].T
    sinT = (sin[:, idx] * sign[None, :]).T
    return _f(cosT), _f(sinT)


def host_inputs(inputs):
    g = lambda k: np.asarray(inputs[k][0], np.float32)
    x = np.asarray(inputs["x"], dtype=np.float32)
    shared = {}
    ffn_names = {1: ("ffn1_w_gate", "ffn1_w_up", "ffn1_w_down"), 2: ("ffn2_w_gate", "ffn2_w_up", "ffn2_w_down")}
    for k in (1, 2):
        wg, wu, wd = (g(nm) for nm in ffn_names[k])
        gg = wg.reshape(8, 128, NF, 128).transpose(2, 1, 0, 3)
        uu = wu.reshape(8, 128, NF, 128).transpose(2, 1, 0, 3)
        shared["wgu%d" % k] = _f(np.stack([gg, uu], axis=2))
        shared["wdc%d" % k] = _f(wd.reshape(NF, 128, 8, 128).transpose(2, 1, 0, 3))
    gnames = ["ffn1_pre_g", "ffn1_post_g", "mix_pre_g", "mix_post_g", "ffn2_pre_g", "ffn2_post_g"]
    shared["gv"] = _f(np.stack([g(nm).reshape(8, 128).T for nm in gnames], axis=1))
    shared["ident"] = np.eye(128, dtype=np.float32)
    w_in = g("w_in")
    swap = np.concatenate([(np.arange(64) + 32) % 64 + 64 * h for h in range(8)])
    shared["wv"] = _f(w_in[:, 1024:1536].reshape(8, 128, 512).transpose(1, 0, 2))
    shared["wo"] = _lhs_chunks(g("w_out"))
    shared["wm"] = _attn_masks()
    shared["wmf"] = np.ascontiguousarray(shared["wm"][:, ::-1, :])
    shared["rep"] = _f(np.stack([np.broadcast_to(g(nm)[None, :], (128, 512))
                                 for nm in ("attn_out_g", "rwkv_lnx_w", "rwkv_lnx_b")], axis=1))
    maps = []
    for c in range(8):
        b, h = c // 2, c % 2
        m = dict(shared)
        m["x"] = _f(x[b] if h == 0 else x[b, ::-1])
        cosT, sinT = _rope_tables(h == 1)
        m["cosT"], m["sinT"] = cosT, sinT
        m["hsel"] = _f(np.tile(np.array([[float(h), float(1 - h)]], np.float32), (128, 1)))
        m.update(_rwkv_host(inputs, h, w_in, swap))
        maps.append(m)
    return maps


def _rwkv_host(inputs, h, w_in, swap):
    g = lambda k: np.asarray(inputs[k][0], np.float32)
    dirs = [0, 1] if h == 0 else [1, 0]
    wq, wk = w_in[:, 0:512], w_in[:, 512:1024]
    cols = []
    for gi in range(4):
        cols.append(wq[:, gi * 128:(gi + 1) * 128])
        cols.append(wq[:, swap][:, gi * 128:(gi + 1) * 128])
    for gi in range(4):
        cols.append(wk[:, gi * 128:(gi + 1) * 128])
        cols.append(wk[:, swap][:, gi * 128:(gi + 1) * 128])
    wr = w_in[:, 1536:]
    rw = [wr[:, 0:1536]]
    for off in (1536, 1664):
        blk = wr[:, off:off + 128]
        rw.append(np.concatenate([blk[:, 64 * dirs[0]:64 * dirs[0] + 64], blk[:, 64 * dirs[1]:64 * dirs[1] + 64]], axis=1))
    rw.append(wr[:, 1792:1920])
    Wall = np.concatenate(cols + rw, axis=1)
    out = {"win": _lhs_chunks(Wall)}
    mp, mn = g("rwkv_mu_prev"), g("rwkv_mu_next")
    if h == 1:
        mp, mn = mn, mp

    def fix(v):
        v = v.copy()
        for off in (1536, 1664):
            blk = v[off:off + 128].copy()
            v[off:off + 128] = np.concatenate([blk[64 * dirs[0]:64 * dirs[0] + 64], blk[64 * dirs[1]:64 * dirs[1] + 64]])
        return v
    mp, mn = fix(mp), fix(mn)
    rwp = np.zeros((128, 64), np.float32)
    rwp[:, 0:15] = mp.reshape(15, 128).T
    rwp[:, 15:30] = mn.reshape(15, 128).T
    w0, a0 = g("rwkv_w0"), g("rwkv_a0")
    for di, d in enumerate(dirs):
        rwp[:, 30 + 4 * di:34 + 4 * di] = w0[d].reshape(4, 128).T
        rwp[:, 38 + 4 * di:42 + 4 * di] = a0[d].reshape(4, 128).T
    rwp[:, 46:50] = g("rwkv_k_k").reshape(4, 128).T
    rwp[:, 50:54] = g("rwkv_k_a").reshape(4, 128).T
    rwp[:, 54:58] = g("rwkv_r_k").reshape(4, 128).T
    out["rwpar"] = rwp
    w2, a2 = g("rwkv_w2"), g("rwkv_a2")
    lora = np.zeros((128, 3, 512), np.float32)
    lora[:, 0, :] = np.concatenate([w2[dirs[0]], w2[dirs[1]]], axis=0)
    lora[:, 1, :] = np.concatenate([a2[dirs[0]], a2[dirs[1]]], axis=0)
    lora[:, 2, :] = g("rwkv_g2")
    out["lora"] = lora
    idx = np.arange(128)
    same = (idx[:, None] // 64) == (idx[None, :] // 64)
    B0 = ((idx[None, :] < idx[:, None]) & same).astype(np.float32)
    I = np.eye(128, dtype=np.float32)
    rwm = np.zeros((128, 2, 5, 128), np.float32)
    for d, Bd in enumerate((B0, B0.T)):
        rwm[:, d, 0] = -Bd
        rwm[:, d, 1] = -Bd.T
        rwm[:, d, 2] = Bd.T
        rwm[:, d, 3] = Bd.T + I
        rwm[:, d, 4] = -(Bd.T + I)
    out["rwmask"] = rwm
    rwc = np.zeros((128, 1218), np.float32)
    rwc[:, 0:512] = (np.arange(512) % 64 != 0).astype(np.float32)[None, :]
    rwc[:, 1024:1152] = same.astype(np.float32)
    rwc[:, 1152:1216] = (idx[:, None] % 64 == np.arange(64)[None, :]).astype(np.float32)
    rwc[:, 1216] = (idx < 64)
    rwc[:, 1217] = (idx >= 64)
    out["rwconst"] = rwc
    return out


_CACHE = {}


def kernel(**inputs):
    if 'nc' not in _CACHE:
        _CACHE['nc'] = build()[0]
    nc = _CACHE['nc']
    maps = host_inputs(inputs)
    res = run_bass_kernel_spmd(nc, maps, core_ids=list(range(8)))
    out = np.zeros((4, S, D), np.float32)
    for c in range(8):
        b, h = c // 2, c % 2
        o = np.asarray(res.results[c]["out"])
        if h == 0:
            out[b, :OWN] = o
        else:
            out[b, OWN:] = o[::-1]
    return out
```

```python
import contextlib
import numpy as np
import ml_dtypes
import concourse.bass as bass
import concourse.mybir as mybir
from concourse.bass_utils import run_bass_kernel_spmd

F32 = mybir.dt.float32
BF16 = mybir.dt.bfloat16
AF = mybir.ActivationFunctionType
ALU = mybir.AluOpType
AX = mybir.AxisListType

D = 1024
DFF = 2816
NF = DFF // 128
S = 4096
OWN = 2048
TT = 512
EPS = 1e-6
GN_EPS = 64e-5
CDEC = float(np.exp(-0.5))
NKT = 16


class Prog:
    NDMA = 16

    def __init__(self):
        self.ops = []
        self.last_w = {}
        self.readers = {}
        self.last_barrier = 0
        self.warn = []
        self.pe_hist = []

    def add(self, eng, fn, r=(), w=(), dma=False, cc=False):
        i = len(self.ops)
        deps = set()
        for k in r:
            j = self.last_w.get(k)
            if j is not None:
                deps.add((j, 'raw'))
        for k in w:
            j = self.last_w.get(k)
            if j is not None:
                deps.add((j, 'waw'))
            for j in self.readers.get(k, ()):
                deps.add((j, 'war'))
        self.ops.append(dict(eng=eng, fn=fn, deps=deps, dma=dma, cc=cc))
        for k in w:
            if isinstance(k, str) and k.startswith('ps'):
                engs = set(self.ops[j]['eng'] for j in self.readers.get(k, ()))
                if len(engs) > 1:
                    self.warn.append(('multi-engine psum readers', k, sorted(engs), i))
        for k in r:
            self.readers.setdefault(k, []).append(i)
        for k in w:
            self.last_w[k] = i
            self.readers[k] = []
        return i

    def barrier(self):
        n = len(self.ops)
        deps = set()
        last = {}
        for i in range(n):
            op = self.ops[i]
            if op['dma']:
                if i >= self.last_barrier:
                    deps.add((i, 'raw'))
            elif op['fn'] is not None:
                last[op['eng']] = i
        for e, i in last.items():
            deps.add((i, 'raw'))
        for e in ['pe', 'act', 'dve', 'pool', 'sp']:
            self.ops.append(dict(eng=e, fn=None, deps=set(deps), dma=False, cc=False))
        self.last_barrier = n

    def emit(self, nc, stack):
        ops = self.ops
        n = len(ops)
        need = [set() for _ in range(n)]
        signaled = [False] * n
        for i, op in enumerate(ops):
            for (j, kind) in op['deps']:
                if j == i:
                    continue
                pj = ops[j]
                same = (pj['eng'] == op['eng'])
                if same and op['eng'] == 'pe' and not pj['dma'] and not op['dma'] and op['fn'] is not None:
                    if kind != 'raw':
                        continue
                need[i].add(j)
            latest = {}
            for j in need[i]:
                pj = ops[j]
                if pj['dma']:
                    continue
                e2 = pj['eng']
                if e2 not in latest or j > latest[e2]:
                    latest[e2] = j
            need[i] = set(j for j in need[i] if ops[j]['dma'] or latest[ops[j]['eng']] == j)
            for j in need[i]:
                signaled[j] = True
        engs = ['pe', 'act', 'dve', 'pool', 'sp']
        csem = {e: stack.enter_context(nc.semaphore("s_" + e)) for e in engs[:4]}
        dsem = {e: [stack.enter_context(nc.semaphore("d_%s%d" % (e, k))) for k in range(self.NDMA)]
                for e in ['sp', 'pool']}
        sig = [None] * n
        ccount = {e: 0 for e in engs}
        dcount = {e: 0 for e in engs}
        prevuse = [None] * n
        for i, op in enumerate(ops):
            e = op['eng']
            if op.get('cc'):
                sig[i] = (stack.enter_context(nc.semaphore("cc_%d" % i)), 1)
            elif op['dma']:
                k = dcount[e]
                dcount[e] += 1
                s = dsem[e][k % self.NDMA]
                sig[i] = (s, 16 * (k // self.NDMA + 1))
                if k >= self.NDMA:
                    prevuse[i] = (s, 16 * (k // self.NDMA))
            elif signaled[i]:
                ccount[e] += 1
                sig[i] = (csem[e], ccount[e])
        per = {e: [i for i in range(n) if ops[i]['eng'] == e] for e in engs}
        self.stats = {e: len(per[e]) for e in engs}
        self.stats['sem'] = dict(ccount)

        def run(e, eng):
            waited = {}
            for i in per[e]:
                op = ops[i]
                ws = []
                if prevuse[i] is not None:
                    ws.append(prevuse[i])
                for j in need[i]:
                    ws.append(sig[j])
                best = {}
                for (s, v) in ws:
                    key = id(s)
                    if v > best.get(key, (None, -1))[1]:
                        best[key] = (s, v)
                for key, (s, v) in best.items():
                    if waited.get(key, -1) >= v:
                        continue
                    eng.wait_ge(s, v)
                    waited[key] = v
                if op['fn'] is None:
                    continue
                ins = op['fn'](eng)
                if sig[i] is not None:
                    if op.get('cc'):
                        ins.then_inc(sig[i][0])
                    else:
                        ins.then_inc(sig[i][0], 16 if op['dma'] else 1)

        with nc.Block() as block:
            @block.tensor
            def _(eng):
                run('pe', eng)

            @block.scalar
            def _(eng):
                run('act', eng)

            @block.vector
            def _(eng):
                run('dve', eng)

            @block.gpsimd
            def _(eng):
                run('pool', eng)

            @block.sync
            def _(eng):
                run('sp', eng)


class Arena:
    def __init__(self, tensor, nbytes):
        self.t = tensor
        self.n = nbytes
        self.off = 0

    def alloc(self, shape, dt=F32):
        esz = mybir.dt.size(dt)
        nel = int(np.prod(shape[1:]))
        nb = (nel * esz + 31) // 32 * 32
        assert self.off + nb <= self.n, ("arena overflow", self.off, nb, self.n)
        ap = self.t[:, self.off // 4:(self.off + nb) // 4]
        self.off += nb
        if dt != F32:
            ap = ap.bitcast(dt)
        ap = ap[:, 0:nel]
        if len(shape) == 3:
            ap = ap.rearrange("p (a b) -> p a b", a=shape[1])
        elif len(shape) == 4:
            ap = ap.rearrange("p (a b c) -> p a b c", a=shape[1], b=shape[2])
        return ap


class Env:
    pass


def build(debug=False, phases=('p1', 'att', 'rwkv', 'p4'), ntiles1=8, rwkv_stop=None):
    nc = bass.Bass("TRN2", target_bir_lowering=False)
    P = Prog()
    stack = contextlib.ExitStack()
    E = Env()

    def din(name, shape, dt=F32):
        return nc.dram_tensor(name, list(shape), dt, kind="ExternalInput").ap()

    def dscr(name, shape, dt=F32, out=False):
        if out:
            return nc.dram_tensor(name, list(shape), dt, kind="ExternalOutput").ap()
        return nc.dram_tensor(name, list(shape), dt).ap()

    x_d = din("x", [S, D])
    wgu_d = [din("wgu%d" % k, [NF, 128, 2, 8, 128]) for k in (1, 2)]
    wdc_d = [din("wdc%d" % k, [8, 128, NF, 128]) for k in (1, 2)]
    win_d = din("win", [31, 128, 8, 128])
    wv_d = din("wv", [128, 8, 512])
    wo_d = din("wo", [8, 128, 8, 128])
    gv_d = din("gv", [128, 6, 8])
    ident_d = din("ident", [128, 128])
    cos_d = din("cosT", [128, S])
    sin_d = din("sinT", [128, S])
    wm_d = din("wm", [20, 128, 512], BF16)
    rep_d = din("rep", [128, 3, 512])
    rwm_d = din("rwmask", [128, 2, 5, 128])
    rwc_d = din("rwconst", [128, 1218])
    rwp_d = din("rwpar", [128, 64])
    lora_d = din("lora", [128, 3, 512])
    out_d = nc.dram_tensor("out", [OWN, D], F32, kind="ExternalOutput").ap()
    h1s_d = dscr("h1s", [8, 128, OWN], out=debug)
    zr_d = dscr("zr", [15, 128, S], out=debug)
    qT_d = dscr("qTs", [4, 128, OWN], BF16, out=debug)
    kT_d = dscr("kTs", [4, 128, NKT * 128], BF16, out=debug)
    v_d = dscr("vs", [NKT * 128, 520], BF16, out=debug)
    yc_d = dscr("yc", [OWN, D], out=debug)
    yf_d = dscr("yf", [OWN, 512], out=debug)
    hsel_d = din("hsel", [128, 2])
    wmf_d = din("wmf", [20, 128, 512], BF16)
    kh_d = dscr("kh", [512, 1024], BF16)
    vh_d = dscr("vh", [1024, 520], BF16)
    zb_d = dscr("zb", [128, 15])
    khg_d = dscr("khg", [1024, 1024], BF16)
    vhg_d = dscr("vhg", [2048, 520], BF16)
    zbg_d = dscr("zbg", [256, 15])
    tf_d = dscr("tf", [128, 256])
    tfg_d = dscr("tfg", [256, 256])
    wgu16_d = [dscr("wgu16_%d" % k, [NF, 128, 2 * 8 * 128], BF16) for k in (1, 2)]
    wdc16_d = [dscr("wdc16_%d" % k, [8, 128, NF * 128], BF16) for k in (1, 2)]
    win16_d = dscr("win16", [31, 128, 8 * 128], BF16)
    wo16_d = dscr("wo16", [8, 128, 8 * 128], BF16)
    PAIRS = [[0, 1], [2, 3], [4, 5], [6, 7]]

    def CC(in_ap, out_ap, r, w):
        return P.add('pool', lambda e: e.collective_compute("AllGather", ALU.bypass, replica_groups=PAIRS,
                                                            ins=[in_ap.opt()], outs=[out_ap.opt()]),
                     r=r, w=w, dma=True, cc=True)

    ARENA_BYTES = 204 * 1024
    arena_t = stack.enter_context(nc.sbuf_tensor("arena", [128, ARENA_BYTES // 4], F32))
    A = Arena(arena_t, ARENA_BYTES)
    ps = [stack.enter_context(nc.psum_tensor("ps%d" % i, [128, 512], F32)) for i in range(8)]

    def MM(out, lhsT, rhs, start, stop, r, w):
        rb0, kk_ = lhsT.base_partition(), lhsT.shape[0]
        for (pb0, pk, pw) in P.pe_hist[-1:]:
            if (rb0 + kk_ <= pb0 or pb0 + pk <= rb0) and pw == w[0]:
                P.warn.append(('row-group conflict', w[0], (pb0, pk), (rb0, kk_), len(P.ops)))
        P.pe_hist.append((rb0, kk_, w[0]))
        P.add('pe', lambda e: e.matmul(out, lhsT, rhs, start=start, stop=stop), r=r, w=w)

    def TR(out, in_, idt, r, w):
        P.pe_hist.append((in_.base_partition(), in_.shape[0], w[0]))
        P.add('pe', lambda e: e.transpose(out, in_, idt), r=r, w=w)

    def ACT(out, in_, func, r, w, bias=None, scale=None):
        kw = {}
        if bias is not None:
            kw['bias'] = bias
        if scale is not None:
            kw['scale'] = scale
        P.add('act', lambda e: e.activation(out, in_, func, **kw), r=r, w=w)

    def TTo(eng, out, in0, in1, op, r, w):
        P.add(eng, lambda e: e.tensor_tensor(out=out, in0=in0, in1=in1, op=op), r=r, w=w)

    def TS(eng, out, in0, s1, s2, op0, op1, r, w):
        if op1 is None:
            P.add(eng, lambda e: e.tensor_scalar(out=out, in0=in0, scalar1=s1, scalar2=None, op0=op0), r=r, w=w)
        else:
            P.add(eng, lambda e: e.tensor_scalar(out=out, in0=in0, scalar1=s1, scalar2=s2, op0=op0, op1=op1),
                  r=r, w=w)

    def STT(out, in0, scalar, in1, op0, op1, r, w):
        P.add('dve', lambda e: e.scalar_tensor_tensor(out=out, in0=in0, scalar=scalar, in1=in1, op0=op0, op1=op1),
              r=r, w=w)

    def CP(eng, out, in_, r, w):
        if eng == 'act':
            P.add('act', lambda e: e.copy(out, in_), r=r, w=w)
        else:
            P.add(eng, lambda e: e.tensor_copy(out, in_), r=r, w=w)

    def RECIP(out, in_, r, w):
        P.add('dve', lambda e: e.reciprocal(out, in_), r=r, w=w)

    def DMA(q, out, in_, r, w):
        return P.add(q, lambda e: e.dma_start(out=out, in_=in_), r=r, w=w, dma=True)

    def MEMSET(eng, ap, val, w):
        P.add(eng, lambda e: e.memset(ap, val), w=w)

    ident = A.alloc([128, 128])
    identb = A.alloc([128, 128], BF16)
    ones = A.alloc([128, 128], BF16)
    gv = A.alloc([128, 6, 8])
    gvh = A.alloc([128, 6, 8])
    eps_t = A.alloc([128, 1])
    gneps_t = A.alloc([128, 1])
    rep = A.alloc([128, 3, 512])
    rwm = A.alloc([128, 2, 5, 128])
    rwc = A.alloc([128, 1218])
    rwp = A.alloc([128, 64])
    lora = A.alloc([128, 3, 512])
    hsel = A.alloc([128, 2])
    DMA('sp', ident, ident_d, [], ['ident'])
    DMA('sp', gv, gv_d, [], ['gv'])
    DMA('sp', rep, rep_d, [], ['rep'])
    DMA('sp', rwm, rwm_d, [], ['rwm'])
    DMA('sp', rwc, rwc_d, [], ['rwc'])
    DMA('sp', rwp, rwp_d, [], ['rwp'])
    DMA('sp', lora, lora_d, [], ['lora'])
    DMA('sp', hsel, hsel_d, [], ['hsel'])
    MEMSET('pool', ones, 1.0, ['ones'])
    MEMSET('pool', eps_t, EPS, ['eps'])
    MEMSET('pool', gneps_t, GN_EPS, ['eps'])
    CP('dve', identb, ident, ['ident'], ['identb'])
    TS('dve', gvh, gv, 0.5, None, ALU.mult, None, ['gv'], ['gvh'])
    A_MARK = A.off

    cnt = {'ps01': 0, 'x': 0, 'z': 0, 'st': 0}
    for k_, v_ in list(locals().items()):
        setattr(E, k_, v_)

    def alloc_ffn_bufs():
        B = {}
        B['xin'] = [A.alloc([128, D]) for _ in range(4)]
        B['xT'] = A.alloc([128, 8, TT])
        B['hn'] = A.alloc([128, 8, TT], BF16)
        B['sq'] = A.alloc([128, 8, TT], BF16)
        B['fb'] = A.alloc([128, 8, TT])
        B['aT'] = A.alloc([128, NF, TT], BF16)
        B['rstd'] = A.alloc([128, TT])
        B['tmp'] = A.alloc([128, TT])
        B['wgu'] = [A.alloc([128, 2, 8, 128], BF16) for _ in range(3)]
        B['wdc'] = [A.alloc([128, NF, 128], BF16) for _ in range(3)]
        B['sgl'] = [A.alloc([128, TT]) for _ in range(2)]
        return B

    def issue_x_loads(B, src_d, tt):
        xin = B['xin']
        for sub in range(4):
            t0 = tt * TT + sub * 128
            DMA('sp', xin[sub], src_d[t0:t0 + 128, :], ['yc'] if src_d is yc_d else [], ['xin%d' % sub])

    def load_T(B, src_d, tt, dst, dstkey, loads_issued=False):
        xin = B['xin']
        if not loads_issued:
            issue_x_loads(B, src_d, tt)
        for sub in range(4):
            b = sub
            for half in range(2):
                pb = cnt['ps01'] % 2
                cnt['ps01'] += 1
                for q in range(4):
                    c = half * 4 + q
                    TR(ps[pb][:, q * 128:(q + 1) * 128], xin[b][:, c * 128:(c + 1) * 128], ident,
                       ['xin%d' % b, 'ident'], ['ps%d' % pb])
                src = ps[pb][:].rearrange("p (q t) -> p q t", q=4)
                d_ = dst[:, half * 4:(half + 1) * 4, sub * 128:(sub + 1) * 128]
                CP('act' if half == 0 else 'dve', d_, src, ['ps%d' % pb],
                   [(dstkey, half * 4 + q) for q in range(4)])

    def rmsnorm_stats(B, src, srckey):
        sq, tmp, rstd = B['sq'], B['tmp'], B['rstd']
        pb = cnt['ps01'] % 2
        cnt['ps01'] += 1
        for c in range(8):
            ACT(sq[:, c, :], src[:, c, :], AF.Square, [(srckey, c)], [('sq', c)])
            MM(ps[pb][:], ones, sq[:, c, :], c == 0, c == 7, [('sq', c), 'ones'], ['ps%d' % pb])
        ACT(tmp, ps[pb][:], AF.Sqrt, ['ps%d' % pb, 'eps'], ['tmp'], bias=eps_t, scale=1.0 / D)
        RECIP(rstd, tmp, ['tmp'], ['rstd'])

    def prenorm(B, gidx):
        xT, hn, rstd = B['xT'], B['hn'], B['rstd']
        rmsnorm_stats(B, xT, 'xT')
        for c in range(8):
            STT(hn[:, c, :], xT[:, c, :], gv[:, gidx, c:c + 1], rstd, ALU.mult, ALU.mult,
                [('xT', c), 'gv', 'rstd'], [('hn', c)])

    def ffn(B, k, gpre, gpost, first):
        xT, hn, fb, aT, rstd = B['xT'], B['hn'], B['fb'], B['aT'], B['rstd']
        wgu, wdc, sgl = B['wgu'], B['wdc'], B['sgl']
        prenorm(B, gpre)

        def ld_gu(j):
            b = j % 3
            flat = wgu[b].rearrange("p a c f -> p (a c f)")
            if first:
                DMA('pool', wgu[b], wgu_d[k][j], [], ['wgu%d' % b])
                DMA('sp', wgu16_d[k][j], flat, ['wgu%d' % b], [('wgu16', k, j)])
            else:
                DMA('sp', flat, wgu16_d[k][j], [('wgu16', k, j)], ['wgu%d' % b])

        def ld_d(c):
            b = c % 3
            flat = wdc[b].rearrange("p j f -> p (j f)")
            if first:
                DMA('pool', wdc[b], wdc_d[k][c], [], ['wdc%d' % b])
                DMA('sp', wdc16_d[k][c], flat, ['wdc%d' % b], [('wdc16', k, c)])
            else:
                DMA('sp', flat, wdc16_d[k][c], [('wdc16', k, c)], ['wdc%d' % b])
        ld_gu(0)
        ld_gu(1)
        for j in range(NF):
            if j + 2 < NF:
                ld_gu(j + 2)
            if j == NF - 3:
                ld_d(0)
            if j == NF - 1:
                ld_d(1)
            b = j % 3
            pg = 2 + (j % 2) * 2
            pu = pg + 1
            for which, pbank in ((0, pg), (1, pu)):
                for c in range(8):
                    MM(ps[pbank][:], wgu[b][:, which, c, :], hn[:, c, :], c == 0, c == 7,
                       ['wgu%d' % b, ('hn', c)], ['ps%d' % pbank])
            sgb = j % 2
            ACT(sgl[sgb], ps[pg][:], AF.Silu, ['ps%d' % pg], ['sgl%d' % sgb])
            TTo('dve', aT[:, j, :], sgl[sgb], ps[pu][:], ALU.mult, ['sgl%d' % sgb, 'ps%d' % pu], [('aT', j)])
        for c in range(8):
            if c + 2 < 8:
                ld_d(c + 2)
            b = c % 3
            pf = 6 + (c % 2)
            for j in range(NF):
                MM(ps[pf][:], wdc[b][:, j, :], aT[:, j, :], j == 0, j == NF - 1,
                   ['wdc%d' % b, ('aT', j)], ['ps%d' % pf])
            CP('act' if c % 2 == 0 else 'dve', fb[:, c, :], ps[pf][:], ['ps%d' % pf], [('fb', c)])
        rmsnorm_stats(B, fb, 'fb')
        for c in range(8):
            ACT(fb[:, c, :], fb[:, c, :], AF.Copy, [('fb', c), 'gvh'], [('fb', c)], scale=gvh[:, gpost, c:c + 1])
            TTo('pool', fb[:, c, :], fb[:, c, :], rstd, ALU.mult, [('fb', c), 'rstd'], [('fb', c)])
            TTo('dve', xT[:, c, :], fb[:, c, :], xT[:, c, :], ALU.add, [('fb', c), ('xT', c)], [('xT', c)])

    if 'p1' in phases:
        B = alloc_ffn_bufs()
        winb = [A.alloc([128, 8, 128], BF16) for _ in range(4)]
        wvb = A.alloc([128, 8, 512], BF16)
        cosb = A.alloc([128, TT])
        sinb = A.alloc([128, TT])
        ra = A.alloc([128, TT])
        rb = A.alloc([128, TT])
        qst = [A.alloc([128, TT], BF16) for _ in range(2)]
        vst = [A.alloc([128, 8, 65], BF16) for _ in range(2)]
        zst = [A.alloc([128, TT]) for _ in range(2)]
        xT, hn = B['xT'], B['hn']
        DMA('pool', wvb, wv_d, [], ['wvb'])
        for b in range(2):
            MEMSET('pool', vst[b], 1.0, ['vst%d' % b])
        zbank = [2, 3, 4, 5]
        for tt in range(min(ntiles1, 4)):
            load_T(B, x_d, tt, xT, 'xT', loads_issued=(tt > 0))
            ffn(B, 0, 0, 1, tt == 0)
            if tt < 4:
                DMA('sp', h1s_d[:, :, tt * TT:(tt + 1) * TT].rearrange("c p t -> p c t"), xT, [('xT', c_) for c_ in range(8)], ['h1s'])
            prenorm(B, 2)
            if tt + 1 < min(ntiles1, 4):
                issue_x_loads(B, x_d, tt + 1)
            if tt < 6:
                DMA('sp', cosb, cos_d[:, tt * TT:(tt + 1) * TT], [], ['cosb'])
                DMA('sp', sinb, sin_d[:, tt * TT:(tt + 1) * TT], [], ['sinb'])
            jobs = []
            if tt < 4:
                jobs += [('q', g, [2 * g, 2 * g + 1]) for g in range(4)]
            if tt < 6:
                jobs += [('k', g, [8 + 2 * g, 9 + 2 * g]) for g in range(4)]
            jobs += [('z', j, [16 + j]) for j in range(15)]
            loads = [ci for (_, _, cis) in jobs for ci in cis]

            def ldw(n):
                if n < len(loads):
                    b = n % 4
                    flat = winb[b].rearrange("p c f -> p (c f)")
                    if tt == 0:
                        DMA('pool', winb[b], win_d[loads[n]], [], ['winb%d' % b])
                        DMA('sp', win16_d[loads[n]], flat, ['winb%d' % b], [('win16', loads[n])])
                    else:
                        DMA('sp', flat, win16_d[loads[n]], [('win16', loads[n])], ['winb%d' % b])
            for n in range(3):
                ldw(n)
            li = 0
            for (kind, g, cis) in jobs:
                banks = []
                for ci in cis:
                    ldw(li + 3)
                    b = li % 4
                    li += 1
                    zb = zbank[cnt['z'] % 4]
                    cnt['z'] += 1
                    banks.append(zb)
                    for c in range(8):
                        MM(ps[zb][:], winb[b][:, c, :], hn[:, c, :], c == 0, c == 7,
                           ['winb%d' % b, ('hn', c)], ['ps%d' % zb])
                sb_ = cnt['st'] % 2
                cnt['st'] += 1
                if kind in ('q', 'k'):
                    TTo('dve', ra, ps[banks[0]][:], cosb, ALU.mult, ['ps%d' % banks[0], 'cosb'], ['ra'])
                    TTo('dve', rb, ps[banks[1]][:], sinb, ALU.mult, ['ps%d' % banks[1], 'sinb'], ['rb'])
                    TTo('pool', qst[sb_], ra, rb, ALU.add, ['ra', 'rb'], ['qst%d' % sb_])
                    dst = qT_d if kind == 'q' else kT_d
                    DMA('sp', dst[g, :, tt * TT:(tt + 1) * TT], qst[sb_], ['qst%d' % sb_], ['qk_d'])
                    if kind == 'k' and tt >= 2:
                        DMA('sp', kh_d[g * 128:(g + 1) * 128, (tt - 2) * TT:(tt - 1) * TT], qst[sb_],
                            ['qst%d' % sb_], ['kh_d'])
                else:
                    CP('act' if g % 2 == 0 else 'dve', zst[sb_], ps[banks[0]][:], ['ps%d' % banks[0]],
                       ['zst%d' % sb_])
                    DMA('sp', zr_d[g, :, tt * TT:(tt + 1) * TT], zst[sb_], ['zst%d' % sb_], ['zr'])
                    if tt == 3:
                        P.add('sp', (lambda e, g=g, sb_=sb_: e.dma_start(
                            out=zb_d[:, g:g + 1], in_=zst[sb_][:, TT - 1:TT], allow_slow_non_contiguous=True)),
                            r=['zst%d' % sb_], w=['zb_d'], dma=True)
            if tt < 6:
                for sub in range(4):
                    zb = zbank[cnt['z'] % 4]
                    cnt['z'] += 1
                    for c in range(8):
                        MM(ps[zb][:], hn[:, c, sub * 128:(sub + 1) * 128], wvb[:, c, :], c == 0, c == 7,
                           [('hn', c), 'wvb'], ['ps%d' % zb])
                    sb_ = cnt['st'] % 2
                    cnt['st'] += 1
                    CP('act' if sub % 2 == 0 else 'dve', vst[sb_][:, :, 0:64],
                       ps[zb][:].rearrange("p (h d) -> p h d", h=8), ['ps%d' % zb], ['vst%d' % sb_])
                    r0 = (tt * 4 + sub) * 128
                    DMA('sp', v_d[r0:r0 + 128, :], vst[sb_].rearrange("p h e -> p (h e)"),
                        ['vst%d' % sb_], ['v_d'])
                    if tt >= 2:
                        DMA('sp', vh_d[r0 - 1024:r0 - 1024 + 128, :], vst[sb_].rearrange("p h e -> p (h e)"),
                            ['vst%d' % sb_], ['vh_d'])
        P.barrier()
        A.off = A_MARK
        CC(kh_d, khg_d, ['kh_d'], ['khg'])
        CC(vh_d, vhg_d, ['vh_d'], ['vhg'])
        CC(zb_d, zbg_d, ['zb_d'], ['zbg'])

    if 'att' in phases:
        qT = A.alloc([128, 4, OWN], BF16)
        kT = A.alloc([128, 4, NKT * 128], BF16)
        va = A.alloc([128, NKT, 520], BF16)
        wm = A.alloc([128, 20, 512], BF16)
        wmf = A.alloc([128, 20, 512], BF16)
        khr = A.alloc([128, 2, 4, 1024], BF16)
        vhr = A.alloc([128, 2, 8, 520], BF16)
        kh = A.alloc([128, 4, 1024], BF16)
        vh = A.alloc([128, 8, 520], BF16)
        oall = A.alloc([128, 16, 512])
        eb = [A.alloc([128, 512], BF16) for _ in range(4)]
        pm = [A.alloc([128, 512], BF16) for _ in range(4)]
        rden = [A.alloc([128, 4]) for _ in range(2)]
        sq32 = A.alloc([128, 512])
        ss = A.alloc([128, 8])
        rs = A.alloc([128, 8])
        DMA('sp', qT, qT_d.rearrange("g p t -> p g t"), ['qk_d'], ['qT'])
        DMA('sp', kT, kT_d.rearrange("g p t -> p g t"), ['qk_d'], ['kT'])
        DMA('sp', va, v_d.rearrange("(n p) f -> p n f", p=128), ['v_d'], ['va'])
        DMA('sp', wm, wm_d.rearrange("j p c -> p j c"), [], ['wm'])
        DMA('sp', wmf, wmf_d.rearrange("j p c -> p j c"), [], ['wmf'])
        def load_halo():
            DMA('sp', khr, khg_d.rearrange("(r g p) t -> p r g t", r=2, g=4), ['khg'], ['khr'])
            DMA('sp', vhr, vhg_d.rearrange("(r n p) f -> p r n f", r=2, n=8), ['vhg'], ['vhr'])
            for (raw, dst, key, n_) in ((khr, kh, 'kh', 4096), (vhr, vh, 'vh', 4160)):
                r0_ = raw[:, 0].rearrange("p a b -> p (a b)")
                r1_ = raw[:, 1].rearrange("p a b -> p (a b)")
                d_ = dst.rearrange("p a b -> p (a b)")
                TS('dve', d_, r0_, hsel[:, 0:1], None, ALU.mult, None, [key + 'r', 'hsel'], [key])
                STT(d_, r1_, hsel[:, 1:2], d_, ALU.mult, ALU.add, [key + 'r', 'hsel', key], [key])

        items = []
        for (hh, qb) in [(hh, qb) for qb in (0, 1) for hh in range(8)] + [(hh, qb) for qb in (2, 3) for hh in range(8)]:
            if True:
                blk_id = qb * 8 + hh
                kts = list(range(max(0, 4 * qb - 8), 4 * qb + 12))
                pvl = [(kt, i) for kt in kts for i in range(4) if abs(kt - 4 * qb - i) <= 8]
                for kt in kts:
                    items.append(dict(hh=hh, qb=qb, kt=kt, J=kt - 4 * qb + 8, ob=6 + blk_id % 2,
                                      first=pvl[0], last=pvl[-1], endblk=(kt == kts[-1]), blk=blk_id))
        NB = 4
        SKEW = 2

        def stage_a(n):
            d = items[n]
            hh, qb, kt, J = d['hh'], d['qb'], d['kt'], d['J']
            g, base = hh // 2, 64 * (hh % 2)
            sbk = 2 + (n % NB)
            b = n % NB
            if kt < NKT:
                kop, kkey, mk, mkey_ = kT[base:base + 64, g, kt * 128:(kt + 1) * 128], 'kT', wm, 'wm'
            else:
                ht = 23 - kt
                kop, kkey, mk, mkey_ = kh[base:base + 64, g, ht * 128:(ht + 1) * 128], 'kh', wmf, 'wmf'
            MM(ps[sbk][:], kop, qT[base:base + 64, g, qb * 512:(qb + 1) * 512], True, True, [kkey, 'qT'],
               ['ps%d' % sbk])
            ACT(eb[b], ps[sbk][:], AF.Exp, ['ps%d' % sbk], ['eb%d' % b], scale=0.125)
            TTo('dve', pm[b], eb[b], mk[:, J, :], ALU.mult, ['eb%d' % b, mkey_], ['pm%d' % b])

        def stage_b(n):
            d = items[n]
            hh, qb, kt, J, ob = d['hh'], d['qb'], d['kt'], d['J'], d['ob']
            b = n % NB
            for i in range(4):
                if abs(J - 8 - i) > 8:
                    continue
                vop, vkey = (va[:, kt, hh * 65:(hh + 1) * 65], 'va') if kt < NKT else \
                    (vh[:, 23 - kt, hh * 65:(hh + 1) * 65], 'vh')
                MM(ps[ob][:, i * 65:(i + 1) * 65], pm[b][:, i * 128:(i + 1) * 128],
                   vop, (kt, i) == d['first'], (kt, i) == d['last'],
                   ['pm%d' % b, vkey], ['ps%d' % ob])
            if d['endblk']:
                o4 = ps[ob][:, 0:260].rearrange("p (i e) -> p i e", e=65)
                rb_ = d['blk'] % 2
                RECIP(rden[rb_].unsqueeze(2), o4[:, :, 64:65], ['ps%d' % ob], ['rden%d' % rb_])
                TTo('dve', oall[:, qb * 4:(qb + 1) * 4, hh * 64:(hh + 1) * 64], o4[:, :, 0:64],
                    rden[rb_].unsqueeze(2).broadcast_to([128, 4, 64]), ALU.mult,
                    ['ps%d' % ob, 'rden%d' % rb_], [('oall', qb)])
        first_halo = min(n for n, d in enumerate(items) if d['kt'] >= NKT)
        for n in range(len(items) + SKEW):
            if n == first_halo:
                load_halo()
            if n < len(items):
                stage_a(n)
            if n >= SKEW:
                stage_b(n - SKEW)
        for qt in range(16):
            o = oall[:, qt, :]
            o3 = o.rearrange("p (h d) -> p h d", h=8)
            ACT(sq32, o, AF.Square, [('oall', qt // 4)], ['sq32'])
            P.add('dve', lambda e: e.tensor_reduce(out=ss, in_=sq32.rearrange("p (h d) -> p h d", h=8),
                                                   axis=AX.X, op=ALU.add), r=['sq32'], w=['ss'])
            ACT(rs, ss, AF.Sqrt, ['ss', 'eps'], ['rs'], bias=eps_t, scale=1.0 / 64)
            RECIP(rs, rs, ['rs'], ['rs'])
            TTo('dve', o3, o3, rs.unsqueeze(2).broadcast_to([128, 8, 64]), ALU.mult,
                [('oall', qt // 4), 'rs'], [('oall', qt // 4)])
            TTo('dve', o, o, rep[:, 0, :], ALU.mult, [('oall', qt // 4), 'rep'], [('oall', qt // 4)])
            DMA('sp', yc_d[qt * 128:(qt + 1) * 128, 0:512], o, [('oall', qt // 4)], ['yc'])
        P.barrier()
        A.off = A_MARK

    if 'rwkv' in phases:
        rwkv_phase(E, rwkv_stop)
        P.barrier()
        A.off = A_MARK

    if 'p4' in phases:
        B = alloc_ffn_bufs()
        wob = [A.alloc([128, 8, 128], BF16) for _ in range(2)]
        ost = [A.alloc([128, D]) for _ in range(2)]
        xT, hn, fb, rstd = B['xT'], B['hn'], B['fb'], B['rstd']
        for tt in range(4):
            load_T(B, yc_d, tt, hn, 'hn')
            def ld_o(dc):
                b_ = dc % 2
                flat = wob[b_].rearrange("p c f -> p (c f)")
                if tt == 0:
                    DMA('pool', wob[b_], wo_d[dc], [], ['wob%d' % b_])
                    DMA('sp', wo16_d[dc], flat, ['wob%d' % b_], [('wo16', dc)])
                else:
                    DMA('sp', flat, wo16_d[dc], [('wo16', dc)], ['wob%d' % b_])
            ld_o(0)
            for dc in range(8):
                if dc + 1 < 8:
                    ld_o(dc + 1)
                b = dc % 2
                pf = 6 + (dc % 2)
                for cc in range(8):
                    MM(ps[pf][:], wob[b][:, cc, :], hn[:, cc, :], cc == 0, cc == 7,
                       ['wob%d' % b, ('hn', cc)], ['ps%d' % pf])
                CP('act' if dc % 2 == 0 else 'dve', fb[:, dc, :], ps[pf][:], ['ps%d' % pf], [('fb', dc)])
            rmsnorm_stats(B, fb, 'fb')
            DMA('sp', xT, h1s_d[:, :, tt * TT:(tt + 1) * TT].rearrange("c p t -> p c t"), ['h1s'], [('xT', c_) for c_ in range(8)])
            for c in range(8):
                ACT(fb[:, c, :], fb[:, c, :], AF.Copy, [('fb', c), 'gv'], [('fb', c)], scale=gv[:, 3, c:c + 1])
                TTo('pool', fb[:, c, :], fb[:, c, :], rstd, ALU.mult, [('fb', c), 'rstd'], [('fb', c)])
                TTo('dve', xT[:, c, :], fb[:, c, :], xT[:, c, :], ALU.add, [('fb', c), ('xT', c)], [('xT', c)])
            ffn(B, 1, 4, 5, tt == 0)
            for sub in range(4):
                ob = cnt['st'] % 2
                cnt['st'] += 1
                for half in range(2):
                    pb = cnt['ps01'] % 2
                    cnt['ps01'] += 1
                    for q in range(4):
                        c = half * 4 + q
                        TR(ps[pb][:, q * 128:(q + 1) * 128], xT[:, c, sub * 128:(sub + 1) * 128], ident,
                           [('xT', c), 'ident'], ['ps%d' % pb])
                    CP('act' if half == 0 else 'dve', ost[ob][:, half * 512:(half + 1) * 512], ps[pb][:],
                       ['ps%d' % pb], ['ost%d' % ob])
                r0 = tt * TT + sub * 128
                DMA('sp', out_d[r0:r0 + 128, :], ost[ob], ['ost%d' % ob], ['out'])

    P.add('sp', None, r=['h1s', 'out', 'yc', 'zr', 'qk_d', 'v_d', 'yf'])
    P.emit(nc, stack)
    stack.close()
    return nc, P


def rwkv_phase(E, debug_stop=None):
    P, A, ps = E.P, E.A, E.ps
    MM, TR, ACT, TTo, TS, STT, CP, RECIP, DMA, MEMSET = (E.MM, E.TR, E.ACT, E.TTo, E.TS, E.STT, E.CP, E.RECIP,
                                                          E.DMA, E.MEMSET)
    ident, identb, rep, rwm, rwc, rwp, lora = E.ident, E.identb, E.rep, E.rwm, E.rwc, E.rwp, E.lora
    eps_t, gneps_t = E.eps_t, E.gneps_t
    zr_d, yc_d, yf_d = E.zr_d, E.yc_d, E.yf_d

    rmask = rwc[:, 0:512]
    blockones = rwc[:, 1024:1152]
    identblk = rwc[:, 1152:1216]
    hsel = rwc[:, 1216:1218]
    mp, mn = rwp[:, 0:15], rwp[:, 15:30]
    k_k, k_a, r_k = rwp[:, 46:50], rwp[:, 50:54], rwp[:, 54:58]
    w2sb, a2sb, g2sb = lora[:, 0, :], lora[:, 1, :], lora[:, 2, :]

    def w0T(di, c):
        return rwp[:, 30 + 4 * di + c:31 + 4 * di + c]

    def a0T(di, c):
        return rwp[:, 38 + 4 * di + c:39 + 4 * di + c]

    c0 = A.alloc([128, 15])
    omka = A.alloc([128, 4])
    rk2 = A.alloc([128, 4])
    GW = 256
    NTG = GW // 128
    NG = 4096 // GW
    NG_OWN = 2048 // GW
    zsb = [A.alloc([128, GW + 2]) for _ in range(3)]
    zcnt = {'z': 0}
    ub = A.alloc([128, 15, GW])
    tw = A.alloc([128, GW])
    T_ = {nm: A.alloc([128, GW]) for nm in
          ('sg', 'av', 'kx', 't1', 't2', 'kkv', 'kd', 'ka', 'cs', 'cs2', 'e0', 'e1', 'e2', 'e3', 'kd0')}
    tot_s = A.alloc([128, GW // 64])
    OB = []
    for _ in range(2):
        d_ = {nm: A.alloc([128, 4, GW], BF16) for nm in ('Rb', 'Kb', 'Kd', 'Ad', 'Kh', 'Ahn', 'vb')}
        d_['gc'] = A.alloc([128, 4, GW // 64])
        d_['bprod'] = A.alloc([128, 4, GW])
        d_['sgd'] = A.alloc([128, GW])
        d_['uvf'] = A.alloc([128, 4, GW])
        OB.append(d_)
    KbT, KhT, AhT, VT = [A.alloc([128, 512], BF16) for _ in range(4)]
    VT32s = [A.alloc([128, 512]) for _ in range(2)]
    gTs = [A.alloc([128, 512]) for _ in range(2)]
    bons = [A.alloc([128, 8]) for _ in range(2)]
    fcnt = {'n': 0}
    bg = {'final': None}
    Xs = [[A.alloc([128, 4, 128], BF16) for _ in range(2)] for _ in range(2)]
    Ys = [[A.alloc([128, 4, 128], BF16) for _ in range(2)] for _ in range(2)]
    Rms = [[A.alloc([128, 4, 128], BF16) for _ in range(2)] for _ in range(2)]
    pkks, qrks, qras = [[A.alloc([128, 4, 128], BF16) for _ in range(2)] for _ in range(3)]
    pvs = [A.alloc([128, 4, 64], BF16) for _ in range(2)]
    u0 = A.alloc([128, 8, 64], BF16)
    wt = A.alloc([128, 8, 64], BF16)
    y0 = A.alloc([128, 512])
    rpT = A.alloc([128, 4, 128])
    GT = A.alloc([128, 4, 2, 64])
    Hs = A.alloc([128, 4, 2, 64])
    Tst = [A.alloc([128, 4, 64]) for _ in range(2)]
    ytiles = [A.alloc([128, 512]) for _ in range(2)]
    yfbs = [A.alloc([128, 512]) for _ in range(2)]
    sqb = A.alloc([128, 512])
    tmpv = A.alloc([128, 512])
    s1 = A.alloc([128, 8])
    s2 = A.alloc([128, 8])
    psb2 = ps[2][:].bitcast(BF16)

    TTo('dve', c0, mp, mn, ALU.add, ['rwp'], ['c0'])
    TS('dve', c0, c0, -1.0, 1.0, ALU.mult, ALU.add, ['c0'], ['c0'])
    TS('dve', omka, k_a, -1.0, 1.0, ALU.mult, ALU.add, ['rwp'], ['omka'])
    TS('dve', rk2, r_k, 0.5, None, ALU.mult, None, ['rwp'], ['rk2'])
    zr2 = A.alloc([128, 2, 15])
    zedge = A.alloc([128, 15])
    zg = E.zbg_d.rearrange("(r p) c -> p r c", r=2)
    DMA('sp', zr2, zg, ['zbg'], ['zr2'])
    DMA('sp', zr2[0:64, :, 12:14], zg[64:128, :, 12:14], ['zbg'], ['zr2'])
    DMA('sp', zr2[64:128, :, 12:14], zg[0:64, :, 12:14], ['zbg'], ['zr2'])
    TS('dve', zedge, zr2[:, 0, :], E.hsel[:, 0:1], None, ALU.mult, None, ['zr2', 'hsel'], ['zedge'])
    STT(zedge, zr2[:, 1, :], E.hsel[:, 1:2], zedge, ALU.mult, ALU.add, ['zr2', 'hsel', 'zedge'], ['zedge'])
    tfr = A.alloc([128, 2, 256])

    pcnt = {'pa': 0}

    def pbank(lo=0):
        b = (6 if lo == 0 else 0) + pcnt['pa'] % 2
        pcnt['pa'] += 1
        return b

    def prep_group(g, di, own, final, pb):
        O_ = OB[pb]
        Rb, Kb, Kd, Ad, Kh, Ahn, vb = (O_[n] for n in ('Rb', 'Kb', 'Kd', 'Ad', 'Kh', 'Ahn', 'vb'))
        gc, bprod, sgd, uvf = O_['gc'], O_['bprod'], O_['sgd'], O_['uvf']
        lo, hi = GW * g - 1, GW * g + GW + 1
        clo, chi = max(lo, 0), min(hi, 2048)
        need = list(range(4, 12)) + [12, 13] + ([0, 1, 2, 3] if own else []) + ([14] if final else [])
        for n_, j in enumerate(need):
            zb = zsb[zcnt['z'] % 3]
            zk = 'zsb%d' % (zcnt['z'] % 3)
            zcnt['z'] += 1
            DMA('sp', zb[:, clo - lo:GW + 2 - (hi - chi)], zr_d[j, :, clo:chi], ['zr'], [zk])
            if g == 0:
                MEMSET('pool', zb[:, 0:1], 0.0, [zk])
            if g == NG_OWN - 1:
                CP('pool', zb[:, GW + 1:GW + 2], zedge[:, j:j + 1], ['zedge'], [zk])
            k = ('ub', j)
            ACT(ub[:, j, :], zb[:, 0:GW], AF.Copy, [zk, 'rwp'], [k], scale=mp[:, j:j + 1])
            STT(ub[:, j, :], zb[:, 2:GW + 2], mn[:, j:j + 1], ub[:, j, :], ALU.mult, ALU.add, [zk, 'rwp', k], [k])
            STT(ub[:, j, :], zb[:, 1:GW + 1], c0[:, j:j + 1], ub[:, j, :], ALU.mult, ALU.add, [zk, 'c0', k], [k])
            yield
        ACT(tw, ub[:, 12, :], AF.Tanh, [('ub', 12)], ['tw'])
        if final:
            ACT(sgd, ub[:, 14, :], AF.Sigmoid, [('ub', 14)], [('sgd', pb)])
        sg, av, kx, t1, t2, kkv, kd, ka = (T_[n] for n in ('sg', 'av', 'kx', 't1', 't2', 'kkv', 'kd', 'ka'))
        cs, cs2, e0, e1, e2, e3, kd0 = (T_[n] for n in ('cs', 'cs2', 'e0', 'e1', 'e2', 'e3', 'kd0'))
        lo64 = 64 * di
        NC64 = GW // 64
        for c in range(4):
            ku = ub[:, 4 + c, :]
            kkey = ('ub', 4 + c)
            cc = slice(c * 128, (c + 1) * 128)
            ACT(kx, ku, AF.Copy, [kkey, 'rwp'], ['kx'], scale=k_k[:, c:c + 1])
            ACT(t1, kx, AF.Square, ['kx'], ['t1'])
            b = pbank()
            MM(ps[b][:, 0:GW], blockones, t1, True, True, ['rwc', 't1'], ['ps%d' % b])
            ACT(t1, ps[b][:, 0:GW], AF.Sqrt, ['ps%d' % b], ['t1'])
            yield
            TS('dve', t1, t1, 1e-12, None, ALU.max, None, ['t1'], ['t1'])
            RECIP(t1, t1, ['t1'], ['t1'])
            TTo('dve', kkv, kx, t1, ALU.mult, ['kx', 't1'], ['kkv'])
            b = pbank(lo64)
            MM(ps[b][:, 0:GW], w2sb[lo64:lo64 + 64, cc], tw[lo64:lo64 + 64, :], True, True, ['lora', 'tw'],
               ['ps%d' % b])
            ACT(sg, ps[b][:, 0:GW], AF.Sigmoid, ['ps%d' % b, 'rwp'], ['sg'], bias=w0T(di, c))
            yield
            b = pbank(lo64)
            MM(ps[b][:, 0:GW], a2sb[lo64:lo64 + 64, cc], ub[lo64:lo64 + 64, 13, :], True, True,
               ['lora', ('ub', 13)], ['ps%d' % b])
            ACT(av, ps[b][:, 0:GW], AF.Sigmoid, ['ps%d' % b, 'rwp'], ['av'], bias=a0T(di, c))
            ACT(t2, av, AF.Identity, ['av', 'rwp', 'omka'], ['t2'], bias=omka[:, c:c + 1], scale=k_a[:, c:c + 1])
            TTo('dve', kd, ku, t2, ALU.mult, [kkey, 't2'], ['kd'])
            TTo('pool', ka, kkv, av, ALU.mult, ['kkv', 'av'], ['ka'])
            yield
            if final:
                od = 1 - di
                b = pbank(64 * od)
                MM(ps[b][:, 0:GW], a2sb[64 * od:64 * od + 64, cc], ub[64 * od:64 * od + 64, 13, :], True, True,
                   ['lora', ('ub', 13)], ['ps%d' % b])
                ACT(kd0, ps[b][:, 0:GW], AF.Sigmoid, ['ps%d' % b, 'rwp'], ['kd0'], bias=a0T(od, c))
                TS('dve', kd0, kd0, k_a[:, c:c + 1], omka[:, c:c + 1], ALU.mult, ALU.add,
                   ['kd0', 'rwp', 'omka'], ['kd0'])
                TTo('dve', kd0, kd0, ku, ALU.mult, ['kd0', kkey], ['kd0'])
                yield
                TTo('dve', kd0, kd0, kd, ALU.add, ['kd0', 'kd'], ['kd0'])
                TTo('dve', kd0, kd0, ub[:, c, :], ALU.mult, ['kd0', ('ub', c)], ['kd0'])
                TS('dve', bprod[:, c, :], kd0, rk2[:, c:c + 1], None, ALU.mult, None, ['kd0', 'rk2'],
                   [('bprod', c, pb)])
                CP('pool', uvf[:, c, :], ub[:, 8 + c, :], [('ub', 8 + c)], [('uvf', c, pb)])
                yield
            P.add('dve', lambda e, cs=cs, sg=sg: e.tensor_tensor_scan(out=cs, data0=rmask[:, 0:GW], data1=sg,
                                                                      initial=0.0, op0=ALU.mult, op1=ALU.add),
                  r=['rwc', 'sg'], w=['cs'])
            CP('dve', tot_s, cs[:, 63::64], ['cs'], ['tot'])
            totb = tot_s.unsqueeze(2).broadcast_to([128, NC64, 64])
            v3 = lambda ap: ap.rearrange("p (a b) -> p a b", a=NC64)
            if di == 0:
                csx, cskey = cs, 'cs'
            else:
                TTo('dve', t2, sg, cs, ALU.subtract, ['sg', 'cs'], ['t2'])
                TTo('dve', v3(cs2), v3(t2), totb, ALU.add, ['t2', 'tot'], ['cs2'])
                csx, cskey = cs2, 'cs2'
            yield
            TTo('dve', e0, csx, sg, ALU.subtract, [cskey, 'sg'], ['e0'])
            TTo('dve', v3(e3), totb, v3(csx), ALU.subtract, ['tot', cskey], ['e3'])
            ACT(e1, csx, AF.Exp, [cskey], ['e1'], scale=-CDEC)
            ACT(e2, csx, AF.Exp, [cskey], ['e2'], scale=CDEC)
            yield
            ACT(e0, e0, AF.Exp, ['e0'], ['e0'], scale=-CDEC)
            ACT(e3, e3, AF.Exp, ['e3'], ['e3'], scale=-CDEC)
            ACT(gc[:, c, :], tot_s, AF.Exp, ['tot'], [('gc', c, pb)], scale=-CDEC)
            yield
            if own:
                TTo('pool', Rb[:, c, :], ub[:, c, :], e1, ALU.mult, [('ub', c), 'e1'], [('Rb', c, pb)])
            TTo('pool', Kb[:, c, :], kkv, e0, ALU.mult, ['kkv', 'e0'], [('Kb', c, pb)])
            TTo('pool', Kd[:, c, :], kd, e2, ALU.mult, ['kd', 'e2'], [('Kd', c, pb)])
            yield
            TTo('pool', Ad[:, c, :], ka, e2, ALU.mult, ['ka', 'e2'], [('Ad', c, pb)])
            TTo('pool', Kh[:, c, :], kd, e3, ALU.mult, ['kd', 'e3'], [('Kh', c, pb)])
            TS('pool', ka, ka, -1.0, 0.0, ALU.mult, ALU.add, ['ka'], ['ka'])
            TTo('pool', Ahn[:, c, :], ka, e3, ALU.mult, ['ka', 'e3'], [('Ahn', c, pb)])
            CP('pool', vb[:, c, :], ub[:, 8 + c, :], [('ub', 8 + c)], [('vb', c, pb)])
            yield

    M = lambda di, k: rwm[:, di, k, :].unsqueeze(1).broadcast_to([128, 4, 128])

    def tile_proc(di, g, tl, outp, final, cur, pb, pump):
        cols = slice(tl * 128, (tl + 1) * 128)
        gt = NTG * g + tl
        O_ = OB[pb]
        Rb, Kb, Kd, Ad, Kh, Ahn, vb = (O_[n] for n in ('Rb', 'Kb', 'Kd', 'Ad', 'Kh', 'Ahn', 'vb'))
        gc, bprod, sgd, uvf = O_['gc'], O_['bprod'], O_['sgd'], O_['uvf']
        allk = lambda nm: [(nm, c, pb) for c in range(4)]
        fb_ = fcnt['n'] % 2 if final else 0
        ytile, ykey = ytiles[fb_], 'ytile%d' % fb_
        VT32, gT, bon, yfb = VT32s[fb_], gTs[fb_], bons[fb_], yfbs[fb_]
        slot_of = lambda hh: 4 * (hh % 2) + hh // 2
        for (src, dstT, nm, eng) in ((Kb, KbT, 'Kb', 'act'), (Kh, KhT, 'Kh', 'dve'), (Ahn, AhT, 'Ahn', 'act'),
                                      (vb, VT, 'vb', 'dve')):
            for c in range(4):
                TR(psb2[:, c * 128:(c + 1) * 128], src[:, c, cols], identb, [(nm, c, pb), 'identb'], ['ps2'])
            CP(eng, dstT, psb2[:, 0:512], ['ps2'], [nm + 'T'])
        if final:
            for c in range(4):
                TR(ps[2][:, c * 128:(c + 1) * 128], uvf[:, c, cols], ident, [('uvf', c, pb), 'ident'], ['ps2'])
            CP('act', VT32, ps[2][:], ['ps2'], [('VT32', fb_)])
            MM(ps[3][:], sgd[:, cols], g2sb, True, True, [('sgd', pb), 'lora'], ['ps3'])
            CP('act', gT, ps[3][:], ['ps3'], ['gT%d' % fb_])
            for c in range(4):
                MM(ps[2][:, 2 * c:2 * c + 2], bprod[:, c, cols], hsel, True, True, [('bprod', c, pb), 'rwc'], ['ps2'])
            CP('dve', bon, ps[2][:, 0:8], ['ps2'], ['bon%d' % fb_])
        def hg_stages(hg):
            BA, BB, BC = ((4, 5, 3), (1, 0, 2))[hg]
            X, Y, Rm = Xs[hg], Ys[hg], Rms[hg]
            pkk, qrk, qra, pv = pkks[hg], qrks[hg], qras[hg], pvs[hg]
            sx = 'g%d' % hg
            base = 64 * hg
            hl = [(2 * i + hg, i) for i in range(4)]

            def blk(bank, i):
                return ps[bank][:, i * 128:(i + 1) * 128]

            def b3(bank):
                return ps[bank][:].rearrange("p (a b) -> p a b", a=4)
            for i, (hh, c) in enumerate(hl):
                MM(blk(BA, i), Kb[base:base + 64, c, cols], Ad[base:base + 64, c, cols], True, True,
                   [('Kb', c, pb), ('Ad', c, pb)], ['ps%d' % BA])
            TTo('dve', X[0], b3(BA), M(di, 0), ALU.mult, ['ps%d' % BA, 'rwm'], ['X0' + sx])
            yield
            for i, (hh, c) in enumerate(hl):
                MM(blk(BB, i), Ad[base:base + 64, c, cols], Kb[base:base + 64, c, cols], True, True,
                   [('Kb', c, pb), ('Ad', c, pb)], ['ps%d' % BB])
            TTo('dve', Y[0], b3(BB), M(di, 1), ALU.mult, ['ps%d' % BB, 'rwm'], ['Y0' + sx])
            TTo('pool', Rm[0], Y[0], identb.unsqueeze(1).broadcast_to([128, 4, 128]), ALU.add,
                ['Y0' + sx, 'identb'], ['R0' + sx])
            yield
            for j in range(1, 6):
                a, b = (j - 1) % 2, j % 2
                for i in range(4):
                    MM(blk(BA, i), Y[a][:, i, :], X[a][:, i, :], True, True, ['X%d' % a + sx, 'Y%d' % a + sx], ['ps%d' % BA])
                CP('act', X[b], b3(BA), ['ps%d' % BA], ['X%d' % b + sx])
                yield
                if j < 5:
                    for i in range(4):
                        MM(blk(BB, i), X[a][:, i, :], Y[a][:, i, :], True, True, ['X%d' % a + sx, 'Y%d' % a + sx], ['ps%d' % BB])
                    CP('act', Y[b], b3(BB), ['ps%d' % BB], ['Y%d' % b + sx])
                    yield
                for i in range(4):
                    MM(blk(BC, i), identb, Rm[a][:, i, :], True, False, ['identb', 'R%d' % a + sx], ['ps%d' % BC])
                    MM(blk(BC, i), X[b][:, i, :], Rm[a][:, i, :], False, True, ['X%d' % b + sx, 'R%d' % a + sx], ['ps%d' % BC])
                CP('act', Rm[b], b3(BC), ['ps%d' % BC], ['R%d' % b + sx])
                yield
            MinvT, mkey = Rm[1], 'R1' + sx
            for i, (hh, c) in enumerate(hl):
                MM(blk(BA, i), Kd[base:base + 64, c, cols], Kb[base:base + 64, c, cols], True, True,
                   [('Kd', c, pb), ('Kb', c, pb)], ['ps%d' % BA])
            TTo('dve', pkk, b3(BA), M(di, 2), ALU.mult, ['ps%d' % BA, 'rwm'], ['pkk' + sx])
            yield
            if outp:
                for i, (hh, c) in enumerate(hl):
                    MM(blk(BB, i), Kd[base:base + 64, c, cols], Rb[base:base + 64, c, cols], True, True,
                       [('Kd', c, pb), ('Rb', c, pb)], ['ps%d' % BB])
                TTo('dve', qrk, b3(BB), M(di, 3), ALU.mult, ['ps%d' % BB, 'rwm'], ['qrk' + sx])
                yield
                for i, (hh, c) in enumerate(hl):
                    MM(blk(BC, i), Ad[base:base + 64, c, cols], Rb[base:base + 64, c, cols], True, True,
                       [('Ad', c, pb), ('Rb', c, pb)], ['ps%d' % BC])
                TTo('dve', qra, b3(BC), M(di, 4), ALU.mult, ['ps%d' % BC, 'rwm'], ['qra' + sx])
                yield
            for i, (hh, c) in enumerate(hl):
                MM(ps[BA][:, i * 64:(i + 1) * 64], pkk[:, i, :], VT[:, hh * 64:(hh + 1) * 64], True, True,
                   ['pkk' + sx, 'vbT'], ['ps%d' % BA])
            CP('act', pv, ps[BA][:, 0:256].rearrange("p (a b) -> p a b", a=4), ['ps%d' % BA], ['pv' + sx])
            yield
            for i, (hh, c) in enumerate(hl):
                MM(ps[BB][:, i * 64:(i + 1) * 64], MinvT[:, i, :], pv[:, i, :], True, True, [mkey, 'pv' + sx], ['ps%d' % BB])
            for i, (hh, c) in enumerate(hl):
                MM(ps[BC][:, i * 64:(i + 1) * 64], MinvT[:, i, :], KbT[:, hh * 64:(hh + 1) * 64],
                   True, True, [mkey, 'KbT'], ['ps%d' % BC])
            CP('act', u0[:, 4 * hg:4 * hg + 4, :], ps[BB][:, 0:256].rearrange("p (a b) -> p a b", a=4),
               ['ps%d' % BB], [('u0', hg)])
            CP('dve', wt[:, 4 * hg:4 * hg + 4, :], ps[BC][:, 0:256].rearrange("p (a b) -> p a b", a=4),
               ['ps%d' % BC], [('wt', hg)])
            yield
            if outp:
                for i, (hh, c) in enumerate(hl):
                    MM(ps[BB][:, i * 64:(i + 1) * 64], qrk[:, i, :], VT[:, hh * 64:(hh + 1) * 64], True, False,
                       ['qrk' + sx, 'vbT'], ['ps%d' % BB])
                    MM(ps[BB][:, i * 64:(i + 1) * 64], qra[:, i, :], u0[:, 4 * hg + i, :], False, True,
                       ['qra' + sx, ('u0', hg)], ['ps%d' % BB])
                CP('act', y0.rearrange("p (i q d) -> p i q d", i=4, q=2)[:, :, hg, :],
                   ps[BB][:, 0:256].rearrange("p (a b) -> p a b", a=4), ['ps%d' % BB], [('y0', hg)])
                yield
                for i, (hh, c) in enumerate(hl):
                    MM(ps[BA][base:base + 64, i * 128:(i + 1) * 128], wt[:, 4 * hg + i, :], qra[:, i, :],
                       True, True, [('wt', hg), 'qra' + sx], ['ps%d' % BA])
                TTo('dve', rpT[base:base + 64, :, :],
                    ps[BA][base:base + 64, :].rearrange("p (a b) -> p a b", a=4),
                    Rb[base:base + 64, :, cols], ALU.add, ['ps%d' % BA] + allk('Rb'), [('rpT', hg)])
            yield
        gens = [hg_stages(0), hg_stages(1)]
        alive = [True, True]
        while any(alive):
            pump(1)
            for k_ in (0, 1):
                if alive[k_]:
                    try:
                        next(gens[k_])
                    except StopIteration:
                        alive[k_] = False
        gbank = {0: 5, 1: 3}
        hbank = {0: 4, 1: 2}
        for ch2 in range(2):
            rows = slice(ch2 * 64, ch2 * 64 + 64)
            gb, hb = gbank[ch2], hbank[ch2]
            for hh in range(8):
                c, base = hh // 2, 64 * (hh % 2)
                sl = slot_of(hh)
                o5 = ps[gb][base:base + 64, c * 64:(c + 1) * 64]
                MM(o5, wt[rows, sl, :], AhT[rows, hh * 64:(hh + 1) * 64], True, True,
                   [('wt', hh % 2), 'AhnT'], ['ps%d' % gb])
                o4 = ps[hb][base:base + 64, c * 64:(c + 1) * 64]
                MM(o4, KhT[rows, hh * 64:(hh + 1) * 64], VT[rows, hh * 64:(hh + 1) * 64], True, False,
                   ['KhT', 'vbT'], ['ps%d' % hb])
                MM(o4, AhT[rows, hh * 64:(hh + 1) * 64], u0[rows, sl, :], False, True,
                   ['AhnT', ('u0', hh % 2)], ['ps%d' % hb])
        for ch2 in range(2):
            gb, hb = gbank[ch2], hbank[ch2]
            for c in range(4):
                slot = 2 * tl + ch2
                STT(GT[:, c, ch2, :], identblk, gc[:, c, slot:slot + 1], ps[gb][:, c * 64:(c + 1) * 64],
                    ALU.mult, ALU.add, ['rwc', ('gc', c, pb), 'ps%d' % gb], ['GT'])
            CP('act', Hs[:, :, ch2, :], ps[hb][:, 0:256].rearrange("p (a b) -> p a b", a=4),
               ['ps%d' % hb], ['Hs'])
        pump(2)
        order = [0, 1] if di == 0 else [1, 0]
        tb = {0: 6, 1: 0}
        yb_ = {0: 7, 1: 1}
        for ch2 in order:
            rows = slice(ch2 * 64, ch2 * 64 + 64)
            Tc, Tn = Tst[cur], Tst[1 - cur]
            kc, kn = 'T%d' % cur, 'T%d' % (1 - cur)
            for par in range(2):
                base = 64 * par
                for c in range(4):
                    hh = 2 * c + par
                    if outp:
                        MM(ps[yb_[par]][rows, hh * 64:(hh + 1) * 64],
                           rpT[base:base + 64, c, ch2 * 64:ch2 * 64 + 64],
                           Tc[base:base + 64, c, :], True, True, [('rpT', par), kc], ['ps%d' % yb_[par]])
                    MM(ps[tb[par]][base:base + 64, c * 64:(c + 1) * 64], GT[base:base + 64, c, ch2, :],
                       Tc[base:base + 64, c, :], True, True, ['GT', kc], ['ps%d' % tb[par]])
            for par in range(2):
                base = 64 * par
                if outp:
                    yv = ytile.rearrange("p (i q d) -> p i q d", i=4, q=2)
                    y0v = y0.rearrange("p (i q d) -> p i q d", i=4, q=2)
                    pyv = ps[yb_[par]][:].rearrange("p (i q d) -> p i q d", i=4, q=2)
                    TTo('dve', yv[rows, :, par, :], pyv[rows, :, par, :], y0v[rows, :, par, :], ALU.add,
                        ['ps%d' % yb_[par], ('y0', 0), ('y0', 1)], [ykey])
                TTo('dve', Tn[base:base + 64, :, :],
                    ps[tb[par]][base:base + 64, 0:256].rearrange("p (a b) -> p a b", a=4),
                    Hs[base:base + 64, :, ch2, :], ALU.add, ['ps%d' % tb[par], 'Hs'], [kn])
            cur = 1 - cur
            pump(1)
        if outp and not final:
            DMA('sp', yf_d[gt * 128:(gt + 1) * 128, :], ytile, [ykey], ['yf'])
        if outp and final:
            def final_gen(ytile=ytile, ykey=ykey, VT32=VT32, gT=gT, bon=bon, yfb=yfb, fb_=fb_, gt=gt):
                yk = 'yfb%d' % fb_
                DMA('sp', yfb, yf_d[gt * 128:(gt + 1) * 128, :], ['yf'], [yk])
                y = ytile
                y3 = y.rearrange("p (h d) -> p h d", h=8)
                TTo('dve', y, y, yfb, ALU.add, [ykey, yk], [ykey])
                P.add('dve', lambda e: e.tensor_reduce(out=s1, in_=y3, axis=AX.X, op=ALU.add), r=[ykey], w=['s1'])
                yield
                TS('dve', s1, s1, -1.0 / 64, None, ALU.mult, None, ['s1'], ['s1'])
                TTo('dve', y3, y3, s1.unsqueeze(2).broadcast_to([128, 8, 64]), ALU.add, [ykey, 's1'], [ykey])
                ACT(sqb, y, AF.Square, [ykey], ['sqb'])
                yield
                P.add('dve', lambda e: e.tensor_reduce(out=s2, in_=sqb.rearrange("p (h d) -> p h d", h=8),
                                                       axis=AX.X, op=ALU.add), r=['sqb'], w=['s2'])
                ACT(s2, s2, AF.Sqrt, ['s2', 'eps'], ['s2'], bias=gneps_t, scale=1.0 / 64)
                yield
                RECIP(s2, s2, ['s2'], ['s2'])
                TTo('dve', y3, y3, s2.unsqueeze(2).broadcast_to([128, 8, 64]), ALU.mult, [ykey, 's2'], [ykey])
                yield
                TTo('dve', y, y, rep[:, 1, :], ALU.mult, [ykey, 'rep'], [ykey])
                TTo('pool', tmpv.rearrange("p (h d) -> p h d", h=8), VT32.rearrange("p (h d) -> p h d", h=8),
                    bon.unsqueeze(2).broadcast_to([128, 8, 64]), ALU.mult, [('VT32', fb_), 'bon%d' % fb_], ['tmpv'])
                yield
                TTo('dve', y, y, rep[:, 2, :], ALU.add, [ykey, 'rep'], [ykey])
                TTo('dve', y, y, tmpv, ALU.add, [ykey, 'tmpv'], [ykey])
                yield
                TTo('dve', y, y, gT, ALU.mult, [ykey, 'gT%d' % fb_], [ykey])
                DMA('sp', yc_d[gt * 128:(gt + 1) * 128, 512:1024], y, [ykey], ['yc'])
                yield
            if bg['final'] is not None:
                for _ in bg['final']:
                    pass
            bg['final'] = final_gen()
            fcnt['n'] += 1
        return cur

    sched = [(g, 0, True, False) for g in range(NG_OWN)]
    if debug_stop != 'F':
        sched += [(g, 1, True, True) for g in range(NG_OWN - 1, -1, -1)]
    cur = 0
    MEMSET('pool', Tst[0], 0.0, ['T0'])
    for _ in prep_group(*sched[0], 0):
        pass
    for idx, (g, di, own, final) in enumerate(sched):
        pb = idx % 2
        pg = prep_group(*sched[idx + 1], (idx + 1) % 2) if idx + 1 < len(sched) else None
        st = {'alive': pg is not None}

        def pump(n, pg=pg, st=st):
            for _ in range(n):
                if st['alive']:
                    try:
                        next(pg)
                    except StopIteration:
                        st['alive'] = False
                if n < 100 and bg['final'] is not None:
                    try:
                        next(bg['final'])
                    except StopIteration:
                        bg['final'] = None
        if idx > 0 and sched[idx - 1][1] != di:
            kc_ = 'T%d' % cur
            tflat = Tst[cur].rearrange("p a b -> p (a b)")
            DMA('sp', E.tf_d, tflat, [kc_], ['tf_d'])
            E.CC(E.tf_d, E.tfg_d, ['tf_d'], ['tfg'])
            DMA('sp', tfr, E.tfg_d.rearrange("(r p) f -> p r f", r=2), ['tfg'], ['tfr'])
            TS('dve', tflat, tfr[:, 0, :], E.hsel[:, 0:1], None, ALU.mult, None, ['tfr', 'hsel'], [kc_])
            STT(tflat, tfr[:, 1, :], E.hsel[:, 1:2], tflat, ALU.mult, ALU.add, ['tfr', 'hsel', kc_], [kc_])
        tls = range(NTG) if di == 0 else range(NTG - 1, -1, -1)
        for tl in tls:
            cur = tile_proc(di, g, tl, own, final, cur, pb, pump)
        pump(10 ** 6)
    if bg['final'] is not None:
        for _ in bg['final']:
            pass


def _f(a):
    return np.ascontiguousarray(a, dtype=np.float32)


def _lhs_chunks(W):
    n = W.shape[1] // 128
    return _f(W.reshape(8, 128, n, 128).transpose(2, 1, 0, 3))


def _attn_masks():
    p = np.arange(128)[:, None]
    c = np.arange(128)[None, :]
    cntm = {}
    for j in range(-8, 9):
        d = 128 * j + p - c
        m = (np.abs(d) <= 64).astype(np.float32)
        m += ((d % 4 == 0) & (np.abs(d) <= 256)).astype(np.float32)
        m += ((d % 16 == 0) & (np.abs(d) <= 1024)).astype(np.float32)
        cntm[j] = m
    wm = np.zeros((20, 128, 512), np.float32)
    for J in range(20):
        for i in range(4):
            j = J - 8 - i
            if abs(j) <= 8:
                wm[J, :, i * 128:(i + 1) * 128] = cntm[j]
    return wm.astype(ml_dtypes.bfloat16)


def _rope_tables(rev):
    pos = np.arange(S, dtype=np.float32)
    if rev:
        pos = pos[::-1].copy()
    inv_freq = (np.float32(10000.0) ** (-np.arange(0, 64, 2, dtype=np.float32) / np.float32(64))).astype(np.float32)
    ang = (pos[:, None] * inv_freq[None, :]).astype(np.float32)
    cos, sin = np.cos(ang).astype(np.float32), np.sin(ang).astype(np.float32)
    idx = np.arange(128) % 32
    sign = np.where((np.arange(128) % 64) < 32, -1.0, 1.0).astype(np.float32)
    cosT = cos[:, idx].T
    sinT = (sin[:, idx] * sign[None, :]).T
    return _f(cosT), _f(sinT)


def host_inputs(inputs):
    g = lambda k: np.asarray(inputs[k][0], np.float32)
    x = np.asarray(inputs["x"], dtype=np.float32)
    shared = {}
    ffn_names = {1: ("ffn1_w_gate", "ffn1_w_up", "ffn1_w_down"), 2: ("ffn2_w_gate", "ffn2_w_up", "ffn2_w_down")}
    for k in (1, 2):
        wg, wu, wd = (g(nm) for nm in ffn_names[k])
        gg = wg.reshape(8, 128, NF, 128).transpose(2, 1, 0, 3)
        uu = wu.reshape(8, 128, NF, 128).transpose(2, 1, 0, 3)
        shared["wgu%d" % k] = _f(np.stack([gg, uu], axis=2))
        shared["wdc%d" % k] = _f(wd.reshape(NF, 128, 8, 128).transpose(2, 1, 0, 3))
    gnames = ["ffn1_pre_g", "ffn1_post_g", "mix_pre_g", "mix_post_g", "ffn2_pre_g", "ffn2_post_g"]
    shared["gv"] = _f(np.stack([g(nm).reshape(8, 128).T for nm in gnames], axis=1))
    shared["ident"] = np.eye(128, dtype=np.float32)
    w_in = g("w_in")
    swap = np.concatenate([(np.arange(64) + 32) % 64 + 64 * h for h in range(8)])
    shared["wv"] = _f(w_in[:, 1024:1536].reshape(8, 128, 512).transpose(1, 0, 2))
    shared["wo"] = _lhs_chunks(g("w_out"))
    shared["wm"] = _attn_masks()
    shared["wmf"] = np.ascontiguousarray(shared["wm"][:, ::-1, :])
    shared["rep"] = _f(np.stack([np.broadcast_to(g(nm)[None, :], (128, 512))
                                 for nm in ("attn_out_g", "rwkv_lnx_w", "rwkv_lnx_b")], axis=1))
    maps = []
    for c in range(8):
        b, h = c // 2, c % 2
        m = dict(shared)
        m["x"] = _f(x[b] if h == 0 else x[b, ::-1])
        cosT, sinT = _rope_tables(h == 1)
        m["cosT"], m["sinT"] = cosT, sinT
        m["hsel"] = _f(np.tile(np.array([[float(h), float(1 - h)]], np.float32), (128, 1)))
        m.update(_rwkv_host(inputs, h, w_in, swap))
        maps.append(m)
    return maps


def _rwkv_host(inputs, h, w_in, swap):
    g = lambda k: np.asarray(inputs[k][0], np.float32)
    dirs = [0, 1] if h == 0 else [1, 0]
    wq, wk = w_in[:, 0:512], w_in[:, 512:1024]
    cols = []
    for gi in range(4):
        cols.append(wq[:, gi * 128:(gi + 1) * 128])
        cols.append(wq[:, swap][:, gi * 128:(gi + 1) * 128])
    for gi in range(4):
        cols.append(wk[:, gi * 128:(gi + 1) * 128])
        cols.append(wk[:, swap][:, gi * 128:(gi + 1) * 128])
    wr = w_in[:, 1536:]
    rw = [wr[:, 0:1536]]
    for off in (1536, 1664):
        blk = wr[:, off:off + 128]
        rw.append(np.concatenate([blk[:, 64 * dirs[0]:64 * dirs[0] + 64], blk[:, 64 * dirs[1]:64 * dirs[1] + 64]], axis=1))
    rw.append(wr[:, 1792:1920])
    Wall = np.concatenate(cols + rw, axis=1)
    out = {"win": _lhs_chunks(Wall)}
    mp, mn = g("rwkv_mu_prev"), g("rwkv_mu_next")
    if h == 1:
        mp, mn = mn, mp

    def fix(v):
        v = v.copy()
        for off in (1536, 1664):
            blk = v[off:off + 128].copy()
            v[off:off + 128] = np.concatenate([blk[64 * dirs[0]:64 * dirs[0] + 64], blk[64 * dirs[1]:64 * dirs[1] + 64]])
        return v
    mp, mn = fix(mp), fix(mn)
    rwp = np.zeros((128, 64), np.float32)
    rwp[:, 0:15] = mp.reshape(15, 128).T
    rwp[:, 15:30] = mn.reshape(15, 128).T
    w0, a0 = g("rwkv_w0"), g("rwkv_a0")
    for di, d in enumerate(dirs):
        rwp[:, 30 + 4 * di:34 + 4 * di] = w0[d].reshape(4, 128).T
        rwp[:, 38 + 4 * di:42 + 4 * di] = a0[d].reshape(4, 128).T
    rwp[:, 46:50] = g("rwkv_k_k").reshape(4, 128).T
    rwp[:, 50:54] = g("rwkv_k_a").reshape(4, 128).T
    rwp[:, 54:58] = g("rwkv_r_k").reshape(4, 128).T
    out["rwpar"] = rwp
    w2, a2 = g("rwkv_w2"), g("rwkv_a2")
    lora = np.zeros((128, 3, 512), np.float32)
    lora[:, 0, :] = np.concatenate([w2[dirs[0]], w2[dirs[1]]], axis=0)
    lora[:, 1, :] = np.concatenate([a2[dirs[0]], a2[dirs[1]]], axis=0)
    lora[:, 2, :] = g("rwkv_g2")
    out["lora"] = lora
    idx = np.arange(128)
    same = (idx[:, None] // 64) == (idx[None, :] // 64)
    B0 = ((idx[None, :] < idx[:, None]) & same).astype(np.float32)
    I = np.eye(128, dtype=np.float32)
    rwm = np.zeros((128, 2, 5, 128), np.float32)
    for d, Bd in enumerate((B0, B0.T)):
        rwm[:, d, 0] = -Bd
        rwm[:, d, 1] = -Bd.T
        rwm[:, d, 2] = Bd.T
        rwm[:, d, 3] = Bd.T + I
        rwm[:, d, 4] = -(Bd.T + I)
    out["rwmask"] = rwm
    rwc = np.zeros((128, 1218), np.float32)
    rwc[:, 0:512] = (np.arange(512) % 64 != 0).astype(np.float32)[None, :]
    rwc[:, 1024:1152] = same.astype(np.float32)
    rwc[:, 1152:1216] = (idx[:, None] % 64 == np.arange(64)[None, :]).astype(np.float32)
    rwc[:, 1216] = (idx < 64)
    rwc[:, 1217] = (idx >= 64)
    out["rwconst"] = rwc
    return out


_CACHE = {}


def kernel(**inputs):
    if 'nc' not in _CACHE:
        _CACHE['nc'] = build()[0]
    nc = _CACHE['nc']
    maps = host_inputs(inputs)
    res = run_bass_kernel_spmd(nc, maps, core_ids=list(range(8)))
    out = np.zeros((4, S, D), np.float32)
    for c in range(8):
        b, h = c // 2, c % 2
        o = np.asarray(res.results[c]["out"])
        if h == 0:
            out[b, :OWN] = o
        else:
            out[b, OWN:] = o[::-1]
    return out
```

```python
import contextlib
import numpy as np
import ml_dtypes
import concourse.bass as bass
import concourse.mybir as mybir
from concourse.bass_utils import run_bass_kernel_spmd

F32 = mybir.dt.float32
BF16 = mybir.dt.bfloat16
AF = mybir.ActivationFunctionType
ALU = mybir.AluOpType
AX = mybir.AxisListType

D = 1024
DFF = 2816
NF = DFF // 128
S = 4096
OWN = 2048
TT = 512
EPS = 1e-6
GN_EPS = 64e-5
CDEC = float(np.exp(-0.5))
NKT = 16


class Prog:
    NDMA = 16

    def __init__(self):
        self.ops = []
        self.last_w = {}
        self.readers = {}
        self.last_barrier = 0
        self.warn = []
        self.pe_hist = []

    def add(self, eng, fn, r=(), w=(), dma=False, cc=False):
        i = len(self.ops)
        deps = set()
        for k in r:
            j = self.last_w.get(k)
            if j is not None:
                deps.add((j, 'raw'))
        for k in w:
            j = self.last_w.get(k)
            if j is not None:
                deps.add((j, 'waw'))
            for j in self.readers.get(k, ()):
                deps.add((j, 'war'))
        self.ops.append(dict(eng=eng, fn=fn, deps=deps, dma=dma, cc=cc))
        for k in w:
            if isinstance(k, str) and k.startswith('ps'):
                engs = set(self.ops[j]['eng'] for j in self.readers.get(k, ()))
                if len(engs) > 1:
                    self.warn.append(('multi-engine psum readers', k, sorted(engs), i))
        for k in r:
            self.readers.setdefault(k, []).append(i)
        for k in w:
            self.last_w[k] = i
            self.readers[k] = []
        return i

    def barrier(self):
        n = len(self.ops)
        deps = set()
        last = {}
        for i in range(n):
            op = self.ops[i]
            if op['dma']:
                if i >= self.last_barrier:
                    deps.add((i, 'raw'))
            elif op['fn'] is not None:
                last[op['eng']] = i
        for e, i in last.items():
            deps.add((i, 'raw'))
        for e in ['pe', 'act', 'dve', 'pool', 'sp']:
            self.ops.append(dict(eng=e, fn=None, deps=set(deps), dma=False, cc=False))
        self.last_barrier = n

    def emit(self, nc, stack):
        ops = self.ops
        n = len(ops)
        need = [set() for _ in range(n)]
        signaled = [False] * n
        for i, op in enumerate(ops):
            for (j, kind) in op['deps']:
                if j == i:
                    continue
                pj = ops[j]
                same = (pj['eng'] == op['eng'])
                if same and op['eng'] == 'pe' and not pj['dma'] and not op['dma'] and op['fn'] is not None:
                    if kind != 'raw':
                        continue
                need[i].add(j)
            latest = {}
            for j in need[i]:
                pj = ops[j]
                if pj['dma']:
                    continue
                e2 = pj['eng']
                if e2 not in latest or j > latest[e2]:
                    latest[e2] = j
            need[i] = set(j for j in need[i] if ops[j]['dma'] or latest[ops[j]['eng']] == j)
            for j in need[i]:
                signaled[j] = True
        engs = ['pe', 'act', 'dve', 'pool', 'sp']
        csem = {e: stack.enter_context(nc.semaphore("s_" + e)) for e in engs[:4]}
        dsem = {e: [stack.enter_context(nc.semaphore("d_%s%d" % (e, k))) for k in range(self.NDMA)]
                for e in ['sp', 'pool']}
        sig = [None] * n
        ccount = {e: 0 for e in engs}
        dcount = {e: 0 for e in engs}
        prevuse = [None] * n
        for i, op in enumerate(ops):
            e = op['eng']
            if op.get('cc'):
                sig[i] = (stack.enter_context(nc.semaphore("cc_%d" % i)), 1)
            elif op['dma']:
                k = dcount[e]
                dcount[e] += 1
                s = dsem[e][k % self.NDMA]
                sig[i] = (s, 16 * (k // self.NDMA + 1))
                if k >= self.NDMA:
                    prevuse[i] = (s, 16 * (k // self.NDMA))
            elif signaled[i]:
                ccount[e] += 1
                sig[i] = (csem[e], ccount[e])
        per = {e: [i for i in range(n) if ops[i]['eng'] == e] for e in engs}
        self.stats = {e: len(per[e]) for e in engs}
        self.stats['sem'] = dict(ccount)

        def run(e, eng):
            waited = {}
            for i in per[e]:
                op = ops[i]
                ws = []
                if prevuse[i] is not None:
                    ws.append(prevuse[i])
                for j in need[i]:
                    ws.append(sig[j])
                best = {}
                for (s, v) in ws:
                    key = id(s)
                    if v > best.get(key, (None, -1))[1]:
                        best[key] = (s, v)
                for key, (s, v) in best.items():
                    if waited.get(key, -1) >= v:
                        continue
                    eng.wait_ge(s, v)
                    waited[key] = v
                if op['fn'] is None:
                    continue
                ins = op['fn'](eng)
                if sig[i] is not None:
                    if op.get('cc'):
                        ins.then_inc(sig[i][0])
                    else:
                        ins.then_inc(sig[i][0], 16 if op['dma'] else 1)

        with nc.Block() as block:
            @block.tensor
            def _(eng):
                run('pe', eng)

            @block.scalar
            def _(eng):
                run('act', eng)

            @block.vector
            def _(eng):
                run('dve', eng)

            @block.gpsimd
            def _(eng):
                run('pool', eng)

            @block.sync
            def _(eng):
                run('sp', eng)


class Arena:
    def __init__(self, tensor, nbytes):
        self.t = tensor
        self.n = nbytes
        self.off = 0

    def alloc(self, shape, dt=F32):
        esz = mybir.dt.size(dt)
        nel = int(np.prod(shape[1:]))
        nb = (nel * esz + 31) // 32 * 32
        assert self.off + nb <= self.n, ("arena overflow", self.off, nb, self.n)
        ap = self.t[:, self.off // 4:(self.off + nb) // 4]
        self.off += nb
        if dt != F32:
            ap = ap.bitcast(dt)
        ap = ap[:, 0:nel]
        if len(shape) == 3:
            ap = ap.rearrange("p (a b) -> p a b", a=shape[1])
        elif len(shape) == 4:
            ap = ap.rearrange("p (a b c) -> p a b c", a=shape[1], b=shape[2])
        return ap


class Env:
    pass


def build(debug=False, phases=('p1', 'att', 'rwkv', 'p4'), ntiles1=8, rwkv_stop=None):
    nc = bass.Bass("TRN2", target_bir_lowering=False)
    P = Prog()
    stack = contextlib.ExitStack()
    E = Env()

    def din(name, shape, dt=F32):
        return nc.dram_tensor(name, list(shape), dt, kind="ExternalInput").ap()

    def dscr(name, shape, dt=F32, out=False):
        if out:
            return nc.dram_tensor(name, list(shape), dt, kind="ExternalOutput").ap()
        return nc.dram_tensor(name, list(shape), dt).ap()

    x_d = din("x", [S, D])
    wgu_d = [din("wgu%d" % k, [NF, 128, 2, 8, 128]) for k in (1, 2)]
    wdc_d = [din("wdc%d" % k, [8, 128, NF, 128]) for k in (1, 2)]
    win_d = din("win", [31, 128, 8, 128])
    wv_d = din("wv", [128, 8, 512])
    wo_d = din("wo", [8, 128, 8, 128])
    gv_d = din("gv", [128, 6, 8])
    ident_d = din("ident", [128, 128])
    cos_d = din("cosT", [128, S])
    sin_d = din("sinT", [128, S])
    wm_d = din("wm", [20, 128, 512], BF16)
    rep_d = din("rep", [128, 3, 512])
    rwm_d = din("rwmask", [128, 2, 5, 128])
    rwc_d = din("rwconst", [128, 1218])
    rwp_d = din("rwpar", [128, 64])
    lora_d = din("lora", [128, 3, 512])
    out_d = nc.dram_tensor("out", [OWN, D], F32, kind="ExternalOutput").ap()
    h1s_d = dscr("h1s", [8, 128, OWN], out=debug)
    zr_d = dscr("zr", [15, 128, S], out=debug)
    qT_d = dscr("qTs", [4, 128, OWN], BF16, out=debug)
    kT_d = dscr("kTs", [4, 128, NKT * 128], BF16, out=debug)
    v_d = dscr("vs", [NKT * 128, 520], BF16, out=debug)
    yc_d = dscr("yc", [OWN, D], out=debug)
    yf_d = dscr("yf", [OWN, 512], out=debug)
    hsel_d = din("hsel", [128, 2])
    wmf_d = din("wmf", [20, 128, 512], BF16)
    kh_d = dscr("kh", [512, 1024], BF16)
    vh_d = dscr("vh", [1024, 520], BF16)
    zb_d = dscr("zb", [128, 15])
    khg_d = dscr("khg", [1024, 1024], BF16)
    vhg_d = dscr("vhg", [2048, 520], BF16)
    zbg_d = dscr("zbg", [256, 15])
    tf_d = dscr("tf", [128, 256])
    tfg_d = dscr("tfg", [256, 256])
    wgu16_d = [dscr("wgu16_%d" % k, [NF, 128, 2 * 8 * 128], BF16) for k in (1, 2)]
    wdc16_d = [dscr("wdc16_%d" % k, [8, 128, NF * 128], BF16) for k in (1, 2)]
    win16_d = dscr("win16", [31, 128, 8 * 128], BF16)
    wo16_d = dscr("wo16", [8, 128, 8 * 128], BF16)
    PAIRS = [[0, 1], [2, 3], [4, 5], [6, 7]]

    def CC(in_ap, out_ap, r, w):
        return P.add('pool', lambda e: e.collective_compute("AllGather", ALU.bypass, replica_groups=PAIRS,
                                                            ins=[in_ap.opt()], outs=[out_ap.opt()]),
                     r=r, w=w, dma=True, cc=True)

    ARENA_BYTES = 207 * 1024
    arena_t = stack.enter_context(nc.sbuf_tensor("arena", [128, ARENA_BYTES // 4], F32))
    A = Arena(arena_t, ARENA_BYTES)
    ps = [stack.enter_context(nc.psum_tensor("ps%d" % i, [128, 512], F32)) for i in range(8)]

    def MM(out, lhsT, rhs, start, stop, r, w):
        rb0, kk_ = lhsT.base_partition(), lhsT.shape[0]
        for (pb0, pk, pw) in P.pe_hist[-1:]:
            if (rb0 + kk_ <= pb0 or pb0 + pk <= rb0) and pw == w[0]:
                P.warn.append(('row-group conflict', w[0], (pb0, pk), (rb0, kk_), len(P.ops)))
        P.pe_hist.append((rb0, kk_, w[0]))
        P.add('pe', lambda e: e.matmul(out, lhsT, rhs, start=start, stop=stop), r=r, w=w)

    def TR(out, in_, idt, r, w):
        P.pe_hist.append((in_.base_partition(), in_.shape[0], w[0]))
        P.add('pe', lambda e: e.transpose(out, in_, idt), r=r, w=w)

    def ACT(out, in_, func, r, w, bias=None, scale=None):
        kw = {}
        if bias is not None:
            kw['bias'] = bias
        if scale is not None:
            kw['scale'] = scale
        P.add('act', lambda e: e.activation(out, in_, func, **kw), r=r, w=w)

    def TTo(eng, out, in0, in1, op, r, w):
        P.add(eng, lambda e: e.tensor_tensor(out=out, in0=in0, in1=in1, op=op), r=r, w=w)

    def TS(eng, out, in0, s1, s2, op0, op1, r, w):
        if op1 is None:
            P.add(eng, lambda e: e.tensor_scalar(out=out, in0=in0, scalar1=s1, scalar2=None, op0=op0), r=r, w=w)
        else:
            P.add(eng, lambda e: e.tensor_scalar(out=out, in0=in0, scalar1=s1, scalar2=s2, op0=op0, op1=op1),
                  r=r, w=w)

    def STT(out, in0, scalar, in1, op0, op1, r, w):
        P.add('dve', lambda e: e.scalar_tensor_tensor(out=out, in0=in0, scalar=scalar, in1=in1, op0=op0, op1=op1),
              r=r, w=w)

    def CP(eng, out, in_, r, w):
        if eng == 'act':
            P.add('act', lambda e: e.copy(out, in_), r=r, w=w)
        else:
            P.add(eng, lambda e: e.tensor_copy(out, in_), r=r, w=w)

    def RECIP(out, in_, r, w):
        P.add('dve', lambda e: e.reciprocal(out, in_), r=r, w=w)

    def DMA(q, out, in_, r, w):
        return P.add(q, lambda e: e.dma_start(out=out, in_=in_), r=r, w=w, dma=True)

    def MEMSET(eng, ap, val, w):
        P.add(eng, lambda e: e.memset(ap, val), w=w)

    ident = A.alloc([128, 128])
    identb = A.alloc([128, 128], BF16)
    ones = A.alloc([128, 128], BF16)
    gv = A.alloc([128, 6, 8])
    gvh = A.alloc([128, 6, 8])
    eps_t = A.alloc([128, 1])
    gneps_t = A.alloc([128, 1])
    rep = A.alloc([128, 3, 512])
    rwm = A.alloc([128, 2, 5, 128])
    rwc = A.alloc([128, 1218])
    rwp = A.alloc([128, 64])
    lora = A.alloc([128, 3, 512])
    hsel = A.alloc([128, 2])
    DMA('sp', ident, ident_d, [], ['ident'])
    DMA('sp', gv, gv_d, [], ['gv'])
    DMA('sp', rep, rep_d, [], ['rep'])
    DMA('sp', rwm, rwm_d, [], ['rwm'])
    DMA('sp', rwc, rwc_d, [], ['rwc'])
    DMA('sp', rwp, rwp_d, [], ['rwp'])
    DMA('sp', lora, lora_d, [], ['lora'])
    DMA('sp', hsel, hsel_d, [], ['hsel'])
    MEMSET('pool', ones, 1.0, ['ones'])
    MEMSET('pool', eps_t, EPS, ['eps'])
    MEMSET('pool', gneps_t, GN_EPS, ['eps'])
    CP('dve', identb, ident, ['ident'], ['identb'])
    TS('dve', gvh, gv, 0.5, None, ALU.mult, None, ['gv'], ['gvh'])
    A_MARK = A.off

    cnt = {'ps01': 0, 'x': 0, 'z': 0, 'st': 0}
    for k_, v_ in list(locals().items()):
        setattr(E, k_, v_)

    def alloc_ffn_bufs():
        B = {}
        B['xin'] = [A.alloc([128, D]) for _ in range(4)]
        B['xT'] = A.alloc([128, 8, TT])
        B['hn'] = A.alloc([128, 8, TT], BF16)
        B['sq'] = A.alloc([128, 8, TT], BF16)
        B['fb'] = A.alloc([128, 8, TT])
        B['aT'] = A.alloc([128, NF, TT], BF16)
        B['rstd'] = A.alloc([128, TT])
        B['tmp'] = A.alloc([128, TT])
        B['wgu'] = [A.alloc([128, 2, 8, 128], BF16) for _ in range(3)]
        B['wdc'] = [A.alloc([128, NF, 128], BF16) for _ in range(3)]
        B['sgl'] = [A.alloc([128, TT]) for _ in range(2)]
        return B

    def issue_x_loads(B, src_d, tt):
        xin = B['xin']
        for sub in range(4):
            t0 = tt * TT + sub * 128
            DMA('sp', xin[sub], src_d[t0:t0 + 128, :], ['yc'] if src_d is yc_d else [], ['xin%d' % sub])

    def load_T(B, src_d, tt, dst, dstkey, loads_issued=False):
        xin = B['xin']
        if not loads_issued:
            issue_x_loads(B, src_d, tt)
        for sub in range(4):
            b = sub
            for half in range(2):
                pb = cnt['ps01'] % 2
                cnt['ps01'] += 1
                for q in range(4):
                    c = half * 4 + q
                    TR(ps[pb][:, q * 128:(q + 1) * 128], xin[b][:, c * 128:(c + 1) * 128], ident,
                       ['xin%d' % b, 'ident'], ['ps%d' % pb])
                src = ps[pb][:].rearrange("p (q t) -> p q t", q=4)
                d_ = dst[:, half * 4:(half + 1) * 4, sub * 128:(sub + 1) * 128]
                CP('act' if half == 0 else 'dve', d_, src, ['ps%d' % pb],
                   [(dstkey, half * 4 + q) for q in range(4)])

    def rmsnorm_stats(B, src, srckey):
        sq, tmp, rstd = B['sq'], B['tmp'], B['rstd']
        pb = cnt['ps01'] % 2
        cnt['ps01'] += 1
        for c in range(8):
            ACT(sq[:, c, :], src[:, c, :], AF.Square, [(srckey, c)], [('sq', c)])
            MM(ps[pb][:], ones, sq[:, c, :], c == 0, c == 7, [('sq', c), 'ones'], ['ps%d' % pb])
        ACT(tmp, ps[pb][:], AF.Sqrt, ['ps%d' % pb, 'eps'], ['tmp'], bias=eps_t, scale=1.0 / D)
        RECIP(rstd, tmp, ['tmp'], ['rstd'])

    def prenorm(B, gidx):
        xT, hn, rstd = B['xT'], B['hn'], B['rstd']
        rmsnorm_stats(B, xT, 'xT')
        for c in range(8):
            STT(hn[:, c, :], xT[:, c, :], gv[:, gidx, c:c + 1], rstd, ALU.mult, ALU.mult,
                [('xT', c), 'gv', 'rstd'], [('hn', c)])

    def ffn(B, k, gpre, gpost, first):
        xT, hn, fb, aT, rstd = B['xT'], B['hn'], B['fb'], B['aT'], B['rstd']
        wgu, wdc, sgl = B['wgu'], B['wdc'], B['sgl']
        prenorm(B, gpre)

        def ld_gu(j):
            b = j % 3
            flat = wgu[b].rearrange("p a c f -> p (a c f)")
            if first:
                DMA('pool', wgu[b], wgu_d[k][j], [], ['wgu%d' % b])
                DMA('sp', wgu16_d[k][j], flat, ['wgu%d' % b], [('wgu16', k, j)])
            else:
                DMA('sp', flat, wgu16_d[k][j], [('wgu16', k, j)], ['wgu%d' % b])

        def ld_d(c):
            b = c % 3
            flat = wdc[b].rearrange("p j f -> p (j f)")
            if first:
                DMA('pool', wdc[b], wdc_d[k][c], [], ['wdc%d' % b])
                DMA('sp', wdc16_d[k][c], flat, ['wdc%d' % b], [('wdc16', k, c)])
            else:
                DMA('sp', flat, wdc16_d[k][c], [('wdc16', k, c)], ['wdc%d' % b])
        ld_gu(0)
        ld_gu(1)
        for j in range(NF):
            if j + 2 < NF:
                ld_gu(j + 2)
            if j == NF - 3:
                ld_d(0)
            if j == NF - 1:
                ld_d(1)
            b = j % 3
            pg = 2 + (j % 2) * 2
            pu = pg + 1
            for which, pbank in ((0, pg), (1, pu)):
                for c in range(8):
                    MM(ps[pbank][:], wgu[b][:, which, c, :], hn[:, c, :], c == 0, c == 7,
                       ['wgu%d' % b, ('hn', c)], ['ps%d' % pbank])
            sgb = j % 2
            ACT(sgl[sgb], ps[pg][:], AF.Silu, ['ps%d' % pg], ['sgl%d' % sgb])
            TTo('dve', aT[:, j, :], sgl[sgb], ps[pu][:], ALU.mult, ['sgl%d' % sgb, 'ps%d' % pu], [('aT', j)])
        for c in range(8):
            if c + 2 < 8:
                ld_d(c + 2)
            b = c % 3
            pf = 6 + (c % 2)
            for j in range(NF):
                MM(ps[pf][:], wdc[b][:, j, :], aT[:, j, :], j == 0, j == NF - 1,
                   ['wdc%d' % b, ('aT', j)], ['ps%d' % pf])
            CP('act' if c % 2 == 0 else 'dve', fb[:, c, :], ps[pf][:], ['ps%d' % pf], [('fb', c)])
        rmsnorm_stats(B, fb, 'fb')
        for c in range(8):
            ACT(fb[:, c, :], fb[:, c, :], AF.Copy, [('fb', c), 'gvh'], [('fb', c)], scale=gvh[:, gpost, c:c + 1])
            TTo('pool', fb[:, c, :], fb[:, c, :], rstd, ALU.mult, [('fb', c), 'rstd'], [('fb', c)])
            TTo('dve', xT[:, c, :], fb[:, c, :], xT[:, c, :], ALU.add, [('fb', c), ('xT', c)], [('xT', c)])

    if 'p1' in phases:
        B = alloc_ffn_bufs()
        winb = [A.alloc([128, 8, 128], BF16) for _ in range(4)]
        wvb = A.alloc([128, 8, 512], BF16)
        cosb = A.alloc([128, TT])
        sinb = A.alloc([128, TT])
        ra = A.alloc([128, TT])
        rb = A.alloc([128, TT])
        qst = [A.alloc([128, TT], BF16) for _ in range(2)]
        vst = [A.alloc([128, 8, 65], BF16) for _ in range(2)]
        zst = [A.alloc([128, TT]) for _ in range(2)]
        xT, hn = B['xT'], B['hn']
        DMA('pool', wvb, wv_d, [], ['wvb'])
        for b in range(2):
            MEMSET('pool', vst[b], 1.0, ['vst%d' % b])
        zbank = [2, 3, 4, 5]
        for tt in range(min(ntiles1, 4)):
            load_T(B, x_d, tt, xT, 'xT', loads_issued=(tt > 0))
            ffn(B, 0, 0, 1, tt == 0)
            if tt < 4:
                DMA('sp', h1s_d[:, :, tt * TT:(tt + 1) * TT].rearrange("c p t -> p c t"), xT, [('xT', c_) for c_ in range(8)], ['h1s'])
            prenorm(B, 2)
            if tt + 1 < min(ntiles1, 4):
                issue_x_loads(B, x_d, tt + 1)
            if tt < 6:
                DMA('sp', cosb, cos_d[:, tt * TT:(tt + 1) * TT], [], ['cosb'])
                DMA('sp', sinb, sin_d[:, tt * TT:(tt + 1) * TT], [], ['sinb'])
            jobs = []
            if tt < 4:
                jobs += [('q', g, [2 * g, 2 * g + 1]) for g in range(4)]
            if tt < 6:
                jobs += [('k', g, [8 + 2 * g, 9 + 2 * g]) for g in range(4)]
            jobs += [('z', j, [16 + j]) for j in range(15)]
            loads = [ci for (_, _, cis) in jobs for ci in cis]

            def ldw(n):
                if n < len(loads):
                    b = n % 4
                    flat = winb[b].rearrange("p c f -> p (c f)")
                    if tt == 0:
                        DMA('pool', winb[b], win_d[loads[n]], [], ['winb%d' % b])
                        DMA('sp', win16_d[loads[n]], flat, ['winb%d' % b], [('win16', loads[n])])
                    else:
                        DMA('sp', flat, win16_d[loads[n]], [('win16', loads[n])], ['winb%d' % b])
            for n in range(3):
                ldw(n)
            li = 0
            for (kind, g, cis) in jobs:
                banks = []
                for ci in cis:
                    ldw(li + 3)
                    b = li % 4
                    li += 1
                    zb = zbank[cnt['z'] % 4]
                    cnt['z'] += 1
                    banks.append(zb)
                    for c in range(8):
                        MM(ps[zb][:], winb[b][:, c, :], hn[:, c, :], c == 0, c == 7,
                           ['winb%d' % b, ('hn', c)], ['ps%d' % zb])
                sb_ = cnt['st'] % 2
                cnt['st'] += 1
                if kind in ('q', 'k'):
                    TTo('dve', ra, ps[banks[0]][:], cosb, ALU.mult, ['ps%d' % banks[0], 'cosb'], ['ra'])
                    TTo('dve', rb, ps[banks[1]][:], sinb, ALU.mult, ['ps%d' % banks[1], 'sinb'], ['rb'])
                    TTo('pool', qst[sb_], ra, rb, ALU.add, ['ra', 'rb'], ['qst%d' % sb_])
                    dst = qT_d if kind == 'q' else kT_d
                    DMA('sp', dst[g, :, tt * TT:(tt + 1) * TT], qst[sb_], ['qst%d' % sb_], ['qk_d'])
                    if kind == 'k' and tt >= 2:
                        DMA('sp', kh_d[g * 128:(g + 1) * 128, (tt - 2) * TT:(tt - 1) * TT], qst[sb_],
                            ['qst%d' % sb_], ['kh_d'])
                else:
                    CP('act' if g % 2 == 0 else 'dve', zst[sb_], ps[banks[0]][:], ['ps%d' % banks[0]],
                       ['zst%d' % sb_])
                    DMA('sp', zr_d[g, :, tt * TT:(tt + 1) * TT], zst[sb_], ['zst%d' % sb_], ['zr'])
                    if tt == 3:
                        P.add('sp', (lambda e, g=g, sb_=sb_: e.dma_start(
                            out=zb_d[:, g:g + 1], in_=zst[sb_][:, TT - 1:TT], allow_slow_non_contiguous=True)),
                            r=['zst%d' % sb_], w=['zb_d'], dma=True)
            if tt < 6:
                for sub in range(4):
                    zb = zbank[cnt['z'] % 4]
                    cnt['z'] += 1
                    for c in range(8):
                        MM(ps[zb][:], hn[:, c, sub * 128:(sub + 1) * 128], wvb[:, c, :], c == 0, c == 7,
                           [('hn', c), 'wvb'], ['ps%d' % zb])
                    sb_ = cnt['st'] % 2
                    cnt['st'] += 1
                    CP('act' if sub % 2 == 0 else 'dve', vst[sb_][:, :, 0:64],
                       ps[zb][:].rearrange("p (h d) -> p h d", h=8), ['ps%d' % zb], ['vst%d' % sb_])
                    r0 = (tt * 4 + sub) * 128
                    DMA('sp', v_d[r0:r0 + 128, :], vst[sb_].rearrange("p h e -> p (h e)"),
                        ['vst%d' % sb_], ['v_d'])
                    if tt >= 2:
                        DMA('sp', vh_d[r0 - 1024:r0 - 1024 + 128, :], vst[sb_].rearrange("p h e -> p (h e)"),
                            ['vst%d' % sb_], ['vh_d'])
        P.barrier()
        A.off = A_MARK
        CC(kh_d, khg_d, ['kh_d'], ['khg'])
        CC(vh_d, vhg_d, ['vh_d'], ['vhg'])
        CC(zb_d, zbg_d, ['zb_d'], ['zbg'])

    if 'att' in phases:
        qT = A.alloc([128, 4, OWN], BF16)
        kT = A.alloc([128, 4, NKT * 128], BF16)
        va = A.alloc([128, NKT, 520], BF16)
        wm = A.alloc([128, 20, 512], BF16)
        wmf = A.alloc([128, 20, 512], BF16)
        khr = A.alloc([128, 2, 4, 1024], BF16)
        vhr = A.alloc([128, 2, 8, 520], BF16)
        kh = A.alloc([128, 4, 1024], BF16)
        vh = A.alloc([128, 8, 520], BF16)
        oall = A.alloc([128, 16, 512])
        eb = [A.alloc([128, 512], BF16) for _ in range(4)]
        pm = [A.alloc([128, 512], BF16) for _ in range(4)]
        rden = [A.alloc([128, 4]) for _ in range(2)]
        oTs = [A.alloc([65, 512]) for _ in range(2)]
        sq32 = A.alloc([128, 512])
        ss = A.alloc([128, 8])
        rs = A.alloc([128, 8])
        DMA('sp', qT, qT_d.rearrange("g p t -> p g t"), ['qk_d'], ['qT'])
        DMA('sp', kT, kT_d.rearrange("g p t -> p g t"), ['qk_d'], ['kT'])
        DMA('sp', va, v_d.rearrange("(n p) f -> p n f", p=128), ['v_d'], ['va'])
        DMA('sp', wm, wm_d.rearrange("j p c -> p j c"), [], ['wm'])
        DMA('sp', wmf, wmf_d.rearrange("j p c -> p j c"), [], ['wmf'])
        def load_halo():
            DMA('sp', khr, khg_d.rearrange("(r g p) t -> p r g t", r=2, g=4), ['khg'], ['khr'])
            DMA('sp', vhr, vhg_d.rearrange("(r n p) f -> p r n f", r=2, n=8), ['vhg'], ['vhr'])
            for (raw, dst, key, n_) in ((khr, kh, 'kh', 4096), (vhr, vh, 'vh', 4160)):
                r0_ = raw[:, 0].rearrange("p a b -> p (a b)")
                r1_ = raw[:, 1].rearrange("p a b -> p (a b)")
                d_ = dst.rearrange("p a b -> p (a b)")
                TS('dve', d_, r0_, hsel[:, 0:1], None, ALU.mult, None, [key + 'r', 'hsel'], [key])
                STT(d_, r1_, hsel[:, 1:2], d_, ALU.mult, ALU.add, [key + 'r', 'hsel', key], [key])

        items = []
        for qb in (0, 1, 2, 3):
            for cpair in range(4):
                kts = list(range(max(0, 4 * qb - 8), 4 * qb + 12))
                for kt in kts:
                    for hh in (2 * cpair, 2 * cpair + 1):
                        items.append(dict(hh=hh, qb=qb, kt=kt, J=kt - 4 * qb + 8, ob=6 + hh % 2,
                                          first=(kt == kts[0]), last=(kt == kts[-1]), endblk=(kt == kts[-1]),
                                          blk=hh % 2))
        NB = 4
        SKEW = 2

        def stage_a(n):
            d = items[n]
            hh, qb, kt, J = d['hh'], d['qb'], d['kt'], d['J']
            g, base = hh // 2, 64 * (hh % 2)
            sbk = 2 + (n % NB)
            b = n % NB
            if kt < NKT:
                kop, kkey, mk, mkey_ = kT[base:base + 64, g, kt * 128:(kt + 1) * 128], 'kT', wm, 'wm'
            else:
                ht = 23 - kt
                kop, kkey, mk, mkey_ = kh[base:base + 64, g, ht * 128:(ht + 1) * 128], 'kh', wmf, 'wmf'
            MM(ps[sbk][:], kop, qT[base:base + 64, g, qb * 512:(qb + 1) * 512], True, True, [kkey, 'qT'],
               ['ps%d' % sbk])
            ACT(eb[b], ps[sbk][:], AF.Exp, ['ps%d' % sbk], ['eb%d' % b], scale=0.125)
            TTo('dve', pm[b], eb[b], mk[:, J, :], ALU.mult, ['eb%d' % b, mkey_], ['pm%d' % b])

        def stage_b(n):
            d = items[n]
            hh, qb, kt, J, ob = d['hh'], d['qb'], d['kt'], d['J'], d['ob']
            b = n % NB
            vop, vkey = (va[:, kt, hh * 65:(hh + 1) * 65], 'va') if kt < NKT else \
                (vh[:, 23 - kt, hh * 65:(hh + 1) * 65], 'vh')
            MM(ps[ob][0:65, :], vop, pm[b], d['first'], d['last'], ['pm%d' % b, vkey], ['ps%d' % ob])
            if d['endblk']:
                rb_ = d['blk']
                tbk = rb_
                CP('act', oTs[rb_][0:65, :], ps[ob][0:65, :], ['ps%d' % ob], ['oT%d' % rb_])
                for i in range(4):
                    TR(ps[tbk][:, i * 65:(i + 1) * 65], oTs[rb_][0:65, i * 128:(i + 1) * 128], ident[0:65, 0:65],
                       ['oT%d' % rb_, 'ident'], ['ps%d' % tbk])
                o4 = ps[tbk][:, 0:260].rearrange("p (i e) -> p i e", e=65)
                RECIP(rden[rb_].unsqueeze(2), o4[:, :, 64:65], ['ps%d' % tbk], ['rden%d' % rb_])
                TTo('dve', oall[:, qb * 4:(qb + 1) * 4, hh * 64:(hh + 1) * 64], o4[:, :, 0:64],
                    rden[rb_].unsqueeze(2).broadcast_to([128, 4, 64]), ALU.mult,
                    ['ps%d' % tbk, 'rden%d' % rb_], [('oall', qb)])
        first_halo = min(n for n, d in enumerate(items) if d['kt'] >= NKT)
        for n in range(0, len(items) + SKEW, 2):
            if n <= first_halo < n + 2:
                load_halo()
            for m in (n, n + 1):
                if m < len(items):
                    stage_a(m)
            for m in (n - SKEW, n + 1 - SKEW):
                if 0 <= m < len(items):
                    stage_b(m)
        for qt in range(16):
            o = oall[:, qt, :]
            o3 = o.rearrange("p (h d) -> p h d", h=8)
            ACT(sq32, o, AF.Square, [('oall', qt // 4)], ['sq32'])
            P.add('dve', lambda e: e.tensor_reduce(out=ss, in_=sq32.rearrange("p (h d) -> p h d", h=8),
                                                   axis=AX.X, op=ALU.add), r=['sq32'], w=['ss'])
            ACT(rs, ss, AF.Sqrt, ['ss', 'eps'], ['rs'], bias=eps_t, scale=1.0 / 64)
            RECIP(rs, rs, ['rs'], ['rs'])
            TTo('dve', o3, o3, rs.unsqueeze(2).broadcast_to([128, 8, 64]), ALU.mult,
                [('oall', qt // 4), 'rs'], [('oall', qt // 4)])
            TTo('dve', o, o, rep[:, 0, :], ALU.mult, [('oall', qt // 4), 'rep'], [('oall', qt // 4)])
            DMA('sp', yc_d[qt * 128:(qt + 1) * 128, 0:512], o, [('oall', qt // 4)], ['yc'])
        P.barrier()
        A.off = A_MARK

    if 'rwkv' in phases:
        rwkv_phase(E, rwkv_stop)
        P.barrier()
        A.off = A_MARK

    if 'p4' in phases:
        B = alloc_ffn_bufs()
        wob = [A.alloc([128, 8, 128], BF16) for _ in range(2)]
        ost = [A.alloc([128, D]) for _ in range(2)]
        xT, hn, fb, rstd = B['xT'], B['hn'], B['fb'], B['rstd']
        for tt in range(4):
            load_T(B, yc_d, tt, hn, 'hn')
            def ld_o(dc):
                b_ = dc % 2
                flat = wob[b_].rearrange("p c f -> p (c f)")
                if tt == 0:
                    DMA('pool', wob[b_], wo_d[dc], [], ['wob%d' % b_])
                    DMA('sp', wo16_d[dc], flat, ['wob%d' % b_], [('wo16', dc)])
                else:
                    DMA('sp', flat, wo16_d[dc], [('wo16', dc)], ['wob%d' % b_])
            ld_o(0)
            for dc in range(8):
                if dc + 1 < 8:
                    ld_o(dc + 1)
                b = dc % 2
                pf = 6 + (dc % 2)
                for cc in range(8):
                    MM(ps[pf][:], wob[b][:, cc, :], hn[:, cc, :], cc == 0, cc == 7,
                       ['wob%d' % b, ('hn', cc)], ['ps%d' % pf])
                CP('act' if dc % 2 == 0 else 'dve', fb[:, dc, :], ps[pf][:], ['ps%d' % pf], [('fb', dc)])
            rmsnorm_stats(B, fb, 'fb')
            DMA('sp', xT, h1s_d[:, :, tt * TT:(tt + 1) * TT].rearrange("c p t -> p c t"), ['h1s'], [('xT', c_) for c_ in range(8)])
            for c in range(8):
                ACT(fb[:, c, :], fb[:, c, :], AF.Copy, [('fb', c), 'gv'], [('fb', c)], scale=gv[:, 3, c:c + 1])
                TTo('pool', fb[:, c, :], fb[:, c, :], rstd, ALU.mult, [('fb', c), 'rstd'], [('fb', c)])
                TTo('dve', xT[:, c, :], fb[:, c, :], xT[:, c, :], ALU.add, [('fb', c), ('xT', c)], [('xT', c)])
            ffn(B, 1, 4, 5, tt == 0)
            for sub in range(4):
                ob = cnt['st'] % 2
                cnt['st'] += 1
                for half in range(2):
                    pb = cnt['ps01'] % 2
                    cnt['ps01'] += 1
                    for q in range(4):
                        c = half * 4 + q
                        TR(ps[pb][:, q * 128:(q + 1) * 128], xT[:, c, sub * 128:(sub + 1) * 128], ident,
                           [('xT', c), 'ident'], ['ps%d' % pb])
                    CP('act' if half == 0 else 'dve', ost[ob][:, half * 512:(half + 1) * 512], ps[pb][:],
                       ['ps%d' % pb], ['ost%d' % ob])
                r0 = tt * TT + sub * 128
                DMA('sp', out_d[r0:r0 + 128, :], ost[ob], ['ost%d' % ob], ['out'])

    P.add('sp', None, r=['h1s', 'out', 'yc', 'zr', 'qk_d', 'v_d', 'yf'])
    P.emit(nc, stack)
    stack.close()
    return nc, P


def rwkv_phase(E, debug_stop=None):
    P, A, ps = E.P, E.A, E.ps
    MM, TR, ACT, TTo, TS, STT, CP, RECIP, DMA, MEMSET = (E.MM, E.TR, E.ACT, E.TTo, E.TS, E.STT, E.CP, E.RECIP,
                                                          E.DMA, E.MEMSET)
    ident, identb, rep, rwm, rwc, rwp, lora = E.ident, E.identb, E.rep, E.rwm, E.rwc, E.rwp, E.lora
    eps_t, gneps_t = E.eps_t, E.gneps_t
    zr_d, yc_d, yf_d = E.zr_d, E.yc_d, E.yf_d

    rmask = rwc[:, 0:512]
    blockones = rwc[:, 1024:1152]
    identblk = rwc[:, 1152:1216]
    hsel = rwc[:, 1216:1218]
    mp, mn = rwp[:, 0:15], rwp[:, 15:30]
    k_k, k_a, r_k = rwp[:, 46:50], rwp[:, 50:54], rwp[:, 54:58]
    w2sb, a2sb, g2sb = lora[:, 0, :], lora[:, 1, :], lora[:, 2, :]

    def w0T(di, c):
        return rwp[:, 30 + 4 * di + c:31 + 4 * di + c]

    def a0T(di, c):
        return rwp[:, 38 + 4 * di + c:39 + 4 * di + c]

    c0 = A.alloc([128, 15])
    omka = A.alloc([128, 4])
    rk2 = A.alloc([128, 4])
    GW = 256
    NTG = GW // 128
    NG = 4096 // GW
    NG_OWN = 2048 // GW
    zsb = [A.alloc([128, GW + 2]) for _ in range(3)]
    zcnt = {'z': 0}
    ub = A.alloc([128, 15, GW])
    tw = A.alloc([128, GW])
    T_ = {nm: A.alloc([128, GW]) for nm in
          ('sg', 'av', 'kx', 't1', 't2', 'kkv', 'kd', 'ka', 'cs', 'cs2', 'e0', 'e1', 'e2', 'e3', 'kd0')}
    tot_s = A.alloc([128, GW // 64])
    OB = []
    for _ in range(2):
        d_ = {nm: A.alloc([128, 4, GW], BF16) for nm in ('Rb', 'Kb', 'Kd', 'Ad', 'Kh', 'Ahn', 'vb')}
        d_['gc'] = A.alloc([128, 4, GW // 64])
        d_['bprod'] = A.alloc([128, 4, GW])
        d_['sgd'] = A.alloc([128, GW])
        d_['uvf'] = A.alloc([128, 4, GW])
        OB.append(d_)
    KbT, KhT, AhT, VT = [A.alloc([128, 512], BF16) for _ in range(4)]
    VT32s = [A.alloc([128, 512]) for _ in range(2)]
    gTs = [A.alloc([128, 512]) for _ in range(2)]
    bons = [A.alloc([128, 8]) for _ in range(2)]
    fcnt = {'n': 0}
    bg = {'final': None}
    Xs = [[A.alloc([128, 4, 128], BF16) for _ in range(2)] for _ in range(2)]
    Ys = [[A.alloc([128, 4, 128], BF16) for _ in range(2)] for _ in range(2)]
    Rms = [[A.alloc([128, 4, 128], BF16) for _ in range(2)] for _ in range(2)]
    pkks, qrks, qras = [[A.alloc([128, 4, 128], BF16) for _ in range(2)] for _ in range(3)]
    pvs = [A.alloc([128, 4, 64], BF16) for _ in range(2)]
    u0 = A.alloc([128, 8, 64], BF16)
    wt = A.alloc([128, 8, 64], BF16)
    y0 = A.alloc([128, 512])
    rpT = A.alloc([128, 4, 128])
    GT = A.alloc([128, 4, 2, 64])
    Hs = A.alloc([128, 4, 2, 64])
    Tst = [A.alloc([128, 4, 64]) for _ in range(2)]
    ytiles = [A.alloc([128, 512]) for _ in range(2)]
    yfbs = [A.alloc([128, 512]) for _ in range(2)]
    sqb = A.alloc([128, 512])
    tmpv = A.alloc([128, 512])
    s1 = A.alloc([128, 8])
    s2 = A.alloc([128, 8])
    psb2 = ps[2][:].bitcast(BF16)

    TTo('dve', c0, mp, mn, ALU.add, ['rwp'], ['c0'])
    TS('dve', c0, c0, -1.0, 1.0, ALU.mult, ALU.add, ['c0'], ['c0'])
    TS('dve', omka, k_a, -1.0, 1.0, ALU.mult, ALU.add, ['rwp'], ['omka'])
    TS('dve', rk2, r_k, 0.5, None, ALU.mult, None, ['rwp'], ['rk2'])
    zr2 = A.alloc([128, 2, 15])
    zedge = A.alloc([128, 15])
    zg = E.zbg_d.rearrange("(r p) c -> p r c", r=2)
    DMA('sp', zr2, zg, ['zbg'], ['zr2'])
    DMA('sp', zr2[0:64, :, 12:14], zg[64:128, :, 12:14], ['zbg'], ['zr2'])
    DMA('sp', zr2[64:128, :, 12:14], zg[0:64, :, 12:14], ['zbg'], ['zr2'])
    TS('dve', zedge, zr2[:, 0, :], E.hsel[:, 0:1], None, ALU.mult, None, ['zr2', 'hsel'], ['zedge'])
    STT(zedge, zr2[:, 1, :], E.hsel[:, 1:2], zedge, ALU.mult, ALU.add, ['zr2', 'hsel', 'zedge'], ['zedge'])
    tfr = A.alloc([128, 2, 256])

    pcnt = {'pa': 0}

    def pbank(lo=0):
        b = (6 if lo == 0 else 0) + pcnt['pa'] % 2
        pcnt['pa'] += 1
        return b

    def prep_group(g, di, own, final, pb):
        O_ = OB[pb]
        Rb, Kb, Kd, Ad, Kh, Ahn, vb = (O_[n] for n in ('Rb', 'Kb', 'Kd', 'Ad', 'Kh', 'Ahn', 'vb'))
        gc, bprod, sgd, uvf = O_['gc'], O_['bprod'], O_['sgd'], O_['uvf']
        lo, hi = GW * g - 1, GW * g + GW + 1
        clo, chi = max(lo, 0), min(hi, 2048)
        need = list(range(4, 12)) + [12, 13] + ([0, 1, 2, 3] if own else []) + ([14] if final else [])
        for n_, j in enumerate(need):
            zb = zsb[zcnt['z'] % 3]
            zk = 'zsb%d' % (zcnt['z'] % 3)
            zcnt['z'] += 1
            DMA('sp', zb[:, clo - lo:GW + 2 - (hi - chi)], zr_d[j, :, clo:chi], ['zr'], [zk])
            if g == 0:
                MEMSET('pool', zb[:, 0:1], 0.0, [zk])
            if g == NG_OWN - 1:
                CP('pool', zb[:, GW + 1:GW + 2], zedge[:, j:j + 1], ['zedge'], [zk])
            k = ('ub', j)
            ACT(ub[:, j, :], zb[:, 0:GW], AF.Copy, [zk, 'rwp'], [k], scale=mp[:, j:j + 1])
            STT(ub[:, j, :], zb[:, 2:GW + 2], mn[:, j:j + 1], ub[:, j, :], ALU.mult, ALU.add, [zk, 'rwp', k], [k])
            STT(ub[:, j, :], zb[:, 1:GW + 1], c0[:, j:j + 1], ub[:, j, :], ALU.mult, ALU.add, [zk, 'c0', k], [k])
            yield
        ACT(tw, ub[:, 12, :], AF.Tanh, [('ub', 12)], ['tw'])
        if final:
            ACT(sgd, ub[:, 14, :], AF.Sigmoid, [('ub', 14)], [('sgd', pb)])
        sg, av, kx, t1, t2, kkv, kd, ka = (T_[n] for n in ('sg', 'av', 'kx', 't1', 't2', 'kkv', 'kd', 'ka'))
        cs, cs2, e0, e1, e2, e3, kd0 = (T_[n] for n in ('cs', 'cs2', 'e0', 'e1', 'e2', 'e3', 'kd0'))
        lo64 = 64 * di
        NC64 = GW // 64
        for c in range(4):
            ku = ub[:, 4 + c, :]
            kkey = ('ub', 4 + c)
            cc = slice(c * 128, (c + 1) * 128)
            ACT(kx, ku, AF.Copy, [kkey, 'rwp'], ['kx'], scale=k_k[:, c:c + 1])
            ACT(t1, kx, AF.Square, ['kx'], ['t1'])
            b = pbank()
            MM(ps[b][:, 0:GW], blockones, t1, True, True, ['rwc', 't1'], ['ps%d' % b])
            ACT(t1, ps[b][:, 0:GW], AF.Sqrt, ['ps%d' % b], ['t1'])
            yield
            TS('dve', t1, t1, 1e-12, None, ALU.max, None, ['t1'], ['t1'])
            RECIP(t1, t1, ['t1'], ['t1'])
            TTo('dve', kkv, kx, t1, ALU.mult, ['kx', 't1'], ['kkv'])
            b = pbank(lo64)
            MM(ps[b][:, 0:GW], w2sb[lo64:lo64 + 64, cc], tw[lo64:lo64 + 64, :], True, True, ['lora', 'tw'],
               ['ps%d' % b])
            ACT(sg, ps[b][:, 0:GW], AF.Sigmoid, ['ps%d' % b, 'rwp'], ['sg'], bias=w0T(di, c))
            yield
            b = pbank(lo64)
            MM(ps[b][:, 0:GW], a2sb[lo64:lo64 + 64, cc], ub[lo64:lo64 + 64, 13, :], True, True,
               ['lora', ('ub', 13)], ['ps%d' % b])
            ACT(av, ps[b][:, 0:GW], AF.Sigmoid, ['ps%d' % b, 'rwp'], ['av'], bias=a0T(di, c))
            ACT(t2, av, AF.Identity, ['av', 'rwp', 'omka'], ['t2'], bias=omka[:, c:c + 1], scale=k_a[:, c:c + 1])
            TTo('dve', kd, ku, t2, ALU.mult, [kkey, 't2'], ['kd'])
            TTo('pool', ka, kkv, av, ALU.mult, ['kkv', 'av'], ['ka'])
            yield
            if final:
                od = 1 - di
                b = pbank(64 * od)
                MM(ps[b][:, 0:GW], a2sb[64 * od:64 * od + 64, cc], ub[64 * od:64 * od + 64, 13, :], True, True,
                   ['lora', ('ub', 13)], ['ps%d' % b])
                ACT(kd0, ps[b][:, 0:GW], AF.Sigmoid, ['ps%d' % b, 'rwp'], ['kd0'], bias=a0T(od, c))
                TS('dve', kd0, kd0, k_a[:, c:c + 1], omka[:, c:c + 1], ALU.mult, ALU.add,
                   ['kd0', 'rwp', 'omka'], ['kd0'])
                TTo('dve', kd0, kd0, ku, ALU.mult, ['kd0', kkey], ['kd0'])
                yield
                TTo('dve', kd0, kd0, kd, ALU.add, ['kd0', 'kd'], ['kd0'])
                TTo('dve', kd0, kd0, ub[:, c, :], ALU.mult, ['kd0', ('ub', c)], ['kd0'])
                TS('dve', bprod[:, c, :], kd0, rk2[:, c:c + 1], None, ALU.mult, None, ['kd0', 'rk2'],
                   [('bprod', c, pb)])
                CP('pool', uvf[:, c, :], ub[:, 8 + c, :], [('ub', 8 + c)], [('uvf', c, pb)])
                yield
            P.add('dve', lambda e, cs=cs, sg=sg: e.tensor_tensor_scan(out=cs, data0=rmask[:, 0:GW], data1=sg,
                                                                      initial=0.0, op0=ALU.mult, op1=ALU.add),
                  r=['rwc', 'sg'], w=['cs'])
            CP('dve', tot_s, cs[:, 63::64], ['cs'], ['tot'])
            totb = tot_s.unsqueeze(2).broadcast_to([128, NC64, 64])
            v3 = lambda ap: ap.rearrange("p (a b) -> p a b", a=NC64)
            if di == 0:
                csx, cskey = cs, 'cs'
            else:
                TTo('dve', t2, sg, cs, ALU.subtract, ['sg', 'cs'], ['t2'])
                TTo('dve', v3(cs2), v3(t2), totb, ALU.add, ['t2', 'tot'], ['cs2'])
                csx, cskey = cs2, 'cs2'
            yield
            TTo('dve', e0, csx, sg, ALU.subtract, [cskey, 'sg'], ['e0'])
            TTo('dve', v3(e3), totb, v3(csx), ALU.subtract, ['tot', cskey], ['e3'])
            ACT(e1, csx, AF.Exp, [cskey], ['e1'], scale=-CDEC)
            ACT(e2, csx, AF.Exp, [cskey], ['e2'], scale=CDEC)
            yield
            ACT(e0, e0, AF.Exp, ['e0'], ['e0'], scale=-CDEC)
            ACT(e3, e3, AF.Exp, ['e3'], ['e3'], scale=-CDEC)
            ACT(gc[:, c, :], tot_s, AF.Exp, ['tot'], [('gc', c, pb)], scale=-CDEC)
            yield
            if own:
                TTo('pool', Rb[:, c, :], ub[:, c, :], e1, ALU.mult, [('ub', c), 'e1'], [('Rb', c, pb)])
            TTo('pool', Kb[:, c, :], kkv, e0, ALU.mult, ['kkv', 'e0'], [('Kb', c, pb)])
            TTo('pool', Kd[:, c, :], kd, e2, ALU.mult, ['kd', 'e2'], [('Kd', c, pb)])
            yield
            TTo('pool', Ad[:, c, :], ka, e2, ALU.mult, ['ka', 'e2'], [('Ad', c, pb)])
            TTo('pool', Kh[:, c, :], kd, e3, ALU.mult, ['kd', 'e3'], [('Kh', c, pb)])
            TS('pool', ka, ka, -1.0, 0.0, ALU.mult, ALU.add, ['ka'], ['ka'])
            TTo('pool', Ahn[:, c, :], ka, e3, ALU.mult, ['ka', 'e3'], [('Ahn', c, pb)])
            CP('pool', vb[:, c, :], ub[:, 8 + c, :], [('ub', 8 + c)], [('vb', c, pb)])
            yield

    M = lambda di, k: rwm[:, di, k, :].unsqueeze(1).broadcast_to([128, 4, 128])

    def tile_proc(di, g, tl, outp, final, cur, pb, pump):
        cols = slice(tl * 128, (tl + 1) * 128)
        gt = NTG * g + tl
        O_ = OB[pb]
        Rb, Kb, Kd, Ad, Kh, Ahn, vb = (O_[n] for n in ('Rb', 'Kb', 'Kd', 'Ad', 'Kh', 'Ahn', 'vb'))
        gc, bprod, sgd, uvf = O_['gc'], O_['bprod'], O_['sgd'], O_['uvf']
        allk = lambda nm: [(nm, c, pb) for c in range(4)]
        fb_ = fcnt['n'] % 2 if final else 0
        ytile, ykey = ytiles[fb_], 'ytile%d' % fb_
        VT32, gT, bon, yfb = VT32s[fb_], gTs[fb_], bons[fb_], yfbs[fb_]
        slot_of = lambda hh: 4 * (hh % 2) + hh // 2
        for (src, dstT, nm, eng) in ((Kb, KbT, 'Kb', 'act'), (Kh, KhT, 'Kh', 'dve'), (Ahn, AhT, 'Ahn', 'act'),
                                      (vb, VT, 'vb', 'dve')):
            for c in range(4):
                TR(psb2[:, c * 128:(c + 1) * 128], src[:, c, cols], identb, [(nm, c, pb), 'identb'], ['ps2'])
            CP(eng, dstT, psb2[:, 0:512], ['ps2'], [nm + 'T'])
        if final:
            for c in range(4):
                TR(ps[2][:, c * 128:(c + 1) * 128], uvf[:, c, cols], ident, [('uvf', c, pb), 'ident'], ['ps2'])
            CP('act', VT32, ps[2][:], ['ps2'], [('VT32', fb_)])
            MM(ps[3][:], sgd[:, cols], g2sb, True, True, [('sgd', pb), 'lora'], ['ps3'])
            CP('act', gT, ps[3][:], ['ps3'], ['gT%d' % fb_])
            for c in range(4):
                MM(ps[2][:, 2 * c:2 * c + 2], bprod[:, c, cols], hsel, True, True, [('bprod', c, pb), 'rwc'], ['ps2'])
            CP('dve', bon, ps[2][:, 0:8], ['ps2'], ['bon%d' % fb_])
        def hg_stages(hg):
            BA, BB, BC = ((4, 5, 3), (1, 0, 2))[hg]
            X, Y, Rm = Xs[hg], Ys[hg], Rms[hg]
            pkk, qrk, qra, pv = pkks[hg], qrks[hg], qras[hg], pvs[hg]
            sx = 'g%d' % hg
            base = 64 * hg
            hl = [(2 * i + hg, i) for i in range(4)]

            def blk(bank, i):
                return ps[bank][:, i * 128:(i + 1) * 128]

            def b3(bank):
                return ps[bank][:].rearrange("p (a b) -> p a b", a=4)
            for i, (hh, c) in enumerate(hl):
                MM(blk(BA, i), Kb[base:base + 64, c, cols], Ad[base:base + 64, c, cols], True, True,
                   [('Kb', c, pb), ('Ad', c, pb)], ['ps%d' % BA])
            TTo('dve', X[0], b3(BA), M(di, 0), ALU.mult, ['ps%d' % BA, 'rwm'], ['X0' + sx])
            yield
            for i, (hh, c) in enumerate(hl):
                MM(blk(BB, i), Ad[base:base + 64, c, cols], Kb[base:base + 64, c, cols], True, True,
                   [('Kb', c, pb), ('Ad', c, pb)], ['ps%d' % BB])
            TTo('dve', Y[0], b3(BB), M(di, 1), ALU.mult, ['ps%d' % BB, 'rwm'], ['Y0' + sx])
            TTo('pool', Rm[0], Y[0], identb.unsqueeze(1).broadcast_to([128, 4, 128]), ALU.add,
                ['Y0' + sx, 'identb'], ['R0' + sx])
            yield
            for j in range(1, 6):
                a, b = (j - 1) % 2, j % 2
                for i in range(4):
                    MM(blk(BA, i), Y[a][:, i, :], X[a][:, i, :], True, True, ['X%d' % a + sx, 'Y%d' % a + sx], ['ps%d' % BA])
                CP('act', X[b], b3(BA), ['ps%d' % BA], ['X%d' % b + sx])
                yield
                if j < 5:
                    for i in range(4):
                        MM(blk(BB, i), X[a][:, i, :], Y[a][:, i, :], True, True, ['X%d' % a + sx, 'Y%d' % a + sx], ['ps%d' % BB])
                    CP('act', Y[b], b3(BB), ['ps%d' % BB], ['Y%d' % b + sx])
                    yield
                for i in range(4):
                    MM(blk(BC, i), identb, Rm[a][:, i, :], True, False, ['identb', 'R%d' % a + sx], ['ps%d' % BC])
                    MM(blk(BC, i), X[b][:, i, :], Rm[a][:, i, :], False, True, ['X%d' % b + sx, 'R%d' % a + sx], ['ps%d' % BC])
                CP('act', Rm[b], b3(BC), ['ps%d' % BC], ['R%d' % b + sx])
                yield
            MinvT, mkey = Rm[1], 'R1' + sx
            for i, (hh, c) in enumerate(hl):
                MM(blk(BA, i), Kd[base:base + 64, c, cols], Kb[base:base + 64, c, cols], True, True,
                   [('Kd', c, pb), ('Kb', c, pb)], ['ps%d' % BA])
            TTo('dve', pkk, b3(BA), M(di, 2), ALU.mult, ['ps%d' % BA, 'rwm'], ['pkk' + sx])
            yield
            if outp:
                for i, (hh, c) in enumerate(hl):
                    MM(blk(BB, i), Kd[base:base + 64, c, cols], Rb[base:base + 64, c, cols], True, True,
                       [('Kd', c, pb), ('Rb', c, pb)], ['ps%d' % BB])
                TTo('dve', qrk, b3(BB), M(di, 3), ALU.mult, ['ps%d' % BB, 'rwm'], ['qrk' + sx])
                yield
                for i, (hh, c) in enumerate(hl):
                    MM(blk(BC, i), Ad[base:base + 64, c, cols], Rb[base:base + 64, c, cols], True, True,
                       [('Ad', c, pb), ('Rb', c, pb)], ['ps%d' % BC])
                TTo('dve', qra, b3(BC), M(di, 4), ALU.mult, ['ps%d' % BC, 'rwm'], ['qra' + sx])
                yield
            for i, (hh, c) in enumerate(hl):
                MM(ps[BA][:, i * 64:(i + 1) * 64], pkk[:, i, :], VT[:, hh * 64:(hh + 1) * 64], True, True,
                   ['pkk' + sx, 'vbT'], ['ps%d' % BA])
            CP('act', pv, ps[BA][:, 0:256].rearrange("p (a b) -> p a b", a=4), ['ps%d' % BA], ['pv' + sx])
            yield
            for i, (hh, c) in enumerate(hl):
                MM(ps[BB][:, i * 64:(i + 1) * 64], MinvT[:, i, :], pv[:, i, :], True, True, [mkey, 'pv' + sx], ['ps%d' % BB])
            for i, (hh, c) in enumerate(hl):
                MM(ps[BC][:, i * 64:(i + 1) * 64], MinvT[:, i, :], KbT[:, hh * 64:(hh + 1) * 64],
                   True, True, [mkey, 'KbT'], ['ps%d' % BC])
            CP('act', u0[:, 4 * hg:4 * hg + 4, :], ps[BB][:, 0:256].rearrange("p (a b) -> p a b", a=4),
               ['ps%d' % BB], [('u0', hg)])
            CP('dve', wt[:, 4 * hg:4 * hg + 4, :], ps[BC][:, 0:256].rearrange("p (a b) -> p a b", a=4),
               ['ps%d' % BC], [('wt', hg)])
            yield
            if outp:
                for i, (hh, c) in enumerate(hl):
                    MM(ps[BB][:, i * 64:(i + 1) * 64], qrk[:, i, :], VT[:, hh * 64:(hh + 1) * 64], True, False,
                       ['qrk' + sx, 'vbT'], ['ps%d' % BB])
                    MM(ps[BB][:, i * 64:(i + 1) * 64], qra[:, i, :], u0[:, 4 * hg + i, :], False, True,
                       ['qra' + sx, ('u0', hg)], ['ps%d' % BB])
                CP('act', y0.rearrange("p (i q d) -> p i q d", i=4, q=2)[:, :, hg, :],
                   ps[BB][:, 0:256].rearrange("p (a b) -> p a b", a=4), ['ps%d' % BB], [('y0', hg)])
                yield
                for i, (hh, c) in enumerate(hl):
                    MM(ps[BA][base:base + 64, i * 128:(i + 1) * 128], wt[:, 4 * hg + i, :], qra[:, i, :],
                       True, True, [('wt', hg), 'qra' + sx], ['ps%d' % BA])
                TTo('dve', rpT[base:base + 64, :, :],
                    ps[BA][base:base + 64, :].rearrange("p (a b) -> p a b", a=4),
                    Rb[base:base + 64, :, cols], ALU.add, ['ps%d' % BA] + allk('Rb'), [('rpT', hg)])
            yield
        gens = [hg_stages(0), hg_stages(1)]
        alive = [True, True]
        while any(alive):
            pump(1)
            for k_ in (0, 1):
                if alive[k_]:
                    try:
                        next(gens[k_])
                    except StopIteration:
                        alive[k_] = False
        gbank = {0: 5, 1: 3}
        hbank = {0: 4, 1: 2}
        for ch2 in range(2):
            rows = slice(ch2 * 64, ch2 * 64 + 64)
            gb, hb = gbank[ch2], hbank[ch2]
            for hh in range(8):
                c, base = hh // 2, 64 * (hh % 2)
                sl = slot_of(hh)
                o5 = ps[gb][base:base + 64, c * 64:(c + 1) * 64]
                MM(o5, wt[rows, sl, :], AhT[rows, hh * 64:(hh + 1) * 64], True, True,
                   [('wt', hh % 2), 'AhnT'], ['ps%d' % gb])
                o4 = ps[hb][base:base + 64, c * 64:(c + 1) * 64]
                MM(o4, KhT[rows, hh * 64:(hh + 1) * 64], VT[rows, hh * 64:(hh + 1) * 64], True, False,
                   ['KhT', 'vbT'], ['ps%d' % hb])
                MM(o4, AhT[rows, hh * 64:(hh + 1) * 64], u0[rows, sl, :], False, True,
                   ['AhnT', ('u0', hh % 2)], ['ps%d' % hb])
        for ch2 in range(2):
            gb, hb = gbank[ch2], hbank[ch2]
            for c in range(4):
                slot = 2 * tl + ch2
                STT(GT[:, c, ch2, :], identblk, gc[:, c, slot:slot + 1], ps[gb][:, c * 64:(c + 1) * 64],
                    ALU.mult, ALU.add, ['rwc', ('gc', c, pb), 'ps%d' % gb], ['GT'])
            CP('act', Hs[:, :, ch2, :], ps[hb][:, 0:256].rearrange("p (a b) -> p a b", a=4),
               ['ps%d' % hb], ['Hs'])
        pump(2)
        order = [0, 1] if di == 0 else [1, 0]
        tb = {0: 6, 1: 0}
        yb_ = {0: 7, 1: 1}
        for ch2 in order:
            rows = slice(ch2 * 64, ch2 * 64 + 64)
            Tc, Tn = Tst[cur], Tst[1 - cur]
            kc, kn = 'T%d' % cur, 'T%d' % (1 - cur)
            for par in range(2):
                base = 64 * par
                for c in range(4):
                    hh = 2 * c + par
                    if outp:
                        MM(ps[yb_[par]][rows, hh * 64:(hh + 1) * 64],
                           rpT[base:base + 64, c, ch2 * 64:ch2 * 64 + 64],
                           Tc[base:base + 64, c, :], True, True, [('rpT', par), kc], ['ps%d' % yb_[par]])
                    MM(ps[tb[par]][base:base + 64, c * 64:(c + 1) * 64], GT[base:base + 64, c, ch2, :],
                       Tc[base:base + 64, c, :], True, True, ['GT', kc], ['ps%d' % tb[par]])
            for par in range(2):
                base = 64 * par
                if outp:
                    yv = ytile.rearrange("p (i q d) -> p i q d", i=4, q=2)
                    y0v = y0.rearrange("p (i q d) -> p i q d", i=4, q=2)
                    pyv = ps[yb_[par]][:].rearrange("p (i q d) -> p i q d", i=4, q=2)
                    TTo('dve', yv[rows, :, par, :], pyv[rows, :, par, :], y0v[rows, :, par, :], ALU.add,
                        ['ps%d' % yb_[par], ('y0', 0), ('y0', 1)], [ykey])
                TTo('dve', Tn[base:base + 64, :, :],
                    ps[tb[par]][base:base + 64, 0:256].rearrange("p (a b) -> p a b", a=4),
                    Hs[base:base + 64, :, ch2, :], ALU.add, ['ps%d' % tb[par], 'Hs'], [kn])
            cur = 1 - cur
            pump(1)
        if outp and not final:
            DMA('sp', yf_d[gt * 128:(gt + 1) * 128, :], ytile, [ykey], ['yf'])
        if outp and final:
            def final_gen(ytile=ytile, ykey=ykey, VT32=VT32, gT=gT, bon=bon, yfb=yfb, fb_=fb_, gt=gt):
                yk = 'yfb%d' % fb_
                DMA('sp', yfb, yf_d[gt * 128:(gt + 1) * 128, :], ['yf'], [yk])
                y = ytile
                y3 = y.rearrange("p (h d) -> p h d", h=8)
                TTo('dve', y, y, yfb, ALU.add, [ykey, yk], [ykey])
                P.add('dve', lambda e: e.tensor_reduce(out=s1, in_=y3, axis=AX.X, op=ALU.add), r=[ykey], w=['s1'])
                yield
                TS('dve', s1, s1, -1.0 / 64, None, ALU.mult, None, ['s1'], ['s1'])
                TTo('dve', y3, y3, s1.unsqueeze(2).broadcast_to([128, 8, 64]), ALU.add, [ykey, 's1'], [ykey])
                ACT(sqb, y, AF.Square, [ykey], ['sqb'])
                yield
                P.add('dve', lambda e: e.tensor_reduce(out=s2, in_=sqb.rearrange("p (h d) -> p h d", h=8),
                                                       axis=AX.X, op=ALU.add), r=['sqb'], w=['s2'])
                ACT(s2, s2, AF.Sqrt, ['s2', 'eps'], ['s2'], bias=gneps_t, scale=1.0 / 64)
                yield
                RECIP(s2, s2, ['s2'], ['s2'])
                TTo('dve', y3, y3, s2.unsqueeze(2).broadcast_to([128, 8, 64]), ALU.mult, [ykey, 's2'], [ykey])
                yield
                TTo('dve', y, y, rep[:, 1, :], ALU.mult, [ykey, 'rep'], [ykey])
                TTo('pool', tmpv.rearrange("p (h d) -> p h d", h=8), VT32.rearrange("p (h d) -> p h d", h=8),
                    bon.unsqueeze(2).broadcast_to([128, 8, 64]), ALU.mult, [('VT32', fb_), 'bon%d' % fb_], ['tmpv'])
                yield
                TTo('dve', y, y, rep[:, 2, :], ALU.add, [ykey, 'rep'], [ykey])
                TTo('dve', y, y, tmpv, ALU.add, [ykey, 'tmpv'], [ykey])
                yield
                TTo('dve', y, y, gT, ALU.mult, [ykey, 'gT%d' % fb_], [ykey])
                DMA('sp', yc_d[gt * 128:(gt + 1) * 128, 512:1024], y, [ykey], ['yc'])
                yield
            if bg['final'] is not None:
                for _ in bg['final']:
                    pass
            bg['final'] = final_gen()
            fcnt['n'] += 1
        return cur

    sched = [(g, 0, True, False) for g in range(NG_OWN)]
    if debug_stop != 'F':
        sched += [(g, 1, True, True) for g in range(NG_OWN - 1, -1, -1)]
    cur = 0
    MEMSET('pool', Tst[0], 0.0, ['T0'])
    for _ in prep_group(*sched[0], 0):
        pass
    for idx, (g, di, own, final) in enumerate(sched):
        pb = idx % 2
        pg = prep_group(*sched[idx + 1], (idx + 1) % 2) if idx + 1 < len(sched) else None
        st = {'alive': pg is not None}

        def pump(n, pg=pg, st=st):
            for _ in range(n):
                if st['alive']:
                    try:
                        next(pg)
                    except StopIteration:
                        st['alive'] = False
                if n < 100 and bg['final'] is not None:
                    try:
                        next(bg['final'])
                    except StopIteration:
                        bg['final'] = None
        if idx > 0 and sched[idx - 1][1] != di:
            kc_ = 'T%d' % cur
            tflat = Tst[cur].rearrange("p a b -> p (a b)")
            DMA('sp', E.tf_d, tflat, [kc_], ['tf_d'])
            E.CC(E.tf_d, E.tfg_d, ['tf_d'], ['tfg'])
            DMA('sp', tfr, E.tfg_d.rearrange("(r p) f -> p r f", r=2), ['tfg'], ['tfr'])
            TS('dve', tflat, tfr[:, 0, :], E.hsel[:, 0:1], None, ALU.mult, None, ['tfr', 'hsel'], [kc_])
            STT(tflat, tfr[:, 1, :], E.hsel[:, 1:2], tflat, ALU.mult, ALU.add, ['tfr', 'hsel', kc_], [kc_])
        tls = range(NTG) if di == 0 else range(NTG - 1, -1, -1)
        for tl in tls:
            cur = tile_proc(di, g, tl, own, final, cur, pb, pump)
        pump(10 ** 6)
    if bg['final'] is not None:
        for _ in bg['final']:
            pass


def _f(a):
    return np.ascontiguousarray(a, dtype=np.float32)


def _lhs_chunks(W):
    n = W.shape[1] // 128
    return _f(W.reshape(8, 128, n, 128).transpose(2, 1, 0, 3))


def _attn_masks():
    p = np.arange(128)[:, None]
    c = np.arange(128)[None, :]
    cntm = {}
    for j in range(-8, 9):
        d = 128 * j + p - c
        m = (np.abs(d) <= 64).astype(np.float32)
        m += ((d % 4 == 0) & (np.abs(d) <= 256)).astype(np.float32)
        m += ((d % 16 == 0) & (np.abs(d) <= 1024)).astype(np.float32)
        cntm[j] = m
    wm = np.zeros((20, 128, 512), np.float32)
    for J in range(20):
        for i in range(4):
            j = J - 8 - i
            if abs(j) <= 8:
                wm[J, :, i * 128:(i + 1) * 128] = cntm[j]
    return wm.astype(ml_dtypes.bfloat16)


def _rope_tables(rev):
    pos = np.arange(S, dtype=np.float32)
    if rev:
        pos = pos[::-1].copy()
    inv_freq = (np.float32(10000.0) ** (-np.arange(0, 64, 2, dtype=np.float32) / np.float32(64))).astype(np.float32)
    ang = (pos[:, None] * inv_freq[None, :]).astype(np.float32)
    cos, sin = np.cos(ang).astype(np.float32), np.sin(ang).astype(np.float32)
    idx = np.arange(128) % 32
    sign = np.where((np.arange(128) % 64) < 32, -1.0, 1.0).astype(np.float32)
    cosT = cos[:, idx].T
    sinT = (sin[:, idx] * sign[None, :]).T
    return _f(cosT), _f(sinT)


def host_inputs(inputs):
    g = lambda k: np.asarray(inputs[k][0], np.float32)
    x = np.asarray(inputs["x"], dtype=np.float32)
    shared = {}
    ffn_names = {1: ("ffn1_w_gate", "ffn1_w_up", "ffn1_w_down"), 2: ("ffn2_w_gate", "ffn2_w_up", "ffn2_w_down")}
    for k in (1, 2):
        wg, wu, wd = (g(nm) for nm in ffn_names[k])
        gg = wg.reshape(8, 128, NF, 128).transpose(2, 1, 0, 3)
        uu = wu.reshape(8, 128, NF, 128).transpose(2, 1, 0, 3)
        shared["wgu%d" % k] = _f(np.stack([gg, uu], axis=2))
        shared["wdc%d" % k] = _f(wd.reshape(NF, 128, 8, 128).transpose(2, 1, 0, 3))
    gnames = ["ffn1_pre_g", "ffn1_post_g", "mix_pre_g", "mix_post_g", "ffn2_pre_g", "ffn2_post_g"]
    shared["gv"] = _f(np.stack([g(nm).reshape(8, 128).T for nm in gnames], axis=1))
    shared["ident"] = np.eye(128, dtype=np.float32)
    w_in = g("w_in")
    swap = np.concatenate([(np.arange(64) + 32) % 64 + 64 * h for h in range(8)])
    shared["wv"] = _f(w_in[:, 1024:1536].reshape(8, 128, 512).transpose(1, 0, 2))
    shared["wo"] = _lhs_chunks(g("w_out"))
    shared["wm"] = _attn_masks()
    shared["wmf"] = np.ascontiguousarray(shared["wm"][:, ::-1, :])
    shared["rep"] = _f(np.stack([np.broadcast_to(g(nm)[None, :], (128, 512))
                                 for nm in ("attn_out_g", "rwkv_lnx_w", "rwkv_lnx_b")], axis=1))
    maps = []
    for c in range(8):
        b, h = c // 2, c % 2
        m = dict(shared)
        m["x"] = _f(x[b] if h == 0 else x[b, ::-1])
        cosT, sinT = _rope_tables(h == 1)
        m["cosT"], m["sinT"] = cosT, sinT
        m["hsel"] = _f(np.tile(np.array([[float(h), float(1 - h)]], np.float32), (128, 1)))
        m.update(_rwkv_host(inputs, h, w_in, swap))
        maps.append(m)
    return maps


def _rwkv_host(inputs, h, w_in, swap):
    g = lambda k: np.asarray(inputs[k][0], np.float32)
    dirs = [0, 1] if h == 0 else [1, 0]
    wq, wk = w_in[:, 0:512], w_in[:, 512:1024]
    cols = []
    for gi in range(4):
        cols.append(wq[:, gi * 128:(gi + 1) * 128])
        cols.append(wq[:, swap][:, gi * 128:(gi + 1) * 128])
    for gi in range(4):
        cols.append(wk[:, gi * 128:(gi + 1) * 128])
        cols.append(wk[:, swap][:, gi * 128:(gi + 1) * 128])
    wr = w_in[:, 1536:]
    rw = [wr[:, 0:1536]]
    for off in (1536, 1664):
        blk = wr[:, off:off + 128]
        rw.append(np.concatenate([blk[:, 64 * dirs[0]:64 * dirs[0] + 64], blk[:, 64 * dirs[1]:64 * dirs[1] + 64]], axis=1))
    rw.append(wr[:, 1792:1920])
    Wall = np.concatenate(cols + rw, axis=1)
    out = {"win": _lhs_chunks(Wall)}
    mp, mn = g("rwkv_mu_prev"), g("rwkv_mu_next")
    if h == 1:
        mp, mn = mn, mp

    def fix(v):
        v = v.copy()
        for off in (1536, 1664):
            blk = v[off:off + 128].copy()
            v[off:off + 128] = np.concatenate([blk[64 * dirs[0]:64 * dirs[0] + 64], blk[64 * dirs[1]:64 * dirs[1] + 64]])
        return v
    mp, mn = fix(mp), fix(mn)
    rwp = np.zeros((128, 64), np.float32)
    rwp[:, 0:15] = mp.reshape(15, 128).T
    rwp[:, 15:30] = mn.reshape(15, 128).T
    w0, a0 = g("rwkv_w0"), g("rwkv_a0")
    for di, d in enumerate(dirs):
        rwp[:, 30 + 4 * di:34 + 4 * di] = w0[d].reshape(4, 128).T
        rwp[:, 38 + 4 * di:42 + 4 * di] = a0[d].reshape(4, 128).T
    rwp[:, 46:50] = g("rwkv_k_k").reshape(4, 128).T
    rwp[:, 50:54] = g("rwkv_k_a").reshape(4, 128).T
    rwp[:, 54:58] = g("rwkv_r_k").reshape(4, 128).T
    out["rwpar"] = rwp
    w2, a2 = g("rwkv_w2"), g("rwkv_a2")
    lora = np.zeros((128, 3, 512), np.float32)
    lora[:, 0, :] = np.concatenate([w2[dirs[0]], w2[dirs[1]]], axis=0)
    lora[:, 1, :] = np.concatenate([a2[dirs[0]], a2[dirs[1]]], axis=0)
    lora[:, 2, :] = g("rwkv_g2")
    out["lora"] = lora
    idx = np.arange(128)
    same = (idx[:, None] // 64) == (idx[None, :] // 64)
    B0 = ((idx[None, :] < idx[:, None]) & same).astype(np.float32)
    I = np.eye(128, dtype=np.float32)
    rwm = np.zeros((128, 2, 5, 128), np.float32)
    for d, Bd in enumerate((B0, B0.T)):
        rwm[:, d, 0] = -Bd
        rwm[:, d, 1] = -Bd.T
        rwm[:, d, 2] = Bd.T
        rwm[:, d, 3] = Bd.T + I
        rwm[:, d, 4] = -(Bd.T + I)
    out["rwmask"] = rwm
    rwc = np.zeros((128, 1218), np.float32)
    rwc[:, 0:512] = (np.arange(512) % 64 != 0).astype(np.float32)[None, :]
    rwc[:, 1024:1152] = same.astype(np.float32)
    rwc[:, 1152:1216] = (idx[:, None] % 64 == np.arange(64)[None, :]).astype(np.float32)
    rwc[:, 1216] = (idx < 64)
    rwc[:, 1217] = (idx >= 64)
    out["rwconst"] = rwc
    return out


_CACHE = {}


def kernel(**inputs):
    if 'nc' not in _CACHE:
        _CACHE['nc'] = build()[0]
    nc = _CACHE['nc']
    maps = host_inputs(inputs)
    res = run_bass_kernel_spmd(nc, maps, core_ids=list(range(8)))
    out = np.zeros((4, S, D), np.float32)
    for c in range(8):
        b, h = c // 2, c % 2
        o = np.asarray(res.results[c]["out"])
        if h == 0:
            out[b, :OWN] = o
        else:
            out[b, OWN:] = o[::-1]
    return out
```

```python
import contextlib
import numpy as np
import ml_dtypes
import concourse.bass as bass
import concourse.mybir as mybir
from concourse.bass_utils import run_bass_kernel_spmd

F32 = mybir.dt.float32
BF16 = mybir.dt.bfloat16
AF = mybir.ActivationFunctionType
ALU = mybir.AluOpType
AX = mybir.AxisListType

D = 1024
DFF = 2816
NF = DFF // 128
S = 4096
OWN = 2048
TT = 512
EPS = 1e-6
GN_EPS = 64e-5
CDEC = float(np.exp(-0.5))
NKT = 16


class Prog:
    NDMA = 16

    def __init__(self):
        self.ops = []
        self.last_w = {}
        self.readers = {}
        self.last_barrier = 0
        self.warn = []
        self.pe_hist = []

    def add(self, eng, fn, r=(), w=(), dma=False, cc=False):
        i = len(self.ops)
        deps = set()
        for k in r:
            j = self.last_w.get(k)
            if j is not None:
                deps.add((j, 'raw'))
        for k in w:
            j = self.last_w.get(k)
            if j is not None:
                deps.add((j, 'waw'))
            for j in self.readers.get(k, ()):
                deps.add((j, 'war'))
        self.ops.append(dict(eng=eng, fn=fn, deps=deps, dma=dma, cc=cc))
        for k in w:
            if isinstance(k, str) and k.startswith('ps'):
                engs = set(self.ops[j]['eng'] for j in self.readers.get(k, ()))
                if len(engs) > 1:
                    self.warn.append(('multi-engine psum readers', k, sorted(engs), i))
        for k in r:
            self.readers.setdefault(k, []).append(i)
        for k in w:
            self.last_w[k] = i
            self.readers[k] = []
        return i

    def barrier(self):
        n = len(self.ops)
        deps = set()
        last = {}
        for i in range(n):
            op = self.ops[i]
            if op['dma']:
                if i >= self.last_barrier:
                    deps.add((i, 'raw'))
            elif op['fn'] is not None:
                last[op['eng']] = i
        for e, i in last.items():
            deps.add((i, 'raw'))
        for e in ['pe', 'act', 'dve', 'pool', 'sp']:
            self.ops.append(dict(eng=e, fn=None, deps=set(deps), dma=False, cc=False))
        self.last_barrier = n

    def emit(self, nc, stack):
        ops = self.ops
        n = len(ops)
        need = [set() for _ in range(n)]
        signaled = [False] * n
        for i, op in enumerate(ops):
            for (j, kind) in op['deps']:
                if j == i:
                    continue
                pj = ops[j]
                same = (pj['eng'] == op['eng'])
                if same and op['eng'] == 'pe' and not pj['dma'] and not op['dma'] and op['fn'] is not None:
                    if kind != 'raw':
                        continue
                need[i].add(j)
            latest = {}
            for j in need[i]:
                pj = ops[j]
                if pj['dma']:
                    continue
                e2 = pj['eng']
                if e2 not in latest or j > latest[e2]:
                    latest[e2] = j
            need[i] = set(j for j in need[i] if ops[j]['dma'] or latest[ops[j]['eng']] == j)
            for j in need[i]:
                signaled[j] = True
        engs = ['pe', 'act', 'dve', 'pool', 'sp']
        csem = {e: stack.enter_context(nc.semaphore("s_" + e)) for e in engs[:4]}
        dsem = {e: [stack.enter_context(nc.semaphore("d_%s%d" % (e, k))) for k in range(self.NDMA)]
                for e in ['sp', 'pool']}
        sig = [None] * n
        ccount = {e: 0 for e in engs}
        dcount = {e: 0 for e in engs}
        prevuse = [None] * n
        for i, op in enumerate(ops):
            e = op['eng']
            if op.get('cc'):
                sig[i] = (stack.enter_context(nc.semaphore("cc_%d" % i)), 1)
            elif op['dma']:
                k = dcount[e]
                dcount[e] += 1
                s = dsem[e][k % self.NDMA]
                sig[i] = (s, 16 * (k // self.NDMA + 1))
                if k >= self.NDMA:
                    prevuse[i] = (s, 16 * (k // self.NDMA))
            elif signaled[i]:
                ccount[e] += 1
                sig[i] = (csem[e], ccount[e])
        per = {e: [i for i in range(n) if ops[i]['eng'] == e] for e in engs}
        self.stats = {e: len(per[e]) for e in engs}
        self.stats['sem'] = dict(ccount)

        def run(e, eng):
            waited = {}
            for i in per[e]:
                op = ops[i]
                ws = []
                if prevuse[i] is not None:
                    ws.append(prevuse[i])
                for j in need[i]:
                    ws.append(sig[j])
                best = {}
                for (s, v) in ws:
                    key = id(s)
                    if v > best.get(key, (None, -1))[1]:
                        best[key] = (s, v)
                for key, (s, v) in best.items():
                    if waited.get(key, -1) >= v:
                        continue
                    eng.wait_ge(s, v)
                    waited[key] = v
                if op['fn'] is None:
                    continue
                ins = op['fn'](eng)
                if sig[i] is not None:
                    if op.get('cc'):
                        ins.then_inc(sig[i][0])
                    else:
                        ins.then_inc(sig[i][0], 16 if op['dma'] else 1)

        with nc.Block() as block:
            @block.tensor
            def _(eng):
                run('pe', eng)

            @block.scalar
            def _(eng):
                run('act', eng)

            @block.vector
            def _(eng):
                run('dve', eng)

            @block.gpsimd
            def _(eng):
                run('pool', eng)

            @block.sync
            def _(eng):
                run('sp', eng)


class Arena:
    def __init__(self, tensor, nbytes):
        self.t = tensor
        self.n = nbytes
        self.off = 0

    def alloc(self, shape, dt=F32):
        esz = mybir.dt.size(dt)
        nel = int(np.prod(shape[1:]))
        nb = (nel * esz + 31) // 32 * 32
        assert self.off + nb <= self.n, ("arena overflow", self.off, nb, self.n)
        ap = self.t[:, self.off // 4:(self.off + nb) // 4]
        self.off += nb
        if dt != F32:
            ap = ap.bitcast(dt)
        ap = ap[:, 0:nel]
        if len(shape) == 3:
            ap = ap.rearrange("p (a b) -> p a b", a=shape[1])
        elif len(shape) == 4:
            ap = ap.rearrange("p (a b c) -> p a b c", a=shape[1], b=shape[2])
        return ap


class Env:
    pass


def build(debug=False, phases=('p1', 'att', 'rwkv', 'p4'), ntiles1=8, rwkv_stop=None):
    nc = bass.Bass("TRN2", target_bir_lowering=False)
    P = Prog()
    stack = contextlib.ExitStack()
    E = Env()

    def din(name, shape, dt=F32):
        return nc.dram_tensor(name, list(shape), dt, kind="ExternalInput").ap()

    def dscr(name, shape, dt=F32, out=False):
        if out:
            return nc.dram_tensor(name, list(shape), dt, kind="ExternalOutput").ap()
        return nc.dram_tensor(name, list(shape), dt).ap()

    x_d = din("x", [S, D])
    wgu_d = [din("wgu%d" % k, [NF, 128, 2, 8, 128]) for k in (1, 2)]
    wdc_d = [din("wdc%d" % k, [8, 128, NF, 128]) for k in (1, 2)]
    win_d = din("win", [31, 128, 8, 128])
    wv_d = din("wv", [128, 8, 512])
    wo_d = din("wo", [8, 128, 8, 128])
    gv_d = din("gv", [128, 6, 8])
    ident_d = din("ident", [128, 128])
    cos_d = din("cosT", [128, S])
    sin_d = din("sinT", [128, S])
    wm_d = din("wm", [20, 128, 512], BF16)
    rep_d = din("rep", [128, 3, 512])
    rwm_d = din("rwmask", [128, 2, 5, 128])
    rwc_d = din("rwconst", [128, 1218])
    rwp_d = din("rwpar", [128, 64])
    lora_d = din("lora", [128, 3, 512])
    out_d = nc.dram_tensor("out", [OWN, D], F32, kind="ExternalOutput").ap()
    h1s_d = dscr("h1s", [8, 128, OWN], out=debug)
    zr_d = dscr("zr", [15, 128, S], out=debug)
    qT_d = dscr("qTs", [4, 128, OWN], BF16, out=debug)
    kT_d = dscr("kTs", [4, 128, NKT * 128], BF16, out=debug)
    v_d = dscr("vs", [NKT * 128, 520], BF16, out=debug)
    yc_d = dscr("yc", [OWN, D], out=debug)
    yf_d = dscr("yf", [OWN, 512], out=debug)
    hsel_d = din("hsel", [128, 2])
    wmf_d = din("wmf", [20, 128, 512], BF16)
    kh_d = dscr("kh", [512, 1024], BF16)
    vh_d = dscr("vh", [1024, 520], BF16)
    zb_d = dscr("zb", [128, 15])
    khg_d = dscr("khg", [1024, 1024], BF16)
    vhg_d = dscr("vhg", [2048, 520], BF16)
    zbg_d = dscr("zbg", [256, 15])
    tf_d = dscr("tf", [128, 256])
    tfg_d = dscr("tfg", [256, 256])
    wgu16_d = [dscr("wgu16_%d" % k, [NF, 128, 2 * 8 * 128], BF16) for k in (1, 2)]
    wdc16_d = [dscr("wdc16_%d" % k, [8, 128, NF * 128], BF16) for k in (1, 2)]
    win16_d = dscr("win16", [31, 128, 8 * 128], BF16)
    wo16_d = dscr("wo16", [8, 128, 8 * 128], BF16)
    PAIRS = [[0, 1], [2, 3], [4, 5], [6, 7]]

    def CC(in_ap, out_ap, r, w):
        return P.add('pool', lambda e: e.collective_compute("AllGather", ALU.bypass, replica_groups=PAIRS,
                                                            ins=[in_ap.opt()], outs=[out_ap.opt()]),
                     r=r, w=w, dma=True, cc=True)

    ARENA_BYTES = 207 * 1024
    arena_t = stack.enter_context(nc.sbuf_tensor("arena", [128, ARENA_BYTES // 4], F32))
    A = Arena(arena_t, ARENA_BYTES)
    ps = [stack.enter_context(nc.psum_tensor("ps%d" % i, [128, 512], F32)) for i in range(8)]

    def MM(out, lhsT, rhs, start, stop, r, w):
        rb0, kk_ = lhsT.base_partition(), lhsT.shape[0]
        for (pb0, pk, pw) in P.pe_hist[-1:]:
            if (rb0 + kk_ <= pb0 or pb0 + pk <= rb0) and pw == w[0]:
                P.warn.append(('row-group conflict', w[0], (pb0, pk), (rb0, kk_), len(P.ops)))
        P.pe_hist.append((rb0, kk_, w[0]))
        P.add('pe', lambda e: e.matmul(out, lhsT, rhs, start=start, stop=stop), r=r, w=w)

    def TR(out, in_, idt, r, w):
        P.pe_hist.append((in_.base_partition(), in_.shape[0], w[0]))
        P.add('pe', lambda e: e.transpose(out, in_, idt), r=r, w=w)

    def ACT(out, in_, func, r, w, bias=None, scale=None):
        kw = {}
        if bias is not None:
            kw['bias'] = bias
        if scale is not None:
            kw['scale'] = scale
        P.add('act', lambda e: e.activation(out, in_, func, **kw), r=r, w=w)

    def TTo(eng, out, in0, in1, op, r, w):
        P.add(eng, lambda e: e.tensor_tensor(out=out, in0=in0, in1=in1, op=op), r=r, w=w)

    def TS(eng, out, in0, s1, s2, op0, op1, r, w):
        if op1 is None:
            P.add(eng, lambda e: e.tensor_scalar(out=out, in0=in0, scalar1=s1, scalar2=None, op0=op0), r=r, w=w)
        else:
            P.add(eng, lambda e: e.tensor_scalar(out=out, in0=in0, scalar1=s1, scalar2=s2, op0=op0, op1=op1),
                  r=r, w=w)

    def STT(out, in0, scalar, in1, op0, op1, r, w):
        P.add('dve', lambda e: e.scalar_tensor_tensor(out=out, in0=in0, scalar=scalar, in1=in1, op0=op0, op1=op1),
              r=r, w=w)

    def CP(eng, out, in_, r, w):
        if eng == 'act':
            P.add('act', lambda e: e.copy(out, in_), r=r, w=w)
        else:
            P.add(eng, lambda e: e.tensor_copy(out, in_), r=r, w=w)

    def RECIP(out, in_, r, w):
        P.add('dve', lambda e: e.reciprocal(out, in_), r=r, w=w)

    def DMA(q, out, in_, r, w):
        return P.add(q, lambda e: e.dma_start(out=out, in_=in_), r=r, w=w, dma=True)

    def MEMSET(eng, ap, val, w):
        P.add(eng, lambda e: e.memset(ap, val), w=w)

    ident = A.alloc([128, 128])
    identb = A.alloc([128, 128], BF16)
    ones = A.alloc([128, 128], BF16)
    gv = A.alloc([128, 6, 8])
    gvh = A.alloc([128, 6, 8])
    eps_t = A.alloc([128, 1])
    gneps_t = A.alloc([128, 1])
    rep = A.alloc([128, 3, 512])
    rwm = A.alloc([128, 2, 5, 128])
    rwc = A.alloc([128, 1218])
    rwp = A.alloc([128, 64])
    lora = A.alloc([128, 3, 512])
    hsel = A.alloc([128, 2])
    DMA('sp', ident, ident_d, [], ['ident'])
    DMA('sp', gv, gv_d, [], ['gv'])
    DMA('sp', rep, rep_d, [], ['rep'])
    DMA('sp', rwm, rwm_d, [], ['rwm'])
    DMA('sp', rwc, rwc_d, [], ['rwc'])
    DMA('sp', rwp, rwp_d, [], ['rwp'])
    DMA('sp', lora, lora_d, [], ['lora'])
    DMA('sp', hsel, hsel_d, [], ['hsel'])
    MEMSET('pool', ones, 1.0, ['ones'])
    MEMSET('pool', eps_t, EPS, ['eps'])
    MEMSET('pool', gneps_t, GN_EPS, ['eps'])
    CP('dve', identb, ident, ['ident'], ['identb'])
    TS('dve', gvh, gv, 0.5, None, ALU.mult, None, ['gv'], ['gvh'])
    A_MARK = A.off

    cnt = {'ps01': 0, 'x': 0, 'z': 0, 'st': 0}
    for k_, v_ in list(locals().items()):
        setattr(E, k_, v_)

    def alloc_ffn_bufs():
        B = {}
        B['xin'] = [A.alloc([128, D]) for _ in range(4)]
        B['xT'] = A.alloc([128, 8, TT])
        B['hn'] = A.alloc([128, 8, TT], BF16)
        B['sq'] = A.alloc([128, 8, TT], BF16)
        B['fb'] = A.alloc([128, 8, TT])
        B['aT'] = A.alloc([128, NF, TT], BF16)
        B['rstd'] = A.alloc([128, TT])
        B['tmp'] = A.alloc([128, TT])
        B['wgu'] = [A.alloc([128, 2, 8, 128], BF16) for _ in range(3)]
        B['wdc'] = [A.alloc([128, NF, 128], BF16) for _ in range(3)]
        B['sgl'] = [A.alloc([128, TT]) for _ in range(2)]
        return B

    def issue_x_loads(B, src_d, tt):
        xin = B['xin']
        for sub in range(4):
            t0 = tt * TT + sub * 128
            DMA('sp', xin[sub], src_d[t0:t0 + 128, :], ['yc'] if src_d is yc_d else [], ['xin%d' % sub])

    def load_T(B, src_d, tt, dst, dstkey, loads_issued=False):
        xin = B['xin']
        if not loads_issued:
            issue_x_loads(B, src_d, tt)
        for sub in range(4):
            b = sub
            for half in range(2):
                pb = cnt['ps01'] % 2
                cnt['ps01'] += 1
                for q in range(4):
                    c = half * 4 + q
                    TR(ps[pb][:, q * 128:(q + 1) * 128], xin[b][:, c * 128:(c + 1) * 128], ident,
                       ['xin%d' % b, 'ident'], ['ps%d' % pb])
                src = ps[pb][:].rearrange("p (q t) -> p q t", q=4)
                d_ = dst[:, half * 4:(half + 1) * 4, sub * 128:(sub + 1) * 128]
                CP('act' if half == 0 else 'dve', d_, src, ['ps%d' % pb],
                   [(dstkey, half * 4 + q) for q in range(4)])

    def rmsnorm_stats(B, src, srckey):
        sq, tmp, rstd = B['sq'], B['tmp'], B['rstd']
        pb = cnt['ps01'] % 2
        cnt['ps01'] += 1
        for c in range(8):
            ACT(sq[:, c, :], src[:, c, :], AF.Square, [(srckey, c)], [('sq', c)])
            MM(ps[pb][:], ones, sq[:, c, :], c == 0, c == 7, [('sq', c), 'ones'], ['ps%d' % pb])
        ACT(tmp, ps[pb][:], AF.Sqrt, ['ps%d' % pb, 'eps'], ['tmp'], bias=eps_t, scale=1.0 / D)
        RECIP(rstd, tmp, ['tmp'], ['rstd'])

    def prenorm(B, gidx):
        xT, hn, rstd = B['xT'], B['hn'], B['rstd']
        rmsnorm_stats(B, xT, 'xT')
        for c in range(8):
            STT(hn[:, c, :], xT[:, c, :], gv[:, gidx, c:c + 1], rstd, ALU.mult, ALU.mult,
                [('xT', c), 'gv', 'rstd'], [('hn', c)])

    def ffn(B, k, gpre, gpost, first):
        xT, hn, fb, aT, rstd = B['xT'], B['hn'], B['fb'], B['aT'], B['rstd']
        wgu, wdc, sgl = B['wgu'], B['wdc'], B['sgl']
        prenorm(B, gpre)

        def ld_gu(j):
            b = j % 3
            flat = wgu[b].rearrange("p a c f -> p (a c f)")
            if first:
                DMA('pool', wgu[b], wgu_d[k][j], [], ['wgu%d' % b])
                DMA('sp', wgu16_d[k][j], flat, ['wgu%d' % b], [('wgu16', k, j)])
            else:
                DMA('sp', flat, wgu16_d[k][j], [('wgu16', k, j)], ['wgu%d' % b])

        def ld_d(c):
            b = c % 3
            flat = wdc[b].rearrange("p j f -> p (j f)")
            if first:
                DMA('pool', wdc[b], wdc_d[k][c], [], ['wdc%d' % b])
                DMA('sp', wdc16_d[k][c], flat, ['wdc%d' % b], [('wdc16', k, c)])
            else:
                DMA('sp', flat, wdc16_d[k][c], [('wdc16', k, c)], ['wdc%d' % b])
        ld_gu(0)
        ld_gu(1)
        for j in range(NF):
            if j + 2 < NF:
                ld_gu(j + 2)
            if j == NF - 3:
                ld_d(0)
            if j == NF - 1:
                ld_d(1)
            b = j % 3
            pg = 2 + (j % 2) * 2
            pu = pg + 1
            for which, pbank in ((0, pg), (1, pu)):
                for c in range(8):
                    MM(ps[pbank][:], wgu[b][:, which, c, :], hn[:, c, :], c == 0, c == 7,
                       ['wgu%d' % b, ('hn', c)], ['ps%d' % pbank])
            sgb = j % 2
            ACT(sgl[sgb], ps[pg][:], AF.Silu, ['ps%d' % pg], ['sgl%d' % sgb])
            TTo('dve', aT[:, j, :], sgl[sgb], ps[pu][:], ALU.mult, ['sgl%d' % sgb, 'ps%d' % pu], [('aT', j)])
        for c in range(8):
            if c + 2 < 8:
                ld_d(c + 2)
            b = c % 3
            pf = 6 + (c % 2)
            for j in range(NF):
                MM(ps[pf][:], wdc[b][:, j, :], aT[:, j, :], j == 0, j == NF - 1,
                   ['wdc%d' % b, ('aT', j)], ['ps%d' % pf])
            CP('act' if c % 2 == 0 else 'dve', fb[:, c, :], ps[pf][:], ['ps%d' % pf], [('fb', c)])
        rmsnorm_stats(B, fb, 'fb')
        for c in range(8):
            ACT(fb[:, c, :], fb[:, c, :], AF.Copy, [('fb', c), 'gvh'], [('fb', c)], scale=gvh[:, gpost, c:c + 1])
            TTo('pool', fb[:, c, :], fb[:, c, :], rstd, ALU.mult, [('fb', c), 'rstd'], [('fb', c)])
            TTo('dve', xT[:, c, :], fb[:, c, :], xT[:, c, :], ALU.add, [('fb', c), ('xT', c)], [('xT', c)])

    if 'p1' in phases:
        B = alloc_ffn_bufs()
        winb = [A.alloc([128, 8, 128], BF16) for _ in range(4)]
        wvb = A.alloc([128, 8, 512], BF16)
        cosb = A.alloc([128, TT])
        sinb = A.alloc([128, TT])
        ra = A.alloc([128, TT])
        rb = A.alloc([128, TT])
        qst = [A.alloc([128, TT], BF16) for _ in range(2)]
        vst = [A.alloc([128, 8, 65], BF16) for _ in range(2)]
        zst = [A.alloc([128, TT]) for _ in range(2)]
        xT, hn = B['xT'], B['hn']
        DMA('pool', wvb, wv_d, [], ['wvb'])
        for b in range(2):
            MEMSET('pool', vst[b], 1.0, ['vst%d' % b])
        zbank = [2, 3, 4, 5]
        for tt in range(min(ntiles1, 4)):
            load_T(B, x_d, tt, xT, 'xT', loads_issued=(tt > 0))
            ffn(B, 0, 0, 1, tt == 0)
            if tt < 4:
                DMA('sp', h1s_d[:, :, tt * TT:(tt + 1) * TT].rearrange("c p t -> p c t"), xT, [('xT', c_) for c_ in range(8)], ['h1s'])
            prenorm(B, 2)
            if tt + 1 < min(ntiles1, 4):
                issue_x_loads(B, x_d, tt + 1)
            if tt < 6:
                DMA('sp', cosb, cos_d[:, tt * TT:(tt + 1) * TT], [], ['cosb'])
                DMA('sp', sinb, sin_d[:, tt * TT:(tt + 1) * TT], [], ['sinb'])
            jobs = []
            if tt < 4:
                jobs += [('q', g, [2 * g, 2 * g + 1]) for g in range(4)]
            if tt < 6:
                jobs += [('k', g, [8 + 2 * g, 9 + 2 * g]) for g in range(4)]
            jobs += [('z', j, [16 + j]) for j in range(15)]
            loads = [ci for (_, _, cis) in jobs for ci in cis]

            def ldw(n):
                if n < len(loads):
                    b = n % 4
                    flat = winb[b].rearrange("p c f -> p (c f)")
                    if tt == 0:
                        DMA('pool', winb[b], win_d[loads[n]], [], ['winb%d' % b])
                        DMA('sp', win16_d[loads[n]], flat, ['winb%d' % b], [('win16', loads[n])])
                    else:
                        DMA('sp', flat, win16_d[loads[n]], [('win16', loads[n])], ['winb%d' % b])
            for n in range(3):
                ldw(n)
            li = 0
            for (kind, g, cis) in jobs:
                banks = []
                for ci in cis:
                    ldw(li + 3)
                    b = li % 4
                    li += 1
                    zb = zbank[cnt['z'] % 4]
                    cnt['z'] += 1
                    banks.append(zb)
                    for c in range(8):
                        MM(ps[zb][:], winb[b][:, c, :], hn[:, c, :], c == 0, c == 7,
                           ['winb%d' % b, ('hn', c)], ['ps%d' % zb])
                sb_ = cnt['st'] % 2
                cnt['st'] += 1
                if kind in ('q', 'k'):
                    TTo('dve', ra, ps[banks[0]][:], cosb, ALU.mult, ['ps%d' % banks[0], 'cosb'], ['ra'])
                    TTo('dve', rb, ps[banks[1]][:], sinb, ALU.mult, ['ps%d' % banks[1], 'sinb'], ['rb'])
                    TTo('pool', qst[sb_], ra, rb, ALU.add, ['ra', 'rb'], ['qst%d' % sb_])
                    dst = qT_d if kind == 'q' else kT_d
                    DMA('sp', dst[g, :, tt * TT:(tt + 1) * TT], qst[sb_], ['qst%d' % sb_], ['qk_d'])
                    if kind == 'k' and tt >= 2:
                        DMA('sp', kh_d[g * 128:(g + 1) * 128, (tt - 2) * TT:(tt - 1) * TT], qst[sb_],
                            ['qst%d' % sb_], ['kh_d'])
                else:
                    CP('act' if g % 2 == 0 else 'dve', zst[sb_], ps[banks[0]][:], ['ps%d' % banks[0]],
                       ['zst%d' % sb_])
                    DMA('sp', zr_d[g, :, tt * TT:(tt + 1) * TT], zst[sb_], ['zst%d' % sb_], ['zr'])
                    if tt == 3:
                        P.add('sp', (lambda e, g=g, sb_=sb_: e.dma_start(
                            out=zb_d[:, g:g + 1], in_=zst[sb_][:, TT - 1:TT], allow_slow_non_contiguous=True)),
                            r=['zst%d' % sb_], w=['zb_d'], dma=True)
            if tt < 6:
                for sub in range(4):
                    zb = zbank[cnt['z'] % 4]
                    cnt['z'] += 1
                    for c in range(8):
                        MM(ps[zb][:], hn[:, c, sub * 128:(sub + 1) * 128], wvb[:, c, :], c == 0, c == 7,
                           [('hn', c), 'wvb'], ['ps%d' % zb])
                    sb_ = cnt['st'] % 2
                    cnt['st'] += 1
                    CP('act' if sub % 2 == 0 else 'dve', vst[sb_][:, :, 0:64],
                       ps[zb][:].rearrange("p (h d) -> p h d", h=8), ['ps%d' % zb], ['vst%d' % sb_])
                    r0 = (tt * 4 + sub) * 128
                    DMA('sp', v_d[r0:r0 + 128, :], vst[sb_].rearrange("p h e -> p (h e)"),
                        ['vst%d' % sb_], ['v_d'])
                    if tt >= 2:
                        DMA('sp', vh_d[r0 - 1024:r0 - 1024 + 128, :], vst[sb_].rearrange("p h e -> p (h e)"),
                            ['vst%d' % sb_], ['vh_d'])
        P.barrier()
        A.off = A_MARK
        CC(kh_d, khg_d, ['kh_d'], ['khg'])
        CC(vh_d, vhg_d, ['vh_d'], ['vhg'])
        CC(zb_d, zbg_d, ['zb_d'], ['zbg'])

    if 'att' in phases:
        qT = A.alloc([128, 4, OWN], BF16)
        kT = A.alloc([128, 4, NKT * 128], BF16)
        va = A.alloc([128, NKT, 520], BF16)
        wm = A.alloc([128, 20, 512], BF16)
        wmf = A.alloc([128, 20, 512], BF16)
        khr = A.alloc([128, 2, 4, 1024], BF16)
        vhr = A.alloc([128, 2, 8, 520], BF16)
        kh = A.alloc([128, 4, 1024], BF16)
        vh = A.alloc([128, 8, 520], BF16)
        oall = A.alloc([128, 16, 512])
        eb = [A.alloc([128, 512], BF16) for _ in range(4)]
        pm = [A.alloc([128, 512], BF16) for _ in range(4)]
        rden = [A.alloc([128, 4]) for _ in range(2)]
        oTs = [A.alloc([65, 512]) for _ in range(2)]
        sq32 = A.alloc([128, 512])
        ss = A.alloc([128, 8])
        rs = A.alloc([128, 8])
        DMA('sp', qT, qT_d.rearrange("g p t -> p g t"), ['qk_d'], ['qT'])
        DMA('sp', kT, kT_d.rearrange("g p t -> p g t"), ['qk_d'], ['kT'])
        DMA('sp', va, v_d.rearrange("(n p) f -> p n f", p=128), ['v_d'], ['va'])
        DMA('sp', wm, wm_d.rearrange("j p c -> p j c"), [], ['wm'])
        DMA('sp', wmf, wmf_d.rearrange("j p c -> p j c"), [], ['wmf'])
        def load_halo():
            DMA('sp', khr, khg_d.rearrange("(r g p) t -> p r g t", r=2, g=4), ['khg'], ['khr'])
            DMA('sp', vhr, vhg_d.rearrange("(r n p) f -> p r n f", r=2, n=8), ['vhg'], ['vhr'])
            for (raw, dst, key, n_) in ((khr, kh, 'kh', 4096), (vhr, vh, 'vh', 4160)):
                r0_ = raw[:, 0].rearrange("p a b -> p (a b)")
                r1_ = raw[:, 1].rearrange("p a b -> p (a b)")
                d_ = dst.rearrange("p a b -> p (a b)")
                TS('dve', d_, r0_, hsel[:, 0:1], None, ALU.mult, None, [key + 'r', 'hsel'], [key])
                STT(d_, r1_, hsel[:, 1:2], d_, ALU.mult, ALU.add, [key + 'r', 'hsel', key], [key])

        items = []
        for qb in (0, 1, 2, 3):
            for cpair in range(4):
                kts = list(range(max(0, 4 * qb - 8), 4 * qb + 12))
                for kt in kts:
                    for hh in (2 * cpair, 2 * cpair + 1):
                        items.append(dict(hh=hh, qb=qb, kt=kt, J=kt - 4 * qb + 8, ob=6 + hh % 2,
                                          first=(kt == kts[0]), last=(kt == kts[-1]), endblk=(kt == kts[-1]),
                                          blk=hh % 2))
        NB = 4
        SKEW = 2

        def stage_a(n):
            d = items[n]
            hh, qb, kt, J = d['hh'], d['qb'], d['kt'], d['J']
            g, base = hh // 2, 64 * (hh % 2)
            sbk = 2 + (n % NB)
            b = n % NB
            if kt < NKT:
                kop, kkey, mk, mkey_ = kT[base:base + 64, g, kt * 128:(kt + 1) * 128], 'kT', wm, 'wm'
            else:
                ht = 23 - kt
                kop, kkey, mk, mkey_ = kh[base:base + 64, g, ht * 128:(ht + 1) * 128], 'kh', wmf, 'wmf'
            MM(ps[sbk][:], kop, qT[base:base + 64, g, qb * 512:(qb + 1) * 512], True, True, [kkey, 'qT'],
               ['ps%d' % sbk])
            ACT(eb[b], ps[sbk][:], AF.Exp, ['ps%d' % sbk], ['eb%d' % b], scale=0.125)
            TTo('dve', pm[b], eb[b], mk[:, J, :], ALU.mult, ['eb%d' % b, mkey_], ['pm%d' % b])

        def stage_b(n):
            d = items[n]
            hh, qb, kt, J, ob = d['hh'], d['qb'], d['kt'], d['J'], d['ob']
            b = n % NB
            vop, vkey = (va[:, kt, hh * 65:(hh + 1) * 65], 'va') if kt < NKT else \
                (vh[:, 23 - kt, hh * 65:(hh + 1) * 65], 'vh')
            MM(ps[ob][0:65, :], vop, pm[b], d['first'], d['last'], ['pm%d' % b, vkey], ['ps%d' % ob])
            if d['endblk']:
                rb_ = d['blk']
                tbk = rb_
                CP('act', oTs[rb_][0:65, :], ps[ob][0:65, :], ['ps%d' % ob], ['oT%d' % rb_])
                for i in range(4):
                    TR(ps[tbk][:, i * 65:(i + 1) * 65], oTs[rb_][0:65, i * 128:(i + 1) * 128], ident[0:65, 0:65],
                       ['oT%d' % rb_, 'ident'], ['ps%d' % tbk])
                o4 = ps[tbk][:, 0:260].rearrange("p (i e) -> p i e", e=65)
                RECIP(rden[rb_].unsqueeze(2), o4[:, :, 64:65], ['ps%d' % tbk], ['rden%d' % rb_])
                TTo('dve', oall[:, qb * 4:(qb + 1) * 4, hh * 64:(hh + 1) * 64], o4[:, :, 0:64],
                    rden[rb_].unsqueeze(2).broadcast_to([128, 4, 64]), ALU.mult,
                    ['ps%d' % tbk, 'rden%d' % rb_], [('oall', qb)])
        first_halo = min(n for n, d in enumerate(items) if d['kt'] >= NKT)
        for n in range(0, len(items) + SKEW, 2):
            if n <= first_halo < n + 2:
                load_halo()
            for m in (n, n + 1):
                if m < len(items):
                    stage_a(m)
            for m in (n - SKEW, n + 1 - SKEW):
                if 0 <= m < len(items):
                    stage_b(m)
        for qt in range(16):
            o = oall[:, qt, :]
            o3 = o.rearrange("p (h d) -> p h d", h=8)
            ACT(sq32, o, AF.Square, [('oall', qt // 4)], ['sq32'])
            P.add('dve', lambda e: e.tensor_reduce(out=ss, in_=sq32.rearrange("p (h d) -> p h d", h=8),
                                                   axis=AX.X, op=ALU.add), r=['sq32'], w=['ss'])
            ACT(rs, ss, AF.Sqrt, ['ss', 'eps'], ['rs'], bias=eps_t, scale=1.0 / 64)
            RECIP(rs, rs, ['rs'], ['rs'])
            TTo('dve', o3, o3, rs.unsqueeze(2).broadcast_to([128, 8, 64]), ALU.mult,
                [('oall', qt // 4), 'rs'], [('oall', qt // 4)])
            TTo('dve', o, o, rep[:, 0, :], ALU.mult, [('oall', qt // 4), 'rep'], [('oall', qt // 4)])
            DMA('sp', yc_d[qt * 128:(qt + 1) * 128, 0:512], o, [('oall', qt // 4)], ['yc'])
        P.barrier()
        A.off = A_MARK

    if 'rwkv' in phases:
        rwkv_phase(E, rwkv_stop)
        P.barrier()
        A.off = A_MARK

    if 'p4' in phases:
        B = alloc_ffn_bufs()
        wob = [A.alloc([128, 8, 128], BF16) for _ in range(2)]
        ost = [A.alloc([128, D]) for _ in range(2)]
        xT, hn, fb, rstd = B['xT'], B['hn'], B['fb'], B['rstd']
        for tt in range(4):
            load_T(B, yc_d, tt, hn, 'hn', loads_issued=(tt > 0))
            def ld_o(dc):
                b_ = dc % 2
                flat = wob[b_].rearrange("p c f -> p (c f)")
                if tt == 0:
                    DMA('pool', wob[b_], wo_d[dc], [], ['wob%d' % b_])
                    DMA('sp', wo16_d[dc], flat, ['wob%d' % b_], [('wo16', dc)])
                else:
                    DMA('sp', flat, wo16_d[dc], [('wo16', dc)], ['wob%d' % b_])
            ld_o(0)
            for dc in range(8):
                if dc + 1 < 8:
                    ld_o(dc + 1)
                b = dc % 2
                pf = 6 + (dc % 2)
                for cc in range(8):
                    MM(ps[pf][:], wob[b][:, cc, :], hn[:, cc, :], cc == 0, cc == 7,
                       ['wob%d' % b, ('hn', cc)], ['ps%d' % pf])
                CP('act' if dc % 2 == 0 else 'dve', fb[:, dc, :], ps[pf][:], ['ps%d' % pf], [('fb', dc)])
            if tt + 1 < 4:
                issue_x_loads(B, yc_d, tt + 1)
            rmsnorm_stats(B, fb, 'fb')
            DMA('sp', xT, h1s_d[:, :, tt * TT:(tt + 1) * TT].rearrange("c p t -> p c t"), ['h1s'], [('xT', c_) for c_ in range(8)])
            for c in range(8):
                ACT(fb[:, c, :], fb[:, c, :], AF.Copy, [('fb', c), 'gv'], [('fb', c)], scale=gv[:, 3, c:c + 1])
                TTo('pool', fb[:, c, :], fb[:, c, :], rstd, ALU.mult, [('fb', c), 'rstd'], [('fb', c)])
                TTo('dve', xT[:, c, :], fb[:, c, :], xT[:, c, :], ALU.add, [('fb', c), ('xT', c)], [('xT', c)])
            ffn(B, 1, 4, 5, tt == 0)
            for sub in range(4):
                ob = cnt['st'] % 2
                cnt['st'] += 1
                for half in range(2):
                    pb = cnt['ps01'] % 2
                    cnt['ps01'] += 1
                    for q in range(4):
                        c = half * 4 + q
                        TR(ps[pb][:, q * 128:(q + 1) * 128], xT[:, c, sub * 128:(sub + 1) * 128], ident,
                           [('xT', c), 'ident'], ['ps%d' % pb])
                    CP('act' if half == 0 else 'dve', ost[ob][:, half * 512:(half + 1) * 512], ps[pb][:],
                       ['ps%d' % pb], ['ost%d' % ob])
                r0 = tt * TT + sub * 128
                DMA('sp', out_d[r0:r0 + 128, :], ost[ob], ['ost%d' % ob], ['out'])

    P.add('sp', None, r=['h1s', 'out', 'yc', 'zr', 'qk_d', 'v_d', 'yf'])
    P.emit(nc, stack)
    stack.close()
    return nc, P


def rwkv_phase(E, debug_stop=None):
    P, A, ps = E.P, E.A, E.ps
    MM, TR, ACT, TTo, TS, STT, CP, RECIP, DMA, MEMSET = (E.MM, E.TR, E.ACT, E.TTo, E.TS, E.STT, E.CP, E.RECIP,
                                                          E.DMA, E.MEMSET)
    ident, identb, rep, rwm, rwc, rwp, lora = E.ident, E.identb, E.rep, E.rwm, E.rwc, E.rwp, E.lora
    eps_t, gneps_t = E.eps_t, E.gneps_t
    zr_d, yc_d, yf_d = E.zr_d, E.yc_d, E.yf_d

    rmask = rwc[:, 0:512]
    blockones = rwc[:, 1024:1152]
    identblk = rwc[:, 1152:1216]
    hsel = rwc[:, 1216:1218]
    mp, mn = rwp[:, 0:15], rwp[:, 15:30]
    k_k, k_a, r_k = rwp[:, 46:50], rwp[:, 50:54], rwp[:, 54:58]
    w2sb, a2sb, g2sb = lora[:, 0, :], lora[:, 1, :], lora[:, 2, :]

    def w0T(di, c):
        return rwp[:, 30 + 4 * di + c:31 + 4 * di + c]

    def a0T(di, c):
        return rwp[:, 38 + 4 * di + c:39 + 4 * di + c]

    c0 = A.alloc([128, 15])
    omka = A.alloc([128, 4])
    rk2 = A.alloc([128, 4])
    GW = 256
    NTG = GW // 128
    NG = 4096 // GW
    NG_OWN = 2048 // GW
    zsb = [A.alloc([128, GW + 2]) for _ in range(3)]
    zcnt = {'z': 0}
    ub = A.alloc([128, 15, GW])
    tw = A.alloc([128, GW])
    T_ = {nm: A.alloc([128, GW]) for nm in
          ('sg', 'av', 'kx', 't1', 't2', 'kkv', 'kd', 'ka', 'cs', 'cs2', 'e0', 'e1', 'e2', 'e3', 'kd0')}
    tot_s = A.alloc([128, GW // 64])
    OB = []
    for _ in range(2):
        d_ = {nm: A.alloc([128, 4, GW], BF16) for nm in ('Rb', 'Kb', 'Kd', 'Ad', 'Kh', 'Ahn', 'vb')}
        d_['gc'] = A.alloc([128, 4, GW // 64])
        d_['bprod'] = A.alloc([128, 4, GW])
        d_['sgd'] = A.alloc([128, GW])
        d_['uvf'] = A.alloc([128, 4, GW])
        OB.append(d_)
    KbT, KhT, AhT, VT = [A.alloc([128, 512], BF16) for _ in range(4)]
    VT32s = [A.alloc([128, 512]) for _ in range(2)]
    gTs = [A.alloc([128, 512]) for _ in range(2)]
    bons = [A.alloc([128, 8]) for _ in range(2)]
    fcnt = {'n': 0}
    bg = {'final': None}
    Xs = [[A.alloc([128, 4, 128], BF16) for _ in range(2)] for _ in range(2)]
    Ys = [[A.alloc([128, 4, 128], BF16) for _ in range(2)] for _ in range(2)]
    Rms = [[A.alloc([128, 4, 128], BF16) for _ in range(2)] for _ in range(2)]
    pkks, qrks, qras = [[A.alloc([128, 4, 128], BF16) for _ in range(2)] for _ in range(3)]
    pvs = [A.alloc([128, 4, 64], BF16) for _ in range(2)]
    u0 = A.alloc([128, 8, 64], BF16)
    wt = A.alloc([128, 8, 64], BF16)
    y0 = A.alloc([128, 512])
    rpT = A.alloc([128, 4, 128])
    GT = A.alloc([128, 4, 2, 64])
    Hs = A.alloc([128, 4, 2, 64])
    Tst = [A.alloc([128, 4, 64]) for _ in range(2)]
    ytiles = [A.alloc([128, 512]) for _ in range(2)]
    yfbs = [A.alloc([128, 512]) for _ in range(2)]
    sqb = A.alloc([128, 512])
    tmpv = A.alloc([128, 512])
    s1 = A.alloc([128, 8])
    s2 = A.alloc([128, 8])
    psb2 = ps[2][:].bitcast(BF16)

    TTo('dve', c0, mp, mn, ALU.add, ['rwp'], ['c0'])
    TS('dve', c0, c0, -1.0, 1.0, ALU.mult, ALU.add, ['c0'], ['c0'])
    TS('dve', omka, k_a, -1.0, 1.0, ALU.mult, ALU.add, ['rwp'], ['omka'])
    TS('dve', rk2, r_k, 0.5, None, ALU.mult, None, ['rwp'], ['rk2'])
    zr2 = A.alloc([128, 2, 15])
    zedge = A.alloc([128, 15])
    zg = E.zbg_d.rearrange("(r p) c -> p r c", r=2)
    DMA('sp', zr2, zg, ['zbg'], ['zr2'])
    DMA('sp', zr2[0:64, :, 12:14], zg[64:128, :, 12:14], ['zbg'], ['zr2'])
    DMA('sp', zr2[64:128, :, 12:14], zg[0:64, :, 12:14], ['zbg'], ['zr2'])
    TS('dve', zedge, zr2[:, 0, :], E.hsel[:, 0:1], None, ALU.mult, None, ['zr2', 'hsel'], ['zedge'])
    STT(zedge, zr2[:, 1, :], E.hsel[:, 1:2], zedge, ALU.mult, ALU.add, ['zr2', 'hsel', 'zedge'], ['zedge'])
    tfr = A.alloc([128, 2, 256])

    pcnt = {'pa': 0}

    def pbank(lo=0):
        b = (6 if lo == 0 else 0) + pcnt['pa'] % 2
        pcnt['pa'] += 1
        return b

    def prep_group(g, di, own, final, pb):
        O_ = OB[pb]
        Rb, Kb, Kd, Ad, Kh, Ahn, vb = (O_[n] for n in ('Rb', 'Kb', 'Kd', 'Ad', 'Kh', 'Ahn', 'vb'))
        gc, bprod, sgd, uvf = O_['gc'], O_['bprod'], O_['sgd'], O_['uvf']
        lo, hi = GW * g - 1, GW * g + GW + 1
        clo, chi = max(lo, 0), min(hi, 2048)
        need = list(range(4, 12)) + [12, 13] + ([0, 1, 2, 3] if own else []) + ([14] if final else [])
        for n_, j in enumerate(need):
            zb = zsb[zcnt['z'] % 3]
            zk = 'zsb%d' % (zcnt['z'] % 3)
            zcnt['z'] += 1
            DMA('sp', zb[:, clo - lo:GW + 2 - (hi - chi)], zr_d[j, :, clo:chi], ['zr'], [zk])
            if g == 0:
                MEMSET('pool', zb[:, 0:1], 0.0, [zk])
            if g == NG_OWN - 1:
                CP('pool', zb[:, GW + 1:GW + 2], zedge[:, j:j + 1], ['zedge'], [zk])
            k = ('ub', j)
            ACT(ub[:, j, :], zb[:, 0:GW], AF.Copy, [zk, 'rwp'], [k], scale=mp[:, j:j + 1])
            STT(ub[:, j, :], zb[:, 2:GW + 2], mn[:, j:j + 1], ub[:, j, :], ALU.mult, ALU.add, [zk, 'rwp', k], [k])
            STT(ub[:, j, :], zb[:, 1:GW + 1], c0[:, j:j + 1], ub[:, j, :], ALU.mult, ALU.add, [zk, 'c0', k], [k])
            yield
        ACT(tw, ub[:, 12, :], AF.Tanh, [('ub', 12)], ['tw'])
        if final:
            ACT(sgd, ub[:, 14, :], AF.Sigmoid, [('ub', 14)], [('sgd', pb)])
        sg, av, kx, t1, t2, kkv, kd, ka = (T_[n] for n in ('sg', 'av', 'kx', 't1', 't2', 'kkv', 'kd', 'ka'))
        cs, cs2, e0, e1, e2, e3, kd0 = (T_[n] for n in ('cs', 'cs2', 'e0', 'e1', 'e2', 'e3', 'kd0'))
        lo64 = 64 * di
        NC64 = GW // 64
        for c in range(4):
            ku = ub[:, 4 + c, :]
            kkey = ('ub', 4 + c)
            cc = slice(c * 128, (c + 1) * 128)
            ACT(kx, ku, AF.Copy, [kkey, 'rwp'], ['kx'], scale=k_k[:, c:c + 1])
            ACT(t1, kx, AF.Square, ['kx'], ['t1'])
            b = pbank()
            MM(ps[b][:, 0:GW], blockones, t1, True, True, ['rwc', 't1'], ['ps%d' % b])
            ACT(t1, ps[b][:, 0:GW], AF.Sqrt, ['ps%d' % b], ['t1'])
            yield
            TS('dve', t1, t1, 1e-12, None, ALU.max, None, ['t1'], ['t1'])
            RECIP(t1, t1, ['t1'], ['t1'])
            TTo('dve', kkv, kx, t1, ALU.mult, ['kx', 't1'], ['kkv'])
            b = pbank(lo64)
            MM(ps[b][:, 0:GW], w2sb[lo64:lo64 + 64, cc], tw[lo64:lo64 + 64, :], True, True, ['lora', 'tw'],
               ['ps%d' % b])
            ACT(sg, ps[b][:, 0:GW], AF.Sigmoid, ['ps%d' % b, 'rwp'], ['sg'], bias=w0T(di, c))
            yield
            b = pbank(lo64)
            MM(ps[b][:, 0:GW], a2sb[lo64:lo64 + 64, cc], ub[lo64:lo64 + 64, 13, :], True, True,
               ['lora', ('ub', 13)], ['ps%d' % b])
            ACT(av, ps[b][:, 0:GW], AF.Sigmoid, ['ps%d' % b, 'rwp'], ['av'], bias=a0T(di, c))
            ACT(t2, av, AF.Identity, ['av', 'rwp', 'omka'], ['t2'], bias=omka[:, c:c + 1], scale=k_a[:, c:c + 1])
            TTo('dve', kd, ku, t2, ALU.mult, [kkey, 't2'], ['kd'])
            TTo('pool', ka, kkv, av, ALU.mult, ['kkv', 'av'], ['ka'])
            yield
            if final:
                od = 1 - di
                b = pbank(64 * od)
                MM(ps[b][:, 0:GW], a2sb[64 * od:64 * od + 64, cc], ub[64 * od:64 * od + 64, 13, :], True, True,
                   ['lora', ('ub', 13)], ['ps%d' % b])
                ACT(kd0, ps[b][:, 0:GW], AF.Sigmoid, ['ps%d' % b, 'rwp'], ['kd0'], bias=a0T(od, c))
                TS('dve', kd0, kd0, k_a[:, c:c + 1], omka[:, c:c + 1], ALU.mult, ALU.add,
                   ['kd0', 'rwp', 'omka'], ['kd0'])
                TTo('dve', kd0, kd0, ku, ALU.mult, ['kd0', kkey], ['kd0'])
                yield
                TTo('dve', kd0, kd0, kd, ALU.add, ['kd0', 'kd'], ['kd0'])
                TTo('dve', kd0, kd0, ub[:, c, :], ALU.mult, ['kd0', ('ub', c)], ['kd0'])
                TS('dve', bprod[:, c, :], kd0, rk2[:, c:c + 1], None, ALU.mult, None, ['kd0', 'rk2'],
                   [('bprod', c, pb)])
                CP('pool', uvf[:, c, :], ub[:, 8 + c, :], [('ub', 8 + c)], [('uvf', c, pb)])
                yield
            P.add('dve', lambda e, cs=cs, sg=sg: e.tensor_tensor_scan(out=cs, data0=rmask[:, 0:GW], data1=sg,
                                                                      initial=0.0, op0=ALU.mult, op1=ALU.add),
                  r=['rwc', 'sg'], w=['cs'])
            CP('dve', tot_s, cs[:, 63::64], ['cs'], ['tot'])
            totb = tot_s.unsqueeze(2).broadcast_to([128, NC64, 64])
            v3 = lambda ap: ap.rearrange("p (a b) -> p a b", a=NC64)
            if di == 0:
                csx, cskey = cs, 'cs'
            else:
                TTo('dve', t2, sg, cs, ALU.subtract, ['sg', 'cs'], ['t2'])
                TTo('dve', v3(cs2), v3(t2), totb, ALU.add, ['t2', 'tot'], ['cs2'])
                csx, cskey = cs2, 'cs2'
            yield
            TTo('dve', e0, csx, sg, ALU.subtract, [cskey, 'sg'], ['e0'])
            TTo('dve', v3(e3), totb, v3(csx), ALU.subtract, ['tot', cskey], ['e3'])
            ACT(e1, csx, AF.Exp, [cskey], ['e1'], scale=-CDEC)
            ACT(e2, csx, AF.Exp, [cskey], ['e2'], scale=CDEC)
            yield
            ACT(e0, e0, AF.Exp, ['e0'], ['e0'], scale=-CDEC)
            ACT(e3, e3, AF.Exp, ['e3'], ['e3'], scale=-CDEC)
            ACT(gc[:, c, :], tot_s, AF.Exp, ['tot'], [('gc', c, pb)], scale=-CDEC)
            yield
            if own:
                TTo('pool', Rb[:, c, :], ub[:, c, :], e1, ALU.mult, [('ub', c), 'e1'], [('Rb', c, pb)])
            TTo('pool', Kb[:, c, :], kkv, e0, ALU.mult, ['kkv', 'e0'], [('Kb', c, pb)])
            TTo('pool', Kd[:, c, :], kd, e2, ALU.mult, ['kd', 'e2'], [('Kd', c, pb)])
            yield
            TTo('pool', Ad[:, c, :], ka, e2, ALU.mult, ['ka', 'e2'], [('Ad', c, pb)])
            TTo('pool', Kh[:, c, :], kd, e3, ALU.mult, ['kd', 'e3'], [('Kh', c, pb)])
            TS('pool', ka, ka, -1.0, 0.0, ALU.mult, ALU.add, ['ka'], ['ka'])
            TTo('pool', Ahn[:, c, :], ka, e3, ALU.mult, ['ka', 'e3'], [('Ahn', c, pb)])
            CP('pool', vb[:, c, :], ub[:, 8 + c, :], [('ub', 8 + c)], [('vb', c, pb)])
            yield

    M = lambda di, k: rwm[:, di, k, :].unsqueeze(1).broadcast_to([128, 4, 128])

    def tile_proc(di, g, tl, outp, final, cur, pb, pump):
        cols = slice(tl * 128, (tl + 1) * 128)
        gt = NTG * g + tl
        O_ = OB[pb]
        Rb, Kb, Kd, Ad, Kh, Ahn, vb = (O_[n] for n in ('Rb', 'Kb', 'Kd', 'Ad', 'Kh', 'Ahn', 'vb'))
        gc, bprod, sgd, uvf = O_['gc'], O_['bprod'], O_['sgd'], O_['uvf']
        allk = lambda nm: [(nm, c, pb) for c in range(4)]
        fb_ = fcnt['n'] % 2 if final else 0
        ytile, ykey = ytiles[fb_], 'ytile%d' % fb_
        VT32, gT, bon, yfb = VT32s[fb_], gTs[fb_], bons[fb_], yfbs[fb_]
        slot_of = lambda hh: 4 * (hh % 2) + hh // 2
        for (src, dstT, nm, eng) in ((Kb, KbT, 'Kb', 'act'), (Kh, KhT, 'Kh', 'dve'), (Ahn, AhT, 'Ahn', 'act'),
                                      (vb, VT, 'vb', 'dve')):
            for c in range(4):
                TR(psb2[:, c * 128:(c + 1) * 128], src[:, c, cols], identb, [(nm, c, pb), 'identb'], ['ps2'])
            CP(eng, dstT, psb2[:, 0:512], ['ps2'], [nm + 'T'])
        if final:
            for c in range(4):
                TR(ps[2][:, c * 128:(c + 1) * 128], uvf[:, c, cols], ident, [('uvf', c, pb), 'ident'], ['ps2'])
            CP('act', VT32, ps[2][:], ['ps2'], [('VT32', fb_)])
            MM(ps[3][:], sgd[:, cols], g2sb, True, True, [('sgd', pb), 'lora'], ['ps3'])
            CP('act', gT, ps[3][:], ['ps3'], ['gT%d' % fb_])
            for c in range(4):
                MM(ps[2][:, 2 * c:2 * c + 2], bprod[:, c, cols], hsel, True, True, [('bprod', c, pb), 'rwc'], ['ps2'])
            CP('dve', bon, ps[2][:, 0:8], ['ps2'], ['bon%d' % fb_])
        def hg_stages(hg):
            BA, BB, BC = ((4, 5, 3), (1, 0, 2))[hg]
            X, Y, Rm = Xs[hg], Ys[hg], Rms[hg]
            pkk, qrk, qra, pv = pkks[hg], qrks[hg], qras[hg], pvs[hg]
            sx = 'g%d' % hg
            base = 64 * hg
            hl = [(2 * i + hg, i) for i in range(4)]

            def blk(bank, i):
                return ps[bank][:, i * 128:(i + 1) * 128]

            def b3(bank):
                return ps[bank][:].rearrange("p (a b) -> p a b", a=4)
            for i, (hh, c) in enumerate(hl):
                MM(blk(BA, i), Kb[base:base + 64, c, cols], Ad[base:base + 64, c, cols], True, True,
                   [('Kb', c, pb), ('Ad', c, pb)], ['ps%d' % BA])
            TTo('dve', X[0], b3(BA), M(di, 0), ALU.mult, ['ps%d' % BA, 'rwm'], ['X0' + sx])
            yield
            for i, (hh, c) in enumerate(hl):
                MM(blk(BB, i), Ad[base:base + 64, c, cols], Kb[base:base + 64, c, cols], True, True,
                   [('Kb', c, pb), ('Ad', c, pb)], ['ps%d' % BB])
            TTo('dve', Y[0], b3(BB), M(di, 1), ALU.mult, ['ps%d' % BB, 'rwm'], ['Y0' + sx])
            TTo('pool', Rm[0], Y[0], identb.unsqueeze(1).broadcast_to([128, 4, 128]), ALU.add,
                ['Y0' + sx, 'identb'], ['R0' + sx])
            yield
            for j in range(1, 6):
                a, b = (j - 1) % 2, j % 2
                for i in range(4):
                    MM(blk(BA, i), Y[a][:, i, :], X[a][:, i, :], True, True, ['X%d' % a + sx, 'Y%d' % a + sx], ['ps%d' % BA])
                CP('act', X[b], b3(BA), ['ps%d' % BA], ['X%d' % b + sx])
                yield
                if j < 5:
                    for i in range(4):
                        MM(blk(BB, i), X[a][:, i, :], Y[a][:, i, :], True, True, ['X%d' % a + sx, 'Y%d' % a + sx], ['ps%d' % BB])
                    CP('act', Y[b], b3(BB), ['ps%d' % BB], ['Y%d' % b + sx])
                    yield
                for i in range(4):
                    MM(blk(BC, i), identb, Rm[a][:, i, :], True, False, ['identb', 'R%d' % a + sx], ['ps%d' % BC])
                    MM(blk(BC, i), X[b][:, i, :], Rm[a][:, i, :], False, True, ['X%d' % b + sx, 'R%d' % a + sx], ['ps%d' % BC])
                CP('act', Rm[b], b3(BC), ['ps%d' % BC], ['R%d' % b + sx])
                yield
            MinvT, mkey = Rm[1], 'R1' + sx
            for i, (hh, c) in enumerate(hl):
                MM(blk(BA, i), Kd[base:base + 64, c, cols], Kb[base:base + 64, c, cols], True, True,
                   [('Kd', c, pb), ('Kb', c, pb)], ['ps%d' % BA])
            TTo('dve', pkk, b3(BA), M(di, 2), ALU.mult, ['ps%d' % BA, 'rwm'], ['pkk' + sx])
            yield
            if outp:
                for i, (hh, c) in enumerate(hl):
                    MM(blk(BB, i), Kd[base:base + 64, c, cols], Rb[base:base + 64, c, cols], True, True,
                       [('Kd', c, pb), ('Rb', c, pb)], ['ps%d' % BB])
                TTo('dve', qrk, b3(BB), M(di, 3), ALU.mult, ['ps%d' % BB, 'rwm'], ['qrk' + sx])
                yield
                for i, (hh, c) in enumerate(hl):
                    MM(blk(BC, i), Ad[base:base + 64, c, cols], Rb[base:base + 64, c, cols], True, True,
                       [('Ad', c, pb), ('Rb', c, pb)], ['ps%d' % BC])
                TTo('dve', qra, b3(BC), M(di, 4), ALU.mult, ['ps%d' % BC, 'rwm'], ['qra' + sx])
                yield
            for i, (hh, c) in enumerate(hl):
                MM(ps[BA][:, i * 64:(i + 1) * 64], pkk[:, i, :], VT[:, hh * 64:(hh + 1) * 64], True, True,
                   ['pkk' + sx, 'vbT'], ['ps%d' % BA])
            CP('act', pv, ps[BA][:, 0:256].rearrange("p (a b) -> p a b", a=4), ['ps%d' % BA], ['pv' + sx])
            yield
            for i, (hh, c) in enumerate(hl):
                MM(ps[BB][:, i * 64:(i + 1) * 64], MinvT[:, i, :], pv[:, i, :], True, True, [mkey, 'pv' + sx], ['ps%d' % BB])
            for i, (hh, c) in enumerate(hl):
                MM(ps[BC][:, i * 64:(i + 1) * 64], MinvT[:, i, :], KbT[:, hh * 64:(hh + 1) * 64],
                   True, True, [mkey, 'KbT'], ['ps%d' % BC])
            CP('act', u0[:, 4 * hg:4 * hg + 4, :], ps[BB][:, 0:256].rearrange("p (a b) -> p a b", a=4),
               ['ps%d' % BB], [('u0', hg)])
            CP('dve', wt[:, 4 * hg:4 * hg + 4, :], ps[BC][:, 0:256].rearrange("p (a b) -> p a b", a=4),
               ['ps%d' % BC], [('wt', hg)])
            yield
            if outp:
                for i, (hh, c) in enumerate(hl):
                    MM(ps[BB][:, i * 64:(i + 1) * 64], qrk[:, i, :], VT[:, hh * 64:(hh + 1) * 64], True, False,
                       ['qrk' + sx, 'vbT'], ['ps%d' % BB])
                    MM(ps[BB][:, i * 64:(i + 1) * 64], qra[:, i, :], u0[:, 4 * hg + i, :], False, True,
                       ['qra' + sx, ('u0', hg)], ['ps%d' % BB])
                CP('act', y0.rearrange("p (i q d) -> p i q d", i=4, q=2)[:, :, hg, :],
                   ps[BB][:, 0:256].rearrange("p (a b) -> p a b", a=4), ['ps%d' % BB], [('y0', hg)])
                yield
                for i, (hh, c) in enumerate(hl):
                    MM(ps[BA][base:base + 64, i * 128:(i + 1) * 128], wt[:, 4 * hg + i, :], qra[:, i, :],
                       True, True, [('wt', hg), 'qra' + sx], ['ps%d' % BA])
                TTo('dve', rpT[base:base + 64, :, :],
                    ps[BA][base:base + 64, :].rearrange("p (a b) -> p a b", a=4),
                    Rb[base:base + 64, :, cols], ALU.add, ['ps%d' % BA] + allk('Rb'), [('rpT', hg)])
            yield
        gens = [hg_stages(0), hg_stages(1)]
        alive = [True, True]
        while any(alive):
            pump(1)
            for k_ in (0, 1):
                if alive[k_]:
                    try:
                        next(gens[k_])
                    except StopIteration:
                        alive[k_] = False
        gbank = {0: 5, 1: 3}
        hbank = {0: 4, 1: 2}
        for ch2 in range(2):
            rows = slice(ch2 * 64, ch2 * 64 + 64)
            gb, hb = gbank[ch2], hbank[ch2]
            for hh in range(8):
                c, base = hh // 2, 64 * (hh % 2)
                sl = slot_of(hh)
                o5 = ps[gb][base:base + 64, c * 64:(c + 1) * 64]
                MM(o5, wt[rows, sl, :], AhT[rows, hh * 64:(hh + 1) * 64], True, True,
                   [('wt', hh % 2), 'AhnT'], ['ps%d' % gb])
                o4 = ps[hb][base:base + 64, c * 64:(c + 1) * 64]
                MM(o4, KhT[rows, hh * 64:(hh + 1) * 64], VT[rows, hh * 64:(hh + 1) * 64], True, False,
                   ['KhT', 'vbT'], ['ps%d' % hb])
                MM(o4, AhT[rows, hh * 64:(hh + 1) * 64], u0[rows, sl, :], False, True,
                   ['AhnT', ('u0', hh % 2)], ['ps%d' % hb])
        for ch2 in range(2):
            gb, hb = gbank[ch2], hbank[ch2]
            for c in range(4):
                slot = 2 * tl + ch2
                STT(GT[:, c, ch2, :], identblk, gc[:, c, slot:slot + 1], ps[gb][:, c * 64:(c + 1) * 64],
                    ALU.mult, ALU.add, ['rwc', ('gc', c, pb), 'ps%d' % gb], ['GT'])
            CP('act', Hs[:, :, ch2, :], ps[hb][:, 0:256].rearrange("p (a b) -> p a b", a=4),
               ['ps%d' % hb], ['Hs'])
        pump(2)
        order = [0, 1] if di == 0 else [1, 0]
        tb = {0: 6, 1: 0}
        yb_ = {0: 7, 1: 1}
        for ch2 in order:
            rows = slice(ch2 * 64, ch2 * 64 + 64)
            Tc, Tn = Tst[cur], Tst[1 - cur]
            kc, kn = 'T%d' % cur, 'T%d' % (1 - cur)
            for par in range(2):
                base = 64 * par
                for c in range(4):
                    hh = 2 * c + par
                    if outp:
                        MM(ps[yb_[par]][rows, hh * 64:(hh + 1) * 64],
                           rpT[base:base + 64, c, ch2 * 64:ch2 * 64 + 64],
                           Tc[base:base + 64, c, :], True, True, [('rpT', par), kc], ['ps%d' % yb_[par]])
                    MM(ps[tb[par]][base:base + 64, c * 64:(c + 1) * 64], GT[base:base + 64, c, ch2, :],
                       Tc[base:base + 64, c, :], True, True, ['GT', kc], ['ps%d' % tb[par]])
            for par in range(2):
                base = 64 * par
                if outp:
                    yv = ytile.rearrange("p (i q d) -> p i q d", i=4, q=2)
                    y0v = y0.rearrange("p (i q d) -> p i q d", i=4, q=2)
                    pyv = ps[yb_[par]][:].rearrange("p (i q d) -> p i q d", i=4, q=2)
                    TTo('dve', yv[rows, :, par, :], pyv[rows, :, par, :], y0v[rows, :, par, :], ALU.add,
                        ['ps%d' % yb_[par], ('y0', 0), ('y0', 1)], [ykey])
                TTo('dve', Tn[base:base + 64, :, :],
                    ps[tb[par]][base:base + 64, 0:256].rearrange("p (a b) -> p a b", a=4),
                    Hs[base:base + 64, :, ch2, :], ALU.add, ['ps%d' % tb[par], 'Hs'], [kn])
            cur = 1 - cur
            pump(1)
        if outp and not final:
            DMA('sp', yf_d[gt * 128:(gt + 1) * 128, :], ytile, [ykey], ['yf'])
        if outp and final:
            def final_gen(ytile=ytile, ykey=ykey, VT32=VT32, gT=gT, bon=bon, yfb=yfb, fb_=fb_, gt=gt):
                yk = 'yfb%d' % fb_
                DMA('sp', yfb, yf_d[gt * 128:(gt + 1) * 128, :], ['yf'], [yk])
                y = ytile
                y3 = y.rearrange("p (h d) -> p h d", h=8)
                TTo('dve', y, y, yfb, ALU.add, [ykey, yk], [ykey])
                P.add('dve', lambda e: e.tensor_reduce(out=s1, in_=y3, axis=AX.X, op=ALU.add), r=[ykey], w=['s1'])
                yield
                TS('dve', s1, s1, -1.0 / 64, None, ALU.mult, None, ['s1'], ['s1'])
                TTo('dve', y3, y3, s1.unsqueeze(2).broadcast_to([128, 8, 64]), ALU.add, [ykey, 's1'], [ykey])
                ACT(sqb, y, AF.Square, [ykey], ['sqb'])
                yield
                P.add('dve', lambda e: e.tensor_reduce(out=s2, in_=sqb.rearrange("p (h d) -> p h d", h=8),
                                                       axis=AX.X, op=ALU.add), r=['sqb'], w=['s2'])
                ACT(s2, s2, AF.Sqrt, ['s2', 'eps'], ['s2'], bias=gneps_t, scale=1.0 / 64)
                yield
                RECIP(s2, s2, ['s2'], ['s2'])
                TTo('dve', y3, y3, s2.unsqueeze(2).broadcast_to([128, 8, 64]), ALU.mult, [ykey, 's2'], [ykey])
                yield
                TTo('dve', y, y, rep[:, 1, :], ALU.mult, [ykey, 'rep'], [ykey])
                TTo('pool', tmpv.rearrange("p (h d) -> p h d", h=8), VT32.rearrange("p (h d) -> p h d", h=8),
                    bon.unsqueeze(2).broadcast_to([128, 8, 64]), ALU.mult, [('VT32', fb_), 'bon%d' % fb_], ['tmpv'])
                yield
                TTo('dve', y, y, rep[:, 2, :], ALU.add, [ykey, 'rep'], [ykey])
                TTo('dve', y, y, tmpv, ALU.add, [ykey, 'tmpv'], [ykey])
                yield
                TTo('dve', y, y, gT, ALU.mult, [ykey, 'gT%d' % fb_], [ykey])
                DMA('sp', yc_d[gt * 128:(gt + 1) * 128, 512:1024], y, [ykey], ['yc'])
                yield
            if bg['final'] is not None:
                for _ in bg['final']:
                    pass
            bg['final'] = final_gen()
            fcnt['n'] += 1
        return cur

    sched = [(g, 0, True, False) for g in range(NG_OWN)]
    if debug_stop != 'F':
        sched += [(g, 1, True, True) for g in range(NG_OWN - 1, -1, -1)]
    cur = 0
    MEMSET('pool', Tst[0], 0.0, ['T0'])
    for _ in prep_group(*sched[0], 0):
        pass
    for idx, (g, di, own, final) in enumerate(sched):
        pb = idx % 2
        pg = prep_group(*sched[idx + 1], (idx + 1) % 2) if idx + 1 < len(sched) else None
        st = {'alive': pg is not None}

        def pump(n, pg=pg, st=st):
            for _ in range(n):
                if st['alive']:
                    try:
                        next(pg)
                    except StopIteration:
                        st['alive'] = False
                if n < 100 and bg['final'] is not None:
                    try:
                        next(bg['final'])
                    except StopIteration:
                        bg['final'] = None
        if idx > 0 and sched[idx - 1][1] != di:
            kc_ = 'T%d' % cur
            tflat = Tst[cur].rearrange("p a b -> p (a b)")
            DMA('sp', E.tf_d, tflat, [kc_], ['tf_d'])
            E.CC(E.tf_d, E.tfg_d, ['tf_d'], ['tfg'])
            DMA('sp', tfr, E.tfg_d.rearrange("(r p) f -> p r f", r=2), ['tfg'], ['tfr'])
            TS('dve', tflat, tfr[:, 0, :], E.hsel[:, 0:1], None, ALU.mult, None, ['tfr', 'hsel'], [kc_])
            STT(tflat, tfr[:, 1, :], E.hsel[:, 1:2], tflat, ALU.mult, ALU.add, ['tfr', 'hsel', kc_], [kc_])
        tls = range(NTG) if di == 0 else range(NTG - 1, -1, -1)
        for tl in tls:
            cur = tile_proc(di, g, tl, own, final, cur, pb, pump)
        pump(10 ** 6)
    if bg['final'] is not None:
        for _ in bg['final']:
            pass


def _f(a):
    return np.ascontiguousarray(a, dtype=np.float32)


def _lhs_chunks(W):
    n = W.shape[1] // 128
    return _f(W.reshape(8, 128, n, 128).transpose(2, 1, 0, 3))


def _attn_masks():
    p = np.arange(128)[:, None]
    c = np.arange(128)[None, :]
    cntm = {}
    for j in range(-8, 9):
        d = 128 * j + p - c
        m = (np.abs(d) <= 64).astype(np.float32)
        m += ((d % 4 == 0) & (np.abs(d) <= 256)).astype(np.float32)
        m += ((d % 16 == 0) & (np.abs(d) <= 1024)).astype(np.float32)
        cntm[j] = m
    wm = np.zeros((20, 128, 512), np.float32)
    for J in range(20):
        for i in range(4):
            j = J - 8 - i
            if abs(j) <= 8:
                wm[J, :, i * 128:(i + 1) * 128] = cntm[j]
    return wm.astype(ml_dtypes.bfloat16)


def _rope_tables(rev):
    pos = np.arange(S, dtype=np.float32)
    if rev:
        pos = pos[::-1].copy()
    inv_freq = (np.float32(10000.0) ** (-np.arange(0, 64, 2, dtype=np.float32) / np.float32(64))).astype(np.float32)
    ang = (pos[:, None] * inv_freq[None, :]).astype(np.float32)
    cos, sin = np.cos(ang).astype(np.float32), np.sin(ang).astype(np.float32)
    idx = np.arange(128) % 32
    sign = np.where((np.arange(128) % 64) < 32, -1.0, 1.0).astype(np.float32)
    cosT = cos[:, idx].T
    sinT = (sin[:, idx] * sign[None, :]).T
    return _f(cosT), _f(sinT)


def host_inputs(inputs):
    g = lambda k: np.asarray(inputs[k][0], np.float32)
    x = np.asarray(inputs["x"], dtype=np.float32)
    shared = {}
    ffn_names = {1: ("ffn1_w_gate", "ffn1_w_up", "ffn1_w_down"), 2: ("ffn2_w_gate", "ffn2_w_up", "ffn2_w_down")}
    for k in (1, 2):
        wg, wu, wd = (g(nm) for nm in ffn_names[k])
        gg = wg.reshape(8, 128, NF, 128).transpose(2, 1, 0, 3)
        uu = wu.reshape(8, 128, NF, 128).transpose(2, 1, 0, 3)
        shared["wgu%d" % k] = _f(np.stack([gg, uu], axis=2))
        shared["wdc%d" % k] = _f(wd.reshape(NF, 128, 8, 128).transpose(2, 1, 0, 3))
    gnames = ["ffn1_pre_g", "ffn1_post_g", "mix_pre_g", "mix_post_g", "ffn2_pre_g", "ffn2_post_g"]
    shared["gv"] = _f(np.stack([g(nm).reshape(8, 128).T for nm in gnames], axis=1))
    shared["ident"] = np.eye(128, dtype=np.float32)
    w_in = g("w_in")
    swap = np.concatenate([(np.arange(64) + 32) % 64 + 64 * h for h in range(8)])
    shared["wv"] = _f(w_in[:, 1024:1536].reshape(8, 128, 512).transpose(1, 0, 2))
    shared["wo"] = _lhs_chunks(g("w_out"))
    shared["wm"] = _attn_masks()
    shared["wmf"] = np.ascontiguousarray(shared["wm"][:, ::-1, :])
    shared["rep"] = _f(np.stack([np.broadcast_to(g(nm)[None, :], (128, 512))
                                 for nm in ("attn_out_g", "rwkv_lnx_w", "rwkv_lnx_b")], axis=1))
    maps = []
    for c in range(8):
        b, h = c // 2, c % 2
        m = dict(shared)
        m["x"] = _f(x[b] if h == 0 else x[b, ::-1])
        cosT, sinT = _rope_tables(h == 1)
        m["cosT"], m["sinT"] = cosT, sinT
        m["hsel"] = _f(np.tile(np.array([[float(h), float(1 - h)]], np.float32), (128, 1)))
        m.update(_rwkv_host(inputs, h, w_in, swap))
        maps.append(m)
    return maps


def _rwkv_host(inputs, h, w_in, swap):
    g = lambda k: np.asarray(inputs[k][0], np.float32)
    dirs = [0, 1] if h == 0 else [1, 0]
    wq, wk = w_in[:, 0:512], w_in[:, 512:1024]
    cols = []
    for gi in range(4):
        cols.append(wq[:, gi * 128:(gi + 1) * 128])
        cols.append(wq[:, swap][:, gi * 128:(gi + 1) * 128])
    for gi in range(4):
        cols.append(wk[:, gi * 128:(gi + 1) * 128])
        cols.append(wk[:, swap][:, gi * 128:(gi + 1) * 128])
    wr = w_in[:, 1536:]
    rw = [wr[:, 0:1536]]
    for off in (1536, 1664):
        blk = wr[:, off:off + 128]
        rw.append(np.concatenate([blk[:, 64 * dirs[0]:64 * dirs[0] + 64], blk[:, 64 * dirs[1]:64 * dirs[1] + 64]], axis=1))
    rw.append(wr[:, 1792:1920])
    Wall = np.concatenate(cols + rw, axis=1)
    out = {"win": _lhs_chunks(Wall)}
    mp, mn = g("rwkv_mu_prev"), g("rwkv_mu_next")
    if h == 1:
        mp, mn = mn, mp

    def fix(v):
        v = v.copy()
        for off in (1536, 1664):
            blk = v[off:off + 128].copy()
            v[off:off + 128] = np.concatenate([blk[64 * dirs[0]:64 * dirs[0] + 64], blk[64 * dirs[1]:64 * dirs[1] + 64]])
        return v
    mp, mn = fix(mp), fix(mn)
    rwp = np.zeros((128, 64), np.float32)
    rwp[:, 0:15] = mp.reshape(15, 128).T
    rwp[:, 15:30] = mn.reshape(15, 128).T
    w0, a0 = g("rwkv_w0"), g("rwkv_a0")
    for di, d in enumerate(dirs):
        rwp[:, 30 + 4 * di:34 + 4 * di] = w0[d].reshape(4, 128).T
        rwp[:, 38 + 4 * di:42 + 4 * di] = a0[d].reshape(4, 128).T
    rwp[:, 46:50] = g("rwkv_k_k").reshape(4, 128).T
    rwp[:, 50:54] = g("rwkv_k_a").reshape(4, 128).T
    rwp[:, 54:58] = g("rwkv_r_k").reshape(4, 128).T
    out["rwpar"] = rwp
    w2, a2 = g("rwkv_w2"), g("rwkv_a2")
    lora = np.zeros((128, 3, 512), np.float32)
    lora[:, 0, :] = np.concatenate([w2[dirs[0]], w2[dirs[1]]], axis=0)
    lora[:, 1, :] = np.concatenate([a2[dirs[0]], a2[dirs[1]]], axis=0)
    lora[:, 2, :] = g("rwkv_g2")
    out["lora"] = lora
    idx = np.arange(128)
    same = (idx[:, None] // 64) == (idx[None, :] // 64)
    B0 = ((idx[None, :] < idx[:, None]) & same).astype(np.float32)
    I = np.eye(128, dtype=np.float32)
    rwm = np.zeros((128, 2, 5, 128), np.float32)
    for d, Bd in enumerate((B0, B0.T)):
        rwm[:, d, 0] = -Bd
        rwm[:, d, 1] = -Bd.T
        rwm[:, d, 2] = Bd.T
        rwm[:, d, 3] = Bd.T + I
        rwm[:, d, 4] = -(Bd.T + I)
    out["rwmask"] = rwm
    rwc = np.zeros((128, 1218), np.float32)
    rwc[:, 0:512] = (np.arange(512) % 64 != 0).astype(np.float32)[None, :]
    rwc[:, 1024:1152] = same.astype(np.float32)
    rwc[:, 1152:1216] = (idx[:, None] % 64 == np.arange(64)[None, :]).astype(np.float32)
    rwc[:, 1216] = (idx < 64)
    rwc[:, 1217] = (idx >= 64)
    out["rwconst"] = rwc
    return out


_CACHE = {}


def kernel(**inputs):
    if 'nc' not in _CACHE:
        _CACHE['nc'] = build()[0]
    nc = _CACHE['nc']
    maps = host_inputs(inputs)
    res = run_bass_kernel_spmd(nc, maps, core_ids=list(range(8)))
    out = np.zeros((4, S, D), np.float32)
    for c in range(8):
        b, h = c // 2, c % 2
        o = np.asarray(res.results[c]["out"])
        if h == 0:
            out[b, :OWN] = o
        else:
            out[b, OWN:] = o[::-1]
    return out
```
